# Optimizing a Trainium2 kernel written in Bass

```python
import math
import numpy as np
import jax
import jax.numpy as jnp
from jax import lax

D_MODEL = 1024
BATCH = 2
SEQ = 8192
DEPTH = 1

GRID_W = 64
CTX_LEN = 256
EPS = 1e-6
GDN_HEADS = 8
GDN_DK = 128
GDN_DV = 128
GDN_WIDTH = GDN_HEADS * GDN_DV
CONV_K = 5
CHUNK = 64
DIFF_HEADS = 8
DIFF_DQK = D_MODEL // DIFF_HEADS // 2
DIFF_DV = 2 * DIFF_DQK
DIFF_QK_WIDTH = 2 * DIFF_HEADS * DIFF_DQK
DIFF_V_WIDTH = DIFF_HEADS * DIFF_DV
Q_BLOCK = 128
ROPE_THETA = 10000.0
IN_SIZES = (3 * GDN_WIDTH, GDN_WIDTH, 2 * GDN_HEADS, 2 * GDN_HEADS,
            DIFF_QK_WIDTH, DIFF_QK_WIDTH, DIFF_V_WIDTH, DIFF_V_WIDTH, 2 * D_MODEL)
IN_COLS = 4 * GDN_WIDTH + 4 * GDN_HEADS + 2 * DIFF_QK_WIDTH + 2 * DIFF_V_WIDTH + 2 * D_MODEL

kernel_name = 'hybrid_gdn_diffattn_prefix_block'


def rms_norm(x, gain=None):
    xf = x.astype(jnp.float32)
    y = xf * lax.rsqrt(jnp.mean(xf * xf, axis=-1, keepdims=True) + EPS)
    if gain is not None:
        y = y * gain.astype(jnp.float32)
    return y.astype(x.dtype)


def l2_norm(x):
    return x * lax.rsqrt(jnp.sum(x * x, axis=-1, keepdims=True) + EPS)


def split_cols(cols):
    return jnp.split(cols, np.cumsum(IN_SIZES)[:-1].tolist(), axis=-1)


def centred_conv(x, w):
    pad = CONV_K // 2
    length = x.shape[1]
    xp = jnp.pad(x, ((0, 0), (pad, pad), (0, 0)))
    return sum(xp[:, j:j + length] * w[j] for j in range(CONV_K))


def axial_rope_tables(length):
    rows = length // GRID_W
    row = jnp.broadcast_to(jnp.arange(rows)[:, None], (rows, GRID_W)).reshape(-1).astype(jnp.float32)
    col = jnp.broadcast_to(jnp.arange(GRID_W)[None, :], (rows, GRID_W)).reshape(-1).astype(jnp.float32)
    n_freq = DIFF_DQK // 4
    inv_freq = ROPE_THETA ** (-jnp.arange(n_freq, dtype=jnp.float32) / n_freq)
    ang_r = row[:, None] * inv_freq
    ang_c = col[:, None] * inv_freq
    return (jnp.cos(ang_r), jnp.sin(ang_r), jnp.cos(ang_c), jnp.sin(ang_c))


def rotate(x, cos, sin):
    x1, x2 = jnp.split(x, 2, axis=-1)
    return jnp.concatenate([x1 * cos - x2 * sin, x2 * cos + x1 * sin], axis=-1)


def axial_rope(x, tables):
    cos_r, sin_r, cos_c, sin_c = [t[None, :, None, None, :].astype(x.dtype) for t in tables]
    xr, xc = jnp.split(x, 2, axis=-1)
    return jnp.concatenate([rotate(xr, cos_r, sin_r), rotate(xc, cos_c, sin_c)], axis=-1)


def gdn_prepare(qkv, a_raw, b_raw, conv_w, a_log, dt_bias):
    b, length, _ = qkv.shape
    qkv = jax.nn.silu(centred_conv(qkv, conv_w)).astype(jnp.float32)
    q, k, v = jnp.split(qkv, 3, axis=-1)
    q = l2_norm(q.reshape(b, length, GDN_HEADS, GDN_DK)) * (GDN_DK ** -0.5)
    k = l2_norm(k.reshape(b, length, GDN_HEADS, GDN_DK))
    v = v.reshape(b, length, GDN_HEADS, GDN_DV)
    g = -jnp.exp(a_log.astype(jnp.float32)) * jax.nn.softplus(
        a_raw.astype(jnp.float32).reshape(b, length, 2, GDN_HEADS) + dt_bias.astype(jnp.float32))
    beta = jax.nn.sigmoid(b_raw.astype(jnp.float32).reshape(b, length, 2, GDN_HEADS))
    return q, k, v, g, beta


def gdn_chunk_scan(q, k, v, g, beta, s0, with_out):
    b, length, h, _ = q.shape
    n = length // CHUNK

    def chunks(t):
        t = t.reshape((b, n, CHUNK, h) + t.shape[3:])
        return jnp.moveaxis(t, (1, 3), (0, 2))

    kc, vc, bc = chunks(k), chunks(v), chunks(beta)
    gc = jnp.cumsum(chunks(g), axis=-1)
    idx = jnp.arange(CHUNK)
    incl = idx[:, None] >= idx[None, :]
    strict = idx[:, None] > idx[None, :]
    decay = jnp.where(incl, jnp.exp(jnp.where(incl, gc[..., :, None] - gc[..., None, :], 0.0)), 0.0)
    a_mat = jnp.where(strict, jnp.einsum('nbhid,nbhjd->nbhij', kc * bc[..., None], kc) * decay, 0.0)
    eye = jnp.eye(CHUNK, dtype=a_mat.dtype)
    t_mat = lax.linalg.triangular_solve(eye + a_mat, jnp.broadcast_to(eye, a_mat.shape),
                                        left_side=True, lower=True)
    u = t_mat @ (vc * bc[..., None])
    w = t_mat @ (kc * (bc * jnp.exp(gc))[..., None])
    g_last = gc[..., -1]
    k_dec = kc * jnp.exp(g_last[..., None] - gc)[..., None]
    xs = (u, w, k_dec, g_last)
    if with_out:
        qc = chunks(q)
        qk = jnp.where(incl, jnp.einsum('nbhid,nbhjd->nbhij', qc, kc) * decay, 0.0)
        xs = xs + (qc * jnp.exp(gc)[..., None], qk)

    def step(s, xs_i):
        u_i, w_i, kd_i, gl_i = xs_i[:4]
        v_new = u_i - jnp.einsum('bhck,bhkv->bhcv', w_i, s)
        s_next = s * jnp.exp(gl_i)[..., None, None] + jnp.einsum('bhck,bhcv->bhkv', kd_i, v_new)
        if with_out:
            qd_i, qk_i = xs_i[4:]
            o = jnp.einsum('bhck,bhkv->bhcv', qd_i, s) + jnp.einsum('bhij,bhjv->bhiv', qk_i, v_new)
            return s_next, o
        return s_next, None

    s_final, o = lax.scan(step, s0, xs)
    if with_out:
        o = jnp.moveaxis(o, (0, 2), (1, 3)).reshape(b, length, h, v.shape[-1])
    return s_final, o


def flip_if(t, rev):
    return t[:, ::-1] if rev else t


def gdn_bidirectional(ctx_in, lat_in, with_ctx_out):
    q_c, k_c, v_c, g_c, beta_c = ctx_in
    q_l, k_l, v_l, g_l, beta_l = lat_in
    b = q_l.shape[0]
    o_lat = 0.0
    o_ctx = 0.0
    for d in range(2):
        rev = d == 1
        s0 = jnp.zeros((b, GDN_HEADS, GDN_DK, GDN_DV), jnp.float32)
        s_ctx, oc = gdn_chunk_scan(flip_if(q_c, rev), flip_if(k_c, rev), flip_if(v_c, rev),
                                   flip_if(g_c[:, :, d], rev), flip_if(beta_c[:, :, d], rev), s0, with_ctx_out)
        _, ol = gdn_chunk_scan(flip_if(q_l, rev), flip_if(k_l, rev), flip_if(v_l, rev),
                               flip_if(g_l[:, :, d], rev), flip_if(beta_l[:, :, d], rev), s_ctx, True)
        o_lat = o_lat + flip_if(ol, rev)
        if with_ctx_out:
            o_ctx = o_ctx + flip_if(oc, rev)
    return o_lat, (o_ctx if with_ctx_out else None)


def diff_heads(raw, gain, rope):
    b, length, _ = raw.shape
    t = rms_norm(raw.reshape(b, length, DIFF_HEADS, 2, DIFF_DQK), gain)
    if rope is not None:
        t = axial_rope(t, rope)
    return t


def diff_attend(q, k, v, lam):
    b, lq = q.shape[:2]
    nb = lq // Q_BLOCK
    qb = jnp.moveaxis(q.reshape((b, nb, Q_BLOCK) + q.shape[2:]), 1, 0)
    scale = DIFF_DQK ** -0.5

    def block(qi):
        s = jnp.einsum('bqhcd,bkhcd->bhcqk', qi, k).astype(jnp.float32) * scale
        p = jax.nn.softmax(s, axis=-1)
        attn = p[:, :, 0] - lam * p[:, :, 1]
        return jnp.einsum('bhqk,bkhv->bqhv', attn.astype(v.dtype), v)

    o = lax.map(block, qb)
    return jnp.moveaxis(o, 0, 1).reshape((b, lq) + v.shape[2:])


def mixer_output(o_a, z_a, o_b, z_b, gate_cols, gdn_norm_w, diff_norm_w, lambda_init, w_oa, w_ob, w_out):
    b, length = z_a.shape[:2]
    y_a = rms_norm(o_a.astype(z_a.dtype), gdn_norm_w) * jax.nn.silu(z_a.reshape(b, length, GDN_HEADS, GDN_DV))
    y_b = rms_norm(o_b, diff_norm_w) * (1.0 - lambda_init) * jax.nn.silu(z_b.reshape(b, length, DIFF_HEADS, DIFF_DV))
    g_a, g_b = jnp.split(jax.nn.sigmoid(gate_cols), 2, axis=-1)
    merged = g_a * (y_a.reshape(b, length, GDN_WIDTH) @ w_oa) + g_b * (y_b.reshape(b, length, DIFF_V_WIDTH) @ w_ob)
    return merged @ w_out


def hybrid_layer(x, ctx, c, c_ctx, layer, w_ada, b_ada, w_in, conv_w, a_log, dt_bias, gdn_norm_w,
                 q_norm_w, k_norm_w, lambda_q1, lambda_k1, lambda_q2, lambda_k2, diff_norm_w,
                 w_oa, w_ob, w_out, rope, update_ctx):
    b, length, _ = x.shape
    n_ctx = ctx.shape[1]
    shift, scale, gate = jnp.split(jax.nn.silu(c) @ w_ada + b_ada, 3, axis=-1)
    shift_c, scale_c, gate_c = jnp.split(jax.nn.silu(c_ctx) @ w_ada + b_ada, 3, axis=-1)
    h = rms_norm(x) * (1.0 + scale[:, None]) + shift[:, None]
    hc = rms_norm(ctx) * (1.0 + scale_c) + shift_c
    qkv_a, z_a, a_raw, b_raw, q_b, k_b, v_b, z_b, gates = split_cols(h @ w_in)
    qkv_ac, z_ac, a_rawc, b_rawc, q_bc, k_bc, v_bc, z_bc, gates_c = split_cols(hc @ w_in)

    o_a, o_ac = gdn_bidirectional(gdn_prepare(qkv_ac, a_rawc, b_rawc, conv_w, a_log, dt_bias),
                                  gdn_prepare(qkv_a, a_raw, b_raw, conv_w, a_log, dt_bias), update_ctx)

    lambda_init = 0.8 - 0.6 * math.exp(-0.3 * layer)
    lam = (jnp.exp(jnp.sum(lambda_q1.astype(jnp.float32) * lambda_k1.astype(jnp.float32)))
           - jnp.exp(jnp.sum(lambda_q2.astype(jnp.float32) * lambda_k2.astype(jnp.float32))) + lambda_init)
    q_l = diff_heads(q_b, q_norm_w, rope)
    k_l = diff_heads(k_b, k_norm_w, rope)
    k_c = diff_heads(k_bc, k_norm_w, None)
    v_l = v_b.reshape(b, length, DIFF_HEADS, DIFF_DV)
    v_c = v_bc.reshape(b, n_ctx, DIFF_HEADS, DIFF_DV)
    o_b = diff_attend(q_l, jnp.concatenate([k_c, k_l], axis=1), jnp.concatenate([v_c, v_l], axis=1), lam)

    x = x + gate[:, None] * mixer_output(o_a, z_a, o_b, z_b, gates, gdn_norm_w, diff_norm_w,
                                         lambda_init, w_oa, w_ob, w_out)
    if update_ctx:
        q_c = diff_heads(q_bc, q_norm_w, None)
        o_bc = diff_attend(q_c, k_c, v_c, lam)
        ctx = ctx + gate_c * mixer_output(o_ac, z_ac, o_bc, z_bc, gates_c, gdn_norm_w, diff_norm_w,
                                          lambda_init, w_oa, w_ob, w_out)
    return x, ctx


def setup_inputs(seed: int = 0) -> dict:
    key = jax.random.key(seed)
    ks = jax.random.split(key, 21)

    def nrm(k, shape, s):
        return jax.random.normal(k, shape, jnp.float32) * s

    dt = jnp.exp(jax.random.uniform(ks[8], (DEPTH, 2, GDN_HEADS), jnp.float32,
                                    minval=math.log(1e-3), maxval=math.log(1e-1)))
    return {
        'x': nrm(ks[0], (BATCH, SEQ, D_MODEL), 1.0),
        'c': nrm(ks[1], (BATCH, D_MODEL), 1.0),
        'ctx': nrm(ks[2], (BATCH, CTX_LEN, D_MODEL), 1.0),
        'c_ctx': nrm(ks[3], (D_MODEL,), 1.0),
        'w_ada': nrm(ks[4], (DEPTH, D_MODEL, 3 * D_MODEL), D_MODEL ** -0.5),
        'b_ada': nrm(ks[5], (DEPTH, 3 * D_MODEL), 0.01),
        'w_in': nrm(ks[6], (DEPTH, D_MODEL, IN_COLS), D_MODEL ** -0.5),
        'conv_w': nrm(ks[7], (DEPTH, CONV_K, 3 * GDN_WIDTH), CONV_K ** -0.5),
        'a_log': jnp.log(jax.random.uniform(ks[9], (DEPTH, 2, GDN_HEADS), jnp.float32, minval=1.0, maxval=16.0)),
        'dt_bias': dt + jnp.log(-jnp.expm1(-dt)),
        'gdn_norm_w': 1.0 + nrm(ks[10], (DEPTH, GDN_DV), 0.02),
        'q_norm_w': 1.0 + nrm(ks[11], (DEPTH, DIFF_DQK), 0.02),
        'k_norm_w': 1.0 + nrm(ks[12], (DEPTH, DIFF_DQK), 0.02),
        'lambda_q1': nrm(ks[13], (DEPTH, DIFF_DQK), 0.1),
        'lambda_k1': nrm(ks[14], (DEPTH, DIFF_DQK), 0.1),
        'lambda_q2': nrm(ks[15], (DEPTH, DIFF_DQK), 0.1),
        'lambda_k2': nrm(ks[16], (DEPTH, DIFF_DQK), 0.1),
        'diff_norm_w': 1.0 + nrm(ks[17], (DEPTH, DIFF_DV), 0.02),
        'w_oa': nrm(ks[18], (DEPTH, GDN_WIDTH, D_MODEL), GDN_WIDTH ** -0.5),
        'w_ob': nrm(ks[19], (DEPTH, DIFF_V_WIDTH, D_MODEL), DIFF_V_WIDTH ** -0.5),
        'w_out': nrm(ks[20], (DEPTH, D_MODEL, D_MODEL), D_MODEL ** -0.5),
    }


def reference(x, c, ctx, c_ctx, w_ada, b_ada, w_in, conv_w, a_log, dt_bias, gdn_norm_w,
              q_norm_w, k_norm_w, lambda_q1, lambda_k1, lambda_q2, lambda_k2, diff_norm_w,
              w_oa, w_ob, w_out):
    rope = axial_rope_tables(x.shape[1])
    for layer in range(DEPTH):
        x, ctx = hybrid_layer(x, ctx, c, c_ctx, layer, w_ada[layer], b_ada[layer], w_in[layer],
                              conv_w[layer], a_log[layer], dt_bias[layer], gdn_norm_w[layer],
                              q_norm_w[layer], k_norm_w[layer], lambda_q1[layer], lambda_k1[layer],
                              lambda_q2[layer], lambda_k2[layer], diff_norm_w[layer],
                              w_oa[layer], w_ob[layer], w_out[layer], rope, layer < DEPTH - 1)
    return x
```

```python
import math
from contextlib import ExitStack

import numpy as np
import concourse.bass as bass
import concourse.mybir as mybir
from concourse.bass_utils import run_bass_kernel_spmd

F32 = mybir.dt.float32
BF16 = mybir.dt.bfloat16
AF = mybir.ActivationFunctionType
ALU = mybir.AluOpType
AX = mybir.AxisListType

D = 1024
L = 8192
NCTX = 256
H = 8
EPS = 1e-6
NV = 100
NKV = 66
OWN0 = 50
S2 = 66
OWN1 = 84
IN_COLS = 10272
LAMBDA_INIT = 0.8 - 0.6 * math.exp(-0.3 * 0)
DEBUG = False
import os
MAXOPS = int(os.environ.get('KMAXOPS', '1000000000'))
TRACE_LO = int(os.environ.get('KTLO', '0'))
TRACE_HI = int(os.environ.get('KTHI', '-1'))


class Buf:
    __slots__ = ("w", "r", "dsem", "dcnt")

    def __init__(self):
        self.w = None
        self.r = {}
        self.dsem = None
        self.dcnt = 0


class Prog:
    def __init__(self, nc, es):
        self.nc = nc
        self.es = es
        self.names = ["pe", "act", "dve", "pool", "sp"]
        self.sem = {n: es.enter_context(nc.semaphore("s_" + n)) for n in self.names}
        self.cnt = {n: 0 for n in self.names}
        self.waited = {n: {} for n in self.names}
        self.q = {n: [] for n in self.names}
        self.nsem = 0
        self.dbufs = []
        self.gidx = 0
        self.maxops = MAXOPS

    def _waits(self, eng, reads, writes):
        need = {}

        def add(tok):
            if tok is None:
                return
            s, v = tok
            k = id(s)
            if k not in need or need[k][1] < v:
                need[k] = (s, v)

        for b in reads:
            add(b.w)
        for b in writes:
            add(b.w)
            for t in b.r.values():
                add(t)
        out = []
        own = id(self.sem[eng])
        for k, (s, v) in need.items():
            if eng == "pe" and k == own:
                continue
            if self.waited[eng].get(k, 0) >= v:
                continue
            self.waited[eng][k] = v
            out.append((s, v))
        return out

    def _commit(self, tok, reads, writes):
        for b in reads:
            b.r[id(tok[0])] = tok
        for b in writes:
            b.w = tok
            b.r = {}

    def op(self, eng, fn, reads=(), writes=()):
        self.gidx += 1
        if TRACE_LO <= self.gidx <= TRACE_HI:
            import inspect
            print("OP", self.gidx, eng, inspect.currentframe().f_back.f_lineno)
        if self.gidx > self.maxops:
            return
        waits = self._waits(eng, reads, writes)
        self.cnt[eng] += 1
        tok = (self.sem[eng], self.cnt[eng])
        self.q[eng].append((waits, fn, self.sem[eng], 1))
        self._commit(tok, reads, writes)

    def dma(self, out_ap, in_ap, reads, writes, dst):
        eng = "sp"
        self.gidx += 1
        if TRACE_LO <= self.gidx <= TRACE_HI:
            import inspect
            print("DMA", self.gidx, inspect.currentframe().f_back.f_lineno)
        if self.gidx > self.maxops:
            return (None, 0)
        waits = self._waits(eng, reads, writes)
        if dst.dsem is None:
            dst.dsem = self.es.enter_context(self.nc.semaphore("d%d" % self.nsem))
            self.nsem += 1
            self.dbufs.append(dst)
        dst.dcnt += 16
        tok = (dst.dsem, dst.dcnt)
        self.q[eng].append((waits, lambda e: e.dma_start(out=out_ap, in_=in_ap), dst.dsem, 16))
        self._commit(tok, reads, writes)
        return tok

    def barrier(self):
        toks = [(self.sem[n], self.cnt[n]) for n in self.names if self.cnt[n] > 0]
        toks += [(b.dsem, b.dcnt) for b in self.dbufs]
        for eng in self.names:
            waits = []
            for (s_, v) in toks:
                k = id(s_)
                if eng == "pe" and k == id(self.sem["pe"]):
                    continue
                if self.waited[eng].get(k, 0) >= v:
                    continue
                self.waited[eng][k] = v
                waits.append((s_, v))
            if waits:
                self.q[eng].append((waits, None, None, 0))

    def final_wait(self, eng, toks):
        toks = [t for t in toks if t[0] is not None]
        if not toks:
            toks = [(self.sem[n], self.cnt[n]) for n in self.names if self.cnt[n] > 0 and n != eng]
            toks += [(b.dsem, b.dcnt) for b in self.dbufs]
        self.q[eng].append((toks, None, None, 0))

    def emit(self):
        nc = self.nc
        with nc.Block() as block:
            def run(name):
                def body(e):
                    for waits, fn, sem, inc in self.q[name]:
                        for s, v in waits:
                            e.wait_ge(s, v)
                        if fn is not None:
                            fn(e).then_inc(sem, inc)
                return body
            block.tensor(run("pe"))
            block.scalar(run("act"))
            block.vector(run("dve"))
            block.gpsimd(run("pool"))
            block.sync(run("sp"))


def build_nc():
    nc = bass.Bass("TRN2", target_bir_lowering=False)

    def din(name, shape, dt=F32):
        return nc.dram_tensor(name, list(shape), dt, kind="ExternalInput").ap()

    XV = din("XV", [NV * 128, D])
    XH = din("XH", [NV * 4, D])
    HM = din("HM", [4, NV])
    GM = din("GM", [128, NV])
    ROPE = din("ROPE", [NKV * 128, 64])
    CT = din("CT", [128, 16])
    WADA = din("WADA", [D, 3 * D])
    BADA = din("BADA", [128, 3 * D])
    WIN = din("WIN", [D, IN_COLS])
    WAB = din("WAB", [2, D, 16])
    CW = din("CW", [2, 3072, 5])
    ALG = din("ALG", [2, 128, 8])
    DTB = din("DTB", [2, 128, 8])
    VECS = din("VECS", [128, 768])
    CST = din("CST", [128, 10 * 128 + 2 * 132])
    WOA = din("WOA", [D, D])
    WOB = din("WOB", [D, D])
    WOUT = din("WOUT", [D, D])
    OUT = nc.dram_tensor("OUT", [2048, D], F32, kind="ExternalOutput").ap()

    KTS = nc.dram_tensor("KTS", [H, 128, NKV * 128], BF16).ap()
    VPS = nc.dram_tensor("VPS", [H, 128, NKV, 129], BF16).ap()
    OS = nc.dram_tensor("OS", [2, 16, 128, D], F32).ap()
    ZS = nc.dram_tensor("ZS", [16, 128, 2048], BF16).ap()
    GS = nc.dram_tensor("GS", [16, 128, 2048], BF16).ap()

    es = ExitStack()
    with es:
        P = Prog(nc, es)

        def sb(name, shape, dt=F32, stack=es):
            return stack.enter_context(nc.sbuf_tensor(name, list(shape), dt))

        psum = [es.enter_context(nc.psum_tensor("ps%d" % i, [128, 512], F32)) for i in range(8)]
        PB = [Buf() for _ in range(8)]

        cst = sb("cst", [128, 10 * 128 + 2 * 132]); b_cst = Buf()
        ident = cst[:, 0:128]
        Jm = cst[:, 128:256]
        ones = cst[:, 256:384]
        Ltri = cst[:, 384:512]
        negMs = cst[:, 512:640]
        negMsT = cst[:, 640:768]
        MiT = cst[:, 768:896]
        BD32 = cst[:, 1160:1288]
        M64 = cst[:, 1288:1416]
        M128 = cst[:, 1416:1544]
        cbf = sb("cbf", [128, 128 + 2 * 132], BF16); b_cbf = Buf()
        identb = cbf[:, 0:128]
        SelM = cbf[:, 128:260]
        SelH = cbf[0:4, 260:392]
        vecs = sb("vecs", [128, 768]); b_vecs = Buf()
        gnw = vecs[:, 0:128]
        qnw = vecs[:, 128:192]
        knw = vecs[:, 192:256]
        dnw = vecs[:, 512:640]
        hm = sb("hm", [4, NV]); b_hm = Buf()
        gm = sb("gm", [128, NV]); b_gm = Buf()
        mod = sb("mod", [128, 5, D]); b_mod = Buf()
        negA = sb("negA", [128, 2, 8]); b_negA = Buf()
        dtb = sb("dtb", [128, 2, 8]); b_dtb = Buf()
        lamt = sb("lamt", [128, 4]); b_lam = Buf()
        epsc = sb("epsc", [128, 1]); b_eps = Buf()

        P.dma(cst[:], CST[:, :], [], [b_cst], b_cst)
        P.dma(vecs[:], VECS[:, :], [], [b_vecs], b_vecs)
        P.dma(hm[:], HM[:, :], [], [b_hm], b_hm)
        P.dma(gm[:], GM[:, :], [], [b_gm], b_gm)
        P.dma(negA[:], ALG.rearrange("s p h -> p s h"), [], [b_negA], b_negA)
        P.dma(dtb[:], DTB.rearrange("s p h -> p s h"), [], [b_dtb], b_dtb)
        P.op("act", lambda e: e.activation(out=negA[:], in_=negA[:], func=AF.Exp), [b_negA], [b_negA])
        P.op("dve", lambda e: e.tensor_scalar_mul(out=negA[:], in0=negA[:], scalar1=-1.0), [b_negA], [b_negA])
        P.op("dve", lambda e: e.tensor_copy(out=cbf[:, 0:128], in_=cst[:, 0:128]), [b_cst], [b_cbf])
        P.op("dve", lambda e: e.tensor_copy(out=cbf[:, 128:392], in_=cst[:, 896:1160]), [b_cst], [b_cbf])
        P.op("pool", lambda e: e.memset(epsc[:], EPS), [], [b_eps])

        ltmp = sb("ltmp", [128, 128]); b_ltmp = Buf()
        P.op("dve", lambda e: e.tensor_tensor(out=ltmp[:, 0:64], in0=vecs[:, 256:320], in1=vecs[:, 320:384], op=ALU.mult), [b_vecs], [b_ltmp])
        P.op("dve", lambda e: e.tensor_tensor(out=ltmp[:, 64:128], in0=vecs[:, 384:448], in1=vecs[:, 448:512], op=ALU.mult), [b_vecs], [b_ltmp])
        P.op("dve", lambda e: e.tensor_reduce(out=lamt[:, 2:4], in_=ltmp[:].rearrange("p (a b) -> p a b", a=2), axis=AX.X, op=ALU.add), [b_ltmp], [b_lam])
        P.op("act", lambda e: e.activation(out=lamt[:, 2:4], in_=lamt[:, 2:4], func=AF.Exp), [b_lam], [b_lam])
        P.op("dve", lambda e: e.tensor_tensor(out=lamt[:, 0:1], in0=lamt[:, 3:4], in1=lamt[:, 2:3], op=ALU.subtract), [b_lam], [b_lam])
        P.op("dve", lambda e: e.tensor_scalar_add(out=lamt[:, 0:1], in0=lamt[:, 0:1], scalar1=-LAMBDA_INIT), [b_lam], [b_lam])
        P.op("dve", lambda e: e.tensor_scalar_mul(out=ltmp[:], in0=vecs[:, 128:256], scalar1=-1.0), [b_vecs, b_lam], [b_ltmp])
        P.op("dve", lambda e: e.tensor_tensor(out=ltmp[:], in0=ltmp[:], in1=vecs[:, 128:256], op=ALU.max), [b_vecs, b_ltmp], [b_ltmp])
        P.op("dve", lambda e: e.tensor_reduce(out=lamt[:, 2:4], in_=ltmp[:].rearrange("p (a b) -> p a b", a=2), axis=AX.X, op=ALU.max), [b_ltmp, b_lam], [b_lam])
        P.op("dve", lambda e: e.scalar_tensor_tensor(out=lamt[:, 1:2], in0=lamt[:, 2:3], scalar=-8.0, in1=lamt[:, 3:4], op0=ALU.mult, op1=ALU.mult), [b_lam], [b_lam])

        with ExitStack() as s0:
            ct = sb("ct", [128, 16], stack=s0); b_ct = Buf()
            cbl = sb("cbl", [128, 16, 128], stack=s0); b_cbl = Buf()
            wst = [sb("wada%d" % i, [128, 3 * D], stack=s0) for i in range(2)]
            b_wst = [Buf(), Buf()]
            bada = sb("bada", [128, 3 * D], stack=s0); b_bada = Buf()
            P.dma(ct[:], CT[:, :], [], [b_ct], b_ct)
            P.dma(bada[:], BADA[:, :], [], [b_bada], b_bada)
            P.op("act", lambda e: e.activation(out=ct[:], in_=ct[:], func=AF.Silu), [b_ct], [b_ct])
            for j in range(16):
                P.op("dve", lambda e, j=j: e.tensor_scalar_mul(out=cbl[:, j, :], in0=ones, scalar1=ct[:, j:j + 1]), [b_ct, b_cst], [b_cbl])
            for k in range(8):
                P.dma(wst[k % 2][:], WADA[k * 128:(k + 1) * 128, :], [], [b_wst[k % 2]], b_wst[k % 2])
                for g in range(6):
                    P.op("pe", lambda e, k=k, g=g: e.matmul(psum[g][:, :], lhsT=cbl[:, k, :], rhs=wst[k % 2][:, g * 512:(g + 1) * 512], start=(k == 0), stop=(k == 7)),
                         [b_cbl, b_wst[k % 2]], [PB[g]])
            for g in range(6):
                dsti = [1, 1, 0, 0, 2, 2][g]
                addc = 1.0 if dsti == 0 else 0.0
                P.op("dve", lambda e, g=g, dsti=dsti: e.tensor_tensor(out=mod[:, dsti, (g % 2) * 512:(g % 2) * 512 + 512], in0=psum[g][:, :], in1=bada[:, g * 512:(g + 1) * 512], op=ALU.add),
                     [PB[g], b_bada], [b_mod])
            P.op("pool", lambda e: e.tensor_scalar_add(out=mod[:, 0, :], in0=mod[:, 0, :], scalar1=1.0), [b_mod], [b_mod])
            for k in range(8):
                P.dma(wst[k % 2][:], WADA[k * 128:(k + 1) * 128, :], [], [b_wst[k % 2]], b_wst[k % 2])
                for g in range(4):
                    P.op("pe", lambda e, k=k, g=g: e.matmul(psum[g][:, :], lhsT=cbl[:, 8 + k, :], rhs=wst[k % 2][:, g * 512:(g + 1) * 512], start=(k == 0), stop=(k == 7)),
                         [b_cbl, b_wst[k % 2]], [PB[g]])
            for g in range(4):
                dsti = [4, 4, 3, 3][g]
                P.op("dve", lambda e, g=g, dsti=dsti: e.tensor_tensor(out=mod[:, dsti, (g % 2) * 512:(g % 2) * 512 + 512], in0=psum[g][:, :], in1=bada[:, g * 512:(g + 1) * 512], op=ALU.add),
                     [PB[g], b_bada], [b_mod])
            P.op("pool", lambda e: e.tensor_scalar_add(out=mod[:, 3, :], in0=mod[:, 3, :], scalar1=1.0), [b_mod], [b_mod])

        P.barrier()
        xt = sb("xt", [128, D]); b_xt = Buf()
        xh = sb("xh", [4, D]); b_xh = Buf()
        junk = sb("junk", [128, D]); b_junk = Buf()
        st = sb("st", [128, 8]); b_st = Buf()
        hb = sb("hb", [128, D], BF16); b_hb = Buf()
        hhb = sb("hhb", [4, D], BF16); b_hhb = Buf()
        HT = sb("HT", [128, 8, 132], BF16); b_HT = Buf()

        def make_HT(v, is_ctx):
            mi = 3 if is_ctx else 0
            P.dma(xt[:], XV[v * 128:(v + 1) * 128, :], [], [b_xt], b_xt)
            P.dma(xh[:], XH[v * 4:(v + 1) * 4, :], [], [b_xh], b_xh)
            for (src, bsrc, np_, col, dst, bdst) in ((xt, b_xt, 128, 0, hb, b_hb), (xh, b_xh, 4, 2, hhb, b_hhb)):
                P.op("act", lambda e, src=src, np_=np_, col=col: e.activation(out=junk[0:np_, :], in_=src[0:np_, :], func=AF.Square, accum_out=st[0:np_, col:col + 1]),
                     [bsrc], [b_junk, b_st])
                P.op("act", lambda e, np_=np_, col=col: e.activation(out=st[0:np_, col + 1:col + 2], in_=st[0:np_, col:col + 1], func=AF.Ln, scale=1.0 / D, bias=epsc[0:np_, 0:1]),
                          [b_st, b_eps], [b_st])
                P.op("act", lambda e, np_=np_, col=col: e.activation(out=st[0:np_, col + 1:col + 2], in_=st[0:np_, col + 1:col + 2], func=AF.Exp, scale=-0.5),
                     [b_st], [b_st])
                if np_ == 4:
                    P.op("dve", lambda e, v=v: e.tensor_tensor(out=st[0:4, 3:4], in0=st[0:4, 3:4], in1=hm[:, v:v + 1], op=ALU.mult), [b_st, b_hm], [b_st])
                P.op("dve", lambda e, src=src, np_=np_, col=col: e.scalar_tensor_tensor(out=junk[0:np_, :], in0=src[0:np_, :], scalar=st[0:np_, col + 1:col + 2], in1=mod[0:np_, mi, :], op0=ALU.mult, op1=ALU.mult),
                     [bsrc, b_st, b_mod, b_junk], [b_junk])
                if np_ == 4:
                    P.op("dve", lambda e, v=v, dst=dst: e.scalar_tensor_tensor(out=dst[0:4, :], in0=mod[0:4, mi + 1, :], scalar=hm[:, v:v + 1], in1=junk[0:4, :], op0=ALU.mult, op1=ALU.add),
                         [b_junk, b_mod, b_hm], [bdst])
                else:
                    P.op("pool", lambda e, dst=dst: e.tensor_tensor(out=dst[:, :], in0=junk[:, :], in1=mod[:, mi + 1, :], op=ALU.add), [b_junk, b_mod], [bdst])
            for k in range(8):
                bank, slot = divmod(k, 3)
                o = psum[bank][:, slot * 132:(slot + 1) * 132]
                P.op("pe", lambda e, k=k, o=o: e.matmul(o, lhsT=hb[:, k * 128:(k + 1) * 128], rhs=SelM, start=True, stop=False), [b_hb, b_cbf], [PB[bank]])
                P.op("pe", lambda e, k=k, o=o: e.matmul(o, lhsT=hhb[0:4, k * 128:(k + 1) * 128], rhs=SelH, start=False, stop=True), [b_hhb, b_cbf], [PB[bank]])
            for bank in range(3):
                n = 3 if bank < 2 else 2
                P.op("act", lambda e, bank=bank, n=n: e.activation(out=HT[:, bank * 3:bank * 3 + n, :], in_=psum[bank][:, 0:n * 132].rearrange("p (a b) -> p a b", a=n), func=AF.Copy),
                     [PB[bank]], [b_HT])

        def load_w_bf16(dst, bdst, col0, ncols, stage, bstage, dcol0=0):
            i = 0
            for k in range(8):
                for c0 in range(0, ncols, 1024):
                    n = min(1024, ncols - c0)
                    s_, bs_ = stage[i % 2], bstage[i % 2]
                    P.dma(s_[:, 0:n], WIN[k * 128:(k + 1) * 128, col0 + c0:col0 + c0 + n], [], [bs_], bs_)
                    eng = ["act", "pool", "dve"][i % 3]
                    if eng == "act":
                        P.op("act", lambda e, s_=s_, k=k, c0=c0, n=n: e.activation(out=dst[:, k, dcol0 + c0:dcol0 + c0 + n], in_=s_[:, 0:n], func=AF.Copy), [bs_], [bdst])
                    else:
                        P.op(eng, lambda e, s_=s_, k=k, c0=c0, n=n: e.tensor_copy(out=dst[:, k, dcol0 + c0:dcol0 + c0 + n], in_=s_[:, 0:n]), [bs_], [bdst])
                    i += 1

        def qk_norm_rope(ps_banks, pbufs, gain, tab, b_tab, dst, b_dst, tmp, b_tmp):
            for hf in range(2):
                P.op("act", lambda e, hf=hf: e.activation(out=tmp[0][:, hf * 512:(hf + 1) * 512], in_=ps_banks[hf][:, :], func=AF.Square), [pbufs[hf]], [b_tmp[0]])
            P.op("dve", lambda e: e.tensor_reduce(out=tmp[3][:, 0:16], in_=tmp[0][:].rearrange("p (g d) -> p g d", d=64), axis=AX.X, op=ALU.add),
                 [b_tmp[0]], [b_tmp[3]])
            P.op("act", lambda e: e.activation(out=tmp[3][:, 0:16], in_=tmp[3][:, 0:16], func=AF.Ln, scale=1.0 / 64, bias=epsc[:, 0:1]), [b_tmp[3], b_eps], [b_tmp[3]])
            P.op("act", lambda e: e.activation(out=tmp[3][:, 0:16], in_=tmp[3][:, 0:16], func=AF.Exp, scale=-0.5), [b_tmp[3]], [b_tmp[3]])
            for hf in range(2):
                P.op("dve", lambda e, hf=hf: e.tensor_tensor(out=tmp[0][:, hf * 512:(hf + 1) * 512].rearrange("p (g d) -> p g d", d=64),
                                                            in0=ps_banks[hf][:, :].rearrange("p (g d) -> p g d", d=64),
                                                            in1=tmp[3][:, hf * 8:(hf + 1) * 8].unsqueeze(2).to_broadcast([128, 8, 64]), op=ALU.mult),
                     [pbufs[hf], b_tmp[3], b_tmp[0]], [b_tmp[0]])
            P.op("pool", lambda e: e.tensor_tensor(out=tmp[1][:].rearrange("p (g d) -> p g d", d=64), in0=tmp[0][:].rearrange("p (g d) -> p g d", d=64),
                                                   in1=gain.unsqueeze(1).to_broadcast([128, 16, 64]), op=ALU.mult), [b_tmp[0], b_vecs], [b_tmp[1]])
            xv = tmp[1][:].rearrange("p (g h t f) -> p g h t f", g=16, h=2, t=2)
            tv = tab[:].rearrange("p (h t f) -> p h t f", h=2, t=2)
            dv = dst[:].rearrange("p (g h t f) -> p g h t f", g=16, h=2, t=2)
            av = tmp[2][:].rearrange("p (g h t f) -> p g h t f", g=16, h=2, t=2)
            for hh in range(2):
                x1 = xv[:, :, hh, 0, :]
                x2 = xv[:, :, hh, 1, :]
                cs = tv[:, hh, 0, :].unsqueeze(1).to_broadcast([128, 16, 16])
                sn = tv[:, hh, 1, :].unsqueeze(1).to_broadcast([128, 16, 16])
                a1 = av[:, :, hh, 0, :]
                a2 = av[:, :, hh, 1, :]
                e1, e2 = ("dve", "pool")
                P.op(e1, lambda e, x1=x1, cs=cs, a1=a1: e.tensor_tensor(out=a1, in0=x1, in1=cs, op=ALU.mult), [b_tmp[1], b_tab], [b_tmp[2]])
                P.op(e2, lambda e, x2=x2, sn=sn, a2=a2: e.tensor_tensor(out=a2, in0=x2, in1=sn, op=ALU.mult), [b_tmp[1], b_tab], [b_tmp[2]])
                P.op(e1, lambda e, a1=a1, a2=a2, hh=hh: e.tensor_tensor(out=dv[:, :, hh, 0, :], in0=a1, in1=a2, op=ALU.subtract), [b_tmp[2]], [b_dst])
                P.op(e1, lambda e, x2=x2, cs=cs, a1=a1: e.tensor_tensor(out=a1, in0=x2, in1=cs, op=ALU.mult), [b_tmp[1], b_tab, b_tmp[2]], [b_tmp[2]])
                P.op(e2, lambda e, x1=x1, sn=sn, a2=a2: e.tensor_tensor(out=a2, in0=x1, in1=sn, op=ALU.mult), [b_tmp[1], b_tab, b_tmp[2]], [b_tmp[2]])
                P.op(e1, lambda e, a1=a1, a2=a2, hh=hh: e.tensor_tensor(out=dv[:, :, hh, 1, :], in0=a1, in1=a2, op=ALU.add), [b_tmp[2]], [b_dst])


        with ExitStack() as s1:
            W1 = sb("W1", [128, 8, 5120], BF16, stack=s1); b_W1 = Buf()
            wab = sb("wab", [128, 2, 8, 16], BF16, stack=s1); b_wab = Buf()
            wabs = sb("wabs", [128, 2, 8, 16], stack=s1); b_wabs = Buf()
            cw = sb("cw", [128, 2, 24, 5], stack=s1); b_cw = Buf()
            stage = [junk, xt]
            bstage = [b_junk, b_xt]
            tab = sb("tab1", [128, 64], stack=s1); b_tab = Buf()
            tmpA = [sb("tmpA1_%d" % i, [128, D], stack=s1) for i in range(3)] + [sb("tmpA31", [128, 16], stack=s1)]
            b_tmpA = [Buf() for _ in range(4)]
            krot = sb("krot1", [128, D], BF16, stack=s1); b_krot = Buf()
            load_w_bf16(W1, b_W1, 0, 3072, stage, bstage, 0)
            load_w_bf16(W1, b_W1, 5152, 2048, stage, bstage, 3072)
            P.dma(wabs[:], WAB.rearrange("s (k p) c -> p s k c", p=128), [], [b_wabs], b_wabs)
            P.op("dve", lambda e: e.tensor_copy(out=wab[:], in_=wabs[:]), [b_wabs], [b_wab])
            P.dma(cw[:], CW.rearrange("s (c p) j -> p s c j", p=128), [], [b_cw], b_cw)

            Sst = sb("Sst", [128, H, 128], stack=s1); b_S = [Buf() for _ in range(H)]
            pre = sb("pre", [128, 3, 132], stack=s1); b_pre = Buf()
            acc = [sb("acc%d" % i, [128, 128], stack=s1) for i in range(2)]; b_acc = [Buf(), Buf()]
            ptmp = sb("ptmp", [128, 128], stack=s1); b_ptmp = Buf()
            qkvT = sb("qkvT", [128, 24, 128], stack=s1); b_qkvT = [Buf() for _ in range(24)]
            gb = sb("gb", [128, 6, 8], stack=s1); b_gb = Buf()
            Wb = [sb("Wb%d" % i, [128, 512], stack=s1) for i in range(2)]; b_Wb = [Buf(), Buf()]
            NY = sb("NY", [128, 256], stack=s1); b_NY = Buf()
            NYm = sb("NYm", [128, 256], stack=s1); b_NYm = Buf()
            TU = sb("TU", [128, 256], stack=s1); b_TU = Buf()
            TU2 = sb("TU2", [128, 256], stack=s1); b_TU2 = Buf()
            Wm = sb("Wm", [128, 256], stack=s1); b_Wm = Buf()
            LgIb = sb("LgIb", [128, 256], stack=s1); b_LgIb = Buf()
            Em = sb("Em", [128, 128], stack=s1); b_Em = Buf()
            ETm = sb("ETm", [128, 128], stack=s1); b_ETm = Buf()
            EMs = sb("EMs", [128, 128], stack=s1); b_EMs = Buf()
            ETs = sb("ETs", [128, 128], stack=s1); b_ETs = Buf()
            ETi = sb("ETi", [128, 128], stack=s1); b_ETi = Buf()
            npos = sb("npos", [128, 256], stack=s1); b_npos = Buf()
            sq = sb("sq", [128, 128], stack=s1); b_sq = Buf()
            rs = sb("rs", [128, 128], stack=s1); b_rs = Buf()
            kTn = sb("kTn", [128, 128], stack=s1); b_kTn = Buf()
            qTn = sb("qTn", [128, 128], stack=s1); b_qTn = Buf()
            Kbp = sb("Kbp", [128, 128], stack=s1); b_Kbp = Buf()
            kd = sb("kd", [128, 128], stack=s1); b_kd = Buf()
            Vb = sb("Vb", [128, 128], stack=s1); b_Vb = Buf()
            TT = sb("TT", [128, 128], stack=s1); b_TT = Buf()
            wT = sb("wT", [128, 128], stack=s1); b_wT = Buf()
            usb = sb("usb", [128, 128], stack=s1); b_usb = Buf()
            vnew = sb("vnew", [128, 128], stack=s1); b_vnew = Buf()
            egl = sb("egl", [128, 1], stack=s1); b_egl = Buf()
            eR = sb("eR", [128, 128], stack=s1); b_eR = Buf()
            QdT = sb("QdT", [128, 128], stack=s1); b_QdT = Buf()
            qkT = sb("qkT", [128, 128], stack=s1); b_qkT = Buf()
            osb = sb("osb", [128, D], stack=s1); b_osb = Buf()
            KTsb = sb("KTsb", [128, H, 128], BF16, stack=s1); b_KTsb = Buf()
            Vp = sb("Vp", [128, H, 129], BF16, stack=s1); b_Vp = Buf()
            b_KTS = Buf(); b_VPS = Buf(); b_OS = Buf()
            P.op("pool", lambda e: e.memset(Vp[:], 1.0), [], [b_Vp])

            for v in range(NV):
                scan = 0 if v < S2 else 1
                is_ctx = v in (0, 1, S2, S2 + 1)
                own = (OWN0 <= v < S2) or (v >= OWN1)
                kv = v < NKV
                if v in (0, S2):
                    for h in range(H):
                        P.op("pool", lambda e, h=h: e.memset(Sst[:, h, :], 0.0), [], [b_S[h]])
                make_HT(v, is_ctx)
                PA = 3
                for k in range(8):
                    P.op("pe", lambda e, k=k, scan=scan: e.matmul(psum[PA][:, 0:16], lhsT=HT[:, k, 2:130], rhs=wab[:, scan, k, :], start=(k == 0), stop=(k == 7)), [b_HT, b_wab], [PB[PA]])
                P.op("dve", lambda e, scan=scan: e.tensor_tensor(out=gb[:, 4, :], in0=psum[PA][:, 0:8], in1=dtb[:, scan, :], op=ALU.add), [PB[PA], b_dtb], [b_gb])
                P.op("act", lambda e: e.activation(out=gb[:, 4, :], in_=gb[:, 4, :], func=AF.Exp), [b_gb], [b_gb])
                P.op("act", lambda e: e.activation(out=gb[:, 4, :], in_=gb[:, 4, :], func=AF.Ln, bias=1.0), [b_gb], [b_gb])
                P.op("dve", lambda e, v=v, scan=scan: e.scalar_tensor_tensor(out=gb[:, 0, :], in0=gb[:, 4, :], scalar=gm[:, v:v + 1], in1=negA[:, scan, :], op0=ALU.mult, op1=ALU.mult), [b_gb, b_gm, b_negA], [b_gb])
                P.op("act", lambda e: e.activation(out=gb[:, 5, :], in_=psum[PA][:, 8:16], func=AF.Sigmoid), [PB[PA], b_gb], [b_gb])
                P.op("dve", lambda e, v=v: e.tensor_scalar_mul(out=gb[:, 1, :], in0=gb[:, 5, :], scalar1=gm[:, v:v + 1]), [b_gb, b_gm], [b_gb])
                P.op("pe", lambda e: e.matmul(psum[PA][:, 16:24], lhsT=Ltri, rhs=gb[:, 0, :], start=True, stop=True), [b_cst, b_gb], [PB[PA]])
                P.op("dve", lambda e: e.tensor_copy(out=gb[:, 2, :], in_=psum[PA][:, 16:24]), [PB[PA], b_gb], [b_gb])
                P.op("act", lambda e: e.activation(out=gb[:, 3, :], in_=psum[PA][:, 16:24], func=AF.Exp), [PB[PA], b_gb], [b_gb])
                P.op("dve", lambda e: e.tensor_tensor(out=gb[:, 3, :], in0=gb[:, 3, :], in1=gb[:, 1, :], op=ALU.mult), [b_gb], [b_gb])

                cbs = list(range(24)) if own else list(range(8, 24))
                for gi in range(0, len(cbs), 3):
                    grp = cbs[gi:gi + 3]
                    bank = 4 + (gi // 3) % 2
                    for si, cb in enumerate(grp):
                        for k in range(8):
                            P.op("pe", lambda e, k=k, cb=cb, si=si, bank=bank: e.matmul(psum[bank][:, si * 132:(si + 1) * 132], lhsT=W1[:, k, cb * 128:(cb + 1) * 128], rhs=HT[:, k, :], start=(k == 0), stop=(k == 7)),
                                 [b_W1, b_HT], [PB[bank]])
                    n = len(grp)
                    P.op("act", lambda e, bank=bank, n=n: e.activation(out=pre[:, 0:n, :], in_=psum[bank][:, 0:n * 132].rearrange("p (a b) -> p a b", a=n), func=AF.Copy), [PB[bank]], [b_pre])
                    for si, cb in enumerate(grp):
                        eng = "dve" if (cb % 2 == 0) else "pool"
                        a_, ba_ = acc[cb % 2], b_acc[cb % 2]
                        P.op(eng, lambda e, si=si, cb=cb, a_=a_, scan=scan: e.tensor_scalar_mul(out=a_[:], in0=pre[:, si, 0:128], scalar1=cw[:, scan, cb, 0:1]), [b_pre, b_cw], [ba_])
                        for j in range(1, 5):
                            if eng == "dve":
                                P.op(eng, lambda e, si=si, cb=cb, a_=a_, j=j, scan=scan: e.scalar_tensor_tensor(out=a_[:], in0=pre[:, si, j:j + 128], scalar=cw[:, scan, cb, j:j + 1], in1=a_[:], op0=ALU.mult, op1=ALU.add),
                                     [b_pre, b_cw, ba_], [ba_])
                            else:
                                P.op(eng, lambda e, si=si, cb=cb, j=j, scan=scan: e.tensor_scalar_mul(out=ptmp[:], in0=pre[:, si, j:j + 128], scalar1=cw[:, scan, cb, j:j + 1]), [b_pre, b_cw], [b_ptmp])
                                P.op(eng, lambda e, a_=a_: e.tensor_tensor(out=a_[:], in0=a_[:], in1=ptmp[:], op=ALU.add), [b_ptmp, ba_], [ba_])
                        P.op("act", lambda e, cb=cb, a_=a_: e.activation(out=qkvT[:, cb, :], in_=a_[:], func=AF.Silu), [ba_], [b_qkvT[cb]])

                if kv:
                    P.dma(tab[:], ROPE[v * 128:(v + 1) * 128, :], [], [b_tab], b_tab)
                    for hf in range(2):
                        for k in range(8):
                            P.op("pe", lambda e, k=k, hf=hf: e.matmul(psum[6 + hf][:, :], lhsT=HT[:, k, 2:130], rhs=W1[:, k, 3072 + hf * 512:3072 + (hf + 1) * 512], start=(k == 0), stop=(k == 7)),
                                 [b_HT, b_W1], [PB[6 + hf]])
                    qk_norm_rope([psum[6], psum[7]], [PB[6], PB[7]], knw, tab, b_tab, krot, b_krot, tmpA, b_tmpA)
                    for h in range(H):
                        bank = 6 + h // 4
                        P.op("pe", lambda e, h=h, bank=bank: e.matmul(psum[bank][:, (h % 4) * 128:(h % 4 + 1) * 128], lhsT=krot[:, h * 128:(h + 1) * 128], rhs=identb, start=True, stop=True),
                             [b_krot, b_cbf], [PB[bank]])
                    for hf in range(2):
                        P.op("act", lambda e, hf=hf: e.activation(out=KTsb[:, hf * 4:(hf + 1) * 4, :], in_=psum[6 + hf][:, :].rearrange("p (a b) -> p a b", a=4), func=AF.Copy), [PB[6 + hf]], [b_KTsb])
                    P.dma(KTS[:, :, v * 128:(v + 1) * 128].rearrange("h p t -> p h t"), KTsb[:], [b_KTsb], [b_KTS], b_KTS)
                    for hf in range(2):
                        for k in range(8):
                            P.op("pe", lambda e, k=k, hf=hf: e.matmul(psum[6 + hf][:, :], lhsT=HT[:, k, 2:130], rhs=W1[:, k, 4096 + hf * 512:4096 + (hf + 1) * 512], start=(k == 0), stop=(k == 7)),
                                 [b_HT, b_W1], [PB[6 + hf]])
                    for hf in range(2):
                        P.op("act", lambda e, hf=hf: e.activation(out=Vp[:, hf * 4:(hf + 1) * 4, 0:128], in_=psum[6 + hf][:, :].rearrange("p (a b) -> p a b", a=4), func=AF.Copy), [PB[6 + hf]], [b_Vp])
                    P.dma(VPS[:, :, v, :].rearrange("h p c -> p h c"), Vp[:], [b_Vp], [b_VPS], b_VPS)

                for h in range(H):
                    qcb, kcb, vcb = h, 8 + h, 16 + h
                    PR, PG, PS_, PT, PU = 0, 1, 2, 3, 7
                    P.op("pool", lambda e, kcb=kcb: e.tensor_tensor(out=sq[:], in0=qkvT[:, kcb, :], in1=qkvT[:, kcb, :], op=ALU.mult), [b_qkvT[kcb]], [b_sq])
                    P.op("pe", lambda e: e.matmul(psum[PT][:, 0:128], lhsT=ones, rhs=sq[:], start=True, stop=True), [b_cst, b_sq], [PB[PT]])
                    P.op("act", lambda e: e.activation(out=rs[:], in_=psum[PT][:, 0:128], func=AF.Ln, bias=epsc[:, 0:1]), [PB[PT], b_eps], [b_rs])
                    P.op("act", lambda e: e.activation(out=rs[:], in_=rs[:], func=AF.Exp, scale=-0.5), [b_rs], [b_rs])
                    P.op("dve", lambda e, kcb=kcb: e.tensor_tensor(out=kTn[:], in0=qkvT[:, kcb, :], in1=rs[:], op=ALU.mult), [b_qkvT[kcb], b_rs], [b_kTn])
                    if own:
                        P.op("pool", lambda e, qcb=qcb: e.tensor_tensor(out=sq[:], in0=qkvT[:, qcb, :], in1=qkvT[:, qcb, :], op=ALU.mult), [b_qkvT[qcb]], [b_sq])
                        P.op("pe", lambda e: e.matmul(psum[PT][:, 128:256], lhsT=ones, rhs=sq[:], start=True, stop=True), [b_cst, b_sq], [PB[PT]])
                        P.op("act", lambda e: e.activation(out=rs[:], in_=psum[PT][:, 128:256], func=AF.Ln, bias=epsc[:, 0:1]), [PB[PT], b_eps], [b_rs])
                        P.op("act", lambda e: e.activation(out=rs[:], in_=rs[:], func=AF.Exp, scale=-0.5), [b_rs], [b_rs])
                        P.op("dve", lambda e, qcb=qcb: e.scalar_tensor_tensor(out=qTn[:], in0=qkvT[:, qcb, :], scalar=128.0 ** -0.5, in1=rs[:], op0=ALU.mult, op1=ALU.mult), [b_qkvT[qcb], b_rs], [b_qTn])
                    P.op("dve", lambda e, h=h: e.tensor_scalar_mul(out=LgIb[:, 0:128], in0=Ltri, scalar1=gb[:, 0, h:h + 1]), [b_cst, b_gb], [b_LgIb])
                    P.op("pool", lambda e, h=h: e.tensor_scalar_mul(out=LgIb[:, 128:256], in0=ident, scalar1=gb[:, 1, h:h + 1]), [b_cst, b_gb], [b_LgIb])
                    P.op("pe", lambda e: e.matmul(psum[PR][:, 0:256], lhsT=ones, rhs=LgIb[:], start=True, stop=True), [b_cst, b_LgIb], [PB[PR]])
                    P.op("dve", lambda e, h=h: e.tensor_scalar(out=npos[:, 0:128], in0=psum[PR][:, 0:128], scalar1=gb[:, 2, h:h + 1], scalar2=0.0, op0=ALU.subtract, op1=ALU.max), [PB[PR], b_gb], [b_npos])
                    P.op("dve", lambda e, h=h: e.tensor_scalar(out=npos[:, 128:256], in0=psum[PR][:, 0:128], scalar1=gb[:, 2, h:h + 1], scalar2=0.0, op0=ALU.subtract, op1=ALU.min), [PB[PR], b_gb], [b_npos])
                    P.op("act", lambda e: e.activation(out=Em[:], in_=npos[:, 0:128], func=AF.Exp, scale=-1.0), [b_npos], [b_Em])
                    P.op("act", lambda e: e.activation(out=ETm[:], in_=npos[:, 128:256], func=AF.Exp), [b_npos], [b_ETm])
                    P.op("pool", lambda e: e.tensor_tensor(out=EMs[:], in0=Em[:], in1=negMs, op=ALU.mult), [b_Em, b_cst], [b_EMs])
                    P.op("pool", lambda e: e.tensor_tensor(out=ETs[:], in0=ETm[:], in1=negMsT, op=ALU.mult), [b_ETm, b_cst], [b_ETs])
                    P.op("pe", lambda e: e.matmul(psum[PG][:, 0:128], lhsT=kTn[:], rhs=kTn[:], start=True, stop=True), [b_kTn], [PB[PG]])
                    if own:
                        P.op("pe", lambda e: e.matmul(psum[PG][:, 128:256], lhsT=kTn[:], rhs=qTn[:], start=True, stop=True), [b_kTn, b_qTn], [PB[PG]])
                    P.op("dve", lambda e, h=h: e.scalar_tensor_tensor(out=NY[:, 0:128], in0=psum[PG][:, 0:128], scalar=gb[:, 1, h:h + 1], in1=EMs[:], op0=ALU.mult, op1=ALU.mult), [PB[PG], b_gb, b_EMs], [b_NY])
                    P.op("dve", lambda e: e.tensor_tensor(out=NY[:, 128:256], in0=psum[PG][:, 0:128], in1=ETs[:], op=ALU.mult), [PB[PG], b_ETs], [b_NY])
                    P.op("dve", lambda e: e.tensor_tensor(out=NY[:, 128:256], in0=NY[:, 128:256], in1=psum[PR][:, 128:256], op=ALU.mult), [PB[PR], b_NY], [b_NY])
                    P.op("pool", lambda e: e.tensor_tensor(out=Wb[0][:, 0:128], in0=NY[:, 0:128], in1=BD32, op=ALU.mult), [b_NY, b_cst], [b_Wb[0]])
                    P.op("pool", lambda e: e.tensor_tensor(out=Wb[0][:, 256:384], in0=NY[:, 128:256], in1=BD32, op=ALU.mult), [b_NY, b_cst], [b_Wb[0]])
                    P.op("pool", lambda e: e.tensor_tensor(out=Wb[1][:, 128:256], in0=Wb[0][:, 0:128], in1=ident, op=ALU.add), [b_Wb[0], b_cst], [b_Wb[1]])
                    P.op("pool", lambda e: e.tensor_tensor(out=Wb[1][:, 384:512], in0=Wb[0][:, 256:384], in1=ident, op=ALU.add), [b_Wb[0], b_cst], [b_Wb[1]])
                    for n in range(4):
                        cur, nxt = Wb[n % 2], Wb[(n + 1) % 2]
                        bc, bn = b_Wb[n % 2], b_Wb[(n + 1) % 2]
                        w_ = 128 if n == 0 else 256
                        P.op("pe", lambda e, cur=cur, w_=w_: e.matmul(psum[PS_][:, 0:w_], lhsT=cur[:, 256:384], rhs=cur[:, 0:w_], start=True, stop=True), [bc], [PB[PS_]])
                        P.op("pe", lambda e, cur=cur, w_=w_: e.matmul(psum[PS_][:, 256:256 + w_], lhsT=cur[:, 0:128], rhs=cur[:, 256:256 + w_], start=True, stop=True), [bc], [PB[PS_]])
                        P.op("act", lambda e, nxt=nxt: e.activation(out=nxt[:].rearrange("p (a b) -> p a b", a=2)[:, :, 0:128], in_=psum[PS_][:, :].rearrange("p (a b) -> p a b", a=2)[:, :, 0:128], func=AF.Copy), [PB[PS_]], [bn])
                        if n > 0:
                            P.op("dve", lambda e, cur=cur, nxt=nxt: e.tensor_tensor(out=nxt[:].rearrange("p (a b) -> p a b", a=2)[:, :, 128:256], in0=psum[PS_][:, :].rearrange("p (a b) -> p a b", a=2)[:, :, 128:256],
                                                                                    in1=cur[:].rearrange("p (a b) -> p a b", a=2)[:, :, 128:256], op=ALU.add), [PB[PS_], bc], [bn])
                    P.op("pe", lambda e: e.matmul(psum[PS_][:, 0:128], lhsT=Wb[0][:, 256:384], rhs=Wb[0][:, 128:256], start=True, stop=True), [b_Wb[0]], [PB[PS_]])
                    P.op("pe", lambda e: e.matmul(psum[PS_][:, 128:256], lhsT=Wb[0][:, 0:128], rhs=Wb[0][:, 384:512], start=True, stop=True), [b_Wb[0]], [PB[PS_]])
                    P.op("dve", lambda e: e.tensor_tensor(out=TU[:].rearrange("p (a b) -> p a b", a=2), in0=psum[PS_][:, 0:256].rearrange("p (a b) -> p a b", a=2),
                                                          in1=Wb[0][:].rearrange("p (a b) -> p a b", a=2)[:, :, 128:256], op=ALU.add), [PB[PS_], b_Wb[0]], [b_TU])
                    P.op("pool", lambda e: e.tensor_tensor(out=NYm[:, 0:128], in0=NY[:, 128:256], in1=M64, op=ALU.mult), [b_NY, b_cst], [b_NYm])
                    P.op("pool", lambda e: e.tensor_tensor(out=NYm[:, 128:256], in0=NY[:, 0:128], in1=M64, op=ALU.mult), [b_NY, b_cst], [b_NYm])
                    P.op("pe", lambda e: e.matmul(psum[PS_][:, 256:384], lhsT=NYm[:, 0:128], rhs=TU[:, 0:128], start=True, stop=True), [b_NYm, b_TU], [PB[PS_]])
                    P.op("pe", lambda e: e.matmul(psum[PS_][:, 384:512], lhsT=NYm[:, 128:256], rhs=TU[:, 128:256], start=True, stop=True), [b_NYm, b_TU], [PB[PS_]])
                    P.op("act", lambda e: e.activation(out=Wm[:], in_=psum[PS_][:, 256:512], func=AF.Copy), [PB[PS_]], [b_Wm])
                    P.op("pe", lambda e: e.matmul(psum[PS_][:, 0:128], lhsT=TU[:, 128:256], rhs=Wm[:, 0:128], start=True, stop=True), [b_TU, b_Wm], [PB[PS_]])
                    P.op("pe", lambda e: e.matmul(psum[PS_][:, 128:256], lhsT=TU[:, 0:128], rhs=Wm[:, 128:256], start=True, stop=True), [b_TU, b_Wm], [PB[PS_]])
                    P.op("dve", lambda e: e.tensor_tensor(out=TU2[:], in0=psum[PS_][:, 0:256], in1=TU[:], op=ALU.add), [PB[PS_], b_TU], [b_TU2])
                    P.op("pool", lambda e: e.tensor_tensor(out=NYm[:, 128:256], in0=NY[:, 0:128], in1=M128, op=ALU.mult), [b_NY, b_cst, b_NYm], [b_NYm])
                    P.op("pe", lambda e: e.matmul(psum[PS_][:, 384:512], lhsT=NYm[:, 128:256], rhs=TU2[:, 128:256], start=True, stop=True), [b_NYm, b_TU2], [PB[PS_]])
                    P.op("act", lambda e: e.activation(out=Wm[:, 128:256], in_=psum[PS_][:, 384:512], func=AF.Copy), [PB[PS_]], [b_Wm])
                    P.op("pe", lambda e: e.matmul(psum[PS_][:, 128:256], lhsT=TU2[:, 0:128], rhs=Wm[:, 128:256], start=True, stop=True), [b_TU2, b_Wm], [PB[PS_]])
                    P.op("dve", lambda e: e.tensor_tensor(out=TT[:], in0=psum[PS_][:, 128:256], in1=TU2[:, 128:256], op=ALU.add), [PB[PS_], b_TU2], [b_TT])
                    P.op("pe", lambda e: e.matmul(psum[PT][:, 256:384], lhsT=kTn[:], rhs=ident, start=True, stop=True), [b_kTn, b_cst], [PB[PT]])
                    P.op("pe", lambda e, vcb=vcb: e.matmul(psum[PT][:, 384:512], lhsT=qkvT[:, vcb, :], rhs=ident, start=True, stop=True), [b_qkvT[vcb], b_cst], [PB[PT]])
                    P.op("dve", lambda e, h=h: e.tensor_scalar_mul(out=Kbp[:], in0=psum[PT][:, 256:384], scalar1=gb[:, 3, h:h + 1]), [PB[PT], b_gb], [b_Kbp])
                    P.op("dve", lambda e: e.tensor_scalar_mul(out=kd[:], in0=psum[PT][:, 256:384], scalar1=ETm[:, 127:128]), [PB[PT], b_ETm], [b_kd])
                    P.op("dve", lambda e, h=h: e.tensor_scalar_mul(out=Vb[:], in0=psum[PT][:, 384:512], scalar1=gb[:, 1, h:h + 1]), [PB[PT], b_gb], [b_Vb])
                    P.op("pe", lambda e: e.matmul(psum[PU][:, 0:128], lhsT=TT[:], rhs=Vb[:], start=True, stop=True), [b_TT, b_Vb], [PB[PU]])
                    P.op("pe", lambda e: e.matmul(psum[PU][:, 128:256], lhsT=Kbp[:], rhs=TT[:], start=True, stop=True), [b_TT, b_Kbp], [PB[PU]])
                    P.op("act", lambda e: e.activation(out=wT[:], in_=psum[PU][:, 128:256], func=AF.Copy), [PB[PU]], [b_wT])
                    P.op("act", lambda e: e.activation(out=usb[:], in_=psum[PU][:, 0:128], func=AF.Copy), [PB[PU]], [b_usb])
                    P.op("pe", lambda e, h=h: e.matmul(psum[PU][:, 256:384], lhsT=wT[:], rhs=Sst[:, h, :], start=True, stop=True), [b_wT, b_S[h]], [PB[PU]])
                    P.op("dve", lambda e: e.tensor_tensor(out=vnew[:], in0=usb[:], in1=psum[PU][:, 256:384], op=ALU.subtract), [b_usb, PB[PU]], [b_vnew])
                    if own:
                        P.op("act", lambda e: e.activation(out=eR[:], in_=psum[PR][:, 0:128], func=AF.Exp), [PB[PR]], [b_eR])
                        P.op("dve", lambda e: e.tensor_tensor(out=QdT[:], in0=qTn[:], in1=eR[:], op=ALU.mult), [b_qTn, b_eR], [b_QdT])
                        P.op("pool", lambda e: e.tensor_tensor(out=ETi[:], in0=ETm[:], in1=MiT, op=ALU.mult), [b_ETm, b_cst], [b_ETi])
                        P.op("dve", lambda e: e.tensor_tensor(out=qkT[:], in0=psum[PG][:, 128:256], in1=ETi[:], op=ALU.mult), [PB[PG], b_ETi], [b_qkT])
                        P.op("pe", lambda e, h=h: e.matmul(psum[PG][:, 256:384], lhsT=QdT[:], rhs=Sst[:, h, :], start=True, stop=False), [b_QdT, b_S[h]], [PB[PG]])
                        P.op("pe", lambda e: e.matmul(psum[PG][:, 256:384], lhsT=qkT[:], rhs=vnew[:], start=False, stop=True), [b_qkT, b_vnew], [PB[PG]])
                        P.op("act", lambda e, h=h: e.activation(out=osb[:, h * 128:(h + 1) * 128], in_=psum[PG][:, 256:384], func=AF.Copy), [PB[PG]], [b_osb])
                    P.op("act", lambda e: e.activation(out=egl[:], in_=psum[PR][:, 127:128], func=AF.Exp), [PB[PR]], [b_egl])
                    P.op("pe", lambda e: e.matmul(psum[PU][:, 384:512], lhsT=kd[:], rhs=vnew[:], start=True, stop=True), [b_kd, b_vnew], [PB[PU]])
                    P.op("dve", lambda e, h=h: e.scalar_tensor_tensor(out=Sst[:, h, :], in0=Sst[:, h, :], scalar=egl[:, 0:1], in1=psum[PU][:, 384:512], op0=ALU.mult, op1=ALU.add), [b_S[h], b_egl, PB[PU]], [b_S[h]])
                if own:
                    oi = (v - OWN0) if scan == 0 else (v - OWN1)
                    P.dma(OS[scan, oi, :, :], osb[:], [b_osb], [b_OS], b_OS)

        P.barrier()

        b_ZS = Buf(); b_GS = Buf()
        YbT = sb("YbT", [128, H, 2048], BF16); b_YbT = Buf()
        s_qt = ExitStack()
        QT = sb("QT", [128, H, 2048], BF16, stack=s_qt); b_QT = Buf()
        with ExitStack() as s2:
            W2 = sb("W2", [128, 8, 3072], BF16, stack=s2); b_W2 = Buf()
            stage = [junk, xt]
            bstage = [b_junk, b_xt]
            tabB = sb("tabB2", [128, 64], stack=s2); b_tabB = Buf()
            tmpB = [sb("tmpB2_%d" % i, [128, D], stack=s2) for i in range(3)] + [sb("tmpB32", [128, 16], stack=s2)]
            b_tmpB = [Buf() for _ in range(4)]
            krotB = sb("krotB2", [128, D], BF16, stack=s2); b_krotB = Buf()
            load_w_bf16(W2, b_W2, 3072, 1024, stage, bstage, 0)
            load_w_bf16(W2, b_W2, 4128, 1024, stage, bstage, 1024)
            load_w_bf16(W2, b_W2, 7200, 1024, stage, bstage, 2048)
            zsb = sb("zsb", [128, 2048], BF16, stack=s2); b_zsb = Buf()
            for i in range(16):
                v = OWN0 + i
                make_HT(v, False)
                P.dma(tabB[:], ROPE[v * 128:(v + 1) * 128, :], [], [b_tabB], b_tabB)
                for hf in range(2):
                    for k in range(8):
                        P.op("pe", lambda e, k=k, hf=hf: e.matmul(psum[6 + hf][:, :], lhsT=HT[:, k, 2:130], rhs=W2[:, k, 1024 + hf * 512:1024 + (hf + 1) * 512], start=(k == 0), stop=(k == 7)),
                             [b_HT, b_W2], [PB[6 + hf]])
                qk_norm_rope([psum[6], psum[7]], [PB[6], PB[7]], qnw, tabB, b_tabB, krotB, b_krotB, tmpB, b_tmpB)
                for h in range(H):
                    bank = 6 + h // 4
                    P.op("pe", lambda e, h=h, bank=bank: e.matmul(psum[bank][:, (h % 4) * 128:(h % 4 + 1) * 128], lhsT=krotB[:, h * 128:(h + 1) * 128], rhs=identb, start=True, stop=True),
                         [b_krotB, b_cbf], [PB[bank]])
                for hf in range(2):
                    P.op("act", lambda e, hf=hf, i=i: e.activation(out=QT[:, hf * 4:(hf + 1) * 4, i * 128:(i + 1) * 128], in_=psum[6 + hf][:, :].rearrange("p (a b) -> p a b", a=4), func=AF.Copy), [PB[6 + hf]], [b_QT])
                for (c0, dstt, bd, off, fn) in ((0, zsb, b_zsb, 0, AF.Silu), (2048, zsb, b_zsb, 1024, AF.Silu)):
                    for hf in range(2):
                        bank = 4 + hf
                        for k in range(8):
                            P.op("pe", lambda e, k=k, hf=hf, c0=c0, bank=bank: e.matmul(psum[bank][:, :], lhsT=HT[:, k, 2:130], rhs=W2[:, k, c0 + hf * 512:c0 + (hf + 1) * 512], start=(k == 0), stop=(k == 7)),
                                 [b_HT, b_W2], [PB[bank]])
                        P.op("act", lambda e, hf=hf, bank=bank, dstt=dstt, off=off, fn=fn: e.activation(out=dstt[:, off + hf * 512:off + (hf + 1) * 512], in_=psum[bank][:, :], func=fn), [PB[bank]], [bd])
                P.dma(ZS[i, :, :], zsb[:], [b_zsb], [b_ZS], b_ZS)

        P.barrier()
        with ExitStack() as s3:
            KTh = sb("KTh", [128, NKV * 128], BF16, stack=s3); b_KTh = Buf()
            Vph = sb("Vph", [128, NKV, 129], BF16, stack=s3); b_Vph = Buf()
            Eb = [sb("Eb%d" % i, [128, 1024], BF16, stack=s3) for i in range(2)]; b_Eb = [Buf(), Buf()]
            zb = sb("zb", [128, 4, 128], BF16, stack=s3); b_zb = Buf()
            fo = sb("fo", [128, 8, 128], stack=s3); b_fo = Buf()
            fs = sb("fs", [128, 8], stack=s3); b_fs = Buf()
            yb = sb("yb", [128, 128], BF16, stack=s3); b_yb = Buf()
            for h in range(H):
                P.dma(KTh[:], KTS[h, :, :], [b_KTS], [b_KTh], b_KTh)
                P.dma(Vph[:], VPS[h, :, :, :], [b_VPS], [b_Vph], b_Vph)
                for qt in range(4):
                    P.dma(zb[:], ZS[qt * 4:(qt + 1) * 4, :, 1024 + h * 128:1024 + (h + 1) * 128].rearrange("a p c -> p a c"), [b_ZS], [b_zb], b_zb)
                    for kb in range(NKV):
                        sbk = (kb % 2) * 2
                        for m in range(2):
                            P.op("pe", lambda e, m=m, kb=kb, qt=qt, h=h, sbk=sbk: e.matmul(psum[sbk + m][:, :], lhsT=KTh[m * 64:(m + 1) * 64, kb * 128:(kb + 1) * 128],
                                                                                   rhs=QT[m * 64:(m + 1) * 64, h, qt * 512:(qt + 1) * 512], start=True, stop=True),
                                 [b_KTh, b_QT], [PB[sbk + m]])
                        for m in range(2):
                            P.op("act", lambda e, m=m, kb=kb, sbk=sbk: e.activation(out=Eb[kb % 2][:, m * 512:(m + 1) * 512], in_=psum[sbk + m][:, :], func=AF.Exp, scale=0.125, bias=lamt[:, 1:2]),
                                 [PB[sbk + m], b_lam], [b_Eb[kb % 2]])
                        for qb in range(4):
                            for m in range(2):
                                P.op("pe", lambda e, m=m, kb=kb, qb=qb: e.matmul(psum[4 + qb][:, m * 129:(m + 1) * 129], lhsT=Eb[kb % 2][:, m * 512 + qb * 128:m * 512 + (qb + 1) * 128],
                                                                                 rhs=Vph[:, kb, :], start=(kb == 0 and m == 0), stop=(kb == NKV - 1 and m == 1)),
                                     [b_Eb[kb % 2], b_Vph], [PB[4 + qb]])
                    for qb in range(4):
                        blk = qt * 4 + qb
                        pso = psum[4 + qb]
                        P.op("dve", lambda e, pso=pso: e.tensor_copy(out=fs[:, 0:1], in_=pso[:, 128:129]), [PB[4 + qb]], [b_fs])
                        P.op("dve", lambda e, pso=pso: e.tensor_copy(out=fs[:, 1:2], in_=pso[:, 257:258]), [PB[4 + qb]], [b_fs])
                        P.op("dve", lambda e: e.reciprocal(out=fs[:, 2:4], in_=fs[:, 0:2]), [b_fs], [b_fs])
                        P.op("dve", lambda e: e.tensor_tensor(out=fs[:, 3:4], in0=fs[:, 3:4], in1=lamt[:, 0:1], op=ALU.mult), [b_fs, b_lam], [b_fs])
                        P.op("dve", lambda e, pso=pso: e.tensor_scalar_mul(out=fo[:, 0, :], in0=pso[:, 129:257], scalar1=fs[:, 3:4]), [PB[4 + qb], b_fs], [b_fo])
                        P.op("dve", lambda e, pso=pso: e.scalar_tensor_tensor(out=fo[:, 1, :], in0=pso[:, 0:128], scalar=fs[:, 2:3], in1=fo[:, 0, :], op0=ALU.mult, op1=ALU.add), [PB[4 + qb], b_fs, b_fo], [b_fo])
                        P.op("act", lambda e: e.activation(out=fo[:, 2, :], in_=fo[:, 1, :], func=AF.Square, accum_out=fs[:, 4:5]), [b_fo, b_fs], [b_fo, b_fs])
                        P.op("act", lambda e: e.activation(out=fs[:, 5:6], in_=fs[:, 4:5], func=AF.Ln, scale=1.0 / 128, bias=epsc[:, 0:1]), [b_fs, b_eps], [b_fs])
                        P.op("act", lambda e: e.activation(out=fs[:, 5:6], in_=fs[:, 5:6], func=AF.Exp, scale=-0.5), [b_fs], [b_fs])
                        P.op("dve", lambda e: e.scalar_tensor_tensor(out=fo[:, 3, :], in0=fo[:, 1, :], scalar=fs[:, 5:6], in1=dnw, op0=ALU.mult, op1=ALU.mult), [b_fo, b_fs, b_vecs], [b_fo])
                        P.op("dve", lambda e, qb=qb: e.scalar_tensor_tensor(out=yb[:], in0=fo[:, 3, :], scalar=(1.0 - LAMBDA_INIT), in1=zb[:, qb, :], op0=ALU.mult, op1=ALU.mult), [b_fo, b_zb], [b_yb])
                        P.op("pe", lambda e: e.matmul(psum[0][:, 0:128], lhsT=yb[:], rhs=identb, start=True, stop=True), [b_yb, b_cbf], [PB[0]])
                        P.op("act", lambda e, h=h, blk=blk: e.activation(out=YbT[:, h, blk * 128:(blk + 1) * 128], in_=psum[0][:, 0:128], func=AF.Copy), [PB[0]], [b_YbT])

        s_qt.close()
        P.barrier()
        with ExitStack() as s4:
            Wo = [sb("Wo%d" % i, [128, 8, D], BF16, stack=s4) for i in range(3)]; b_Wo = [Buf() for _ in range(3)]
            stage = [junk, xt]
            bstage = [b_junk, b_xt]
            YaT = sb("YaT", [128, H, 128], BF16, stack=s4); b_YaT = Buf()
            Wg = sb("Wg", [128, 8, 2048], BF16, stack=s4); b_Wg = Buf()
            load_w_bf16(Wg, b_Wg, 8224, 2048, stage, bstage, 0)
            ii = 0
            for wi, src in enumerate((WOA, WOB, WOUT)):
                for k in range(8):
                    s_, bs_ = stage[ii % 2], bstage[ii % 2]
                    P.dma(s_[:], src[k * 128:(k + 1) * 128, :], [], [bs_], bs_)
                    P.op(["dve", "pool"][ii % 2], lambda e, s_=s_, wi=wi, k=k: e.tensor_copy(out=Wo[wi][:, k, :], in_=s_[:]), [bs_], [b_Wo[wi]])
                    ii += 1
            o1 = sb("o1", [128, D], stack=s4); b_o1 = Buf()
            o2 = sb("o2", [128, D], stack=s4); b_o2 = Buf()
            zs4 = sb("zs4", [128, 1024], BF16, stack=s4); b_zs4 = Buf()
            gs4 = sb("gs4", [128, 2048], BF16, stack=s4); b_gs4 = Buf()
            ya = sb("ya", [128, D], BF16, stack=s4); b_ya = Buf()
            mg = sb("mg", [128, D], stack=s4); b_mg = Buf()
            mgb = sb("mgb", [128, D], BF16, stack=s4); b_mgb = Buf()
            mT = sb("mT", [128, 8, 128], BF16, stack=s4); b_mT = Buf()
            res = sb("res", [128, D], stack=s4); b_res = Buf()
            b_OUT = Buf()
            out_toks = []
            for i in range(16):
                P.dma(o1[:], OS[0, i, :, :], [b_OS], [b_o1], b_o1)
                P.dma(o2[:], OS[1, 15 - i, :, :], [b_OS], [b_o2], b_o2)
                P.dma(zs4[:], ZS[i, :, 0:1024], [b_ZS], [b_zs4], b_zs4)
                make_HT(OWN0 + i, False)
                for g4 in range(4):
                    bank = 6 + g4 % 2
                    for k in range(8):
                        P.op("pe", lambda e, k=k, g4=g4, bank=bank: e.matmul(psum[bank][:, :], lhsT=HT[:, k, 2:130], rhs=Wg[:, k, g4 * 512:(g4 + 1) * 512], start=(k == 0), stop=(k == 7)), [b_HT, b_Wg], [PB[bank]])
                    P.op("act", lambda e, g4=g4, bank=bank: e.activation(out=gs4[:, g4 * 512:(g4 + 1) * 512], in_=psum[bank][:, :], func=AF.Sigmoid), [PB[bank]], [b_gs4])
                for hf in range(2):
                    P.op("pe", lambda e, hf=hf: e.matmul(psum[hf][:, :], lhsT=Jm, rhs=o2[:, hf * 512:(hf + 1) * 512], start=True, stop=True), [b_cst, b_o2], [PB[hf]])
                    P.op("dve", lambda e, hf=hf: e.tensor_tensor(out=o1[:, hf * 512:(hf + 1) * 512], in0=o1[:, hf * 512:(hf + 1) * 512], in1=psum[hf][:, :], op=ALU.add), [PB[hf], b_o1], [b_o1])
                P.op("act", lambda e: e.activation(out=junk[:], in_=o1[:], func=AF.Square), [b_o1], [b_junk])
                P.op("dve", lambda e: e.tensor_reduce(out=st[:, 0:8], in_=junk[:].rearrange("p (g d) -> p g d", d=128), axis=AX.X, op=ALU.add), [b_junk], [b_st])
                P.op("act", lambda e: e.activation(out=st[:, 0:8], in_=st[:, 0:8], func=AF.Ln, scale=1.0 / 128, bias=epsc[:, 0:1]), [b_st, b_eps], [b_st])
                P.op("act", lambda e: e.activation(out=st[:, 0:8], in_=st[:, 0:8], func=AF.Exp, scale=-0.5), [b_st], [b_st])
                P.op("dve", lambda e: e.tensor_tensor(out=junk[:].rearrange("p (g d) -> p g d", d=128), in0=o1[:].rearrange("p (g d) -> p g d", d=128), in1=st[:, 0:8].unsqueeze(2).to_broadcast([128, 8, 128]), op=ALU.mult), [b_o1, b_st], [b_junk])
                P.op("pool", lambda e: e.tensor_tensor(out=junk[:].rearrange("p (g d) -> p g d", d=128), in0=junk[:].rearrange("p (g d) -> p g d", d=128), in1=gnw.unsqueeze(1).to_broadcast([128, 8, 128]), op=ALU.mult), [b_junk, b_vecs], [b_junk])
                P.op("dve", lambda e: e.tensor_tensor(out=ya[:], in0=junk[:], in1=zs4[:], op=ALU.mult), [b_junk, b_zs4], [b_ya])
                for h in range(H):
                    bank = h // 4
                    P.op("pe", lambda e, h=h, bank=bank: e.matmul(psum[bank][:, (h % 4) * 128:(h % 4 + 1) * 128], lhsT=ya[:, h * 128:(h + 1) * 128], rhs=identb, start=True, stop=True), [b_ya, b_cbf], [PB[bank]])
                for hf in range(2):
                    P.op("act", lambda e, hf=hf, i=i: e.activation(out=YaT[:, hf * 4:(hf + 1) * 4, :], in_=psum[hf][:, :].rearrange("p (a b) -> p a b", a=4), func=AF.Copy), [PB[hf]], [b_YaT])
                for (wi, Y, bY, pb0, c0_) in ((0, YaT, b_YaT, 2, 0), (1, YbT, b_YbT, 4, i * 128)):
                    for hf in range(2):
                        for k in range(8):
                            P.op("pe", lambda e, wi=wi, Y=Y, hf=hf, k=k, pb0=pb0, c0_=c0_: e.matmul(psum[pb0 + hf][:, :], lhsT=Y[:, k, c0_:c0_ + 128], rhs=Wo[wi][:, k, hf * 512:(hf + 1) * 512], start=(k == 0), stop=(k == 7)),
                                 [bY, b_Wo[wi]], [PB[pb0 + hf]])
                for hf in range(2):
                    P.op("dve", lambda e, hf=hf: e.tensor_tensor(out=mg[:, hf * 512:(hf + 1) * 512], in0=psum[2 + hf][:, :], in1=gs4[:, hf * 512:(hf + 1) * 512], op=ALU.mult), [PB[2 + hf], b_gs4], [b_mg])
                    P.op("dve", lambda e, hf=hf: e.tensor_tensor(out=junk[:, hf * 512:(hf + 1) * 512], in0=psum[4 + hf][:, :], in1=gs4[:, 1024 + hf * 512:1024 + (hf + 1) * 512], op=ALU.mult), [PB[4 + hf], b_gs4], [b_junk])
                P.op("pool", lambda e: e.tensor_tensor(out=mgb[:], in0=mg[:], in1=junk[:], op=ALU.add), [b_mg, b_junk], [b_mgb])
                for k in range(8):
                    bank = 6 + k // 4
                    P.op("pe", lambda e, k=k, bank=bank: e.matmul(psum[bank][:, (k % 4) * 128:(k % 4 + 1) * 128], lhsT=mgb[:, k * 128:(k + 1) * 128], rhs=identb, start=True, stop=True), [b_mgb, b_cbf], [PB[bank]])
                for hf in range(2):
                    P.op("act", lambda e, hf=hf: e.activation(out=mT[:, hf * 4:(hf + 1) * 4, :], in_=psum[6 + hf][:, :].rearrange("p (a b) -> p a b", a=4), func=AF.Copy), [PB[6 + hf]], [b_mT])
                for hf in range(2):
                    for k in range(8):
                        P.op("pe", lambda e, hf=hf, k=k: e.matmul(psum[hf][:, :], lhsT=mT[:, k, :], rhs=Wo[2][:, k, hf * 512:(hf + 1) * 512], start=(k == 0), stop=(k == 7)), [b_mT, b_Wo[2]], [PB[hf]])
                    P.op("dve", lambda e, hf=hf: e.tensor_tensor(out=res[:, hf * 512:(hf + 1) * 512], in0=psum[hf][:, :], in1=mod[:, 2, hf * 512:(hf + 1) * 512], op=ALU.mult), [PB[hf], b_mod], [b_res])
                P.op("pool", lambda e: e.tensor_tensor(out=res[:], in0=res[:], in1=xt[:], op=ALU.add), [b_res, b_xt], [b_res])
                tok = P.dma(OUT[i * 128:(i + 1) * 128, :], res[:], [b_res], [b_OUT], b_OUT)
            P.final_wait("sp", [tok])

        print('total ops', P.gidx, {n: len(P.q[n]) for n in P.names})
        P.emit()
    return nc


def _consts():
    i = np.arange(128)
    ident = np.eye(128, dtype=np.float32)
    J = ident[::-1].copy()
    ones = np.ones((128, 128), np.float32)
    Ltri = (i[:, None] <= i[None, :]).astype(np.float32)
    negMs = -(i[:, None] > i[None, :]).astype(np.float32)
    negMsT = -(i[None, :] > i[:, None]).astype(np.float32)
    MiT = (i[None, :] >= i[:, None]).astype(np.float32)
    SelM = np.zeros((128, 132), np.float32)
    SelM[i, i + 2] = 1
    SelH = np.zeros((128, 132), np.float32)
    SelH[0, 0] = 1
    SelH[1, 1] = 1
    SelH[2, 130] = 1
    SelH[3, 131] = 1
    blk = lambda b: ((i[:, None] // b) == (i[None, :] // b)).astype(np.float32)
    BD32 = blk(32)
    M64 = blk(64) - blk(32)
    M128 = 1.0 - blk(64)
    return np.concatenate([ident, J, ones, Ltri, negMs, negMsT, MiT, SelM, SelH, BD32, M64, M128], axis=1)


def _rope_table(pos):
    pos = np.asarray(pos)
    row = (pos // 64).astype(np.float32)
    col = (pos % 64).astype(np.float32)
    inv = (10000.0 ** (-np.arange(16, dtype=np.float32) / 16)).astype(np.float32)
    ar = row[:, None] * inv
    ac = col[:, None] * inv
    t = np.concatenate([np.cos(ar), np.sin(ar), np.cos(ac), np.sin(ac)], axis=1).astype(np.float32)
    t[pos < 0] = np.array([1.0] * 16 + [0.0] * 16 + [1.0] * 16 + [0.0] * 16, np.float32)
    return t


def _core_plan(r):
    fwd_first = r >= 2
    own = list(range(16 * r, 16 * r + 16))
    left = list(range(0, 16 * r))
    right = list(range(16 * r + 16, 64))

    def scan(forward, nslots, filler):
        rev = not forward
        vis = [("c", 0, rev, 1.0), ("c", 1, rev, 1.0)]
        if rev:
            vis = [("c", 1, rev, 1.0), ("c", 0, rev, 1.0)]
        pref = left if forward else right[::-1]
        fill = [("l", b, rev, 0.0) for b in filler]
        vis += fill[:nslots - len(pref)] + [("l", b, rev, 1.0) for b in pref]
        o = own if forward else own[::-1]
        vis += [("l", b, rev, 1.0) for b in o]
        return vis

    if fwd_first:
        filler = right[::-1]
        v1 = scan(True, 48, filler)
        v2 = scan(False, 16, [own[0]] * 16)
    else:
        filler = left
        v1 = scan(False, 48, filler)
        v2 = scan(True, 16, [own[0]] * 16)
    assert len(v1) == 66 and len(v2) == 34
    return v1 + v2, fwd_first


_NC_CACHE = {}


def kernel(x, c, ctx, c_ctx, w_ada, b_ada, w_in, conv_w, a_log, dt_bias, gdn_norm_w,
           q_norm_w, k_norm_w, lambda_q1, lambda_k1, lambda_q2, lambda_k2, diff_norm_w,
           w_oa, w_ob, w_out):
    f = lambda a: np.ascontiguousarray(np.asarray(a, dtype=np.float32))
    x, c, ctx, c_ctx = f(x), f(c), f(ctx), f(c_ctx)
    w_ada, b_ada, w_in, conv_w = f(w_ada)[0], f(b_ada)[0], f(w_in)[0], f(conv_w)[0]
    a_log, dt_bias = f(a_log)[0], f(dt_bias)[0]
    vec = lambda a: f(a)[0]
    if "nc" not in _NC_CACHE:
        _NC_CACHE["nc"] = build_nc()
    nc = _NC_CACHE["nc"]
    cst = _consts()
    vecs = np.zeros((768,), np.float32)
    vecs[0:128] = vec(gdn_norm_w)
    vecs[128:192] = vec(q_norm_w)
    vecs[192:256] = vec(k_norm_w)
    vecs[256:320] = vec(lambda_q1)
    vecs[320:384] = vec(lambda_k1)
    vecs[384:448] = vec(lambda_q2)
    vecs[448:512] = vec(lambda_k2)
    vecs[512:640] = vec(diff_norm_w)
    VECS = np.ascontiguousarray(np.broadcast_to(vecs, (128, 768)))
    BADA = np.ascontiguousarray(np.broadcast_to(b_ada, (128, 3 * D)))
    in_maps = []
    plans = []
    for core in range(8):
        b, r = divmod(core, 4)
        plan, fwd_first = _core_plan(r)
        plans.append((plan, fwd_first))
        XV = np.zeros((NV * 128, D), np.float32)
        XH = np.zeros((NV * 4, D), np.float32)
        HM = np.zeros((4, NV), np.float32)
        GM = np.zeros((128, NV), np.float32)
        ROPE = np.zeros((NKV * 128, 64), np.float32)
        for v, (seq, blk, rev, gmask) in enumerate(plan):
            src = ctx[b] if seq == "c" else x[b]
            n = src.shape[0]
            idx = np.arange(blk * 128, blk * 128 + 128)
            halo = np.array([blk * 128 - 2, blk * 128 - 1, blk * 128 + 128, blk * 128 + 129])
            if rev:
                idx = idx[::-1]
                halo = halo[::-1]
            XV[v * 128:(v + 1) * 128] = src[idx]
            valid = (halo >= 0) & (halo < n)
            XH[v * 4:(v + 1) * 4][valid] = src[halo[valid]]
            HM[:, v] = valid.astype(np.float32)
            GM[:, v] = gmask
            if v < NKV:
                pos = idx if seq == "l" else -np.ones(128, np.int64)
                ROPE[v * 128:(v + 1) * 128] = _rope_table(pos)
        d1 = 0 if fwd_first else 1
        dirs = (d1, 1 - d1)
        WAB = np.stack([np.concatenate([w_in[:, 4096 + d * 8:4096 + d * 8 + 8], w_in[:, 4112 + d * 8:4112 + d * 8 + 8]], axis=1) for d in dirs])
        cwt = np.ascontiguousarray(conv_w.T)
        CW = np.stack([cwt if d == 0 else np.ascontiguousarray(cwt[:, ::-1]) for d in dirs])
        ALG = np.stack([np.broadcast_to(a_log[d], (128, 8)) for d in dirs])
        DTB = np.stack([np.broadcast_to(dt_bias[d], (128, 8)) for d in dirs])
        cv = np.stack([c[b], c_ctx])
        CT = np.ascontiguousarray(cv.reshape(2, 8, 128).transpose(2, 0, 1).reshape(128, 16))
        in_maps.append({
            "XV": XV, "XH": XH, "HM": HM, "GM": GM, "ROPE": ROPE, "CT": CT, "WADA": w_ada, "BADA": BADA,
            "WIN": w_in, "WAB": np.ascontiguousarray(WAB), "CW": np.ascontiguousarray(CW),
            "ALG": np.ascontiguousarray(ALG), "DTB": np.ascontiguousarray(DTB), "VECS": VECS, "CST": cst,
            "WOA": f(w_oa)[0], "WOB": f(w_ob)[0], "WOUT": f(w_out)[0],
        })
    if _NC_CACHE.get("maps_only"):
        return in_maps, plans
    res = run_bass_kernel_spmd(nc, in_maps, core_ids=list(range(8)))
    out = np.zeros((2, L, D), np.float32)
    for core in range(8):
        b, r = divmod(core, 4)
        plan, fwd_first = plans[core]
        o = np.asarray(res.results[core]["OUT"])
        for i in range(16):
            seq, blk, rev, _ = plan[OWN0 + i]
            idx = np.arange(blk * 128, blk * 128 + 128)
            if rev:
                idx = idx[::-1]
            out[b, idx] = o[i * 128:(i + 1) * 128]
    return out
```

```python
import math
from contextlib import ExitStack

import numpy as np
import concourse.bass as bass
import concourse.mybir as mybir
from concourse.bass_utils import run_bass_kernel_spmd

F32 = mybir.dt.float32
BF16 = mybir.dt.bfloat16
AF = mybir.ActivationFunctionType
ALU = mybir.AluOpType
AX = mybir.AxisListType

D = 1024
L = 8192
NCTX = 256
H = 8
EPS = 1e-6
NV = 100
NKV = 66
OWN0 = 50
S2 = 66
OWN1 = 84
IN_COLS = 10272
LAMBDA_INIT = 0.8 - 0.6 * math.exp(-0.3 * 0)
DEBUG = False
import os
MAXOPS = int(os.environ.get('KMAXOPS', '1000000000'))
TRACE_LO = int(os.environ.get('KTLO', '0'))
TRACE_HI = int(os.environ.get('KTHI', '-1'))


class Buf:
    __slots__ = ("w", "r", "dsem", "dcnt")

    def __init__(self):
        self.w = None
        self.r = {}
        self.dsem = None
        self.dcnt = 0


class Prog:
    def __init__(self, nc, es):
        self.nc = nc
        self.es = es
        self.names = ["pe", "act", "dve", "pool", "sp"]
        self.sem = {n: es.enter_context(nc.semaphore("s_" + n)) for n in self.names}
        self.cnt = {n: 0 for n in self.names}
        self.waited = {n: {} for n in self.names}
        self.q = {n: [] for n in self.names}
        self.nsem = 0
        self.dbufs = []
        self.gidx = 0
        self.maxops = MAXOPS

    def _waits(self, eng, reads, writes):
        need = {}

        def add(tok):
            if tok is None:
                return
            s, v = tok
            k = id(s)
            if k not in need or need[k][1] < v:
                need[k] = (s, v)

        for b in reads:
            add(b.w)
        for b in writes:
            add(b.w)
            for t in b.r.values():
                add(t)
        out = []
        own = id(self.sem[eng])
        for k, (s, v) in need.items():
            if eng == "pe" and k == own:
                continue
            if self.waited[eng].get(k, 0) >= v:
                continue
            self.waited[eng][k] = v
            out.append((s, v))
        return out

    def _commit(self, tok, reads, writes):
        for b in reads:
            b.r[id(tok[0])] = tok
        for b in writes:
            b.w = tok
            b.r = {}

    def op(self, eng, fn, reads=(), writes=()):
        self.gidx += 1
        if TRACE_LO <= self.gidx <= TRACE_HI:
            import inspect
            print("OP", self.gidx, eng, inspect.currentframe().f_back.f_lineno)
        if self.gidx > self.maxops:
            return
        waits = self._waits(eng, reads, writes)
        self.cnt[eng] += 1
        tok = (self.sem[eng], self.cnt[eng])
        self.q[eng].append((waits, fn, self.sem[eng], 1))
        self._commit(tok, reads, writes)

    def dma(self, out_ap, in_ap, reads, writes, dst):
        eng = "sp"
        self.gidx += 1
        if TRACE_LO <= self.gidx <= TRACE_HI:
            import inspect
            print("DMA", self.gidx, inspect.currentframe().f_back.f_lineno)
        if self.gidx > self.maxops:
            return (None, 0)
        waits = self._waits(eng, reads, writes)
        if dst.dsem is None:
            dst.dsem = self.es.enter_context(self.nc.semaphore("d%d" % self.nsem))
            self.nsem += 1
            self.dbufs.append(dst)
        dst.dcnt += 16
        tok = (dst.dsem, dst.dcnt)
        self.q[eng].append((waits, lambda e: e.dma_start(out=out_ap, in_=in_ap), dst.dsem, 16))
        self._commit(tok, reads, writes)
        return tok

    def barrier(self):
        toks = [(self.sem[n], self.cnt[n]) for n in self.names if self.cnt[n] > 0]
        toks += [(b.dsem, b.dcnt) for b in self.dbufs]
        for eng in self.names:
            waits = []
            for (s_, v) in toks:
                k = id(s_)
                if eng == "pe" and k == id(self.sem["pe"]):
                    continue
                if self.waited[eng].get(k, 0) >= v:
                    continue
                self.waited[eng][k] = v
                waits.append((s_, v))
            if waits:
                self.q[eng].append((waits, None, None, 0))

    def final_wait(self, eng, toks):
        toks = [t for t in toks if t[0] is not None]
        if not toks:
            toks = [(self.sem[n], self.cnt[n]) for n in self.names if self.cnt[n] > 0 and n != eng]
            toks += [(b.dsem, b.dcnt) for b in self.dbufs]
        self.q[eng].append((toks, None, None, 0))

    def emit(self):
        nc = self.nc
        with nc.Block() as block:
            def run(name):
                def body(e):
                    for waits, fn, sem, inc in self.q[name]:
                        for s, v in waits:
                            e.wait_ge(s, v)
                        if fn is not None:
                            fn(e).then_inc(sem, inc)
                return body
            block.tensor(run("pe"))
            block.scalar(run("act"))
            block.vector(run("dve"))
            block.gpsimd(run("pool"))
            block.sync(run("sp"))


def build_nc():
    nc = bass.Bass("TRN2", target_bir_lowering=False)

    def din(name, shape, dt=F32):
        return nc.dram_tensor(name, list(shape), dt, kind="ExternalInput").ap()

    XV = din("XV", [NV * 128, D])
    XH = din("XH", [NV * 4, D])
    HM = din("HM", [4, NV])
    GM = din("GM", [128, NV])
    ROPE = din("ROPE", [NKV * 128, 64])
    CT = din("CT", [128, 16])
    WADA = din("WADA", [D, 3 * D])
    BADA = din("BADA", [128, 3 * D])
    WIN = din("WIN", [D, IN_COLS])
    WAB = din("WAB", [2, D, 16])
    CW = din("CW", [2, 3072, 5])
    ALG = din("ALG", [2, 128, 8])
    DTB = din("DTB", [2, 128, 8])
    VECS = din("VECS", [128, 768])
    CST = din("CST", [128, 10 * 128 + 2 * 132])
    WOA = din("WOA", [D, D])
    WOB = din("WOB", [D, D])
    WOUT = din("WOUT", [D, D])
    OUT = nc.dram_tensor("OUT", [2048, D], F32, kind="ExternalOutput").ap()

    KTS = nc.dram_tensor("KTS", [H, 128, NKV * 128], BF16).ap()
    VPS = nc.dram_tensor("VPS", [H, 128, NKV, 129], BF16).ap()
    OS = nc.dram_tensor("OS", [2, 16, 128, D], F32).ap()
    ZS = nc.dram_tensor("ZS", [16, 128, 2048], BF16).ap()
    GS = nc.dram_tensor("GS", [16, 128, 2048], BF16).ap()

    es = ExitStack()
    with es:
        P = Prog(nc, es)

        def sb(name, shape, dt=F32, stack=es):
            return stack.enter_context(nc.sbuf_tensor(name, list(shape), dt))

        psum = [es.enter_context(nc.psum_tensor("ps%d" % i, [128, 512], F32)) for i in range(8)]
        PB = [Buf() for _ in range(8)]

        cst = sb("cst", [128, 10 * 128 + 2 * 132]); b_cst = Buf()
        ident = cst[:, 0:128]
        Jm = cst[:, 128:256]
        ones = cst[:, 256:384]
        Ltri = cst[:, 384:512]
        negMs = cst[:, 512:640]
        negMsT = cst[:, 640:768]
        MiT = cst[:, 768:896]
        BD32 = cst[:, 1160:1288]
        M64 = cst[:, 1288:1416]
        M128 = cst[:, 1416:1544]
        cbf = sb("cbf", [128, 128 + 2 * 132], BF16); b_cbf = Buf()
        identb = cbf[:, 0:128]
        SelM = cbf[:, 128:260]
        SelH = cbf[0:4, 260:392]
        vecs = sb("vecs", [128, 768]); b_vecs = Buf()
        gnw = vecs[:, 0:128]
        qnw = vecs[:, 128:192]
        knw = vecs[:, 192:256]
        dnw = vecs[:, 512:640]
        hm = sb("hm", [4, NV]); b_hm = Buf()
        gm = sb("gm", [128, NV]); b_gm = Buf()
        mod = sb("mod", [128, 5, D]); b_mod = Buf()
        negA = sb("negA", [128, 2, 8]); b_negA = Buf()
        dtb = sb("dtb", [128, 2, 8]); b_dtb = Buf()
        lamt = sb("lamt", [128, 4]); b_lam = Buf()
        epsc = sb("epsc", [128, 1]); b_eps = Buf()

        P.dma(cst[:], CST[:, :], [], [b_cst], b_cst)
        P.dma(vecs[:], VECS[:, :], [], [b_vecs], b_vecs)
        P.dma(hm[:], HM[:, :], [], [b_hm], b_hm)
        P.dma(gm[:], GM[:, :], [], [b_gm], b_gm)
        P.dma(negA[:], ALG.rearrange("s p h -> p s h"), [], [b_negA], b_negA)
        P.dma(dtb[:], DTB.rearrange("s p h -> p s h"), [], [b_dtb], b_dtb)
        P.op("act", lambda e: e.activation(out=negA[:], in_=negA[:], func=AF.Exp), [b_negA], [b_negA])
        P.op("dve", lambda e: e.tensor_scalar_mul(out=negA[:], in0=negA[:], scalar1=-1.0), [b_negA], [b_negA])
        P.op("dve", lambda e: e.tensor_copy(out=cbf[:, 0:128], in_=cst[:, 0:128]), [b_cst], [b_cbf])
        P.op("dve", lambda e: e.tensor_copy(out=cbf[:, 128:392], in_=cst[:, 896:1160]), [b_cst], [b_cbf])
        P.op("pool", lambda e: e.memset(epsc[:], EPS), [], [b_eps])

        ltmp = sb("ltmp", [128, 128]); b_ltmp = Buf()
        P.op("dve", lambda e: e.tensor_tensor(out=ltmp[:, 0:64], in0=vecs[:, 256:320], in1=vecs[:, 320:384], op=ALU.mult), [b_vecs], [b_ltmp])
        P.op("dve", lambda e: e.tensor_tensor(out=ltmp[:, 64:128], in0=vecs[:, 384:448], in1=vecs[:, 448:512], op=ALU.mult), [b_vecs], [b_ltmp])
        P.op("dve", lambda e: e.tensor_reduce(out=lamt[:, 2:4], in_=ltmp[:].rearrange("p (a b) -> p a b", a=2), axis=AX.X, op=ALU.add), [b_ltmp], [b_lam])
        P.op("act", lambda e: e.activation(out=lamt[:, 2:4], in_=lamt[:, 2:4], func=AF.Exp), [b_lam], [b_lam])
        P.op("dve", lambda e: e.tensor_tensor(out=lamt[:, 0:1], in0=lamt[:, 3:4], in1=lamt[:, 2:3], op=ALU.subtract), [b_lam], [b_lam])
        P.op("dve", lambda e: e.tensor_scalar_add(out=lamt[:, 0:1], in0=lamt[:, 0:1], scalar1=-LAMBDA_INIT), [b_lam], [b_lam])
        P.op("dve", lambda e: e.tensor_scalar_mul(out=ltmp[:], in0=vecs[:, 128:256], scalar1=-1.0), [b_vecs, b_lam], [b_ltmp])
        P.op("dve", lambda e: e.tensor_tensor(out=ltmp[:], in0=ltmp[:], in1=vecs[:, 128:256], op=ALU.max), [b_vecs, b_ltmp], [b_ltmp])
        P.op("dve", lambda e: e.tensor_reduce(out=lamt[:, 2:4], in_=ltmp[:].rearrange("p (a b) -> p a b", a=2), axis=AX.X, op=ALU.max), [b_ltmp, b_lam], [b_lam])
        P.op("dve", lambda e: e.scalar_tensor_tensor(out=lamt[:, 1:2], in0=lamt[:, 2:3], scalar=-8.0, in1=lamt[:, 3:4], op0=ALU.mult, op1=ALU.mult), [b_lam], [b_lam])

        with ExitStack() as s0:
            ct = sb("ct", [128, 16], stack=s0); b_ct = Buf()
            cbl = sb("cbl", [128, 16, 128], stack=s0); b_cbl = Buf()
            wst = [sb("wada%d" % i, [128, 3 * D], stack=s0) for i in range(2)]
            b_wst = [Buf(), Buf()]
            bada = sb("bada", [128, 3 * D], stack=s0); b_bada = Buf()
            P.dma(ct[:], CT[:, :], [], [b_ct], b_ct)
            P.dma(bada[:], BADA[:, :], [], [b_bada], b_bada)
            P.op("act", lambda e: e.activation(out=ct[:], in_=ct[:], func=AF.Silu), [b_ct], [b_ct])
            for j in range(16):
                P.op("dve", lambda e, j=j: e.tensor_scalar_mul(out=cbl[:, j, :], in0=ones, scalar1=ct[:, j:j + 1]), [b_ct, b_cst], [b_cbl])
            for k in range(8):
                P.dma(wst[k % 2][:], WADA[k * 128:(k + 1) * 128, :], [], [b_wst[k % 2]], b_wst[k % 2])
                for g in range(6):
                    P.op("pe", lambda e, k=k, g=g: e.matmul(psum[g][:, :], lhsT=cbl[:, k, :], rhs=wst[k % 2][:, g * 512:(g + 1) * 512], start=(k == 0), stop=(k == 7)),
                         [b_cbl, b_wst[k % 2]], [PB[g]])
            for g in range(6):
                dsti = [1, 1, 0, 0, 2, 2][g]
                addc = 1.0 if dsti == 0 else 0.0
                P.op("dve", lambda e, g=g, dsti=dsti: e.tensor_tensor(out=mod[:, dsti, (g % 2) * 512:(g % 2) * 512 + 512], in0=psum[g][:, :], in1=bada[:, g * 512:(g + 1) * 512], op=ALU.add),
                     [PB[g], b_bada], [b_mod])
            P.op("pool", lambda e: e.tensor_scalar_add(out=mod[:, 0, :], in0=mod[:, 0, :], scalar1=1.0), [b_mod], [b_mod])
            for k in range(8):
                P.dma(wst[k % 2][:], WADA[k * 128:(k + 1) * 128, :], [], [b_wst[k % 2]], b_wst[k % 2])
                for g in range(4):
                    P.op("pe", lambda e, k=k, g=g: e.matmul(psum[g][:, :], lhsT=cbl[:, 8 + k, :], rhs=wst[k % 2][:, g * 512:(g + 1) * 512], start=(k == 0), stop=(k == 7)),
                         [b_cbl, b_wst[k % 2]], [PB[g]])
            for g in range(4):
                dsti = [4, 4, 3, 3][g]
                P.op("dve", lambda e, g=g, dsti=dsti: e.tensor_tensor(out=mod[:, dsti, (g % 2) * 512:(g % 2) * 512 + 512], in0=psum[g][:, :], in1=bada[:, g * 512:(g + 1) * 512], op=ALU.add),
                     [PB[g], b_bada], [b_mod])
            P.op("pool", lambda e: e.tensor_scalar_add(out=mod[:, 3, :], in0=mod[:, 3, :], scalar1=1.0), [b_mod], [b_mod])

        P.barrier()
        xt = sb("xt", [128, D]); b_xt = Buf()
        xh = sb("xh", [4, D]); b_xh = Buf()
        junk = sb("junk", [128, D]); b_junk = Buf()
        st = sb("st", [128, 8]); b_st = Buf()
        hb = sb("hb", [128, D], BF16); b_hb = Buf()
        hhb = sb("hhb", [4, D], BF16); b_hhb = Buf()
        HT = sb("HT", [128, 8, 132], BF16); b_HT = Buf()

        def make_HT(v, is_ctx):
            mi = 3 if is_ctx else 0
            P.dma(xt[:], XV[v * 128:(v + 1) * 128, :], [], [b_xt], b_xt)
            P.dma(xh[:], XH[v * 4:(v + 1) * 4, :], [], [b_xh], b_xh)
            for (src, bsrc, np_, col, dst, bdst) in ((xt, b_xt, 128, 0, hb, b_hb), (xh, b_xh, 4, 2, hhb, b_hhb)):
                P.op("act", lambda e, src=src, np_=np_, col=col: e.activation(out=junk[0:np_, :], in_=src[0:np_, :], func=AF.Square, accum_out=st[0:np_, col:col + 1]),
                     [bsrc], [b_junk, b_st])
                P.op("act", lambda e, np_=np_, col=col: e.activation(out=st[0:np_, col + 1:col + 2], in_=st[0:np_, col:col + 1], func=AF.Ln, scale=1.0 / D, bias=epsc[0:np_, 0:1]),
                          [b_st, b_eps], [b_st])
                P.op("act", lambda e, np_=np_, col=col: e.activation(out=st[0:np_, col + 1:col + 2], in_=st[0:np_, col + 1:col + 2], func=AF.Exp, scale=-0.5),
                     [b_st], [b_st])
                if np_ == 4:
                    P.op("dve", lambda e, v=v: e.tensor_tensor(out=st[0:4, 3:4], in0=st[0:4, 3:4], in1=hm[:, v:v + 1], op=ALU.mult), [b_st, b_hm], [b_st])
                P.op("dve", lambda e, src=src, np_=np_, col=col: e.scalar_tensor_tensor(out=junk[0:np_, :], in0=src[0:np_, :], scalar=st[0:np_, col + 1:col + 2], in1=mod[0:np_, mi, :], op0=ALU.mult, op1=ALU.mult),
                     [bsrc, b_st, b_mod, b_junk], [b_junk])
                if np_ == 4:
                    P.op("dve", lambda e, v=v, dst=dst: e.scalar_tensor_tensor(out=dst[0:4, :], in0=mod[0:4, mi + 1, :], scalar=hm[:, v:v + 1], in1=junk[0:4, :], op0=ALU.mult, op1=ALU.add),
                         [b_junk, b_mod, b_hm], [bdst])
                else:
                    P.op("pool", lambda e, dst=dst: e.tensor_tensor(out=dst[:, :], in0=junk[:, :], in1=mod[:, mi + 1, :], op=ALU.add), [b_junk, b_mod], [bdst])
            for k in range(8):
                bank, slot = divmod(k, 3)
                o = psum[bank][:, slot * 132:(slot + 1) * 132]
                P.op("pe", lambda e, k=k, o=o: e.matmul(o, lhsT=hb[:, k * 128:(k + 1) * 128], rhs=SelM, start=True, stop=False), [b_hb, b_cbf], [PB[bank]])
                P.op("pe", lambda e, k=k, o=o: e.matmul(o, lhsT=hhb[0:4, k * 128:(k + 1) * 128], rhs=SelH, start=False, stop=True), [b_hhb, b_cbf], [PB[bank]])
            for bank in range(3):
                n = 3 if bank < 2 else 2
                P.op("act", lambda e, bank=bank, n=n: e.activation(out=HT[:, bank * 3:bank * 3 + n, :], in_=psum[bank][:, 0:n * 132].rearrange("p (a b) -> p a b", a=n), func=AF.Copy),
                     [PB[bank]], [b_HT])

        def load_w_bf16(dst, bdst, col0, ncols, stage, bstage, dcol0=0):
            i = 0
            for k in range(8):
                for c0 in range(0, ncols, 1024):
                    n = min(1024, ncols - c0)
                    s_, bs_ = stage[i % 2], bstage[i % 2]
                    P.dma(s_[:, 0:n], WIN[k * 128:(k + 1) * 128, col0 + c0:col0 + c0 + n], [], [bs_], bs_)
                    eng = ["act", "pool", "dve"][i % 3]
                    if eng == "act":
                        P.op("act", lambda e, s_=s_, k=k, c0=c0, n=n: e.activation(out=dst[:, k, dcol0 + c0:dcol0 + c0 + n], in_=s_[:, 0:n], func=AF.Copy), [bs_], [bdst])
                    else:
                        P.op(eng, lambda e, s_=s_, k=k, c0=c0, n=n: e.tensor_copy(out=dst[:, k, dcol0 + c0:dcol0 + c0 + n], in_=s_[:, 0:n]), [bs_], [bdst])
                    i += 1

        def qk_norm_rope(ps_banks, pbufs, gain, tab, b_tab, dst, b_dst, tmp, b_tmp):
            for hf in range(2):
                P.op("act", lambda e, hf=hf: e.activation(out=tmp[0][:, hf * 512:(hf + 1) * 512], in_=ps_banks[hf][:, :], func=AF.Square), [pbufs[hf]], [b_tmp[0]])
            P.op("dve", lambda e: e.tensor_reduce(out=tmp[3][:, 0:16], in_=tmp[0][:].rearrange("p (g d) -> p g d", d=64), axis=AX.X, op=ALU.add),
                 [b_tmp[0]], [b_tmp[3]])
            P.op("act", lambda e: e.activation(out=tmp[3][:, 0:16], in_=tmp[3][:, 0:16], func=AF.Ln, scale=1.0 / 64, bias=epsc[:, 0:1]), [b_tmp[3], b_eps], [b_tmp[3]])
            P.op("act", lambda e: e.activation(out=tmp[3][:, 0:16], in_=tmp[3][:, 0:16], func=AF.Exp, scale=-0.5), [b_tmp[3]], [b_tmp[3]])
            for hf in range(2):
                P.op("dve", lambda e, hf=hf: e.tensor_tensor(out=tmp[0][:, hf * 512:(hf + 1) * 512].rearrange("p (g d) -> p g d", d=64),
                                                            in0=ps_banks[hf][:, :].rearrange("p (g d) -> p g d", d=64),
                                                            in1=tmp[3][:, hf * 8:(hf + 1) * 8].unsqueeze(2).to_broadcast([128, 8, 64]), op=ALU.mult),
                     [pbufs[hf], b_tmp[3], b_tmp[0]], [b_tmp[0]])
            P.op("pool", lambda e: e.tensor_tensor(out=tmp[1][:].rearrange("p (g d) -> p g d", d=64), in0=tmp[0][:].rearrange("p (g d) -> p g d", d=64),
                                                   in1=gain.unsqueeze(1).to_broadcast([128, 16, 64]), op=ALU.mult), [b_tmp[0], b_vecs], [b_tmp[1]])
            xv = tmp[1][:].rearrange("p (g h t f) -> p g h t f", g=16, h=2, t=2)
            tv = tab[:].rearrange("p (h t f) -> p h t f", h=2, t=2)
            dv = dst[:].rearrange("p (g h t f) -> p g h t f", g=16, h=2, t=2)
            av = tmp[2][:].rearrange("p (g h t f) -> p g h t f", g=16, h=2, t=2)
            for hh in range(2):
                x1 = xv[:, :, hh, 0, :]
                x2 = xv[:, :, hh, 1, :]
                cs = tv[:, hh, 0, :].unsqueeze(1).to_broadcast([128, 16, 16])
                sn = tv[:, hh, 1, :].unsqueeze(1).to_broadcast([128, 16, 16])
                a1 = av[:, :, hh, 0, :]
                a2 = av[:, :, hh, 1, :]
                e1, e2 = ("dve", "pool")
                P.op(e1, lambda e, x1=x1, cs=cs, a1=a1: e.tensor_tensor(out=a1, in0=x1, in1=cs, op=ALU.mult), [b_tmp[1], b_tab], [b_tmp[2]])
                P.op(e2, lambda e, x2=x2, sn=sn, a2=a2: e.tensor_tensor(out=a2, in0=x2, in1=sn, op=ALU.mult), [b_tmp[1], b_tab], [b_tmp[2]])
                P.op(e1, lambda e, a1=a1, a2=a2, hh=hh: e.tensor_tensor(out=dv[:, :, hh, 0, :], in0=a1, in1=a2, op=ALU.subtract), [b_tmp[2]], [b_dst])
                P.op(e1, lambda e, x2=x2, cs=cs, a1=a1: e.tensor_tensor(out=a1, in0=x2, in1=cs, op=ALU.mult), [b_tmp[1], b_tab, b_tmp[2]], [b_tmp[2]])
                P.op(e2, lambda e, x1=x1, sn=sn, a2=a2: e.tensor_tensor(out=a2, in0=x1, in1=sn, op=ALU.mult), [b_tmp[1], b_tab, b_tmp[2]], [b_tmp[2]])
                P.op(e1, lambda e, a1=a1, a2=a2, hh=hh: e.tensor_tensor(out=dv[:, :, hh, 1, :], in0=a1, in1=a2, op=ALU.add), [b_tmp[2]], [b_dst])


        with ExitStack() as s1:
            W1 = sb("W1", [128, 8, 5120], BF16, stack=s1); b_W1 = Buf()
            wab = sb("wab", [128, 2, 8, 16], BF16, stack=s1); b_wab = Buf()
            wabs = sb("wabs", [128, 2, 8, 16], stack=s1); b_wabs = Buf()
            cw = sb("cw", [128, 2, 24, 5], stack=s1); b_cw = Buf()
            stage = [junk, xt]
            bstage = [b_junk, b_xt]
            tab = sb("tab1", [128, 64], stack=s1); b_tab = Buf()
            tmpA = [sb("tmpA1_%d" % i, [128, D], stack=s1) for i in range(3)] + [sb("tmpA31", [128, 16], stack=s1)]
            b_tmpA = [Buf() for _ in range(4)]
            krot = sb("krot1", [128, D], BF16, stack=s1); b_krot = Buf()
            load_w_bf16(W1, b_W1, 0, 3072, stage, bstage, 0)
            load_w_bf16(W1, b_W1, 5152, 2048, stage, bstage, 3072)
            P.dma(wabs[:], WAB.rearrange("s (k p) c -> p s k c", p=128), [], [b_wabs], b_wabs)
            P.op("dve", lambda e: e.tensor_copy(out=wab[:], in_=wabs[:]), [b_wabs], [b_wab])
            P.dma(cw[:], CW.rearrange("s (c p) j -> p s c j", p=128), [], [b_cw], b_cw)

            Sst = sb("Sst", [128, H, 128], stack=s1); b_S = [Buf() for _ in range(H)]
            pre = sb("pre", [128, 3, 132], stack=s1); b_pre = Buf()
            acc = [sb("acc%d" % i, [128, 128], stack=s1) for i in range(2)]; b_acc = [Buf(), Buf()]
            ptmp = sb("ptmp", [128, 128], stack=s1); b_ptmp = Buf()
            qkvT = sb("qkvT", [128, 24, 128], stack=s1); b_qkvT = [Buf() for _ in range(24)]
            gb = sb("gb", [128, 6, 8], stack=s1); b_gb = Buf()
            from types import SimpleNamespace
            TS = []
            for si_ in range(2):
                T = SimpleNamespace()
                T.Wb = [sb("Wb%d_%d" % (i, si_), [128, 512], stack=s1) for i in range(2)]; T.b_Wb = [Buf(), Buf()]
                T.NY = sb("NY_%d" % si_, [128, 256], stack=s1); T.b_NY = Buf()
                T.NYm = sb("NYm_%d" % si_, [128, 256], stack=s1); T.b_NYm = Buf()
                T.TU = sb("TU_%d" % si_, [128, 256], stack=s1); T.b_TU = Buf()
                T.TU2 = sb("TU2_%d" % si_, [128, 256], stack=s1); T.b_TU2 = Buf()
                T.Wm = sb("Wm_%d" % si_, [128, 256], stack=s1); T.b_Wm = Buf()
                T.LgIb = sb("LgIb_%d" % si_, [128, 256], stack=s1); T.b_LgIb = Buf()
                T.Em = sb("Em_%d" % si_, [128, 128], stack=s1); T.b_Em = Buf()
                T.ETm = sb("ETm_%d" % si_, [128, 128], stack=s1); T.b_ETm = Buf()
                T.EMs = sb("EMs_%d" % si_, [128, 128], stack=s1); T.b_EMs = Buf()
                T.ETs = sb("ETs_%d" % si_, [128, 128], stack=s1); T.b_ETs = Buf()
                T.ETi = sb("ETi_%d" % si_, [128, 128], stack=s1); T.b_ETi = Buf()
                T.npos = T.LgIb; T.b_npos = T.b_LgIb
                T.sq = sb("sq_%d" % si_, [128, 128], stack=s1); T.b_sq = Buf()
                T.rs = sb("rs_%d" % si_, [128, 128], stack=s1); T.b_rs = Buf()
                T.kTn = sb("kTn_%d" % si_, [128, 128], stack=s1); T.b_kTn = Buf()
                T.qTn = sb("qTn_%d" % si_, [128, 128], stack=s1); T.b_qTn = Buf()
                T.Kbp = sb("Kbp_%d" % si_, [128, 128], stack=s1); T.b_Kbp = Buf()
                T.kd = sb("kd_%d" % si_, [128, 128], stack=s1); T.b_kd = Buf()
                T.Vb = sb("Vb_%d" % si_, [128, 128], stack=s1); T.b_Vb = Buf()
                T.TT = sb("TT_%d" % si_, [128, 128], stack=s1); T.b_TT = Buf()
                T.wT = sb("wT_%d" % si_, [128, 128], stack=s1); T.b_wT = Buf()
                T.usb = sb("usb_%d" % si_, [128, 128], stack=s1); T.b_usb = Buf()
                T.vnew = sb("vnew_%d" % si_, [128, 128], stack=s1); T.b_vnew = Buf()
                T.egl = sb("egl_%d" % si_, [128, 1], stack=s1); T.b_egl = Buf()
                T.eR = sb("eR_%d" % si_, [128, 128], stack=s1); T.b_eR = Buf()
                T.QdT = sb("QdT_%d" % si_, [128, 128], stack=s1); T.b_QdT = Buf()
                T.qkT = sb("qkT_%d" % si_, [128, 128], stack=s1); T.b_qkT = Buf()
                T.banks = (0, 1, 2, 3) if si_ == 0 else (4, 5, 6, 7)
                TS.append(T)
            osb = junk; b_osb = b_junk
            KTsb = hb[:].rearrange("p (h t) -> p h t", h=H); b_KTsb = b_hb
            Vp = sb("Vp", [128, H, 129], BF16, stack=s1); b_Vp = Buf()
            b_KTS = Buf(); b_VPS = Buf(); b_OS = Buf()
            P.op("pool", lambda e: e.memset(Vp[:], 1.0), [], [b_Vp])

            for v in range(NV):
                scan = 0 if v < S2 else 1
                is_ctx = v in (0, 1, S2, S2 + 1)
                own = (OWN0 <= v < S2) or (v >= OWN1)
                kv = v < NKV
                if v in (0, S2):
                    for h in range(H):
                        P.op("pool", lambda e, h=h: e.memset(Sst[:, h, :], 0.0), [], [b_S[h]])
                make_HT(v, is_ctx)
                PA = 3
                for k in range(8):
                    P.op("pe", lambda e, k=k, scan=scan: e.matmul(psum[PA][:, 0:16], lhsT=HT[:, k, 2:130], rhs=wab[:, scan, k, :], start=(k == 0), stop=(k == 7)), [b_HT, b_wab], [PB[PA]])
                P.op("dve", lambda e, scan=scan: e.tensor_tensor(out=gb[:, 4, :], in0=psum[PA][:, 0:8], in1=dtb[:, scan, :], op=ALU.add), [PB[PA], b_dtb], [b_gb])
                P.op("act", lambda e: e.activation(out=gb[:, 4, :], in_=gb[:, 4, :], func=AF.Exp), [b_gb], [b_gb])
                P.op("act", lambda e: e.activation(out=gb[:, 4, :], in_=gb[:, 4, :], func=AF.Ln, bias=1.0), [b_gb], [b_gb])
                P.op("dve", lambda e, v=v, scan=scan: e.scalar_tensor_tensor(out=gb[:, 0, :], in0=gb[:, 4, :], scalar=gm[:, v:v + 1], in1=negA[:, scan, :], op0=ALU.mult, op1=ALU.mult), [b_gb, b_gm, b_negA], [b_gb])
                P.op("act", lambda e: e.activation(out=gb[:, 5, :], in_=psum[PA][:, 8:16], func=AF.Sigmoid), [PB[PA], b_gb], [b_gb])
                P.op("dve", lambda e, v=v: e.tensor_scalar_mul(out=gb[:, 1, :], in0=gb[:, 5, :], scalar1=gm[:, v:v + 1]), [b_gb, b_gm], [b_gb])
                P.op("pe", lambda e: e.matmul(psum[PA][:, 16:24], lhsT=Ltri, rhs=gb[:, 0, :], start=True, stop=True), [b_cst, b_gb], [PB[PA]])
                P.op("dve", lambda e: e.tensor_copy(out=gb[:, 2, :], in_=psum[PA][:, 16:24]), [PB[PA], b_gb], [b_gb])
                P.op("act", lambda e: e.activation(out=gb[:, 3, :], in_=psum[PA][:, 16:24], func=AF.Exp), [PB[PA], b_gb], [b_gb])
                P.op("dve", lambda e: e.tensor_tensor(out=gb[:, 3, :], in0=gb[:, 3, :], in1=gb[:, 1, :], op=ALU.mult), [b_gb], [b_gb])

                cbs = list(range(24)) if own else list(range(8, 24))
                for gi in range(0, len(cbs), 3):
                    grp = cbs[gi:gi + 3]
                    bank = 4 + (gi // 3) % 2
                    for si, cb in enumerate(grp):
                        for k in range(8):
                            P.op("pe", lambda e, k=k, cb=cb, si=si, bank=bank: e.matmul(psum[bank][:, si * 132:(si + 1) * 132], lhsT=W1[:, k, cb * 128:(cb + 1) * 128], rhs=HT[:, k, :], start=(k == 0), stop=(k == 7)),
                                 [b_W1, b_HT], [PB[bank]])
                    n = len(grp)
                    P.op("act", lambda e, bank=bank, n=n: e.activation(out=pre[:, 0:n, :], in_=psum[bank][:, 0:n * 132].rearrange("p (a b) -> p a b", a=n), func=AF.Copy), [PB[bank]], [b_pre])
                    for si, cb in enumerate(grp):
                        eng = "dve" if (cb % 2 == 0) else "pool"
                        a_, ba_ = acc[cb % 2], b_acc[cb % 2]
                        P.op(eng, lambda e, si=si, cb=cb, a_=a_, scan=scan: e.tensor_scalar_mul(out=a_[:], in0=pre[:, si, 0:128], scalar1=cw[:, scan, cb, 0:1]), [b_pre, b_cw], [ba_])
                        for j in range(1, 5):
                            if eng == "dve":
                                P.op(eng, lambda e, si=si, cb=cb, a_=a_, j=j, scan=scan: e.scalar_tensor_tensor(out=a_[:], in0=pre[:, si, j:j + 128], scalar=cw[:, scan, cb, j:j + 1], in1=a_[:], op0=ALU.mult, op1=ALU.add),
                                     [b_pre, b_cw, ba_], [ba_])
                            else:
                                P.op(eng, lambda e, si=si, cb=cb, j=j, scan=scan: e.tensor_scalar_mul(out=ptmp[:], in0=pre[:, si, j:j + 128], scalar1=cw[:, scan, cb, j:j + 1]), [b_pre, b_cw], [b_ptmp])
                                P.op(eng, lambda e, a_=a_: e.tensor_tensor(out=a_[:], in0=a_[:], in1=ptmp[:], op=ALU.add), [b_ptmp, ba_], [ba_])
                        P.op("act", lambda e, cb=cb, a_=a_: e.activation(out=qkvT[:, cb, :], in_=a_[:], func=AF.Silu), [ba_], [b_qkvT[cb]])

                if kv:
                    P.dma(tab[:], ROPE[v * 128:(v + 1) * 128, :], [], [b_tab], b_tab)
                    for hf in range(2):
                        for k in range(8):
                            P.op("pe", lambda e, k=k, hf=hf: e.matmul(psum[6 + hf][:, :], lhsT=HT[:, k, 2:130], rhs=W1[:, k, 3072 + hf * 512:3072 + (hf + 1) * 512], start=(k == 0), stop=(k == 7)),
                                 [b_HT, b_W1], [PB[6 + hf]])
                    qk_norm_rope([psum[6], psum[7]], [PB[6], PB[7]], knw, tab, b_tab, krot, b_krot, tmpA, b_tmpA)
                    for h in range(H):
                        bank = 6 + h // 4
                        P.op("pe", lambda e, h=h, bank=bank: e.matmul(psum[bank][:, (h % 4) * 128:(h % 4 + 1) * 128], lhsT=krot[:, h * 128:(h + 1) * 128], rhs=identb, start=True, stop=True),
                             [b_krot, b_cbf], [PB[bank]])
                    for hf in range(2):
                        P.op("act", lambda e, hf=hf: e.activation(out=KTsb[:, hf * 4:(hf + 1) * 4, :], in_=psum[6 + hf][:, :].rearrange("p (a b) -> p a b", a=4), func=AF.Copy), [PB[6 + hf]], [b_KTsb])
                    P.dma(KTS[:, :, v * 128:(v + 1) * 128].rearrange("h p t -> p h t"), KTsb[:], [b_KTsb], [b_KTS], b_KTS)
                    for hf in range(2):
                        for k in range(8):
                            P.op("pe", lambda e, k=k, hf=hf: e.matmul(psum[6 + hf][:, :], lhsT=HT[:, k, 2:130], rhs=W1[:, k, 4096 + hf * 512:4096 + (hf + 1) * 512], start=(k == 0), stop=(k == 7)),
                                 [b_HT, b_W1], [PB[6 + hf]])
                    for hf in range(2):
                        P.op("act", lambda e, hf=hf: e.activation(out=Vp[:, hf * 4:(hf + 1) * 4, 0:128], in_=psum[6 + hf][:, :].rearrange("p (a b) -> p a b", a=4), func=AF.Copy), [PB[6 + hf]], [b_Vp])
                    P.dma(VPS[:, :, v, :].rearrange("h p c -> p h c"), Vp[:], [b_Vp], [b_VPS], b_VPS)

                def head_steps(h, T):
                        qcb, kcb, vcb = h, 8 + h, 16 + h
                        PR, PG, PS_, PU = T.banks
                        P.op("pool", lambda e, kcb=kcb: e.tensor_tensor(out=T.sq[:], in0=qkvT[:, kcb, :], in1=qkvT[:, kcb, :], op=ALU.mult), [b_qkvT[kcb]], [T.b_sq])
                        yield
                        P.op("pe", lambda e: e.matmul(psum[PR][:, 256:384], lhsT=ones, rhs=T.sq[:], start=True, stop=True), [b_cst, T.b_sq], [PB[PR]])
                        yield
                        P.op("act", lambda e: e.activation(out=T.rs[:], in_=psum[PR][:, 256:384], func=AF.Ln, bias=epsc[:, 0:1]), [PB[PR], b_eps], [T.b_rs])
                        yield
                        P.op("act", lambda e: e.activation(out=T.rs[:], in_=T.rs[:], func=AF.Exp, scale=-0.5), [T.b_rs], [T.b_rs])
                        yield
                        P.op("dve", lambda e, kcb=kcb: e.tensor_tensor(out=T.kTn[:], in0=qkvT[:, kcb, :], in1=T.rs[:], op=ALU.mult), [b_qkvT[kcb], T.b_rs], [T.b_kTn])
                        yield
                        if own:
                            P.op("pool", lambda e, qcb=qcb: e.tensor_tensor(out=T.sq[:], in0=qkvT[:, qcb, :], in1=qkvT[:, qcb, :], op=ALU.mult), [b_qkvT[qcb]], [T.b_sq])
                            yield
                            P.op("pe", lambda e: e.matmul(psum[PR][:, 384:512], lhsT=ones, rhs=T.sq[:], start=True, stop=True), [b_cst, T.b_sq], [PB[PR]])
                            yield
                            P.op("act", lambda e: e.activation(out=T.rs[:], in_=psum[PR][:, 384:512], func=AF.Ln, bias=epsc[:, 0:1]), [PB[PR], b_eps], [T.b_rs])
                            yield
                            P.op("act", lambda e: e.activation(out=T.rs[:], in_=T.rs[:], func=AF.Exp, scale=-0.5), [T.b_rs], [T.b_rs])
                            yield
                            P.op("dve", lambda e, qcb=qcb: e.scalar_tensor_tensor(out=T.qTn[:], in0=qkvT[:, qcb, :], scalar=128.0 ** -0.5, in1=T.rs[:], op0=ALU.mult, op1=ALU.mult), [b_qkvT[qcb], T.b_rs], [T.b_qTn])
                            yield
                        P.op("dve", lambda e, h=h: e.tensor_scalar_mul(out=T.LgIb[:, 0:128], in0=Ltri, scalar1=gb[:, 0, h:h + 1]), [b_cst, b_gb], [T.b_LgIb])
                        yield
                        P.op("pool", lambda e, h=h: e.tensor_scalar_mul(out=T.LgIb[:, 128:256], in0=ident, scalar1=gb[:, 1, h:h + 1]), [b_cst, b_gb], [T.b_LgIb])
                        yield
                        P.op("pe", lambda e: e.matmul(psum[PR][:, 0:256], lhsT=ones, rhs=T.LgIb[:], start=True, stop=True), [b_cst, T.b_LgIb], [PB[PR]])
                        yield
                        P.op("dve", lambda e, h=h: e.tensor_scalar(out=T.npos[:, 0:128], in0=psum[PR][:, 0:128], scalar1=gb[:, 2, h:h + 1], scalar2=0.0, op0=ALU.subtract, op1=ALU.max), [PB[PR], b_gb], [T.b_npos])
                        yield
                        P.op("dve", lambda e, h=h: e.tensor_scalar(out=T.npos[:, 128:256], in0=psum[PR][:, 0:128], scalar1=gb[:, 2, h:h + 1], scalar2=0.0, op0=ALU.subtract, op1=ALU.min), [PB[PR], b_gb], [T.b_npos])
                        yield
                        P.op("act", lambda e: e.activation(out=T.Em[:], in_=T.npos[:, 0:128], func=AF.Exp, scale=-1.0), [T.b_npos], [T.b_Em])
                        yield
                        P.op("act", lambda e: e.activation(out=T.ETm[:], in_=T.npos[:, 128:256], func=AF.Exp), [T.b_npos], [T.b_ETm])
                        yield
                        P.op("pool", lambda e: e.tensor_tensor(out=T.EMs[:], in0=T.Em[:], in1=negMs, op=ALU.mult), [T.b_Em, b_cst], [T.b_EMs])
                        yield
                        P.op("pool", lambda e: e.tensor_tensor(out=T.ETs[:], in0=T.ETm[:], in1=negMsT, op=ALU.mult), [T.b_ETm, b_cst], [T.b_ETs])
                        yield
                        P.op("pe", lambda e: e.matmul(psum[PG][:, 0:128], lhsT=T.kTn[:], rhs=T.kTn[:], start=True, stop=True), [T.b_kTn], [PB[PG]])
                        yield
                        if own:
                            P.op("pe", lambda e: e.matmul(psum[PG][:, 128:256], lhsT=T.kTn[:], rhs=T.qTn[:], start=True, stop=True), [T.b_kTn, T.b_qTn], [PB[PG]])
                            yield
                        P.op("dve", lambda e, h=h: e.scalar_tensor_tensor(out=T.NY[:, 0:128], in0=psum[PG][:, 0:128], scalar=gb[:, 1, h:h + 1], in1=T.EMs[:], op0=ALU.mult, op1=ALU.mult), [PB[PG], b_gb, T.b_EMs], [T.b_NY])
                        yield
                        P.op("dve", lambda e: e.tensor_tensor(out=T.NY[:, 128:256], in0=psum[PG][:, 0:128], in1=T.ETs[:], op=ALU.mult), [PB[PG], T.b_ETs], [T.b_NY])
                        yield
                        P.op("dve", lambda e: e.tensor_tensor(out=T.NY[:, 128:256], in0=T.NY[:, 128:256], in1=psum[PR][:, 128:256], op=ALU.mult), [PB[PR], T.b_NY], [T.b_NY])
                        yield
                        P.op("pool", lambda e: e.tensor_tensor(out=T.Wb[0][:, 0:128], in0=T.NY[:, 0:128], in1=BD32, op=ALU.mult), [T.b_NY, b_cst], [T.b_Wb[0]])
                        yield
                        P.op("pool", lambda e: e.tensor_tensor(out=T.Wb[0][:, 256:384], in0=T.NY[:, 128:256], in1=BD32, op=ALU.mult), [T.b_NY, b_cst], [T.b_Wb[0]])
                        yield
                        P.op("pool", lambda e: e.tensor_tensor(out=T.Wb[1][:, 128:256], in0=T.Wb[0][:, 0:128], in1=ident, op=ALU.add), [T.b_Wb[0], b_cst], [T.b_Wb[1]])
                        yield
                        P.op("pool", lambda e: e.tensor_tensor(out=T.Wb[1][:, 384:512], in0=T.Wb[0][:, 256:384], in1=ident, op=ALU.add), [T.b_Wb[0], b_cst], [T.b_Wb[1]])
                        yield
                        for n in range(4):
                            cur, nxt = T.Wb[n % 2], T.Wb[(n + 1) % 2]
                            bc, bn = T.b_Wb[n % 2], T.b_Wb[(n + 1) % 2]
                            w_ = 128 if n == 0 else 256
                            P.op("pe", lambda e, cur=cur, w_=w_: e.matmul(psum[PS_][:, 0:w_], lhsT=cur[:, 256:384], rhs=cur[:, 0:w_], start=True, stop=True), [bc], [PB[PS_]])
                            yield
                            P.op("pe", lambda e, cur=cur, w_=w_: e.matmul(psum[PS_][:, 256:256 + w_], lhsT=cur[:, 0:128], rhs=cur[:, 256:256 + w_], start=True, stop=True), [bc], [PB[PS_]])
                            yield
                            P.op("act", lambda e, nxt=nxt: e.activation(out=nxt[:].rearrange("p (a b) -> p a b", a=2)[:, :, 0:128], in_=psum[PS_][:, :].rearrange("p (a b) -> p a b", a=2)[:, :, 0:128], func=AF.Copy), [PB[PS_]], [bn])
                            yield
                            if n > 0:
                                P.op("dve", lambda e, cur=cur, nxt=nxt: e.tensor_tensor(out=nxt[:].rearrange("p (a b) -> p a b", a=2)[:, :, 128:256], in0=psum[PS_][:, :].rearrange("p (a b) -> p a b", a=2)[:, :, 128:256],
                                                                                        in1=cur[:].rearrange("p (a b) -> p a b", a=2)[:, :, 128:256], op=ALU.add), [PB[PS_], bc], [bn])
                                yield
                        P.op("pe", lambda e: e.matmul(psum[PS_][:, 0:128], lhsT=T.Wb[0][:, 256:384], rhs=T.Wb[0][:, 128:256], start=True, stop=True), [T.b_Wb[0]], [PB[PS_]])
                        yield
                        P.op("pe", lambda e: e.matmul(psum[PS_][:, 128:256], lhsT=T.Wb[0][:, 0:128], rhs=T.Wb[0][:, 384:512], start=True, stop=True), [T.b_Wb[0]], [PB[PS_]])
                        yield
                        P.op("dve", lambda e: e.tensor_tensor(out=T.TU[:].rearrange("p (a b) -> p a b", a=2), in0=psum[PS_][:, 0:256].rearrange("p (a b) -> p a b", a=2),
                                                              in1=T.Wb[0][:].rearrange("p (a b) -> p a b", a=2)[:, :, 128:256], op=ALU.add), [PB[PS_], T.b_Wb[0]], [T.b_TU])
                        yield
                        P.op("pool", lambda e: e.tensor_tensor(out=T.NYm[:, 0:128], in0=T.NY[:, 128:256], in1=M64, op=ALU.mult), [T.b_NY, b_cst], [T.b_NYm])
                        yield
                        P.op("pool", lambda e: e.tensor_tensor(out=T.NYm[:, 128:256], in0=T.NY[:, 0:128], in1=M64, op=ALU.mult), [T.b_NY, b_cst], [T.b_NYm])
                        yield
                        P.op("pe", lambda e: e.matmul(psum[PS_][:, 256:384], lhsT=T.NYm[:, 0:128], rhs=T.TU[:, 0:128], start=True, stop=True), [T.b_NYm, T.b_TU], [PB[PS_]])
                        yield
                        P.op("pe", lambda e: e.matmul(psum[PS_][:, 384:512], lhsT=T.NYm[:, 128:256], rhs=T.TU[:, 128:256], start=True, stop=True), [T.b_NYm, T.b_TU], [PB[PS_]])
                        yield
                        P.op("act", lambda e: e.activation(out=T.Wm[:], in_=psum[PS_][:, 256:512], func=AF.Copy), [PB[PS_]], [T.b_Wm])
                        yield
                        P.op("pe", lambda e: e.matmul(psum[PS_][:, 0:128], lhsT=T.TU[:, 128:256], rhs=T.Wm[:, 0:128], start=True, stop=True), [T.b_TU, T.b_Wm], [PB[PS_]])
                        yield
                        P.op("pe", lambda e: e.matmul(psum[PS_][:, 128:256], lhsT=T.TU[:, 0:128], rhs=T.Wm[:, 128:256], start=True, stop=True), [T.b_TU, T.b_Wm], [PB[PS_]])
                        yield
                        P.op("dve", lambda e: e.tensor_tensor(out=T.TU2[:], in0=psum[PS_][:, 0:256], in1=T.TU[:], op=ALU.add), [PB[PS_], T.b_TU], [T.b_TU2])
                        yield
                        P.op("pool", lambda e: e.tensor_tensor(out=T.NYm[:, 128:256], in0=T.NY[:, 0:128], in1=M128, op=ALU.mult), [T.b_NY, b_cst, T.b_NYm], [T.b_NYm])
                        yield
                        P.op("pe", lambda e: e.matmul(psum[PS_][:, 384:512], lhsT=T.NYm[:, 128:256], rhs=T.TU2[:, 128:256], start=True, stop=True), [T.b_NYm, T.b_TU2], [PB[PS_]])
                        yield
                        P.op("act", lambda e: e.activation(out=T.Wm[:, 128:256], in_=psum[PS_][:, 384:512], func=AF.Copy), [PB[PS_]], [T.b_Wm])
                        yield
                        P.op("pe", lambda e: e.matmul(psum[PS_][:, 128:256], lhsT=T.TU2[:, 0:128], rhs=T.Wm[:, 128:256], start=True, stop=True), [T.b_TU2, T.b_Wm], [PB[PS_]])
                        yield
                        P.op("dve", lambda e: e.tensor_tensor(out=T.TT[:], in0=psum[PS_][:, 128:256], in1=T.TU2[:, 128:256], op=ALU.add), [PB[PS_], T.b_TU2], [T.b_TT])
                        yield
                        P.op("pe", lambda e: e.matmul(psum[PR][:, 256:384], lhsT=T.kTn[:], rhs=ident, start=True, stop=True), [T.b_kTn, b_cst], [PB[PR]])
                        yield
                        P.op("pe", lambda e, vcb=vcb: e.matmul(psum[PR][:, 384:512], lhsT=qkvT[:, vcb, :], rhs=ident, start=True, stop=True), [b_qkvT[vcb], b_cst], [PB[PR]])
                        yield
                        P.op("dve", lambda e, h=h: e.tensor_scalar_mul(out=T.Kbp[:], in0=psum[PR][:, 256:384], scalar1=gb[:, 3, h:h + 1]), [PB[PR], b_gb], [T.b_Kbp])
                        yield
                        P.op("dve", lambda e: e.tensor_scalar_mul(out=T.kd[:], in0=psum[PR][:, 256:384], scalar1=T.ETm[:, 127:128]), [PB[PR], T.b_ETm], [T.b_kd])
                        yield
                        P.op("dve", lambda e, h=h: e.tensor_scalar_mul(out=T.Vb[:], in0=psum[PR][:, 384:512], scalar1=gb[:, 1, h:h + 1]), [PB[PR], b_gb], [T.b_Vb])
                        yield
                        P.op("pe", lambda e: e.matmul(psum[PU][:, 0:128], lhsT=T.TT[:], rhs=T.Vb[:], start=True, stop=True), [T.b_TT, T.b_Vb], [PB[PU]])
                        yield
                        P.op("pe", lambda e: e.matmul(psum[PU][:, 128:256], lhsT=T.Kbp[:], rhs=T.TT[:], start=True, stop=True), [T.b_TT, T.b_Kbp], [PB[PU]])
                        yield
                        P.op("act", lambda e: e.activation(out=T.wT[:], in_=psum[PU][:, 128:256], func=AF.Copy), [PB[PU]], [T.b_wT])
                        yield
                        P.op("act", lambda e: e.activation(out=T.usb[:], in_=psum[PU][:, 0:128], func=AF.Copy), [PB[PU]], [T.b_usb])
                        yield
                        P.op("pe", lambda e, h=h: e.matmul(psum[PU][:, 256:384], lhsT=T.wT[:], rhs=Sst[:, h, :], start=True, stop=True), [T.b_wT, b_S[h]], [PB[PU]])
                        yield
                        P.op("dve", lambda e: e.tensor_tensor(out=T.vnew[:], in0=T.usb[:], in1=psum[PU][:, 256:384], op=ALU.subtract), [T.b_usb, PB[PU]], [T.b_vnew])
                        yield
                        if own:
                            P.op("act", lambda e: e.activation(out=T.eR[:], in_=psum[PR][:, 0:128], func=AF.Exp), [PB[PR]], [T.b_eR])
                            yield
                            P.op("dve", lambda e: e.tensor_tensor(out=T.QdT[:], in0=T.qTn[:], in1=T.eR[:], op=ALU.mult), [T.b_qTn, T.b_eR], [T.b_QdT])
                            yield
                            P.op("pool", lambda e: e.tensor_tensor(out=T.ETi[:], in0=T.ETm[:], in1=MiT, op=ALU.mult), [T.b_ETm, b_cst], [T.b_ETi])
                            yield
                            P.op("dve", lambda e: e.tensor_tensor(out=T.qkT[:], in0=psum[PG][:, 128:256], in1=T.ETi[:], op=ALU.mult), [PB[PG], T.b_ETi], [T.b_qkT])
                            yield
                            P.op("pe", lambda e, h=h: e.matmul(psum[PG][:, 256:384], lhsT=T.QdT[:], rhs=Sst[:, h, :], start=True, stop=False), [T.b_QdT, b_S[h]], [PB[PG]])
                            yield
                            P.op("pe", lambda e: e.matmul(psum[PG][:, 256:384], lhsT=T.qkT[:], rhs=T.vnew[:], start=False, stop=True), [T.b_qkT, T.b_vnew], [PB[PG]])
                            yield
                            P.op("act", lambda e, h=h: e.activation(out=osb[:, h * 128:(h + 1) * 128], in_=psum[PG][:, 256:384], func=AF.Copy), [PB[PG]], [b_osb])
                            yield
                        P.op("act", lambda e: e.activation(out=T.egl[:], in_=psum[PR][:, 127:128], func=AF.Exp), [PB[PR]], [T.b_egl])
                        yield
                        P.op("pe", lambda e: e.matmul(psum[PU][:, 384:512], lhsT=T.kd[:], rhs=T.vnew[:], start=True, stop=True), [T.b_kd, T.b_vnew], [PB[PU]])
                        yield
                        P.op("dve", lambda e, h=h: e.scalar_tensor_tensor(out=Sst[:, h, :], in0=Sst[:, h, :], scalar=T.egl[:, 0:1], in1=psum[PU][:, 384:512], op0=ALU.mult, op1=ALU.add), [b_S[h], T.b_egl, PB[PU]], [b_S[h]])
                        yield
                for h0 in range(0, H, 2):
                    gens = [head_steps(h0, TS[0]), head_steps(h0 + 1, TS[1])]
                    alive = [True, True]
                    while any(alive):
                        for gi_ in range(2):
                            if alive[gi_]:
                                try:
                                    next(gens[gi_])
                                except StopIteration:
                                    alive[gi_] = False
                if own:
                    oi = (v - OWN0) if scan == 0 else (v - OWN1)
                    P.dma(OS[scan, oi, :, :], osb[:], [b_osb], [b_OS], b_OS)

        P.barrier()

        b_ZS = Buf(); b_GS = Buf()
        YbT = sb("YbT", [128, H, 2048], BF16); b_YbT = Buf()
        s_qt = ExitStack()
        QT = sb("QT", [128, H, 2048], BF16, stack=s_qt); b_QT = Buf()
        with ExitStack() as s2:
            W2 = sb("W2", [128, 8, 3072], BF16, stack=s2); b_W2 = Buf()
            stage = [junk, xt]
            bstage = [b_junk, b_xt]
            tabB = sb("tabB2", [128, 64], stack=s2); b_tabB = Buf()
            tmpB = [sb("tmpB2_%d" % i, [128, D], stack=s2) for i in range(3)] + [sb("tmpB32", [128, 16], stack=s2)]
            b_tmpB = [Buf() for _ in range(4)]
            krotB = sb("krotB2", [128, D], BF16, stack=s2); b_krotB = Buf()
            load_w_bf16(W2, b_W2, 3072, 1024, stage, bstage, 0)
            load_w_bf16(W2, b_W2, 4128, 1024, stage, bstage, 1024)
            load_w_bf16(W2, b_W2, 7200, 1024, stage, bstage, 2048)
            zsb = sb("zsb", [128, 2048], BF16, stack=s2); b_zsb = Buf()
            for i in range(16):
                v = OWN0 + i
                make_HT(v, False)
                P.dma(tabB[:], ROPE[v * 128:(v + 1) * 128, :], [], [b_tabB], b_tabB)
                for hf in range(2):
                    for k in range(8):
                        P.op("pe", lambda e, k=k, hf=hf: e.matmul(psum[6 + hf][:, :], lhsT=HT[:, k, 2:130], rhs=W2[:, k, 1024 + hf * 512:1024 + (hf + 1) * 512], start=(k == 0), stop=(k == 7)),
                             [b_HT, b_W2], [PB[6 + hf]])
                qk_norm_rope([psum[6], psum[7]], [PB[6], PB[7]], qnw, tabB, b_tabB, krotB, b_krotB, tmpB, b_tmpB)
                for h in range(H):
                    bank = 6 + h // 4
                    P.op("pe", lambda e, h=h, bank=bank: e.matmul(psum[bank][:, (h % 4) * 128:(h % 4 + 1) * 128], lhsT=krotB[:, h * 128:(h + 1) * 128], rhs=identb, start=True, stop=True),
                         [b_krotB, b_cbf], [PB[bank]])
                for hf in range(2):
                    P.op("act", lambda e, hf=hf, i=i: e.activation(out=QT[:, hf * 4:(hf + 1) * 4, i * 128:(i + 1) * 128], in_=psum[6 + hf][:, :].rearrange("p (a b) -> p a b", a=4), func=AF.Copy), [PB[6 + hf]], [b_QT])
                for (c0, dstt, bd, off, fn) in ((0, zsb, b_zsb, 0, AF.Silu), (2048, zsb, b_zsb, 1024, AF.Silu)):
                    for hf in range(2):
                        bank = 4 + hf
                        for k in range(8):
                            P.op("pe", lambda e, k=k, hf=hf, c0=c0, bank=bank: e.matmul(psum[bank][:, :], lhsT=HT[:, k, 2:130], rhs=W2[:, k, c0 + hf * 512:c0 + (hf + 1) * 512], start=(k == 0), stop=(k == 7)),
                                 [b_HT, b_W2], [PB[bank]])
                        P.op("act", lambda e, hf=hf, bank=bank, dstt=dstt, off=off, fn=fn: e.activation(out=dstt[:, off + hf * 512:off + (hf + 1) * 512], in_=psum[bank][:, :], func=fn), [PB[bank]], [bd])
                P.dma(ZS[i, :, :], zsb[:], [b_zsb], [b_ZS], b_ZS)

        P.barrier()
        with ExitStack() as s3:
            KTh = sb("KTh", [128, NKV * 128], BF16, stack=s3); b_KTh = Buf()
            Vph = sb("Vph", [128, NKV, 129], BF16, stack=s3); b_Vph = Buf()
            Eb = [sb("Eb%d" % i, [128, 1024], BF16, stack=s3) for i in range(2)]; b_Eb = [Buf(), Buf()]
            zb = sb("zb", [128, 4, 128], BF16, stack=s3); b_zb = Buf()
            fo = sb("fo", [128, 8, 128], stack=s3); b_fo = Buf()
            fs = sb("fs", [128, 8], stack=s3); b_fs = Buf()
            yb = sb("yb", [128, 128], BF16, stack=s3); b_yb = Buf()
            for h in range(H):
                P.dma(KTh[:], KTS[h, :, :], [b_KTS], [b_KTh], b_KTh)
                P.dma(Vph[:], VPS[h, :, :, :], [b_VPS], [b_Vph], b_Vph)
                for qt in range(4):
                    P.dma(zb[:], ZS[qt * 4:(qt + 1) * 4, :, 1024 + h * 128:1024 + (h + 1) * 128].rearrange("a p c -> p a c"), [b_ZS], [b_zb], b_zb)
                    for kb in range(NKV):
                        sbk = (kb % 2) * 2
                        for m in range(2):
                            P.op("pe", lambda e, m=m, kb=kb, qt=qt, h=h, sbk=sbk: e.matmul(psum[sbk + m][:, :], lhsT=KTh[m * 64:(m + 1) * 64, kb * 128:(kb + 1) * 128],
                                                                                   rhs=QT[m * 64:(m + 1) * 64, h, qt * 512:(qt + 1) * 512], start=True, stop=True),
                                 [b_KTh, b_QT], [PB[sbk + m]])
                        for m in range(2):
                            P.op("act", lambda e, m=m, kb=kb, sbk=sbk: e.activation(out=Eb[kb % 2][:, m * 512:(m + 1) * 512], in_=psum[sbk + m][:, :], func=AF.Exp, scale=0.125, bias=lamt[:, 1:2]),
                                 [PB[sbk + m], b_lam], [b_Eb[kb % 2]])
                        for qb in range(4):
                            for m in range(2):
                                P.op("pe", lambda e, m=m, kb=kb, qb=qb: e.matmul(psum[4 + qb][:, m * 129:(m + 1) * 129], lhsT=Eb[kb % 2][:, m * 512 + qb * 128:m * 512 + (qb + 1) * 128],
                                                                                 rhs=Vph[:, kb, :], start=(kb == 0 and m == 0), stop=(kb == NKV - 1 and m == 1)),
                                     [b_Eb[kb % 2], b_Vph], [PB[4 + qb]])
                    for qb in range(4):
                        blk = qt * 4 + qb
                        pso = psum[4 + qb]
                        P.op("dve", lambda e, pso=pso: e.tensor_copy(out=fs[:, 0:1], in_=pso[:, 128:129]), [PB[4 + qb]], [b_fs])
                        P.op("dve", lambda e, pso=pso: e.tensor_copy(out=fs[:, 1:2], in_=pso[:, 257:258]), [PB[4 + qb]], [b_fs])
                        P.op("dve", lambda e: e.reciprocal(out=fs[:, 2:4], in_=fs[:, 0:2]), [b_fs], [b_fs])
                        P.op("dve", lambda e: e.tensor_tensor(out=fs[:, 3:4], in0=fs[:, 3:4], in1=lamt[:, 0:1], op=ALU.mult), [b_fs, b_lam], [b_fs])
                        P.op("dve", lambda e, pso=pso: e.tensor_scalar_mul(out=fo[:, 0, :], in0=pso[:, 129:257], scalar1=fs[:, 3:4]), [PB[4 + qb], b_fs], [b_fo])
                        P.op("dve", lambda e, pso=pso: e.scalar_tensor_tensor(out=fo[:, 1, :], in0=pso[:, 0:128], scalar=fs[:, 2:3], in1=fo[:, 0, :], op0=ALU.mult, op1=ALU.add), [PB[4 + qb], b_fs, b_fo], [b_fo])
                        P.op("act", lambda e: e.activation(out=fo[:, 2, :], in_=fo[:, 1, :], func=AF.Square, accum_out=fs[:, 4:5]), [b_fo, b_fs], [b_fo, b_fs])
                        P.op("act", lambda e: e.activation(out=fs[:, 5:6], in_=fs[:, 4:5], func=AF.Ln, scale=1.0 / 128, bias=epsc[:, 0:1]), [b_fs, b_eps], [b_fs])
                        P.op("act", lambda e: e.activation(out=fs[:, 5:6], in_=fs[:, 5:6], func=AF.Exp, scale=-0.5), [b_fs], [b_fs])
                        P.op("dve", lambda e: e.scalar_tensor_tensor(out=fo[:, 3, :], in0=fo[:, 1, :], scalar=fs[:, 5:6], in1=dnw, op0=ALU.mult, op1=ALU.mult), [b_fo, b_fs, b_vecs], [b_fo])
                        P.op("dve", lambda e, qb=qb: e.scalar_tensor_tensor(out=yb[:], in0=fo[:, 3, :], scalar=(1.0 - LAMBDA_INIT), in1=zb[:, qb, :], op0=ALU.mult, op1=ALU.mult), [b_fo, b_zb], [b_yb])
                        P.op("pe", lambda e: e.matmul(psum[0][:, 0:128], lhsT=yb[:], rhs=identb, start=True, stop=True), [b_yb, b_cbf], [PB[0]])
                        P.op("act", lambda e, h=h, blk=blk: e.activation(out=YbT[:, h, blk * 128:(blk + 1) * 128], in_=psum[0][:, 0:128], func=AF.Copy), [PB[0]], [b_YbT])

        s_qt.close()
        P.barrier()
        with ExitStack() as s4:
            Wo = [sb("Wo%d" % i, [128, 8, D], BF16, stack=s4) for i in range(3)]; b_Wo = [Buf() for _ in range(3)]
            stage = [junk, xt]
            bstage = [b_junk, b_xt]
            YaT = sb("YaT", [128, H, 128], BF16, stack=s4); b_YaT = Buf()
            Wg = sb("Wg", [128, 8, 2048], BF16, stack=s4); b_Wg = Buf()
            load_w_bf16(Wg, b_Wg, 8224, 2048, stage, bstage, 0)
            ii = 0
            for wi, src in enumerate((WOA, WOB, WOUT)):
                for k in range(8):
                    s_, bs_ = stage[ii % 2], bstage[ii % 2]
                    P.dma(s_[:], src[k * 128:(k + 1) * 128, :], [], [bs_], bs_)
                    P.op(["dve", "pool"][ii % 2], lambda e, s_=s_, wi=wi, k=k: e.tensor_copy(out=Wo[wi][:, k, :], in_=s_[:]), [bs_], [b_Wo[wi]])
                    ii += 1
            o1 = sb("o1", [128, D], stack=s4); b_o1 = Buf()
            o2 = sb("o2", [128, D], stack=s4); b_o2 = Buf()
            zs4 = sb("zs4", [128, 1024], BF16, stack=s4); b_zs4 = Buf()
            gs4 = sb("gs4", [128, 2048], BF16, stack=s4); b_gs4 = Buf()
            ya = sb("ya", [128, D], BF16, stack=s4); b_ya = Buf()
            mg = sb("mg", [128, D], stack=s4); b_mg = Buf()
            mgb = sb("mgb", [128, D], BF16, stack=s4); b_mgb = Buf()
            mT = sb("mT", [128, 8, 128], BF16, stack=s4); b_mT = Buf()
            res = sb("res", [128, D], stack=s4); b_res = Buf()
            b_OUT = Buf()
            out_toks = []
            for i in range(16):
                P.dma(o1[:], OS[0, i, :, :], [b_OS], [b_o1], b_o1)
                P.dma(o2[:], OS[1, 15 - i, :, :], [b_OS], [b_o2], b_o2)
                P.dma(zs4[:], ZS[i, :, 0:1024], [b_ZS], [b_zs4], b_zs4)
                make_HT(OWN0 + i, False)
                for g4 in range(4):
                    bank = 6 + g4 % 2
                    for k in range(8):
                        P.op("pe", lambda e, k=k, g4=g4, bank=bank: e.matmul(psum[bank][:, :], lhsT=HT[:, k, 2:130], rhs=Wg[:, k, g4 * 512:(g4 + 1) * 512], start=(k == 0), stop=(k == 7)), [b_HT, b_Wg], [PB[bank]])
                    P.op("act", lambda e, g4=g4, bank=bank: e.activation(out=gs4[:, g4 * 512:(g4 + 1) * 512], in_=psum[bank][:, :], func=AF.Sigmoid), [PB[bank]], [b_gs4])
                for hf in range(2):
                    P.op("pe", lambda e, hf=hf: e.matmul(psum[hf][:, :], lhsT=Jm, rhs=o2[:, hf * 512:(hf + 1) * 512], start=True, stop=True), [b_cst, b_o2], [PB[hf]])
                    P.op("dve", lambda e, hf=hf: e.tensor_tensor(out=o1[:, hf * 512:(hf + 1) * 512], in0=o1[:, hf * 512:(hf + 1) * 512], in1=psum[hf][:, :], op=ALU.add), [PB[hf], b_o1], [b_o1])
                P.op("act", lambda e: e.activation(out=junk[:], in_=o1[:], func=AF.Square), [b_o1], [b_junk])
                P.op("dve", lambda e: e.tensor_reduce(out=st[:, 0:8], in_=junk[:].rearrange("p (g d) -> p g d", d=128), axis=AX.X, op=ALU.add), [b_junk], [b_st])
                P.op("act", lambda e: e.activation(out=st[:, 0:8], in_=st[:, 0:8], func=AF.Ln, scale=1.0 / 128, bias=epsc[:, 0:1]), [b_st, b_eps], [b_st])
                P.op("act", lambda e: e.activation(out=st[:, 0:8], in_=st[:, 0:8], func=AF.Exp, scale=-0.5), [b_st], [b_st])
                P.op("dve", lambda e: e.tensor_tensor(out=junk[:].rearrange("p (g d) -> p g d", d=128), in0=o1[:].rearrange("p (g d) -> p g d", d=128), in1=st[:, 0:8].unsqueeze(2).to_broadcast([128, 8, 128]), op=ALU.mult), [b_o1, b_st], [b_junk])
                P.op("pool", lambda e: e.tensor_tensor(out=junk[:].rearrange("p (g d) -> p g d", d=128), in0=junk[:].rearrange("p (g d) -> p g d", d=128), in1=gnw.unsqueeze(1).to_broadcast([128, 8, 128]), op=ALU.mult), [b_junk, b_vecs], [b_junk])
                P.op("dve", lambda e: e.tensor_tensor(out=ya[:], in0=junk[:], in1=zs4[:], op=ALU.mult), [b_junk, b_zs4], [b_ya])
                for h in range(H):
                    bank = h // 4
                    P.op("pe", lambda e, h=h, bank=bank: e.matmul(psum[bank][:, (h % 4) * 128:(h % 4 + 1) * 128], lhsT=ya[:, h * 128:(h + 1) * 128], rhs=identb, start=True, stop=True), [b_ya, b_cbf], [PB[bank]])
                for hf in range(2):
                    P.op("act", lambda e, hf=hf, i=i: e.activation(out=YaT[:, hf * 4:(hf + 1) * 4, :], in_=psum[hf][:, :].rearrange("p (a b) -> p a b", a=4), func=AF.Copy), [PB[hf]], [b_YaT])
                for (wi, Y, bY, pb0, c0_) in ((0, YaT, b_YaT, 2, 0), (1, YbT, b_YbT, 4, i * 128)):
                    for hf in range(2):
                        for k in range(8):
                            P.op("pe", lambda e, wi=wi, Y=Y, hf=hf, k=k, pb0=pb0, c0_=c0_: e.matmul(psum[pb0 + hf][:, :], lhsT=Y[:, k, c0_:c0_ + 128], rhs=Wo[wi][:, k, hf * 512:(hf + 1) * 512], start=(k == 0), stop=(k == 7)),
                                 [bY, b_Wo[wi]], [PB[pb0 + hf]])
                for hf in range(2):
                    P.op("dve", lambda e, hf=hf: e.tensor_tensor(out=mg[:, hf * 512:(hf + 1) * 512], in0=psum[2 + hf][:, :], in1=gs4[:, hf * 512:(hf + 1) * 512], op=ALU.mult), [PB[2 + hf], b_gs4], [b_mg])
                    P.op("dve", lambda e, hf=hf: e.tensor_tensor(out=junk[:, hf * 512:(hf + 1) * 512], in0=psum[4 + hf][:, :], in1=gs4[:, 1024 + hf * 512:1024 + (hf + 1) * 512], op=ALU.mult), [PB[4 + hf], b_gs4], [b_junk])
                P.op("pool", lambda e: e.tensor_tensor(out=mgb[:], in0=mg[:], in1=junk[:], op=ALU.add), [b_mg, b_junk], [b_mgb])
                for k in range(8):
                    bank = 6 + k // 4
                    P.op("pe", lambda e, k=k, bank=bank: e.matmul(psum[bank][:, (k % 4) * 128:(k % 4 + 1) * 128], lhsT=mgb[:, k * 128:(k + 1) * 128], rhs=identb, start=True, stop=True), [b_mgb, b_cbf], [PB[bank]])
                for hf in range(2):
                    P.op("act", lambda e, hf=hf: e.activation(out=mT[:, hf * 4:(hf + 1) * 4, :], in_=psum[6 + hf][:, :].rearrange("p (a b) -> p a b", a=4), func=AF.Copy), [PB[6 + hf]], [b_mT])
                for hf in range(2):
                    for k in range(8):
                        P.op("pe", lambda e, hf=hf, k=k: e.matmul(psum[hf][:, :], lhsT=mT[:, k, :], rhs=Wo[2][:, k, hf * 512:(hf + 1) * 512], start=(k == 0), stop=(k == 7)), [b_mT, b_Wo[2]], [PB[hf]])
                    P.op("dve", lambda e, hf=hf: e.tensor_tensor(out=res[:, hf * 512:(hf + 1) * 512], in0=psum[hf][:, :], in1=mod[:, 2, hf * 512:(hf + 1) * 512], op=ALU.mult), [PB[hf], b_mod], [b_res])
                P.op("pool", lambda e: e.tensor_tensor(out=res[:], in0=res[:], in1=xt[:], op=ALU.add), [b_res, b_xt], [b_res])
                tok = P.dma(OUT[i * 128:(i + 1) * 128, :], res[:], [b_res], [b_OUT], b_OUT)
            P.final_wait("sp", [tok])

        print('total ops', P.gidx, {n: len(P.q[n]) for n in P.names})
        P.emit()
    return nc


def _consts():
    i = np.arange(128)
    ident = np.eye(128, dtype=np.float32)
    J = ident[::-1].copy()
    ones = np.ones((128, 128), np.float32)
    Ltri = (i[:, None] <= i[None, :]).astype(np.float32)
    negMs = -(i[:, None] > i[None, :]).astype(np.float32)
    negMsT = -(i[None, :] > i[:, None]).astype(np.float32)
    MiT = (i[None, :] >= i[:, None]).astype(np.float32)
    SelM = np.zeros((128, 132), np.float32)
    SelM[i, i + 2] = 1
    SelH = np.zeros((128, 132), np.float32)
    SelH[0, 0] = 1
    SelH[1, 1] = 1
    SelH[2, 130] = 1
    SelH[3, 131] = 1
    blk = lambda b: ((i[:, None] // b) == (i[None, :] // b)).astype(np.float32)
    BD32 = blk(32)
    M64 = blk(64) - blk(32)
    M128 = 1.0 - blk(64)
    return np.concatenate([ident, J, ones, Ltri, negMs, negMsT, MiT, SelM, SelH, BD32, M64, M128], axis=1)


def _rope_table(pos):
    pos = np.asarray(pos)
    row = (pos // 64).astype(np.float32)
    col = (pos % 64).astype(np.float32)
    inv = (10000.0 ** (-np.arange(16, dtype=np.float32) / 16)).astype(np.float32)
    ar = row[:, None] * inv
    ac = col[:, None] * inv
    t = np.concatenate([np.cos(ar), np.sin(ar), np.cos(ac), np.sin(ac)], axis=1).astype(np.float32)
    t[pos < 0] = np.array([1.0] * 16 + [0.0] * 16 + [1.0] * 16 + [0.0] * 16, np.float32)
    return t


def _core_plan(r):
    fwd_first = r >= 2
    own = list(range(16 * r, 16 * r + 16))
    left = list(range(0, 16 * r))
    right = list(range(16 * r + 16, 64))

    def scan(forward, nslots, filler):
        rev = not forward
        vis = [("c", 0, rev, 1.0), ("c", 1, rev, 1.0)]
        if rev:
            vis = [("c", 1, rev, 1.0), ("c", 0, rev, 1.0)]
        pref = left if forward else right[::-1]
        fill = [("l", b, rev, 0.0) for b in filler]
        vis += fill[:nslots - len(pref)] + [("l", b, rev, 1.0) for b in pref]
        o = own if forward else own[::-1]
        vis += [("l", b, rev, 1.0) for b in o]
        return vis

    if fwd_first:
        filler = right[::-1]
        v1 = scan(True, 48, filler)
        v2 = scan(False, 16, [own[0]] * 16)
    else:
        filler = left
        v1 = scan(False, 48, filler)
        v2 = scan(True, 16, [own[0]] * 16)
    assert len(v1) == 66 and len(v2) == 34
    return v1 + v2, fwd_first


_NC_CACHE = {}


def kernel(x, c, ctx, c_ctx, w_ada, b_ada, w_in, conv_w, a_log, dt_bias, gdn_norm_w,
           q_norm_w, k_norm_w, lambda_q1, lambda_k1, lambda_q2, lambda_k2, diff_norm_w,
           w_oa, w_ob, w_out):
    f = lambda a: np.ascontiguousarray(np.asarray(a, dtype=np.float32))
    x, c, ctx, c_ctx = f(x), f(c), f(ctx), f(c_ctx)
    w_ada, b_ada, w_in, conv_w = f(w_ada)[0], f(b_ada)[0], f(w_in)[0], f(conv_w)[0]
    a_log, dt_bias = f(a_log)[0], f(dt_bias)[0]
    vec = lambda a: f(a)[0]
    if "nc" not in _NC_CACHE:
        _NC_CACHE["nc"] = build_nc()
    nc = _NC_CACHE["nc"]
    cst = _consts()
    vecs = np.zeros((768,), np.float32)
    vecs[0:128] = vec(gdn_norm_w)
    vecs[128:192] = vec(q_norm_w)
    vecs[192:256] = vec(k_norm_w)
    vecs[256:320] = vec(lambda_q1)
    vecs[320:384] = vec(lambda_k1)
    vecs[384:448] = vec(lambda_q2)
    vecs[448:512] = vec(lambda_k2)
    vecs[512:640] = vec(diff_norm_w)
    VECS = np.ascontiguousarray(np.broadcast_to(vecs, (128, 768)))
    BADA = np.ascontiguousarray(np.broadcast_to(b_ada, (128, 3 * D)))
    in_maps = []
    plans = []
    for core in range(8):
        b, r = divmod(core, 4)
        plan, fwd_first = _core_plan(r)
        plans.append((plan, fwd_first))
        XV = np.zeros((NV * 128, D), np.float32)
        XH = np.zeros((NV * 4, D), np.float32)
        HM = np.zeros((4, NV), np.float32)
        GM = np.zeros((128, NV), np.float32)
        ROPE = np.zeros((NKV * 128, 64), np.float32)
        for v, (seq, blk, rev, gmask) in enumerate(plan):
            src = ctx[b] if seq == "c" else x[b]
            n = src.shape[0]
            idx = np.arange(blk * 128, blk * 128 + 128)
            halo = np.array([blk * 128 - 2, blk * 128 - 1, blk * 128 + 128, blk * 128 + 129])
            if rev:
                idx = idx[::-1]
                halo = halo[::-1]
            XV[v * 128:(v + 1) * 128] = src[idx]
            valid = (halo >= 0) & (halo < n)
            XH[v * 4:(v + 1) * 4][valid] = src[halo[valid]]
            HM[:, v] = valid.astype(np.float32)
            GM[:, v] = gmask
            if v < NKV:
                pos = idx if seq == "l" else -np.ones(128, np.int64)
                ROPE[v * 128:(v + 1) * 128] = _rope_table(pos)
        d1 = 0 if fwd_first else 1
        dirs = (d1, 1 - d1)
        WAB = np.stack([np.concatenate([w_in[:, 4096 + d * 8:4096 + d * 8 + 8], w_in[:, 4112 + d * 8:4112 + d * 8 + 8]], axis=1) for d in dirs])
        cwt = np.ascontiguousarray(conv_w.T)
        CW = np.stack([cwt if d == 0 else np.ascontiguousarray(cwt[:, ::-1]) for d in dirs])
        ALG = np.stack([np.broadcast_to(a_log[d], (128, 8)) for d in dirs])
        DTB = np.stack([np.broadcast_to(dt_bias[d], (128, 8)) for d in dirs])
        cv = np.stack([c[b], c_ctx])
        CT = np.ascontiguousarray(cv.reshape(2, 8, 128).transpose(2, 0, 1).reshape(128, 16))
        in_maps.append({
            "XV": XV, "XH": XH, "HM": HM, "GM": GM, "ROPE": ROPE, "CT": CT, "WADA": w_ada, "BADA": BADA,
            "WIN": w_in, "WAB": np.ascontiguousarray(WAB), "CW": np.ascontiguousarray(CW),
            "ALG": np.ascontiguousarray(ALG), "DTB": np.ascontiguousarray(DTB), "VECS": VECS, "CST": cst,
            "WOA": f(w_oa)[0], "WOB": f(w_ob)[0], "WOUT": f(w_out)[0],
        })
    if _NC_CACHE.get("maps_only"):
        return in_maps, plans
    res = run_bass_kernel_spmd(nc, in_maps, core_ids=list(range(8)))
    out = np.zeros((2, L, D), np.float32)
    for core in range(8):
        b, r = divmod(core, 4)
        plan, fwd_first = plans[core]
        o = np.asarray(res.results[core]["OUT"])
        for i in range(16):
            seq, blk, rev, _ = plan[OWN0 + i]
            idx = np.arange(blk * 128, blk * 128 + 128)
            if rev:
                idx = idx[::-1]
            out[b, idx] = o[i * 128:(i + 1) * 128]
    return out
```

```python
import math
from contextlib import ExitStack

import numpy as np
import concourse.bass as bass
import concourse.mybir as mybir
from concourse.bass_utils import run_bass_kernel_spmd

F32 = mybir.dt.float32
BF16 = mybir.dt.bfloat16
AF = mybir.ActivationFunctionType
ALU = mybir.AluOpType
AX = mybir.AxisListType

D = 1024
L = 8192
NCTX = 256
H = 8
EPS = 1e-6
NV = 100
NKV = 66
OWN0 = 50
S2 = 66
OWN1 = 84
IN_COLS = 10272
LAMBDA_INIT = 0.8 - 0.6 * math.exp(-0.3 * 0)
DEBUG = False
import os
MAXOPS = int(os.environ.get('KMAXOPS', '1000000000'))
TRACE_LO = int(os.environ.get('KTLO', '0'))
TRACE_HI = int(os.environ.get('KTHI', '-1'))


class Buf:
    __slots__ = ("w", "r", "dsem", "dcnt")

    def __init__(self):
        self.w = None
        self.r = {}
        self.dsem = None
        self.dcnt = 0


class Prog:
    def __init__(self, nc, es):
        self.nc = nc
        self.es = es
        self.names = ["pe", "act", "dve", "pool", "sp"]
        self.sem = {n: es.enter_context(nc.semaphore("s_" + n)) for n in self.names}
        self.cnt = {n: 0 for n in self.names}
        self.waited = {n: {} for n in self.names}
        self.q = {n: [] for n in self.names}
        self.nsem = 0
        self.dbufs = []
        self.gidx = 0
        self.maxops = MAXOPS

    def _waits(self, eng, reads, writes):
        need = {}

        def add(tok):
            if tok is None:
                return
            s, v = tok
            k = id(s)
            if k not in need or need[k][1] < v:
                need[k] = (s, v)

        for b in reads:
            add(b.w)
        for b in writes:
            add(b.w)
            for t in b.r.values():
                add(t)
        out = []
        own = id(self.sem[eng])
        for k, (s, v) in need.items():
            if eng == "pe" and k == own:
                continue
            if self.waited[eng].get(k, 0) >= v:
                continue
            self.waited[eng][k] = v
            out.append((s, v))
        return out

    def _commit(self, tok, reads, writes):
        for b in reads:
            b.r[id(tok[0])] = tok
        for b in writes:
            b.w = tok
            b.r = {}

    def op(self, eng, fn, reads=(), writes=()):
        self.gidx += 1
        if TRACE_LO <= self.gidx <= TRACE_HI:
            import inspect
            print("OP", self.gidx, eng, inspect.currentframe().f_back.f_lineno)
        if self.gidx > self.maxops:
            return
        waits = self._waits(eng, reads, writes)
        self.cnt[eng] += 1
        tok = (self.sem[eng], self.cnt[eng])
        self.q[eng].append((waits, fn, self.sem[eng], 1))
        self._commit(tok, reads, writes)

    def dma(self, out_ap, in_ap, reads, writes, dst):
        eng = "sp"
        self.gidx += 1
        if TRACE_LO <= self.gidx <= TRACE_HI:
            import inspect
            print("DMA", self.gidx, inspect.currentframe().f_back.f_lineno)
        if self.gidx > self.maxops:
            return (None, 0)
        waits = self._waits(eng, reads, writes)
        if dst.dsem is None:
            dst.dsem = self.es.enter_context(self.nc.semaphore("d%d" % self.nsem))
            self.nsem += 1
            self.dbufs.append(dst)
        dst.dcnt += 16
        tok = (dst.dsem, dst.dcnt)
        self.q[eng].append((waits, lambda e: e.dma_start(out=out_ap, in_=in_ap), dst.dsem, 16))
        self._commit(tok, reads, writes)
        return tok

    def barrier(self):
        toks = [(self.sem[n], self.cnt[n]) for n in self.names if self.cnt[n] > 0]
        toks += [(b.dsem, b.dcnt) for b in self.dbufs]
        for eng in self.names:
            waits = []
            for (s_, v) in toks:
                k = id(s_)
                if eng == "pe" and k == id(self.sem["pe"]):
                    continue
                if self.waited[eng].get(k, 0) >= v:
                    continue
                self.waited[eng][k] = v
                waits.append((s_, v))
            if waits:
                self.q[eng].append((waits, None, None, 0))

    def final_wait(self, eng, toks):
        toks = [t for t in toks if t[0] is not None]
        if not toks:
            toks = [(self.sem[n], self.cnt[n]) for n in self.names if self.cnt[n] > 0 and n != eng]
            toks += [(b.dsem, b.dcnt) for b in self.dbufs]
        self.q[eng].append((toks, None, None, 0))

    def emit(self):
        nc = self.nc
        with nc.Block() as block:
            def run(name):
                def body(e):
                    for waits, fn, sem, inc in self.q[name]:
                        for s, v in waits:
                            e.wait_ge(s, v)
                        if fn is not None:
                            fn(e).then_inc(sem, inc)
                return body
            block.tensor(run("pe"))
            block.scalar(run("act"))
            block.vector(run("dve"))
            block.gpsimd(run("pool"))
            block.sync(run("sp"))


def build_nc():
    nc = bass.Bass("TRN2", target_bir_lowering=False)

    def din(name, shape, dt=F32):
        return nc.dram_tensor(name, list(shape), dt, kind="ExternalInput").ap()

    XV = din("XV", [NV * 128, D])
    XH = din("XH", [NV * 4, D])
    HM = din("HM", [4, NV])
    GM = din("GM", [128, NV])
    ROPE = din("ROPE", [NKV * 128, 64])
    CT = din("CT", [128, 16])
    WADA = din("WADA", [D, 3 * D])
    BADA = din("BADA", [128, 3 * D])
    WIN = din("WIN", [D, IN_COLS])
    WAB = din("WAB", [2, D, 16])
    CW = din("CW", [2, 3072, 5])
    ALG = din("ALG", [2, 128, 8])
    DTB = din("DTB", [2, 128, 8])
    VECS = din("VECS", [128, 768])
    CST = din("CST", [128, 10 * 128 + 2 * 132])
    WOA = din("WOA", [D, D])
    WOB = din("WOB", [D, D])
    WOUT = din("WOUT", [D, D])
    OUT = nc.dram_tensor("OUT", [2048, D], F32, kind="ExternalOutput").ap()

    KTS = nc.dram_tensor("KTS", [H, 128, NKV * 128], BF16).ap()
    VPS = nc.dram_tensor("VPS", [H, 128, NKV, 129], BF16).ap()
    OS = nc.dram_tensor("OS", [2, 16, 128, D], F32).ap()
    ZS = nc.dram_tensor("ZS", [16, 128, 2048], BF16).ap()
    GS = nc.dram_tensor("GS", [16, 128, 2048], BF16).ap()

    es = ExitStack()
    with es:
        P = Prog(nc, es)

        def sb(name, shape, dt=F32, stack=es):
            return stack.enter_context(nc.sbuf_tensor(name, list(shape), dt))

        psum = [es.enter_context(nc.psum_tensor("ps%d" % i, [128, 512], F32)) for i in range(8)]
        PB = [Buf() for _ in range(8)]

        cst = sb("cst", [128, 10 * 128 + 2 * 132]); b_cst = Buf()
        ident = cst[:, 0:128]
        Jm = cst[:, 128:256]
        ones = cst[:, 256:384]
        Ltri = cst[:, 384:512]
        negMs = cst[:, 512:640]
        negMsT = cst[:, 640:768]
        MiT = cst[:, 768:896]
        BD32 = cst[:, 1160:1288]
        M64 = cst[:, 1288:1416]
        M128 = cst[:, 1416:1544]
        cbf = sb("cbf", [128, 128 + 2 * 132], BF16); b_cbf = Buf()
        identb = cbf[:, 0:128]
        SelM = cbf[:, 128:260]
        SelH = cbf[0:4, 260:392]
        vecs = sb("vecs", [128, 768]); b_vecs = Buf()
        gnw = vecs[:, 0:128]
        qnw = vecs[:, 128:192]
        knw = vecs[:, 192:256]
        dnw = vecs[:, 512:640]
        hm = sb("hm", [4, NV]); b_hm = Buf()
        gm = sb("gm", [128, NV]); b_gm = Buf()
        mod = sb("mod", [128, 5, D]); b_mod = Buf()
        negA = sb("negA", [128, 2, 8]); b_negA = Buf()
        dtb = sb("dtb", [128, 2, 8]); b_dtb = Buf()
        lamt = sb("lamt", [128, 4]); b_lam = Buf()
        epsc = sb("epsc", [128, 1]); b_eps = Buf()

        P.dma(cst[:], CST[:, :], [], [b_cst], b_cst)
        P.dma(vecs[:], VECS[:, :], [], [b_vecs], b_vecs)
        P.dma(hm[:], HM[:, :], [], [b_hm], b_hm)
        P.dma(gm[:], GM[:, :], [], [b_gm], b_gm)
        P.dma(negA[:], ALG.rearrange("s p h -> p s h"), [], [b_negA], b_negA)
        P.dma(dtb[:], DTB.rearrange("s p h -> p s h"), [], [b_dtb], b_dtb)
        P.op("act", lambda e: e.activation(out=negA[:], in_=negA[:], func=AF.Exp), [b_negA], [b_negA])
        P.op("dve", lambda e: e.tensor_scalar_mul(out=negA[:], in0=negA[:], scalar1=-1.0), [b_negA], [b_negA])
        P.op("dve", lambda e: e.tensor_copy(out=cbf[:, 0:128], in_=cst[:, 0:128]), [b_cst], [b_cbf])
        P.op("dve", lambda e: e.tensor_copy(out=cbf[:, 128:392], in_=cst[:, 896:1160]), [b_cst], [b_cbf])
        P.op("pool", lambda e: e.memset(epsc[:], EPS), [], [b_eps])

        ltmp = sb("ltmp", [128, 128]); b_ltmp = Buf()
        P.op("dve", lambda e: e.tensor_tensor(out=ltmp[:, 0:64], in0=vecs[:, 256:320], in1=vecs[:, 320:384], op=ALU.mult), [b_vecs], [b_ltmp])
        P.op("dve", lambda e: e.tensor_tensor(out=ltmp[:, 64:128], in0=vecs[:, 384:448], in1=vecs[:, 448:512], op=ALU.mult), [b_vecs], [b_ltmp])
        P.op("dve", lambda e: e.tensor_reduce(out=lamt[:, 2:4], in_=ltmp[:].rearrange("p (a b) -> p a b", a=2), axis=AX.X, op=ALU.add), [b_ltmp], [b_lam])
        P.op("act", lambda e: e.activation(out=lamt[:, 2:4], in_=lamt[:, 2:4], func=AF.Exp), [b_lam], [b_lam])
        P.op("dve", lambda e: e.tensor_tensor(out=lamt[:, 0:1], in0=lamt[:, 3:4], in1=lamt[:, 2:3], op=ALU.subtract), [b_lam], [b_lam])
        P.op("dve", lambda e: e.tensor_scalar_add(out=lamt[:, 0:1], in0=lamt[:, 0:1], scalar1=-LAMBDA_INIT), [b_lam], [b_lam])
        P.op("dve", lambda e: e.tensor_scalar_mul(out=ltmp[:], in0=vecs[:, 128:256], scalar1=-1.0), [b_vecs, b_lam], [b_ltmp])
        P.op("dve", lambda e: e.tensor_tensor(out=ltmp[:], in0=ltmp[:], in1=vecs[:, 128:256], op=ALU.max), [b_vecs, b_ltmp], [b_ltmp])
        P.op("dve", lambda e: e.tensor_reduce(out=lamt[:, 2:4], in_=ltmp[:].rearrange("p (a b) -> p a b", a=2), axis=AX.X, op=ALU.max), [b_ltmp, b_lam], [b_lam])
        P.op("dve", lambda e: e.scalar_tensor_tensor(out=lamt[:, 1:2], in0=lamt[:, 2:3], scalar=-8.0, in1=lamt[:, 3:4], op0=ALU.mult, op1=ALU.mult), [b_lam], [b_lam])

        with ExitStack() as s0:
            ct = sb("ct", [128, 16], stack=s0); b_ct = Buf()
            cbl = sb("cbl", [128, 16, 128], stack=s0); b_cbl = Buf()
            wst = [sb("wada%d" % i, [128, 3 * D], stack=s0) for i in range(2)]
            b_wst = [Buf(), Buf()]
            bada = sb("bada", [128, 3 * D], stack=s0); b_bada = Buf()
            P.dma(ct[:], CT[:, :], [], [b_ct], b_ct)
            P.dma(bada[:], BADA[:, :], [], [b_bada], b_bada)
            P.op("act", lambda e: e.activation(out=ct[:], in_=ct[:], func=AF.Silu), [b_ct], [b_ct])
            for j in range(16):
                P.op("dve", lambda e, j=j: e.tensor_scalar_mul(out=cbl[:, j, :], in0=ones, scalar1=ct[:, j:j + 1]), [b_ct, b_cst], [b_cbl])
            for k in range(8):
                P.dma(wst[k % 2][:], WADA[k * 128:(k + 1) * 128, :], [], [b_wst[k % 2]], b_wst[k % 2])
                for g in range(6):
                    P.op("pe", lambda e, k=k, g=g: e.matmul(psum[g][:, :], lhsT=cbl[:, k, :], rhs=wst[k % 2][:, g * 512:(g + 1) * 512], start=(k == 0), stop=(k == 7)),
                         [b_cbl, b_wst[k % 2]], [PB[g]])
            for g in range(6):
                dsti = [1, 1, 0, 0, 2, 2][g]
                addc = 1.0 if dsti == 0 else 0.0
                P.op("dve", lambda e, g=g, dsti=dsti: e.tensor_tensor(out=mod[:, dsti, (g % 2) * 512:(g % 2) * 512 + 512], in0=psum[g][:, :], in1=bada[:, g * 512:(g + 1) * 512], op=ALU.add),
                     [PB[g], b_bada], [b_mod])
            P.op("pool", lambda e: e.tensor_scalar_add(out=mod[:, 0, :], in0=mod[:, 0, :], scalar1=1.0), [b_mod], [b_mod])
            for k in range(8):
                P.dma(wst[k % 2][:], WADA[k * 128:(k + 1) * 128, :], [], [b_wst[k % 2]], b_wst[k % 2])
                for g in range(4):
                    P.op("pe", lambda e, k=k, g=g: e.matmul(psum[g][:, :], lhsT=cbl[:, 8 + k, :], rhs=wst[k % 2][:, g * 512:(g + 1) * 512], start=(k == 0), stop=(k == 7)),
                         [b_cbl, b_wst[k % 2]], [PB[g]])
            for g in range(4):
                dsti = [4, 4, 3, 3][g]
                P.op("dve", lambda e, g=g, dsti=dsti: e.tensor_tensor(out=mod[:, dsti, (g % 2) * 512:(g % 2) * 512 + 512], in0=psum[g][:, :], in1=bada[:, g * 512:(g + 1) * 512], op=ALU.add),
                     [PB[g], b_bada], [b_mod])
            P.op("pool", lambda e: e.tensor_scalar_add(out=mod[:, 3, :], in0=mod[:, 3, :], scalar1=1.0), [b_mod], [b_mod])

        P.barrier()
        xt = sb("xt", [128, D]); b_xt = Buf()
        xh = sb("xh", [4, D]); b_xh = Buf()
        junk = sb("junk", [128, D]); b_junk = Buf()
        st = sb("st", [128, 8]); b_st = Buf()
        hb = sb("hb", [128, D], BF16); b_hb = Buf()
        hhb = sb("hhb", [4, D], BF16); b_hhb = Buf()
        HT = sb("HT", [128, 8, 132], BF16); b_HT = Buf()

        def make_HT(v, is_ctx):
            mi = 3 if is_ctx else 0
            P.dma(xt[:], XV[v * 128:(v + 1) * 128, :], [], [b_xt], b_xt)
            P.dma(xh[:], XH[v * 4:(v + 1) * 4, :], [], [b_xh], b_xh)
            for (src, bsrc, np_, col, dst, bdst) in ((xt, b_xt, 128, 0, hb, b_hb), (xh, b_xh, 4, 2, hhb, b_hhb)):
                P.op("act", lambda e, src=src, np_=np_, col=col: e.activation(out=junk[0:np_, :], in_=src[0:np_, :], func=AF.Square, accum_out=st[0:np_, col:col + 1]),
                     [bsrc], [b_junk, b_st])
                P.op("act", lambda e, np_=np_, col=col: e.activation(out=st[0:np_, col + 1:col + 2], in_=st[0:np_, col:col + 1], func=AF.Ln, scale=1.0 / D, bias=epsc[0:np_, 0:1]),
                          [b_st, b_eps], [b_st])
                P.op("act", lambda e, np_=np_, col=col: e.activation(out=st[0:np_, col + 1:col + 2], in_=st[0:np_, col + 1:col + 2], func=AF.Exp, scale=-0.5),
                     [b_st], [b_st])
                if np_ == 4:
                    P.op("dve", lambda e, v=v: e.tensor_tensor(out=st[0:4, 3:4], in0=st[0:4, 3:4], in1=hm[:, v:v + 1], op=ALU.mult), [b_st, b_hm], [b_st])
                P.op("dve", lambda e, src=src, np_=np_, col=col: e.scalar_tensor_tensor(out=junk[0:np_, :], in0=src[0:np_, :], scalar=st[0:np_, col + 1:col + 2], in1=mod[0:np_, mi, :], op0=ALU.mult, op1=ALU.mult),
                     [bsrc, b_st, b_mod, b_junk], [b_junk])
                if np_ == 4:
                    P.op("dve", lambda e, v=v, dst=dst: e.scalar_tensor_tensor(out=dst[0:4, :], in0=mod[0:4, mi + 1, :], scalar=hm[:, v:v + 1], in1=junk[0:4, :], op0=ALU.mult, op1=ALU.add),
                         [b_junk, b_mod, b_hm], [bdst])
                else:
                    P.op("pool", lambda e, dst=dst: e.tensor_tensor(out=dst[:, :], in0=junk[:, :], in1=mod[:, mi + 1, :], op=ALU.add), [b_junk, b_mod], [bdst])
            for k in range(8):
                bank, slot = divmod(k, 3)
                o = psum[bank][:, slot * 132:(slot + 1) * 132]
                P.op("pe", lambda e, k=k, o=o: e.matmul(o, lhsT=hb[:, k * 128:(k + 1) * 128], rhs=SelM, start=True, stop=False), [b_hb, b_cbf], [PB[bank]])
                P.op("pe", lambda e, k=k, o=o: e.matmul(o, lhsT=hhb[0:4, k * 128:(k + 1) * 128], rhs=SelH, start=False, stop=True), [b_hhb, b_cbf], [PB[bank]])
            for bank in range(3):
                n = 3 if bank < 2 else 2
                P.op("act", lambda e, bank=bank, n=n: e.activation(out=HT[:, bank * 3:bank * 3 + n, :], in_=psum[bank][:, 0:n * 132].rearrange("p (a b) -> p a b", a=n), func=AF.Copy),
                     [PB[bank]], [b_HT])

        def load_w_bf16(dst, bdst, col0, ncols, stage, bstage, dcol0=0):
            i = 0
            for k in range(8):
                for c0 in range(0, ncols, 1024):
                    n = min(1024, ncols - c0)
                    s_, bs_ = stage[i % 2], bstage[i % 2]
                    P.dma(s_[:, 0:n], WIN[k * 128:(k + 1) * 128, col0 + c0:col0 + c0 + n], [], [bs_], bs_)
                    eng = ["act", "pool", "dve"][i % 3]
                    if eng == "act":
                        P.op("act", lambda e, s_=s_, k=k, c0=c0, n=n: e.activation(out=dst[:, k, dcol0 + c0:dcol0 + c0 + n], in_=s_[:, 0:n], func=AF.Copy), [bs_], [bdst])
                    else:
                        P.op(eng, lambda e, s_=s_, k=k, c0=c0, n=n: e.tensor_copy(out=dst[:, k, dcol0 + c0:dcol0 + c0 + n], in_=s_[:, 0:n]), [bs_], [bdst])
                    i += 1

        def qk_norm_rope(ps_banks, pbufs, gain, tab, b_tab, dst, b_dst, tmp, b_tmp):
            for hf in range(2):
                P.op("act", lambda e, hf=hf: e.activation(out=tmp[0][:, hf * 512:(hf + 1) * 512], in_=ps_banks[hf][:, :], func=AF.Square), [pbufs[hf]], [b_tmp[0]])
            P.op("dve", lambda e: e.tensor_reduce(out=tmp[3][:, 0:16], in_=tmp[0][:].rearrange("p (g d) -> p g d", d=64), axis=AX.X, op=ALU.add),
                 [b_tmp[0]], [b_tmp[3]])
            P.op("act", lambda e: e.activation(out=tmp[3][:, 0:16], in_=tmp[3][:, 0:16], func=AF.Ln, scale=1.0 / 64, bias=epsc[:, 0:1]), [b_tmp[3], b_eps], [b_tmp[3]])
            P.op("act", lambda e: e.activation(out=tmp[3][:, 0:16], in_=tmp[3][:, 0:16], func=AF.Exp, scale=-0.5), [b_tmp[3]], [b_tmp[3]])
            for hf in range(2):
                P.op("dve", lambda e, hf=hf: e.tensor_tensor(out=tmp[0][:, hf * 512:(hf + 1) * 512].rearrange("p (g d) -> p g d", d=64),
                                                            in0=ps_banks[hf][:, :].rearrange("p (g d) -> p g d", d=64),
                                                            in1=tmp[3][:, hf * 8:(hf + 1) * 8].unsqueeze(2).to_broadcast([128, 8, 64]), op=ALU.mult),
                     [pbufs[hf], b_tmp[3], b_tmp[0]], [b_tmp[0]])
            P.op("pool", lambda e: e.tensor_tensor(out=tmp[1][:].rearrange("p (g d) -> p g d", d=64), in0=tmp[0][:].rearrange("p (g d) -> p g d", d=64),
                                                   in1=gain.unsqueeze(1).to_broadcast([128, 16, 64]), op=ALU.mult), [b_tmp[0], b_vecs], [b_tmp[1]])
            xv = tmp[1][:].rearrange("p (g h t f) -> p g h t f", g=16, h=2, t=2)
            tv = tab[:].rearrange("p (h t f) -> p h t f", h=2, t=2)
            dv = dst[:].rearrange("p (g h t f) -> p g h t f", g=16, h=2, t=2)
            av = tmp[2][:].rearrange("p (g h t f) -> p g h t f", g=16, h=2, t=2)
            for hh in range(2):
                x1 = xv[:, :, hh, 0, :]
                x2 = xv[:, :, hh, 1, :]
                cs = tv[:, hh, 0, :].unsqueeze(1).to_broadcast([128, 16, 16])
                sn = tv[:, hh, 1, :].unsqueeze(1).to_broadcast([128, 16, 16])
                a1 = av[:, :, hh, 0, :]
                a2 = av[:, :, hh, 1, :]
                e1, e2 = ("dve", "pool")
                P.op(e1, lambda e, x1=x1, cs=cs, a1=a1: e.tensor_tensor(out=a1, in0=x1, in1=cs, op=ALU.mult), [b_tmp[1], b_tab], [b_tmp[2]])
                P.op(e2, lambda e, x2=x2, sn=sn, a2=a2: e.tensor_tensor(out=a2, in0=x2, in1=sn, op=ALU.mult), [b_tmp[1], b_tab], [b_tmp[2]])
                P.op(e1, lambda e, a1=a1, a2=a2, hh=hh: e.tensor_tensor(out=dv[:, :, hh, 0, :], in0=a1, in1=a2, op=ALU.subtract), [b_tmp[2]], [b_dst])
                P.op(e1, lambda e, x2=x2, cs=cs, a1=a1: e.tensor_tensor(out=a1, in0=x2, in1=cs, op=ALU.mult), [b_tmp[1], b_tab, b_tmp[2]], [b_tmp[2]])
                P.op(e2, lambda e, x1=x1, sn=sn, a2=a2: e.tensor_tensor(out=a2, in0=x1, in1=sn, op=ALU.mult), [b_tmp[1], b_tab, b_tmp[2]], [b_tmp[2]])
                P.op(e1, lambda e, a1=a1, a2=a2, hh=hh: e.tensor_tensor(out=dv[:, :, hh, 1, :], in0=a1, in1=a2, op=ALU.add), [b_tmp[2]], [b_dst])


        with ExitStack() as s1:
            W1 = sb("W1", [128, 8, 5120], BF16, stack=s1); b_W1 = Buf()
            wab = sb("wab", [128, 2, 8, 16], BF16, stack=s1); b_wab = Buf()
            wabs = sb("wabs", [128, 2, 8, 16], stack=s1); b_wabs = Buf()
            cw = sb("cw", [128, 2, 24, 5], stack=s1); b_cw = Buf()
            stage = [junk, xt]
            bstage = [b_junk, b_xt]
            tab = sb("tab1", [128, 64], stack=s1); b_tab = Buf()
            tmpA = [sb("tmpA1_%d" % i, [128, D], stack=s1) for i in range(3)] + [sb("tmpA31", [128, 16], stack=s1)]
            b_tmpA = [Buf() for _ in range(4)]
            krot = sb("krot1", [128, D], BF16, stack=s1); b_krot = Buf()
            load_w_bf16(W1, b_W1, 0, 3072, stage, bstage, 0)
            load_w_bf16(W1, b_W1, 5152, 2048, stage, bstage, 3072)
            P.dma(wabs[:], WAB.rearrange("s (k p) c -> p s k c", p=128), [], [b_wabs], b_wabs)
            P.op("dve", lambda e: e.tensor_copy(out=wab[:], in_=wabs[:]), [b_wabs], [b_wab])
            P.dma(cw[:], CW.rearrange("s (c p) j -> p s c j", p=128), [], [b_cw], b_cw)

            Sst = sb("Sst", [128, H, 128], stack=s1); b_S = [Buf() for _ in range(H)]
            pre = sb("pre", [128, 3, 132], stack=s1); b_pre = Buf()
            acc = [sb("acc%d" % i, [128, 128], stack=s1) for i in range(2)]; b_acc = [Buf(), Buf()]
            ptmp = sb("ptmp", [128, 128], stack=s1); b_ptmp = Buf()
            qkvT = sb("qkvT", [128, 24, 128], stack=s1); b_qkvT = [Buf() for _ in range(24)]
            gb = sb("gb", [128, 6, 8], stack=s1); b_gb = Buf()
            from types import SimpleNamespace
            TS = []
            for si_ in range(2):
                T = SimpleNamespace()
                T.Wb = [sb("Wb%d_%d" % (i, si_), [128, 512], stack=s1) for i in range(2)]; T.b_Wb = [Buf(), Buf()]
                T.NY = sb("NY_%d" % si_, [128, 256], stack=s1); T.b_NY = Buf()
                T.NYm = sb("NYm_%d" % si_, [128, 256], stack=s1); T.b_NYm = Buf()
                T.TU = sb("TU_%d" % si_, [128, 256], stack=s1); T.b_TU = Buf()
                T.TU2 = sb("TU2_%d" % si_, [128, 256], stack=s1); T.b_TU2 = Buf()
                T.Wm = sb("Wm_%d" % si_, [128, 256], stack=s1); T.b_Wm = Buf()
                T.LgIb = sb("LgIb_%d" % si_, [128, 256], stack=s1); T.b_LgIb = Buf()
                T.Em = sb("Em_%d" % si_, [128, 128], stack=s1); T.b_Em = Buf()
                T.ETm = sb("ETm_%d" % si_, [128, 128], stack=s1); T.b_ETm = Buf()
                T.EMs = sb("EMs_%d" % si_, [128, 128], stack=s1); T.b_EMs = Buf()
                T.ETs = sb("ETs_%d" % si_, [128, 128], stack=s1); T.b_ETs = Buf()
                T.ETi = sb("ETi_%d" % si_, [128, 128], stack=s1); T.b_ETi = Buf()
                T.npos = T.LgIb; T.b_npos = T.b_LgIb
                T.sq = sb("sq_%d" % si_, [128, 128], stack=s1); T.b_sq = Buf()
                T.rs = sb("rs_%d" % si_, [128, 128], stack=s1); T.b_rs = Buf()
                T.kTn = sb("kTn_%d" % si_, [128, 128], stack=s1); T.b_kTn = Buf()
                T.qTn = sb("qTn_%d" % si_, [128, 128], stack=s1); T.b_qTn = Buf()
                T.Kbp = sb("Kbp_%d" % si_, [128, 128], stack=s1); T.b_Kbp = Buf()
                T.kd = sb("kd_%d" % si_, [128, 128], stack=s1); T.b_kd = Buf()
                T.Vb = sb("Vb_%d" % si_, [128, 128], stack=s1); T.b_Vb = Buf()
                T.TT = sb("TT_%d" % si_, [128, 128], stack=s1); T.b_TT = Buf()
                T.wT = sb("wT_%d" % si_, [128, 128], stack=s1); T.b_wT = Buf()
                T.usb = sb("usb_%d" % si_, [128, 128], stack=s1); T.b_usb = Buf()
                T.vnew = sb("vnew_%d" % si_, [128, 128], stack=s1); T.b_vnew = Buf()
                T.egl = sb("egl_%d" % si_, [128, 1], stack=s1); T.b_egl = Buf()
                T.eR = sb("eR_%d" % si_, [128, 128], stack=s1); T.b_eR = Buf()
                T.QdT = sb("QdT_%d" % si_, [128, 128], stack=s1); T.b_QdT = Buf()
                T.qkT = sb("qkT_%d" % si_, [128, 128], stack=s1); T.b_qkT = Buf()
                T.banks = (0, 1, 2, 3) if si_ == 0 else (4, 5, 6, 7)
                TS.append(T)
            osb = junk; b_osb = b_junk
            KTsb = hb[:].rearrange("p (h t) -> p h t", h=H); b_KTsb = b_hb
            Vp = sb("Vp", [128, H, 129], BF16, stack=s1); b_Vp = Buf()
            b_KTS = Buf(); b_VPS = Buf(); b_OS = Buf()
            P.op("pool", lambda e: e.memset(Vp[:], 1.0), [], [b_Vp])

            for v in range(NV):
                scan = 0 if v < S2 else 1
                is_ctx = v in (0, 1, S2, S2 + 1)
                own = (OWN0 <= v < S2) or (v >= OWN1)
                kv = v < NKV
                if v in (0, S2):
                    for h in range(H):
                        P.op("pool", lambda e, h=h: e.memset(Sst[:, h, :], 0.0), [], [b_S[h]])
                make_HT(v, is_ctx)
                PA = 3
                for k in range(8):
                    P.op("pe", lambda e, k=k, scan=scan: e.matmul(psum[PA][:, 0:16], lhsT=HT[:, k, 2:130], rhs=wab[:, scan, k, :], start=(k == 0), stop=(k == 7)), [b_HT, b_wab], [PB[PA]])
                P.op("dve", lambda e, scan=scan: e.tensor_tensor(out=gb[:, 4, :], in0=psum[PA][:, 0:8], in1=dtb[:, scan, :], op=ALU.add), [PB[PA], b_dtb], [b_gb])
                P.op("act", lambda e: e.activation(out=gb[:, 4, :], in_=gb[:, 4, :], func=AF.Exp), [b_gb], [b_gb])
                P.op("act", lambda e: e.activation(out=gb[:, 4, :], in_=gb[:, 4, :], func=AF.Ln, bias=1.0), [b_gb], [b_gb])
                P.op("dve", lambda e, v=v, scan=scan: e.scalar_tensor_tensor(out=gb[:, 0, :], in0=gb[:, 4, :], scalar=gm[:, v:v + 1], in1=negA[:, scan, :], op0=ALU.mult, op1=ALU.mult), [b_gb, b_gm, b_negA], [b_gb])
                P.op("act", lambda e: e.activation(out=gb[:, 5, :], in_=psum[PA][:, 8:16], func=AF.Sigmoid), [PB[PA], b_gb], [b_gb])
                P.op("dve", lambda e, v=v: e.tensor_scalar_mul(out=gb[:, 1, :], in0=gb[:, 5, :], scalar1=gm[:, v:v + 1]), [b_gb, b_gm], [b_gb])
                P.op("pe", lambda e: e.matmul(psum[PA][:, 16:24], lhsT=Ltri, rhs=gb[:, 0, :], start=True, stop=True), [b_cst, b_gb], [PB[PA]])
                P.op("dve", lambda e: e.tensor_copy(out=gb[:, 2, :], in_=psum[PA][:, 16:24]), [PB[PA], b_gb], [b_gb])
                P.op("act", lambda e: e.activation(out=gb[:, 3, :], in_=psum[PA][:, 16:24], func=AF.Exp), [PB[PA], b_gb], [b_gb])
                P.op("dve", lambda e: e.tensor_tensor(out=gb[:, 3, :], in0=gb[:, 3, :], in1=gb[:, 1, :], op=ALU.mult), [b_gb], [b_gb])

                cbs = list(range(24)) if own else list(range(8, 24))
                for gi in range(0, len(cbs), 3):
                    grp = cbs[gi:gi + 3]
                    bank = 4 + (gi // 3) % 2
                    for si, cb in enumerate(grp):
                        for k in range(8):
                            P.op("pe", lambda e, k=k, cb=cb, si=si, bank=bank: e.matmul(psum[bank][:, si * 132:(si + 1) * 132], lhsT=W1[:, k, cb * 128:(cb + 1) * 128], rhs=HT[:, k, :], start=(k == 0), stop=(k == 7)),
                                 [b_W1, b_HT], [PB[bank]])
                    n = len(grp)
                    P.op("act", lambda e, bank=bank, n=n: e.activation(out=pre[:, 0:n, :], in_=psum[bank][:, 0:n * 132].rearrange("p (a b) -> p a b", a=n), func=AF.Copy), [PB[bank]], [b_pre])
                    for si, cb in enumerate(grp):
                        eng = "dve" if (cb % 2 == 0) else "pool"
                        a_, ba_ = acc[cb % 2], b_acc[cb % 2]
                        P.op(eng, lambda e, si=si, cb=cb, a_=a_, scan=scan: e.tensor_scalar_mul(out=a_[:], in0=pre[:, si, 0:128], scalar1=cw[:, scan, cb, 0:1]), [b_pre, b_cw], [ba_])
                        for j in range(1, 5):
                            if eng == "dve":
                                P.op(eng, lambda e, si=si, cb=cb, a_=a_, j=j, scan=scan: e.scalar_tensor_tensor(out=a_[:], in0=pre[:, si, j:j + 128], scalar=cw[:, scan, cb, j:j + 1], in1=a_[:], op0=ALU.mult, op1=ALU.add),
                                     [b_pre, b_cw, ba_], [ba_])
                            else:
                                P.op(eng, lambda e, si=si, cb=cb, j=j, scan=scan: e.tensor_scalar_mul(out=ptmp[:], in0=pre[:, si, j:j + 128], scalar1=cw[:, scan, cb, j:j + 1]), [b_pre, b_cw], [b_ptmp])
                                P.op(eng, lambda e, a_=a_: e.tensor_tensor(out=a_[:], in0=a_[:], in1=ptmp[:], op=ALU.add), [b_ptmp, ba_], [ba_])
                        P.op("act", lambda e, cb=cb, a_=a_: e.activation(out=qkvT[:, cb, :], in_=a_[:], func=AF.Silu), [ba_], [b_qkvT[cb]])

                if kv:
                    P.dma(tab[:], ROPE[v * 128:(v + 1) * 128, :], [], [b_tab], b_tab)
                    for hf in range(2):
                        for k in range(8):
                            P.op("pe", lambda e, k=k, hf=hf: e.matmul(psum[6 + hf][:, :], lhsT=HT[:, k, 2:130], rhs=W1[:, k, 3072 + hf * 512:3072 + (hf + 1) * 512], start=(k == 0), stop=(k == 7)),
                                 [b_HT, b_W1], [PB[6 + hf]])
                    qk_norm_rope([psum[6], psum[7]], [PB[6], PB[7]], knw, tab, b_tab, krot, b_krot, tmpA, b_tmpA)
                    for h in range(H):
                        bank = 6 + h // 4
                        P.op("pe", lambda e, h=h, bank=bank: e.matmul(psum[bank][:, (h % 4) * 128:(h % 4 + 1) * 128], lhsT=krot[:, h * 128:(h + 1) * 128], rhs=identb, start=True, stop=True),
                             [b_krot, b_cbf], [PB[bank]])
                    for hf in range(2):
                        P.op("act", lambda e, hf=hf: e.activation(out=KTsb[:, hf * 4:(hf + 1) * 4, :], in_=psum[6 + hf][:, :].rearrange("p (a b) -> p a b", a=4), func=AF.Copy), [PB[6 + hf]], [b_KTsb])
                    P.dma(KTS[:, :, v * 128:(v + 1) * 128].rearrange("h p t -> p h t"), KTsb[:], [b_KTsb], [b_KTS], b_KTS)
                    for hf in range(2):
                        for k in range(8):
                            P.op("pe", lambda e, k=k, hf=hf: e.matmul(psum[6 + hf][:, :], lhsT=HT[:, k, 2:130], rhs=W1[:, k, 4096 + hf * 512:4096 + (hf + 1) * 512], start=(k == 0), stop=(k == 7)),
                                 [b_HT, b_W1], [PB[6 + hf]])
                    for hf in range(2):
                        P.op("act", lambda e, hf=hf: e.activation(out=Vp[:, hf * 4:(hf + 1) * 4, 0:128], in_=psum[6 + hf][:, :].rearrange("p (a b) -> p a b", a=4), func=AF.Copy), [PB[6 + hf]], [b_Vp])
                    P.dma(VPS[:, :, v, :].rearrange("h p c -> p h c"), Vp[:], [b_Vp], [b_VPS], b_VPS)

                def head_steps(h, T):
                        qcb, kcb, vcb = h, 8 + h, 16 + h
                        PR, PG, PS_, PU = T.banks
                        P.op("pool", lambda e, kcb=kcb: e.tensor_tensor(out=T.sq[:], in0=qkvT[:, kcb, :], in1=qkvT[:, kcb, :], op=ALU.mult), [b_qkvT[kcb]], [T.b_sq])
                        yield
                        P.op("pe", lambda e: e.matmul(psum[PR][:, 256:384], lhsT=ones, rhs=T.sq[:], start=True, stop=True), [b_cst, T.b_sq], [PB[PR]])
                        yield
                        P.op("act", lambda e: e.activation(out=T.rs[:], in_=psum[PR][:, 256:384], func=AF.Ln, bias=epsc[:, 0:1]), [PB[PR], b_eps], [T.b_rs])
                        yield
                        P.op("act", lambda e: e.activation(out=T.rs[:], in_=T.rs[:], func=AF.Exp, scale=-0.5), [T.b_rs], [T.b_rs])
                        yield
                        P.op("dve", lambda e, kcb=kcb: e.tensor_tensor(out=T.kTn[:], in0=qkvT[:, kcb, :], in1=T.rs[:], op=ALU.mult), [b_qkvT[kcb], T.b_rs], [T.b_kTn])
                        yield
                        if own:
                            P.op("pool", lambda e, qcb=qcb: e.tensor_tensor(out=T.sq[:], in0=qkvT[:, qcb, :], in1=qkvT[:, qcb, :], op=ALU.mult), [b_qkvT[qcb]], [T.b_sq])
                            yield
                            P.op("pe", lambda e: e.matmul(psum[PR][:, 384:512], lhsT=ones, rhs=T.sq[:], start=True, stop=True), [b_cst, T.b_sq], [PB[PR]])
                            yield
                            P.op("act", lambda e: e.activation(out=T.rs[:], in_=psum[PR][:, 384:512], func=AF.Ln, bias=epsc[:, 0:1]), [PB[PR], b_eps], [T.b_rs])
                            yield
                            P.op("act", lambda e: e.activation(out=T.rs[:], in_=T.rs[:], func=AF.Exp, scale=-0.5), [T.b_rs], [T.b_rs])
                            yield
                            P.op("dve", lambda e, qcb=qcb: e.scalar_tensor_tensor(out=T.qTn[:], in0=qkvT[:, qcb, :], scalar=128.0 ** -0.5, in1=T.rs[:], op0=ALU.mult, op1=ALU.mult), [b_qkvT[qcb], T.b_rs], [T.b_qTn])
                            yield
                        P.op("dve", lambda e, h=h: e.tensor_scalar_mul(out=T.LgIb[:, 0:128], in0=Ltri, scalar1=gb[:, 0, h:h + 1]), [b_cst, b_gb], [T.b_LgIb])
                        yield
                        P.op("pool", lambda e, h=h: e.tensor_scalar_mul(out=T.LgIb[:, 128:256], in0=ident, scalar1=gb[:, 1, h:h + 1]), [b_cst, b_gb], [T.b_LgIb])
                        yield
                        P.op("pe", lambda e: e.matmul(psum[PR][:, 0:256], lhsT=ones, rhs=T.LgIb[:], start=True, stop=True), [b_cst, T.b_LgIb], [PB[PR]])
                        yield
                        P.op("dve", lambda e, h=h: e.tensor_scalar(out=T.npos[:, 0:128], in0=psum[PR][:, 0:128], scalar1=gb[:, 2, h:h + 1], scalar2=0.0, op0=ALU.subtract, op1=ALU.max), [PB[PR], b_gb], [T.b_npos])
                        yield
                        P.op("dve", lambda e, h=h: e.tensor_scalar(out=T.npos[:, 128:256], in0=psum[PR][:, 0:128], scalar1=gb[:, 2, h:h + 1], scalar2=0.0, op0=ALU.subtract, op1=ALU.min), [PB[PR], b_gb], [T.b_npos])
                        yield
                        P.op("act", lambda e: e.activation(out=T.Em[:], in_=T.npos[:, 0:128], func=AF.Exp, scale=-1.0), [T.b_npos], [T.b_Em])
                        yield
                        P.op("act", lambda e: e.activation(out=T.ETm[:], in_=T.npos[:, 128:256], func=AF.Exp), [T.b_npos], [T.b_ETm])
                        yield
                        P.op("pool", lambda e: e.tensor_tensor(out=T.EMs[:], in0=T.Em[:], in1=negMs, op=ALU.mult), [T.b_Em, b_cst], [T.b_EMs])
                        yield
                        P.op("pool", lambda e: e.tensor_tensor(out=T.ETs[:], in0=T.ETm[:], in1=negMsT, op=ALU.mult), [T.b_ETm, b_cst], [T.b_ETs])
                        yield
                        P.op("pe", lambda e: e.matmul(psum[PG][:, 0:128], lhsT=T.kTn[:], rhs=T.kTn[:], start=True, stop=True), [T.b_kTn], [PB[PG]])
                        yield
                        if own:
                            P.op("pe", lambda e: e.matmul(psum[PG][:, 128:256], lhsT=T.kTn[:], rhs=T.qTn[:], start=True, stop=True), [T.b_kTn, T.b_qTn], [PB[PG]])
                            yield
                        P.op("dve", lambda e, h=h: e.scalar_tensor_tensor(out=T.NY[:, 0:128], in0=psum[PG][:, 0:128], scalar=gb[:, 1, h:h + 1], in1=T.EMs[:], op0=ALU.mult, op1=ALU.mult), [PB[PG], b_gb, T.b_EMs], [T.b_NY])
                        yield
                        P.op("dve", lambda e: e.tensor_tensor(out=T.NY[:, 128:256], in0=psum[PG][:, 0:128], in1=T.ETs[:], op=ALU.mult), [PB[PG], T.b_ETs], [T.b_NY])
                        yield
                        P.op("dve", lambda e: e.tensor_tensor(out=T.NY[:, 128:256], in0=T.NY[:, 128:256], in1=psum[PR][:, 128:256], op=ALU.mult), [PB[PR], T.b_NY], [T.b_NY])
                        yield
                        P.op("pool", lambda e: e.tensor_tensor(out=T.Wb[0][:, 0:128], in0=T.NY[:, 0:128], in1=BD32, op=ALU.mult), [T.b_NY, b_cst], [T.b_Wb[0]])
                        yield
                        P.op("pool", lambda e: e.tensor_tensor(out=T.Wb[0][:, 256:384], in0=T.NY[:, 128:256], in1=BD32, op=ALU.mult), [T.b_NY, b_cst], [T.b_Wb[0]])
                        yield
                        P.op("pool", lambda e: e.tensor_tensor(out=T.Wb[1][:, 128:256], in0=T.Wb[0][:, 0:128], in1=ident, op=ALU.add), [T.b_Wb[0], b_cst], [T.b_Wb[1]])
                        yield
                        P.op("pool", lambda e: e.tensor_tensor(out=T.Wb[1][:, 384:512], in0=T.Wb[0][:, 256:384], in1=ident, op=ALU.add), [T.b_Wb[0], b_cst], [T.b_Wb[1]])
                        yield
                        for n in range(4):
                            cur, nxt = T.Wb[n % 2], T.Wb[(n + 1) % 2]
                            bc, bn = T.b_Wb[n % 2], T.b_Wb[(n + 1) % 2]
                            w_ = 128 if n == 0 else 256
                            P.op("pe", lambda e, cur=cur, w_=w_: e.matmul(psum[PS_][:, 0:w_], lhsT=cur[:, 256:384], rhs=cur[:, 0:w_], start=True, stop=True), [bc], [PB[PS_]])
                            yield
                            P.op("pe", lambda e, cur=cur, w_=w_: e.matmul(psum[PS_][:, 256:256 + w_], lhsT=cur[:, 0:128], rhs=cur[:, 256:256 + w_], start=True, stop=True), [bc], [PB[PS_]])
                            yield
                            P.op("act", lambda e, nxt=nxt: e.activation(out=nxt[:].rearrange("p (a b) -> p a b", a=2)[:, :, 0:128], in_=psum[PS_][:, :].rearrange("p (a b) -> p a b", a=2)[:, :, 0:128], func=AF.Copy), [PB[PS_]], [bn])
                            yield
                            if n > 0:
                                P.op("dve", lambda e, cur=cur, nxt=nxt: e.tensor_tensor(out=nxt[:].rearrange("p (a b) -> p a b", a=2)[:, :, 128:256], in0=psum[PS_][:, :].rearrange("p (a b) -> p a b", a=2)[:, :, 128:256],
                                                                                        in1=cur[:].rearrange("p (a b) -> p a b", a=2)[:, :, 128:256], op=ALU.add), [PB[PS_], bc], [bn])
                                yield
                        P.op("pe", lambda e: e.matmul(psum[PS_][:, 0:128], lhsT=T.Wb[0][:, 256:384], rhs=T.Wb[0][:, 128:256], start=True, stop=True), [T.b_Wb[0]], [PB[PS_]])
                        yield
                        P.op("pe", lambda e: e.matmul(psum[PS_][:, 128:256], lhsT=T.Wb[0][:, 0:128], rhs=T.Wb[0][:, 384:512], start=True, stop=True), [T.b_Wb[0]], [PB[PS_]])
                        yield
                        P.op("dve", lambda e: e.tensor_tensor(out=T.TU[:].rearrange("p (a b) -> p a b", a=2), in0=psum[PS_][:, 0:256].rearrange("p (a b) -> p a b", a=2),
                                                              in1=T.Wb[0][:].rearrange("p (a b) -> p a b", a=2)[:, :, 128:256], op=ALU.add), [PB[PS_], T.b_Wb[0]], [T.b_TU])
                        yield
                        P.op("pool", lambda e: e.tensor_tensor(out=T.NYm[:, 0:128], in0=T.NY[:, 128:256], in1=M64, op=ALU.mult), [T.b_NY, b_cst], [T.b_NYm])
                        yield
                        P.op("pool", lambda e: e.tensor_tensor(out=T.NYm[:, 128:256], in0=T.NY[:, 0:128], in1=M64, op=ALU.mult), [T.b_NY, b_cst], [T.b_NYm])
                        yield
                        P.op("pe", lambda e: e.matmul(psum[PS_][:, 256:384], lhsT=T.NYm[:, 0:128], rhs=T.TU[:, 0:128], start=True, stop=True), [T.b_NYm, T.b_TU], [PB[PS_]])
                        yield
                        P.op("pe", lambda e: e.matmul(psum[PS_][:, 384:512], lhsT=T.NYm[:, 128:256], rhs=T.TU[:, 128:256], start=True, stop=True), [T.b_NYm, T.b_TU], [PB[PS_]])
                        yield
                        P.op("act", lambda e: e.activation(out=T.Wm[:], in_=psum[PS_][:, 256:512], func=AF.Copy), [PB[PS_]], [T.b_Wm])
                        yield
                        P.op("pe", lambda e: e.matmul(psum[PS_][:, 0:128], lhsT=T.TU[:, 128:256], rhs=T.Wm[:, 0:128], start=True, stop=True), [T.b_TU, T.b_Wm], [PB[PS_]])
                        yield
                        P.op("pe", lambda e: e.matmul(psum[PS_][:, 128:256], lhsT=T.TU[:, 0:128], rhs=T.Wm[:, 128:256], start=True, stop=True), [T.b_TU, T.b_Wm], [PB[PS_]])
                        yield
                        P.op("dve", lambda e: e.tensor_tensor(out=T.TU2[:], in0=psum[PS_][:, 0:256], in1=T.TU[:], op=ALU.add), [PB[PS_], T.b_TU], [T.b_TU2])
                        yield
                        P.op("pool", lambda e: e.tensor_tensor(out=T.NYm[:, 128:256], in0=T.NY[:, 0:128], in1=M128, op=ALU.mult), [T.b_NY, b_cst, T.b_NYm], [T.b_NYm])
                        yield
                        P.op("pe", lambda e: e.matmul(psum[PS_][:, 384:512], lhsT=T.NYm[:, 128:256], rhs=T.TU2[:, 128:256], start=True, stop=True), [T.b_NYm, T.b_TU2], [PB[PS_]])
                        yield
                        P.op("act", lambda e: e.activation(out=T.Wm[:, 128:256], in_=psum[PS_][:, 384:512], func=AF.Copy), [PB[PS_]], [T.b_Wm])
                        yield
                        P.op("pe", lambda e: e.matmul(psum[PS_][:, 128:256], lhsT=T.TU2[:, 0:128], rhs=T.Wm[:, 128:256], start=True, stop=True), [T.b_TU2, T.b_Wm], [PB[PS_]])
                        yield
                        P.op("dve", lambda e: e.tensor_tensor(out=T.TT[:], in0=psum[PS_][:, 128:256], in1=T.TU2[:, 128:256], op=ALU.add), [PB[PS_], T.b_TU2], [T.b_TT])
                        yield
                        P.op("pe", lambda e: e.matmul(psum[PR][:, 256:384], lhsT=T.kTn[:], rhs=ident, start=True, stop=True), [T.b_kTn, b_cst], [PB[PR]])
                        yield
                        P.op("pe", lambda e, vcb=vcb: e.matmul(psum[PR][:, 384:512], lhsT=qkvT[:, vcb, :], rhs=ident, start=True, stop=True), [b_qkvT[vcb], b_cst], [PB[PR]])
                        yield
                        P.op("dve", lambda e, h=h: e.tensor_scalar_mul(out=T.Kbp[:], in0=psum[PR][:, 256:384], scalar1=gb[:, 3, h:h + 1]), [PB[PR], b_gb], [T.b_Kbp])
                        yield
                        P.op("dve", lambda e: e.tensor_scalar_mul(out=T.kd[:], in0=psum[PR][:, 256:384], scalar1=T.ETm[:, 127:128]), [PB[PR], T.b_ETm], [T.b_kd])
                        yield
                        P.op("dve", lambda e, h=h: e.tensor_scalar_mul(out=T.Vb[:], in0=psum[PR][:, 384:512], scalar1=gb[:, 1, h:h + 1]), [PB[PR], b_gb], [T.b_Vb])
                        yield
                        P.op("pe", lambda e: e.matmul(psum[PU][:, 0:128], lhsT=T.TT[:], rhs=T.Vb[:], start=True, stop=True), [T.b_TT, T.b_Vb], [PB[PU]])
                        yield
                        P.op("pe", lambda e: e.matmul(psum[PU][:, 128:256], lhsT=T.Kbp[:], rhs=T.TT[:], start=True, stop=True), [T.b_TT, T.b_Kbp], [PB[PU]])
                        yield
                        P.op("act", lambda e: e.activation(out=T.wT[:], in_=psum[PU][:, 128:256], func=AF.Copy), [PB[PU]], [T.b_wT])
                        yield
                        P.op("act", lambda e: e.activation(out=T.usb[:], in_=psum[PU][:, 0:128], func=AF.Copy), [PB[PU]], [T.b_usb])
                        yield
                        P.op("pe", lambda e, h=h: e.matmul(psum[PU][:, 256:384], lhsT=T.wT[:], rhs=Sst[:, h, :], start=True, stop=True), [T.b_wT, b_S[h]], [PB[PU]])
                        yield
                        P.op("dve", lambda e: e.tensor_tensor(out=T.vnew[:], in0=T.usb[:], in1=psum[PU][:, 256:384], op=ALU.subtract), [T.b_usb, PB[PU]], [T.b_vnew])
                        yield
                        if own:
                            P.op("act", lambda e: e.activation(out=T.eR[:], in_=psum[PR][:, 0:128], func=AF.Exp), [PB[PR]], [T.b_eR])
                            yield
                            P.op("dve", lambda e: e.tensor_tensor(out=T.QdT[:], in0=T.qTn[:], in1=T.eR[:], op=ALU.mult), [T.b_qTn, T.b_eR], [T.b_QdT])
                            yield
                            P.op("pool", lambda e: e.tensor_tensor(out=T.ETi[:], in0=T.ETm[:], in1=MiT, op=ALU.mult), [T.b_ETm, b_cst], [T.b_ETi])
                            yield
                            P.op("dve", lambda e: e.tensor_tensor(out=T.qkT[:], in0=psum[PG][:, 128:256], in1=T.ETi[:], op=ALU.mult), [PB[PG], T.b_ETi], [T.b_qkT])
                            yield
                            P.op("pe", lambda e, h=h: e.matmul(psum[PG][:, 256:384], lhsT=T.QdT[:], rhs=Sst[:, h, :], start=True, stop=False), [T.b_QdT, b_S[h]], [PB[PG]])
                            yield
                            P.op("pe", lambda e: e.matmul(psum[PG][:, 256:384], lhsT=T.qkT[:], rhs=T.vnew[:], start=False, stop=True), [T.b_qkT, T.b_vnew], [PB[PG]])
                            yield
                            P.op("act", lambda e, h=h: e.activation(out=osb[:, h * 128:(h + 1) * 128], in_=psum[PG][:, 256:384], func=AF.Copy), [PB[PG]], [b_osb])
                            yield
                        P.op("act", lambda e: e.activation(out=T.egl[:], in_=psum[PR][:, 127:128], func=AF.Exp), [PB[PR]], [T.b_egl])
                        yield
                        P.op("pe", lambda e: e.matmul(psum[PU][:, 384:512], lhsT=T.kd[:], rhs=T.vnew[:], start=True, stop=True), [T.b_kd, T.b_vnew], [PB[PU]])
                        yield
                        P.op("dve", lambda e, h=h: e.scalar_tensor_tensor(out=Sst[:, h, :], in0=Sst[:, h, :], scalar=T.egl[:, 0:1], in1=psum[PU][:, 384:512], op0=ALU.mult, op1=ALU.add), [b_S[h], T.b_egl, PB[PU]], [b_S[h]])
                        yield
                for h0 in range(0, H, 2):
                    gens = [head_steps(h0, TS[0]), head_steps(h0 + 1, TS[1])]
                    alive = [True, True]
                    while any(alive):
                        for gi_ in range(2):
                            if alive[gi_]:
                                try:
                                    next(gens[gi_])
                                except StopIteration:
                                    alive[gi_] = False
                if own:
                    oi = (v - OWN0) if scan == 0 else (v - OWN1)
                    P.dma(OS[scan, oi, :, :], osb[:], [b_osb], [b_OS], b_OS)

        P.barrier()

        b_ZS = Buf(); b_GS = Buf()
        YbT = sb("YbT", [128, H, 2048], BF16); b_YbT = Buf()
        s_qt = ExitStack()
        QT = sb("QT", [128, H, 2048], BF16, stack=s_qt); b_QT = Buf()
        with ExitStack() as s2:
            W2 = sb("W2", [128, 8, 3072], BF16, stack=s2); b_W2 = Buf()
            stage = [junk, xt]
            bstage = [b_junk, b_xt]
            tabB = sb("tabB2", [128, 64], stack=s2); b_tabB = Buf()
            tmpB = [sb("tmpB2_%d" % i, [128, D], stack=s2) for i in range(3)] + [sb("tmpB32", [128, 16], stack=s2)]
            b_tmpB = [Buf() for _ in range(4)]
            krotB = sb("krotB2", [128, D], BF16, stack=s2); b_krotB = Buf()
            load_w_bf16(W2, b_W2, 3072, 1024, stage, bstage, 0)
            load_w_bf16(W2, b_W2, 4128, 1024, stage, bstage, 1024)
            load_w_bf16(W2, b_W2, 7200, 1024, stage, bstage, 2048)
            zsb = sb("zsb", [128, 2048], BF16, stack=s2); b_zsb = Buf()
            for i in range(16):
                v = OWN0 + i
                make_HT(v, False)
                P.dma(tabB[:], ROPE[v * 128:(v + 1) * 128, :], [], [b_tabB], b_tabB)
                for hf in range(2):
                    for k in range(8):
                        P.op("pe", lambda e, k=k, hf=hf: e.matmul(psum[6 + hf][:, :], lhsT=HT[:, k, 2:130], rhs=W2[:, k, 1024 + hf * 512:1024 + (hf + 1) * 512], start=(k == 0), stop=(k == 7)),
                             [b_HT, b_W2], [PB[6 + hf]])
                qk_norm_rope([psum[6], psum[7]], [PB[6], PB[7]], qnw, tabB, b_tabB, krotB, b_krotB, tmpB, b_tmpB)
                for h in range(H):
                    bank = 6 + h // 4
                    P.op("pe", lambda e, h=h, bank=bank: e.matmul(psum[bank][:, (h % 4) * 128:(h % 4 + 1) * 128], lhsT=krotB[:, h * 128:(h + 1) * 128], rhs=identb, start=True, stop=True),
                         [b_krotB, b_cbf], [PB[bank]])
                for hf in range(2):
                    P.op("act", lambda e, hf=hf, i=i: e.activation(out=QT[:, hf * 4:(hf + 1) * 4, i * 128:(i + 1) * 128], in_=psum[6 + hf][:, :].rearrange("p (a b) -> p a b", a=4), func=AF.Copy), [PB[6 + hf]], [b_QT])
                for (c0, dstt, bd, off, fn) in ((0, zsb, b_zsb, 0, AF.Silu), (2048, zsb, b_zsb, 1024, AF.Silu)):
                    for hf in range(2):
                        bank = 4 + hf
                        for k in range(8):
                            P.op("pe", lambda e, k=k, hf=hf, c0=c0, bank=bank: e.matmul(psum[bank][:, :], lhsT=HT[:, k, 2:130], rhs=W2[:, k, c0 + hf * 512:c0 + (hf + 1) * 512], start=(k == 0), stop=(k == 7)),
                                 [b_HT, b_W2], [PB[bank]])
                        P.op("act", lambda e, hf=hf, bank=bank, dstt=dstt, off=off, fn=fn: e.activation(out=dstt[:, off + hf * 512:off + (hf + 1) * 512], in_=psum[bank][:, :], func=fn), [PB[bank]], [bd])
                P.dma(ZS[i, :, :], zsb[:], [b_zsb], [b_ZS], b_ZS)

        P.barrier()
        with ExitStack() as s3:
            KTh = sb("KTh", [128, NKV * 128], BF16, stack=s3); b_KTh = Buf()
            Vph = sb("Vph", [128, NKV, 129], BF16, stack=s3); b_Vph = Buf()
            Eb = [sb("Eb%d" % i, [128, 1024], BF16, stack=s3) for i in range(2)]; b_Eb = [Buf(), Buf()]
            zb = sb("zb", [128, 4, 128], BF16, stack=s3); b_zb = Buf()
            fo = sb("fo", [128, 8, 128], stack=s3); b_fo = Buf()
            fs = sb("fs", [128, 8], stack=s3); b_fs = Buf()
            yb = sb("yb", [128, 128], BF16, stack=s3); b_yb = Buf()
            for h in range(H):
                P.dma(KTh[:], KTS[h, :, :], [b_KTS], [b_KTh], b_KTh)
                P.dma(Vph[:], VPS[h, :, :, :], [b_VPS], [b_Vph], b_Vph)
                for qt in range(4):
                    P.dma(zb[:], ZS[qt * 4:(qt + 1) * 4, :, 1024 + h * 128:1024 + (h + 1) * 128].rearrange("a p c -> p a c"), [b_ZS], [b_zb], b_zb)
                    def emit_qk(kb):
                        sbk = (kb % 2) * 2
                        for m in range(2):
                            P.op("pe", lambda e, m=m, kb=kb, qt=qt, h=h, sbk=sbk: e.matmul(psum[sbk + m][:, :], lhsT=KTh[m * 64:(m + 1) * 64, kb * 128:(kb + 1) * 128],
                                                                                   rhs=QT[m * 64:(m + 1) * 64, h, qt * 512:(qt + 1) * 512], start=True, stop=True),
                                 [b_KTh, b_QT], [PB[sbk + m]])
                    emit_qk(0)
                    for kb in range(NKV):
                        sbk = (kb % 2) * 2
                        if kb + 1 < NKV:
                            emit_qk(kb + 1)
                        for m in range(2):
                            P.op("act", lambda e, m=m, kb=kb, sbk=sbk: e.activation(out=Eb[kb % 2][:, m * 512:(m + 1) * 512], in_=psum[sbk + m][:, :], func=AF.Exp, scale=0.125, bias=lamt[:, 1:2]),
                                 [PB[sbk + m], b_lam], [b_Eb[kb % 2]])
                        for qb in range(4):
                            for m in range(2):
                                P.op("pe", lambda e, m=m, kb=kb, qb=qb: e.matmul(psum[4 + qb][:, m * 129:(m + 1) * 129], lhsT=Eb[kb % 2][:, m * 512 + qb * 128:m * 512 + (qb + 1) * 128],
                                                                                 rhs=Vph[:, kb, :], start=(kb == 0 and m == 0), stop=(kb == NKV - 1 and m == 1)),
                                     [b_Eb[kb % 2], b_Vph], [PB[4 + qb]])
                    for qb in range(4):
                        blk = qt * 4 + qb
                        pso = psum[4 + qb]
                        P.op("dve", lambda e, pso=pso: e.tensor_copy(out=fs[:, 0:1], in_=pso[:, 128:129]), [PB[4 + qb]], [b_fs])
                        P.op("dve", lambda e, pso=pso: e.tensor_copy(out=fs[:, 1:2], in_=pso[:, 257:258]), [PB[4 + qb]], [b_fs])
                        P.op("dve", lambda e: e.reciprocal(out=fs[:, 2:4], in_=fs[:, 0:2]), [b_fs], [b_fs])
                        P.op("dve", lambda e: e.tensor_tensor(out=fs[:, 3:4], in0=fs[:, 3:4], in1=lamt[:, 0:1], op=ALU.mult), [b_fs, b_lam], [b_fs])
                        P.op("dve", lambda e, pso=pso: e.tensor_scalar_mul(out=fo[:, 0, :], in0=pso[:, 129:257], scalar1=fs[:, 3:4]), [PB[4 + qb], b_fs], [b_fo])
                        P.op("dve", lambda e, pso=pso: e.scalar_tensor_tensor(out=fo[:, 1, :], in0=pso[:, 0:128], scalar=fs[:, 2:3], in1=fo[:, 0, :], op0=ALU.mult, op1=ALU.add), [PB[4 + qb], b_fs, b_fo], [b_fo])
                        P.op("act", lambda e: e.activation(out=fo[:, 2, :], in_=fo[:, 1, :], func=AF.Square, accum_out=fs[:, 4:5]), [b_fo, b_fs], [b_fo, b_fs])
                        P.op("act", lambda e: e.activation(out=fs[:, 5:6], in_=fs[:, 4:5], func=AF.Ln, scale=1.0 / 128, bias=epsc[:, 0:1]), [b_fs, b_eps], [b_fs])
                        P.op("act", lambda e: e.activation(out=fs[:, 5:6], in_=fs[:, 5:6], func=AF.Exp, scale=-0.5), [b_fs], [b_fs])
                        P.op("dve", lambda e: e.scalar_tensor_tensor(out=fo[:, 3, :], in0=fo[:, 1, :], scalar=fs[:, 5:6], in1=dnw, op0=ALU.mult, op1=ALU.mult), [b_fo, b_fs, b_vecs], [b_fo])
                        P.op("dve", lambda e, qb=qb: e.scalar_tensor_tensor(out=yb[:], in0=fo[:, 3, :], scalar=(1.0 - LAMBDA_INIT), in1=zb[:, qb, :], op0=ALU.mult, op1=ALU.mult), [b_fo, b_zb], [b_yb])
                        P.op("pe", lambda e: e.matmul(psum[0][:, 0:128], lhsT=yb[:], rhs=identb, start=True, stop=True), [b_yb, b_cbf], [PB[0]])
                        P.op("act", lambda e, h=h, blk=blk: e.activation(out=YbT[:, h, blk * 128:(blk + 1) * 128], in_=psum[0][:, 0:128], func=AF.Copy), [PB[0]], [b_YbT])

        s_qt.close()
        P.barrier()
        with ExitStack() as s4:
            Wo = [sb("Wo%d" % i, [128, 8, D], BF16, stack=s4) for i in range(3)]; b_Wo = [Buf() for _ in range(3)]
            stage = [junk, xt]
            bstage = [b_junk, b_xt]
            YaT = sb("YaT", [128, H, 128], BF16, stack=s4); b_YaT = Buf()
            Wg = sb("Wg", [128, 8, 2048], BF16, stack=s4); b_Wg = Buf()
            load_w_bf16(Wg, b_Wg, 8224, 2048, stage, bstage, 0)
            ii = 0
            for wi, src in enumerate((WOA, WOB, WOUT)):
                for k in range(8):
                    s_, bs_ = stage[ii % 2], bstage[ii % 2]
                    P.dma(s_[:], src[k * 128:(k + 1) * 128, :], [], [bs_], bs_)
                    P.op(["dve", "pool"][ii % 2], lambda e, s_=s_, wi=wi, k=k: e.tensor_copy(out=Wo[wi][:, k, :], in_=s_[:]), [bs_], [b_Wo[wi]])
                    ii += 1
            o1 = sb("o1", [128, D], stack=s4); b_o1 = Buf()
            o2 = sb("o2", [128, D], stack=s4); b_o2 = Buf()
            zs4 = sb("zs4", [128, 1024], BF16, stack=s4); b_zs4 = Buf()
            gs4 = sb("gs4", [128, 2048], BF16, stack=s4); b_gs4 = Buf()
            ya = sb("ya", [128, D], BF16, stack=s4); b_ya = Buf()
            mg = sb("mg", [128, D], stack=s4); b_mg = Buf()
            mgb = sb("mgb", [128, D], BF16, stack=s4); b_mgb = Buf()
            mT = sb("mT", [128, 8, 128], BF16, stack=s4); b_mT = Buf()
            res = sb("res", [128, D], stack=s4); b_res = Buf()
            b_OUT = Buf()
            out_toks = []
            for i in range(16):
                P.dma(o1[:], OS[0, i, :, :], [b_OS], [b_o1], b_o1)
                P.dma(o2[:], OS[1, 15 - i, :, :], [b_OS], [b_o2], b_o2)
                P.dma(zs4[:], ZS[i, :, 0:1024], [b_ZS], [b_zs4], b_zs4)
                make_HT(OWN0 + i, False)
                for g4 in range(4):
                    bank = 6 + g4 % 2
                    for k in range(8):
                        P.op("pe", lambda e, k=k, g4=g4, bank=bank: e.matmul(psum[bank][:, :], lhsT=HT[:, k, 2:130], rhs=Wg[:, k, g4 * 512:(g4 + 1) * 512], start=(k == 0), stop=(k == 7)), [b_HT, b_Wg], [PB[bank]])
                    P.op("act", lambda e, g4=g4, bank=bank: e.activation(out=gs4[:, g4 * 512:(g4 + 1) * 512], in_=psum[bank][:, :], func=AF.Sigmoid), [PB[bank]], [b_gs4])
                for hf in range(2):
                    P.op("pe", lambda e, hf=hf: e.matmul(psum[hf][:, :], lhsT=Jm, rhs=o2[:, hf * 512:(hf + 1) * 512], start=True, stop=True), [b_cst, b_o2], [PB[hf]])
                    P.op("dve", lambda e, hf=hf: e.tensor_tensor(out=o1[:, hf * 512:(hf + 1) * 512], in0=o1[:, hf * 512:(hf + 1) * 512], in1=psum[hf][:, :], op=ALU.add), [PB[hf], b_o1], [b_o1])
                P.op("act", lambda e: e.activation(out=junk[:], in_=o1[:], func=AF.Square), [b_o1], [b_junk])
                P.op("dve", lambda e: e.tensor_reduce(out=st[:, 0:8], in_=junk[:].rearrange("p (g d) -> p g d", d=128), axis=AX.X, op=ALU.add), [b_junk], [b_st])
                P.op("act", lambda e: e.activation(out=st[:, 0:8], in_=st[:, 0:8], func=AF.Ln, scale=1.0 / 128, bias=epsc[:, 0:1]), [b_st, b_eps], [b_st])
                P.op("act", lambda e: e.activation(out=st[:, 0:8], in_=st[:, 0:8], func=AF.Exp, scale=-0.5), [b_st], [b_st])
                P.op("dve", lambda e: e.tensor_tensor(out=junk[:].rearrange("p (g d) -> p g d", d=128), in0=o1[:].rearrange("p (g d) -> p g d", d=128), in1=st[:, 0:8].unsqueeze(2).to_broadcast([128, 8, 128]), op=ALU.mult), [b_o1, b_st], [b_junk])
                P.op("pool", lambda e: e.tensor_tensor(out=junk[:].rearrange("p (g d) -> p g d", d=128), in0=junk[:].rearrange("p (g d) -> p g d", d=128), in1=gnw.unsqueeze(1).to_broadcast([128, 8, 128]), op=ALU.mult), [b_junk, b_vecs], [b_junk])
                P.op("dve", lambda e: e.tensor_tensor(out=ya[:], in0=junk[:], in1=zs4[:], op=ALU.mult), [b_junk, b_zs4], [b_ya])
                for h in range(H):
                    bank = h // 4
                    P.op("pe", lambda e, h=h, bank=bank: e.matmul(psum[bank][:, (h % 4) * 128:(h % 4 + 1) * 128], lhsT=ya[:, h * 128:(h + 1) * 128], rhs=identb, start=True, stop=True), [b_ya, b_cbf], [PB[bank]])
                for hf in range(2):
                    P.op("act", lambda e, hf=hf, i=i: e.activation(out=YaT[:, hf * 4:(hf + 1) * 4, :], in_=psum[hf][:, :].rearrange("p (a b) -> p a b", a=4), func=AF.Copy), [PB[hf]], [b_YaT])
                for (wi, Y, bY, pb0, c0_) in ((0, YaT, b_YaT, 2, 0), (1, YbT, b_YbT, 4, i * 128)):
                    for hf in range(2):
                        for k in range(8):
                            P.op("pe", lambda e, wi=wi, Y=Y, hf=hf, k=k, pb0=pb0, c0_=c0_: e.matmul(psum[pb0 + hf][:, :], lhsT=Y[:, k, c0_:c0_ + 128], rhs=Wo[wi][:, k, hf * 512:(hf + 1) * 512], start=(k == 0), stop=(k == 7)),
                                 [bY, b_Wo[wi]], [PB[pb0 + hf]])
                for hf in range(2):
                    P.op("dve", lambda e, hf=hf: e.tensor_tensor(out=mg[:, hf * 512:(hf + 1) * 512], in0=psum[2 + hf][:, :], in1=gs4[:, hf * 512:(hf + 1) * 512], op=ALU.mult), [PB[2 + hf], b_gs4], [b_mg])
                    P.op("dve", lambda e, hf=hf: e.tensor_tensor(out=junk[:, hf * 512:(hf + 1) * 512], in0=psum[4 + hf][:, :], in1=gs4[:, 1024 + hf * 512:1024 + (hf + 1) * 512], op=ALU.mult), [PB[4 + hf], b_gs4], [b_junk])
                P.op("pool", lambda e: e.tensor_tensor(out=mgb[:], in0=mg[:], in1=junk[:], op=ALU.add), [b_mg, b_junk], [b_mgb])
                for k in range(8):
                    bank = 6 + k // 4
                    P.op("pe", lambda e, k=k, bank=bank: e.matmul(psum[bank][:, (k % 4) * 128:(k % 4 + 1) * 128], lhsT=mgb[:, k * 128:(k + 1) * 128], rhs=identb, start=True, stop=True), [b_mgb, b_cbf], [PB[bank]])
                for hf in range(2):
                    P.op("act", lambda e, hf=hf: e.activation(out=mT[:, hf * 4:(hf + 1) * 4, :], in_=psum[6 + hf][:, :].rearrange("p (a b) -> p a b", a=4), func=AF.Copy), [PB[6 + hf]], [b_mT])
                for hf in range(2):
                    for k in range(8):
                        P.op("pe", lambda e, hf=hf, k=k: e.matmul(psum[hf][:, :], lhsT=mT[:, k, :], rhs=Wo[2][:, k, hf * 512:(hf + 1) * 512], start=(k == 0), stop=(k == 7)), [b_mT, b_Wo[2]], [PB[hf]])
                    P.op("dve", lambda e, hf=hf: e.tensor_tensor(out=res[:, hf * 512:(hf + 1) * 512], in0=psum[hf][:, :], in1=mod[:, 2, hf * 512:(hf + 1) * 512], op=ALU.mult), [PB[hf], b_mod], [b_res])
                P.op("pool", lambda e: e.tensor_tensor(out=res[:], in0=res[:], in1=xt[:], op=ALU.add), [b_res, b_xt], [b_res])
                tok = P.dma(OUT[i * 128:(i + 1) * 128, :], res[:], [b_res], [b_OUT], b_OUT)
            P.final_wait("sp", [tok])

        print('total ops', P.gidx, {n: len(P.q[n]) for n in P.names})
        P.emit()
    return nc


def _consts():
    i = np.arange(128)
    ident = np.eye(128, dtype=np.float32)
    J = ident[::-1].copy()
    ones = np.ones((128, 128), np.float32)
    Ltri = (i[:, None] <= i[None, :]).astype(np.float32)
    negMs = -(i[:, None] > i[None, :]).astype(np.float32)
    negMsT = -(i[None, :] > i[:, None]).astype(np.float32)
    MiT = (i[None, :] >= i[:, None]).astype(np.float32)
    SelM = np.zeros((128, 132), np.float32)
    SelM[i, i + 2] = 1
    SelH = np.zeros((128, 132), np.float32)
    SelH[0, 0] = 1
    SelH[1, 1] = 1
    SelH[2, 130] = 1
    SelH[3, 131] = 1
    blk = lambda b: ((i[:, None] // b) == (i[None, :] // b)).astype(np.float32)
    BD32 = blk(32)
    M64 = blk(64) - blk(32)
    M128 = 1.0 - blk(64)
    return np.concatenate([ident, J, ones, Ltri, negMs, negMsT, MiT, SelM, SelH, BD32, M64, M128], axis=1)


def _rope_table(pos):
    pos = np.asarray(pos)
    row = (pos // 64).astype(np.float32)
    col = (pos % 64).astype(np.float32)
    inv = (10000.0 ** (-np.arange(16, dtype=np.float32) / 16)).astype(np.float32)
    ar = row[:, None] * inv
    ac = col[:, None] * inv
    t = np.concatenate([np.cos(ar), np.sin(ar), np.cos(ac), np.sin(ac)], axis=1).astype(np.float32)
    t[pos < 0] = np.array([1.0] * 16 + [0.0] * 16 + [1.0] * 16 + [0.0] * 16, np.float32)
    return t


def _core_plan(r):
    fwd_first = r >= 2
    own = list(range(16 * r, 16 * r + 16))
    left = list(range(0, 16 * r))
    right = list(range(16 * r + 16, 64))

    def scan(forward, nslots, filler):
        rev = not forward
        vis = [("c", 0, rev, 1.0), ("c", 1, rev, 1.0)]
        if rev:
            vis = [("c", 1, rev, 1.0), ("c", 0, rev, 1.0)]
        pref = left if forward else right[::-1]
        fill = [("l", b, rev, 0.0) for b in filler]
        vis += fill[:nslots - len(pref)] + [("l", b, rev, 1.0) for b in pref]
        o = own if forward else own[::-1]
        vis += [("l", b, rev, 1.0) for b in o]
        return vis

    if fwd_first:
        filler = right[::-1]
        v1 = scan(True, 48, filler)
        v2 = scan(False, 16, [own[0]] * 16)
    else:
        filler = left
        v1 = scan(False, 48, filler)
        v2 = scan(True, 16, [own[0]] * 16)
    assert len(v1) == 66 and len(v2) == 34
    return v1 + v2, fwd_first


_NC_CACHE = {}


def kernel(x, c, ctx, c_ctx, w_ada, b_ada, w_in, conv_w, a_log, dt_bias, gdn_norm_w,
           q_norm_w, k_norm_w, lambda_q1, lambda_k1, lambda_q2, lambda_k2, diff_norm_w,
           w_oa, w_ob, w_out):
    f = lambda a: np.ascontiguousarray(np.asarray(a, dtype=np.float32))
    x, c, ctx, c_ctx = f(x), f(c), f(ctx), f(c_ctx)
    w_ada, b_ada, w_in, conv_w = f(w_ada)[0], f(b_ada)[0], f(w_in)[0], f(conv_w)[0]
    a_log, dt_bias = f(a_log)[0], f(dt_bias)[0]
    vec = lambda a: f(a)[0]
    if "nc" not in _NC_CACHE:
        _NC_CACHE["nc"] = build_nc()
    nc = _NC_CACHE["nc"]
    cst = _consts()
    vecs = np.zeros((768,), np.float32)
    vecs[0:128] = vec(gdn_norm_w)
    vecs[128:192] = vec(q_norm_w)
    vecs[192:256] = vec(k_norm_w)
    vecs[256:320] = vec(lambda_q1)
    vecs[320:384] = vec(lambda_k1)
    vecs[384:448] = vec(lambda_q2)
    vecs[448:512] = vec(lambda_k2)
    vecs[512:640] = vec(diff_norm_w)
    VECS = np.ascontiguousarray(np.broadcast_to(vecs, (128, 768)))
    BADA = np.ascontiguousarray(np.broadcast_to(b_ada, (128, 3 * D)))
    in_maps = []
    plans = []
    for core in range(8):
        b, r = divmod(core, 4)
        plan, fwd_first = _core_plan(r)
        plans.append((plan, fwd_first))
        XV = np.zeros((NV * 128, D), np.float32)
        XH = np.zeros((NV * 4, D), np.float32)
        HM = np.zeros((4, NV), np.float32)
        GM = np.zeros((128, NV), np.float32)
        ROPE = np.zeros((NKV * 128, 64), np.float32)
        for v, (seq, blk, rev, gmask) in enumerate(plan):
            src = ctx[b] if seq == "c" else x[b]
            n = src.shape[0]
            idx = np.arange(blk * 128, blk * 128 + 128)
            halo = np.array([blk * 128 - 2, blk * 128 - 1, blk * 128 + 128, blk * 128 + 129])
            if rev:
                idx = idx[::-1]
                halo = halo[::-1]
            XV[v * 128:(v + 1) * 128] = src[idx]
            valid = (halo >= 0) & (halo < n)
            XH[v * 4:(v + 1) * 4][valid] = src[halo[valid]]
            HM[:, v] = valid.astype(np.float32)
            GM[:, v] = gmask
            if v < NKV:
                pos = idx if seq == "l" else -np.ones(128, np.int64)
                ROPE[v * 128:(v + 1) * 128] = _rope_table(pos)
        d1 = 0 if fwd_first else 1
        dirs = (d1, 1 - d1)
        WAB = np.stack([np.concatenate([w_in[:, 4096 + d * 8:4096 + d * 8 + 8], w_in[:, 4112 + d * 8:4112 + d * 8 + 8]], axis=1) for d in dirs])
        cwt = np.ascontiguousarray(conv_w.T)
        CW = np.stack([cwt if d == 0 else np.ascontiguousarray(cwt[:, ::-1]) for d in dirs])
        ALG = np.stack([np.broadcast_to(a_log[d], (128, 8)) for d in dirs])
        DTB = np.stack([np.broadcast_to(dt_bias[d], (128, 8)) for d in dirs])
        cv = np.stack([c[b], c_ctx])
        CT = np.ascontiguousarray(cv.reshape(2, 8, 128).transpose(2, 0, 1).reshape(128, 16))
        in_maps.append({
            "XV": XV, "XH": XH, "HM": HM, "GM": GM, "ROPE": ROPE, "CT": CT, "WADA": w_ada, "BADA": BADA,
            "WIN": w_in, "WAB": np.ascontiguousarray(WAB), "CW": np.ascontiguousarray(CW),
            "ALG": np.ascontiguousarray(ALG), "DTB": np.ascontiguousarray(DTB), "VECS": VECS, "CST": cst,
            "WOA": f(w_oa)[0], "WOB": f(w_ob)[0], "WOUT": f(w_out)[0],
        })
    if _NC_CACHE.get("maps_only"):
        return in_maps, plans
    res = run_bass_kernel_spmd(nc, in_maps, core_ids=list(range(8)))
    out = np.zeros((2, L, D), np.float32)
    for core in range(8):
        b, r = divmod(core, 4)
        plan, fwd_first = plans[core]
        o = np.asarray(res.results[core]["OUT"])
        for i in range(16):
            seq, blk, rev, _ = plan[OWN0 + i]
            idx = np.arange(blk * 128, blk * 128 + 128)
            if rev:
                idx = idx[::-1]
            out[b, idx] = o[i * 128:(i + 1) * 128]
    return out
```

```python
import math
from contextlib import ExitStack

import numpy as np
import concourse.bass as bass
import concourse.mybir as mybir
from concourse.bass_utils import run_bass_kernel_spmd

F32 = mybir.dt.float32
BF16 = mybir.dt.bfloat16
AF = mybir.ActivationFunctionType
ALU = mybir.AluOpType
AX = mybir.AxisListType

D = 1024
L = 8192
NCTX = 256
H = 8
EPS = 1e-6
NV = 100
NKV = 66
OWN0 = 50
S2 = 66
OWN1 = 84
IN_COLS = 10272
LAMBDA_INIT = 0.8 - 0.6 * math.exp(-0.3 * 0)
DEBUG = False
import os
MAXOPS = int(os.environ.get('KMAXOPS', '1000000000'))
TRACE_LO = int(os.environ.get('KTLO', '0'))
TRACE_HI = int(os.environ.get('KTHI', '-1'))


class Buf:
    __slots__ = ("w", "r", "dsem", "dcnt")

    def __init__(self):
        self.w = None
        self.r = {}
        self.dsem = None
        self.dcnt = 0


class Prog:
    def __init__(self, nc, es):
        self.nc = nc
        self.es = es
        self.names = ["pe", "act", "dve", "pool", "sp"]
        self.sem = {n: es.enter_context(nc.semaphore("s_" + n)) for n in self.names}
        self.cnt = {n: 0 for n in self.names}
        self.waited = {n: {} for n in self.names}
        self.q = {n: [] for n in self.names}
        self.nsem = 0
        self.dbufs = []
        self.gidx = 0
        self.maxops = MAXOPS

    def _waits(self, eng, reads, writes):
        need = {}

        def add(tok):
            if tok is None:
                return
            s, v = tok
            k = id(s)
            if k not in need or need[k][1] < v:
                need[k] = (s, v)

        for b in reads:
            add(b.w)
        for b in writes:
            add(b.w)
            for t in b.r.values():
                add(t)
        out = []
        own = id(self.sem[eng])
        for k, (s, v) in need.items():
            if eng == "pe" and k == own:
                continue
            if self.waited[eng].get(k, 0) >= v:
                continue
            self.waited[eng][k] = v
            out.append((s, v))
        return out

    def _commit(self, tok, reads, writes):
        for b in reads:
            b.r[id(tok[0])] = tok
        for b in writes:
            b.w = tok
            b.r = {}

    def op(self, eng, fn, reads=(), writes=()):
        self.gidx += 1
        if TRACE_LO <= self.gidx <= TRACE_HI:
            import inspect
            print("OP", self.gidx, eng, inspect.currentframe().f_back.f_lineno)
        if self.gidx > self.maxops:
            return
        waits = self._waits(eng, reads, writes)
        self.cnt[eng] += 1
        tok = (self.sem[eng], self.cnt[eng])
        self.q[eng].append((waits, fn, self.sem[eng], 1))
        self._commit(tok, reads, writes)

    def dma(self, out_ap, in_ap, reads, writes, dst):
        eng = "sp"
        self.gidx += 1
        if TRACE_LO <= self.gidx <= TRACE_HI:
            import inspect
            print("DMA", self.gidx, inspect.currentframe().f_back.f_lineno)
        if self.gidx > self.maxops:
            return (None, 0)
        waits = self._waits(eng, reads, writes)
        if dst.dsem is None:
            dst.dsem = self.es.enter_context(self.nc.semaphore("d%d" % self.nsem))
            self.nsem += 1
            self.dbufs.append(dst)
        dst.dcnt += 16
        tok = (dst.dsem, dst.dcnt)
        self.q[eng].append((waits, lambda e: e.dma_start(out=out_ap, in_=in_ap), dst.dsem, 16))
        self._commit(tok, reads, writes)
        return tok

    def barrier(self):
        toks = [(self.sem[n], self.cnt[n]) for n in self.names if self.cnt[n] > 0]
        toks += [(b.dsem, b.dcnt) for b in self.dbufs]
        for eng in self.names:
            waits = []
            for (s_, v) in toks:
                k = id(s_)
                if eng == "pe" and k == id(self.sem["pe"]):
                    continue
                if self.waited[eng].get(k, 0) >= v:
                    continue
                self.waited[eng][k] = v
                waits.append((s_, v))
            if waits:
                self.q[eng].append((waits, None, None, 0))

    def final_wait(self, eng, toks):
        toks = [t for t in toks if t[0] is not None]
        if not toks:
            toks = [(self.sem[n], self.cnt[n]) for n in self.names if self.cnt[n] > 0 and n != eng]
            toks += [(b.dsem, b.dcnt) for b in self.dbufs]
        self.q[eng].append((toks, None, None, 0))

    def emit(self):
        nc = self.nc
        with nc.Block() as block:
            def run(name):
                def body(e):
                    for waits, fn, sem, inc in self.q[name]:
                        for s, v in waits:
                            e.wait_ge(s, v)
                        if fn is not None:
                            fn(e).then_inc(sem, inc)
                return body
            block.tensor(run("pe"))
            block.scalar(run("act"))
            block.vector(run("dve"))
            block.gpsimd(run("pool"))
            block.sync(run("sp"))


def build_nc():
    nc = bass.Bass("TRN2", target_bir_lowering=False)

    def din(name, shape, dt=F32):
        return nc.dram_tensor(name, list(shape), dt, kind="ExternalInput").ap()

    XV = din("XV", [NV * 128, D])
    XH = din("XH", [NV * 4, D])
    HM = din("HM", [4, NV])
    GM = din("GM", [128, NV])
    ROPE = din("ROPE", [NKV * 128, 64])
    CT = din("CT", [128, 16])
    WADA = din("WADA", [D, 3 * D])
    BADA = din("BADA", [128, 3 * D])
    WIN = din("WIN", [D, IN_COLS])
    WAB = din("WAB", [2, D, 16])
    CW = din("CW", [2, 3072, 5])
    ALG = din("ALG", [2, 128, 8])
    DTB = din("DTB", [2, 128, 8])
    VECS = din("VECS", [128, 768])
    CST = din("CST", [128, 10 * 128 + 2 * 132])
    WOA = din("WOA", [D, D])
    WOB = din("WOB", [D, D])
    WOUT = din("WOUT", [D, D])
    OUT = nc.dram_tensor("OUT", [2048, D], F32, kind="ExternalOutput").ap()

    KTS = nc.dram_tensor("KTS", [H, 128, NKV * 128], BF16).ap()
    VPS = nc.dram_tensor("VPS", [H, 128, NKV, 129], BF16).ap()
    OS = nc.dram_tensor("OS", [2, 16, 128, D], F32).ap()
    ZS = nc.dram_tensor("ZS", [16, 128, 2048], BF16).ap()
    GS = nc.dram_tensor("GS", [16, 128, 2048], BF16).ap()

    es = ExitStack()
    with es:
        P = Prog(nc, es)

        def sb(name, shape, dt=F32, stack=es):
            return stack.enter_context(nc.sbuf_tensor(name, list(shape), dt))

        psum = [es.enter_context(nc.psum_tensor("ps%d" % i, [128, 512], F32)) for i in range(8)]
        PB = [Buf() for _ in range(8)]

        cst = sb("cst", [128, 10 * 128 + 2 * 132]); b_cst = Buf()
        ident = cst[:, 0:128]
        Jm = cst[:, 128:256]
        ones = cst[:, 256:384]
        Ltri = cst[:, 384:512]
        negMs = cst[:, 512:640]
        negMsT = cst[:, 640:768]
        MiT = cst[:, 768:896]
        BD32 = cst[:, 1160:1288]
        M64 = cst[:, 1288:1416]
        M128 = cst[:, 1416:1544]
        cbf = sb("cbf", [128, 128 + 2 * 132], BF16); b_cbf = Buf()
        identb = cbf[:, 0:128]
        SelM = cbf[:, 128:260]
        SelH = cbf[0:4, 260:392]
        vecs = sb("vecs", [128, 768]); b_vecs = Buf()
        gnw = vecs[:, 0:128]
        qnw = vecs[:, 128:192]
        knw = vecs[:, 192:256]
        dnw = vecs[:, 512:640]
        hm = sb("hm", [4, NV]); b_hm = Buf()
        gm = sb("gm", [128, NV]); b_gm = Buf()
        mod = sb("mod", [128, 5, D]); b_mod = Buf()
        negA = sb("negA", [128, 2, 8]); b_negA = Buf()
        dtb = sb("dtb", [128, 2, 8]); b_dtb = Buf()
        lamt = sb("lamt", [128, 4]); b_lam = Buf()
        epsc = sb("epsc", [128, 1]); b_eps = Buf()

        P.dma(cst[:], CST[:, :], [], [b_cst], b_cst)
        P.dma(vecs[:], VECS[:, :], [], [b_vecs], b_vecs)
        P.dma(hm[:], HM[:, :], [], [b_hm], b_hm)
        P.dma(gm[:], GM[:, :], [], [b_gm], b_gm)
        P.dma(negA[:], ALG.rearrange("s p h -> p s h"), [], [b_negA], b_negA)
        P.dma(dtb[:], DTB.rearrange("s p h -> p s h"), [], [b_dtb], b_dtb)
        P.op("act", lambda e: e.activation(out=negA[:], in_=negA[:], func=AF.Exp), [b_negA], [b_negA])
        P.op("dve", lambda e: e.tensor_scalar_mul(out=negA[:], in0=negA[:], scalar1=-1.0), [b_negA], [b_negA])
        P.op("dve", lambda e: e.tensor_copy(out=cbf[:, 0:128], in_=cst[:, 0:128]), [b_cst], [b_cbf])
        P.op("dve", lambda e: e.tensor_copy(out=cbf[:, 128:392], in_=cst[:, 896:1160]), [b_cst], [b_cbf])
        P.op("pool", lambda e: e.memset(epsc[:], EPS), [], [b_eps])

        ltmp = sb("ltmp", [128, 128]); b_ltmp = Buf()
        P.op("dve", lambda e: e.tensor_tensor(out=ltmp[:, 0:64], in0=vecs[:, 256:320], in1=vecs[:, 320:384], op=ALU.mult), [b_vecs], [b_ltmp])
        P.op("dve", lambda e: e.tensor_tensor(out=ltmp[:, 64:128], in0=vecs[:, 384:448], in1=vecs[:, 448:512], op=ALU.mult), [b_vecs], [b_ltmp])
        P.op("dve", lambda e: e.tensor_reduce(out=lamt[:, 2:4], in_=ltmp[:].rearrange("p (a b) -> p a b", a=2), axis=AX.X, op=ALU.add), [b_ltmp], [b_lam])
        P.op("act", lambda e: e.activation(out=lamt[:, 2:4], in_=lamt[:, 2:4], func=AF.Exp), [b_lam], [b_lam])
        P.op("dve", lambda e: e.tensor_tensor(out=lamt[:, 0:1], in0=lamt[:, 3:4], in1=lamt[:, 2:3], op=ALU.subtract), [b_lam], [b_lam])
        P.op("dve", lambda e: e.tensor_scalar_add(out=lamt[:, 0:1], in0=lamt[:, 0:1], scalar1=-LAMBDA_INIT), [b_lam], [b_lam])
        P.op("dve", lambda e: e.tensor_scalar_mul(out=ltmp[:], in0=vecs[:, 128:256], scalar1=-1.0), [b_vecs, b_lam], [b_ltmp])
        P.op("dve", lambda e: e.tensor_tensor(out=ltmp[:], in0=ltmp[:], in1=vecs[:, 128:256], op=ALU.max), [b_vecs, b_ltmp], [b_ltmp])
        P.op("dve", lambda e: e.tensor_reduce(out=lamt[:, 2:4], in_=ltmp[:].rearrange("p (a b) -> p a b", a=2), axis=AX.X, op=ALU.max), [b_ltmp, b_lam], [b_lam])
        P.op("dve", lambda e: e.scalar_tensor_tensor(out=lamt[:, 1:2], in0=lamt[:, 2:3], scalar=-8.0, in1=lamt[:, 3:4], op0=ALU.mult, op1=ALU.mult), [b_lam], [b_lam])

        with ExitStack() as s0:
            ct = sb("ct", [128, 16], stack=s0); b_ct = Buf()
            cbl = sb("cbl", [128, 16, 128], stack=s0); b_cbl = Buf()
            wst = [sb("wada%d" % i, [128, 3 * D], stack=s0) for i in range(2)]
            b_wst = [Buf(), Buf()]
            bada = sb("bada", [128, 3 * D], stack=s0); b_bada = Buf()
            P.dma(ct[:], CT[:, :], [], [b_ct], b_ct)
            P.dma(bada[:], BADA[:, :], [], [b_bada], b_bada)
            P.op("act", lambda e: e.activation(out=ct[:], in_=ct[:], func=AF.Silu), [b_ct], [b_ct])
            for j in range(16):
                P.op("dve", lambda e, j=j: e.tensor_scalar_mul(out=cbl[:, j, :], in0=ones, scalar1=ct[:, j:j + 1]), [b_ct, b_cst], [b_cbl])
            for k in range(8):
                P.dma(wst[k % 2][:], WADA[k * 128:(k + 1) * 128, :], [], [b_wst[k % 2]], b_wst[k % 2])
                for g in range(6):
                    P.op("pe", lambda e, k=k, g=g: e.matmul(psum[g][:, :], lhsT=cbl[:, k, :], rhs=wst[k % 2][:, g * 512:(g + 1) * 512], start=(k == 0), stop=(k == 7)),
                         [b_cbl, b_wst[k % 2]], [PB[g]])
            for g in range(6):
                dsti = [1, 1, 0, 0, 2, 2][g]
                addc = 1.0 if dsti == 0 else 0.0
                P.op("dve", lambda e, g=g, dsti=dsti: e.tensor_tensor(out=mod[:, dsti, (g % 2) * 512:(g % 2) * 512 + 512], in0=psum[g][:, :], in1=bada[:, g * 512:(g + 1) * 512], op=ALU.add),
                     [PB[g], b_bada], [b_mod])
            P.op("pool", lambda e: e.tensor_scalar_add(out=mod[:, 0, :], in0=mod[:, 0, :], scalar1=1.0), [b_mod], [b_mod])
            for k in range(8):
                P.dma(wst[k % 2][:], WADA[k * 128:(k + 1) * 128, :], [], [b_wst[k % 2]], b_wst[k % 2])
                for g in range(4):
                    P.op("pe", lambda e, k=k, g=g: e.matmul(psum[g][:, :], lhsT=cbl[:, 8 + k, :], rhs=wst[k % 2][:, g * 512:(g + 1) * 512], start=(k == 0), stop=(k == 7)),
                         [b_cbl, b_wst[k % 2]], [PB[g]])
            for g in range(4):
                dsti = [4, 4, 3, 3][g]
                P.op("dve", lambda e, g=g, dsti=dsti: e.tensor_tensor(out=mod[:, dsti, (g % 2) * 512:(g % 2) * 512 + 512], in0=psum[g][:, :], in1=bada[:, g * 512:(g + 1) * 512], op=ALU.add),
                     [PB[g], b_bada], [b_mod])
            P.op("pool", lambda e: e.tensor_scalar_add(out=mod[:, 3, :], in0=mod[:, 3, :], scalar1=1.0), [b_mod], [b_mod])

        P.barrier()
        xt = sb("xt", [128, D]); b_xt = Buf()
        xh = sb("xh", [4, D]); b_xh = Buf()
        junk = sb("junk", [128, D]); b_junk = Buf()
        st = sb("st", [128, 8]); b_st = Buf()
        hb = sb("hb", [128, D], BF16); b_hb = Buf()
        hhb = sb("hhb", [4, D], BF16); b_hhb = Buf()
        HT = sb("HT", [128, 8, 132], BF16); b_HT = Buf()

        def make_HT(v, is_ctx):
            mi = 3 if is_ctx else 0
            P.dma(xt[:], XV[v * 128:(v + 1) * 128, :], [], [b_xt], b_xt)
            P.dma(xh[:], XH[v * 4:(v + 1) * 4, :], [], [b_xh], b_xh)
            for (src, bsrc, np_, col, dst, bdst) in ((xt, b_xt, 128, 0, hb, b_hb), (xh, b_xh, 4, 2, hhb, b_hhb)):
                P.op("act", lambda e, src=src, np_=np_, col=col: e.activation(out=junk[0:np_, :], in_=src[0:np_, :], func=AF.Square, accum_out=st[0:np_, col:col + 1]),
                     [bsrc], [b_junk, b_st])
                P.op("act", lambda e, np_=np_, col=col: e.activation(out=st[0:np_, col + 1:col + 2], in_=st[0:np_, col:col + 1], func=AF.Ln, scale=1.0 / D, bias=epsc[0:np_, 0:1]),
                          [b_st, b_eps], [b_st])
                P.op("act", lambda e, np_=np_, col=col: e.activation(out=st[0:np_, col + 1:col + 2], in_=st[0:np_, col + 1:col + 2], func=AF.Exp, scale=-0.5),
                     [b_st], [b_st])
                if np_ == 4:
                    P.op("dve", lambda e, v=v: e.tensor_tensor(out=st[0:4, 3:4], in0=st[0:4, 3:4], in1=hm[:, v:v + 1], op=ALU.mult), [b_st, b_hm], [b_st])
                P.op("dve", lambda e, src=src, np_=np_, col=col: e.scalar_tensor_tensor(out=junk[0:np_, :], in0=src[0:np_, :], scalar=st[0:np_, col + 1:col + 2], in1=mod[0:np_, mi, :], op0=ALU.mult, op1=ALU.mult),
                     [bsrc, b_st, b_mod, b_junk], [b_junk])
                if np_ == 4:
                    P.op("dve", lambda e, v=v, dst=dst: e.scalar_tensor_tensor(out=dst[0:4, :], in0=mod[0:4, mi + 1, :], scalar=hm[:, v:v + 1], in1=junk[0:4, :], op0=ALU.mult, op1=ALU.add),
                         [b_junk, b_mod, b_hm], [bdst])
                else:
                    P.op("pool", lambda e, dst=dst: e.tensor_tensor(out=dst[:, :], in0=junk[:, :], in1=mod[:, mi + 1, :], op=ALU.add), [b_junk, b_mod], [bdst])
            for k in range(8):
                bank, slot = divmod(k, 3)
                o = psum[bank][:, slot * 132:(slot + 1) * 132]
                P.op("pe", lambda e, k=k, o=o: e.matmul(o, lhsT=hb[:, k * 128:(k + 1) * 128], rhs=SelM, start=True, stop=False), [b_hb, b_cbf], [PB[bank]])
                P.op("pe", lambda e, k=k, o=o: e.matmul(o, lhsT=hhb[0:4, k * 128:(k + 1) * 128], rhs=SelH, start=False, stop=True), [b_hhb, b_cbf], [PB[bank]])
            for bank in range(3):
                n = 3 if bank < 2 else 2
                P.op("act", lambda e, bank=bank, n=n: e.activation(out=HT[:, bank * 3:bank * 3 + n, :], in_=psum[bank][:, 0:n * 132].rearrange("p (a b) -> p a b", a=n), func=AF.Copy),
                     [PB[bank]], [b_HT])

        def load_w_bf16(dst, bdst, col0, ncols, stage, bstage, dcol0=0):
            i = 0
            for k in range(8):
                for c0 in range(0, ncols, 1024):
                    n = min(1024, ncols - c0)
                    s_, bs_ = stage[i % 2], bstage[i % 2]
                    P.dma(s_[:, 0:n], WIN[k * 128:(k + 1) * 128, col0 + c0:col0 + c0 + n], [], [bs_], bs_)
                    eng = ["act", "pool", "dve"][i % 3]
                    if eng == "act":
                        P.op("act", lambda e, s_=s_, k=k, c0=c0, n=n: e.activation(out=dst[:, k, dcol0 + c0:dcol0 + c0 + n], in_=s_[:, 0:n], func=AF.Copy), [bs_], [bdst])
                    else:
                        P.op(eng, lambda e, s_=s_, k=k, c0=c0, n=n: e.tensor_copy(out=dst[:, k, dcol0 + c0:dcol0 + c0 + n], in_=s_[:, 0:n]), [bs_], [bdst])
                    i += 1

        def qk_norm_rope(ps_banks, pbufs, gain, tab, b_tab, dst, b_dst, tmp, b_tmp):
            for hf in range(2):
                P.op("act", lambda e, hf=hf: e.activation(out=tmp[0][:, hf * 512:(hf + 1) * 512], in_=ps_banks[hf][:, :], func=AF.Square), [pbufs[hf]], [b_tmp[0]])
            P.op("dve", lambda e: e.tensor_reduce(out=tmp[3][:, 0:16], in_=tmp[0][:].rearrange("p (g d) -> p g d", d=64), axis=AX.X, op=ALU.add),
                 [b_tmp[0]], [b_tmp[3]])
            P.op("act", lambda e: e.activation(out=tmp[3][:, 0:16], in_=tmp[3][:, 0:16], func=AF.Ln, scale=1.0 / 64, bias=epsc[:, 0:1]), [b_tmp[3], b_eps], [b_tmp[3]])
            P.op("act", lambda e: e.activation(out=tmp[3][:, 0:16], in_=tmp[3][:, 0:16], func=AF.Exp, scale=-0.5), [b_tmp[3]], [b_tmp[3]])
            for hf in range(2):
                P.op("dve", lambda e, hf=hf: e.tensor_tensor(out=tmp[0][:, hf * 512:(hf + 1) * 512].rearrange("p (g d) -> p g d", d=64),
                                                            in0=ps_banks[hf][:, :].rearrange("p (g d) -> p g d", d=64),
                                                            in1=tmp[3][:, hf * 8:(hf + 1) * 8].unsqueeze(2).to_broadcast([128, 8, 64]), op=ALU.mult),
                     [pbufs[hf], b_tmp[3], b_tmp[0]], [b_tmp[0]])
            P.op("pool", lambda e: e.tensor_tensor(out=tmp[1][:].rearrange("p (g d) -> p g d", d=64), in0=tmp[0][:].rearrange("p (g d) -> p g d", d=64),
                                                   in1=gain.unsqueeze(1).to_broadcast([128, 16, 64]), op=ALU.mult), [b_tmp[0], b_vecs], [b_tmp[1]])
            xv = tmp[1][:].rearrange("p (g h t f) -> p g h t f", g=16, h=2, t=2)
            tv = tab[:].rearrange("p (h t f) -> p h t f", h=2, t=2)
            dv = dst[:].rearrange("p (g h t f) -> p g h t f", g=16, h=2, t=2)
            av = tmp[2][:].rearrange("p (g h t f) -> p g h t f", g=16, h=2, t=2)
            for hh in range(2):
                x1 = xv[:, :, hh, 0, :]
                x2 = xv[:, :, hh, 1, :]
                cs = tv[:, hh, 0, :].unsqueeze(1).to_broadcast([128, 16, 16])
                sn = tv[:, hh, 1, :].unsqueeze(1).to_broadcast([128, 16, 16])
                a1 = av[:, :, hh, 0, :]
                a2 = av[:, :, hh, 1, :]
                e1, e2 = ("dve", "pool")
                P.op(e1, lambda e, x1=x1, cs=cs, a1=a1: e.tensor_tensor(out=a1, in0=x1, in1=cs, op=ALU.mult), [b_tmp[1], b_tab], [b_tmp[2]])
                P.op(e2, lambda e, x2=x2, sn=sn, a2=a2: e.tensor_tensor(out=a2, in0=x2, in1=sn, op=ALU.mult), [b_tmp[1], b_tab], [b_tmp[2]])
                P.op(e1, lambda e, a1=a1, a2=a2, hh=hh: e.tensor_tensor(out=dv[:, :, hh, 0, :], in0=a1, in1=a2, op=ALU.subtract), [b_tmp[2]], [b_dst])
                P.op(e1, lambda e, x2=x2, cs=cs, a1=a1: e.tensor_tensor(out=a1, in0=x2, in1=cs, op=ALU.mult), [b_tmp[1], b_tab, b_tmp[2]], [b_tmp[2]])
                P.op(e2, lambda e, x1=x1, sn=sn, a2=a2: e.tensor_tensor(out=a2, in0=x1, in1=sn, op=ALU.mult), [b_tmp[1], b_tab, b_tmp[2]], [b_tmp[2]])
                P.op(e1, lambda e, a1=a1, a2=a2, hh=hh: e.tensor_tensor(out=dv[:, :, hh, 1, :], in0=a1, in1=a2, op=ALU.add), [b_tmp[2]], [b_dst])


        with ExitStack() as s1:
            W1 = sb("W1", [128, 8, 5120], BF16, stack=s1); b_W1 = Buf()
            wab = sb("wab", [128, 2, 8, 16], BF16, stack=s1); b_wab = Buf()
            wabs = sb("wabs", [128, 2, 8, 16], stack=s1); b_wabs = Buf()
            cw = sb("cw", [128, 2, 24, 5], stack=s1); b_cw = Buf()
            stage = [junk, xt]
            bstage = [b_junk, b_xt]
            tab = sb("tab1", [128, 64], stack=s1); b_tab = Buf()
            tmpA = [sb("tmpA1_%d" % i, [128, D], stack=s1) for i in range(3)] + [sb("tmpA31", [128, 16], stack=s1)]
            b_tmpA = [Buf() for _ in range(4)]
            krot = sb("krot1", [128, D], BF16, stack=s1); b_krot = Buf()
            load_w_bf16(W1, b_W1, 0, 3072, stage, bstage, 0)
            load_w_bf16(W1, b_W1, 5152, 2048, stage, bstage, 3072)
            P.dma(wabs[:], WAB.rearrange("s (k p) c -> p s k c", p=128), [], [b_wabs], b_wabs)
            P.op("dve", lambda e: e.tensor_copy(out=wab[:], in_=wabs[:]), [b_wabs], [b_wab])
            P.dma(cw[:], CW.rearrange("s (c p) j -> p s c j", p=128), [], [b_cw], b_cw)

            Sst = sb("Sst", [128, H, 128], stack=s1); b_S = [Buf() for _ in range(H)]
            pre = sb("pre", [128, 3, 132], stack=s1); b_pre = Buf()
            acc = [sb("acc%d" % i, [128, 128], stack=s1) for i in range(2)]; b_acc = [Buf(), Buf()]
            ptmp = sb("ptmp", [128, 128], stack=s1); b_ptmp = Buf()
            qkvT = sb("qkvT", [128, 24, 128], stack=s1); b_qkvT = [Buf() for _ in range(24)]
            gb = sb("gb", [128, 6, 8], stack=s1); b_gb = Buf()
            from types import SimpleNamespace
            TS = []
            for si_ in range(2):
                T = SimpleNamespace()
                T.Wb = [sb("Wb%d_%d" % (i, si_), [128, 512], stack=s1) for i in range(2)]; T.b_Wb = [Buf(), Buf()]
                T.NY = sb("NY_%d" % si_, [128, 256], stack=s1); T.b_NY = Buf()
                T.NYm = sb("NYm_%d" % si_, [128, 256], stack=s1); T.b_NYm = Buf()
                T.TU = sb("TU_%d" % si_, [128, 256], stack=s1); T.b_TU = Buf()
                T.TU2 = sb("TU2_%d" % si_, [128, 256], stack=s1); T.b_TU2 = Buf()
                T.Wm = sb("Wm_%d" % si_, [128, 256], stack=s1); T.b_Wm = Buf()
                T.LgIb = sb("LgIb_%d" % si_, [128, 256], stack=s1); T.b_LgIb = Buf()
                T.Em = sb("Em_%d" % si_, [128, 128], stack=s1); T.b_Em = Buf()
                T.ETm = sb("ETm_%d" % si_, [128, 128], stack=s1); T.b_ETm = Buf()
                T.EMs = sb("EMs_%d" % si_, [128, 128], stack=s1); T.b_EMs = Buf()
                T.ETs = sb("ETs_%d" % si_, [128, 128], stack=s1); T.b_ETs = Buf()
                T.ETi = sb("ETi_%d" % si_, [128, 128], stack=s1); T.b_ETi = Buf()
                T.npos = T.LgIb; T.b_npos = T.b_LgIb
                T.sq = sb("sq_%d" % si_, [128, 128], stack=s1); T.b_sq = Buf()
                T.rs = sb("rs_%d" % si_, [128, 128], stack=s1); T.b_rs = Buf()
                T.kTn = sb("kTn_%d" % si_, [128, 128], stack=s1); T.b_kTn = Buf()
                T.qTn = sb("qTn_%d" % si_, [128, 128], stack=s1); T.b_qTn = Buf()
                T.Kbp = sb("Kbp_%d" % si_, [128, 128], stack=s1); T.b_Kbp = Buf()
                T.kd = sb("kd_%d" % si_, [128, 128], stack=s1); T.b_kd = Buf()
                T.Vb = sb("Vb_%d" % si_, [128, 128], stack=s1); T.b_Vb = Buf()
                T.TT = sb("TT_%d" % si_, [128, 128], stack=s1); T.b_TT = Buf()
                T.wT = sb("wT_%d" % si_, [128, 128], stack=s1); T.b_wT = Buf()
                T.usb = sb("usb_%d" % si_, [128, 128], stack=s1); T.b_usb = Buf()
                T.vnew = sb("vnew_%d" % si_, [128, 128], stack=s1); T.b_vnew = Buf()
                T.egl = sb("egl_%d" % si_, [128, 1], stack=s1); T.b_egl = Buf()
                T.eR = sb("eR_%d" % si_, [128, 128], stack=s1); T.b_eR = Buf()
                T.QdT = sb("QdT_%d" % si_, [128, 128], stack=s1); T.b_QdT = Buf()
                T.qkT = sb("qkT_%d" % si_, [128, 128], stack=s1); T.b_qkT = Buf()
                T.banks = (0, 1, 2, 2) if si_ == 0 else (3, 4, 5, 5)
                TS.append(T)
            osb = junk; b_osb = b_junk
            KTsb = hb[:].rearrange("p (h t) -> p h t", h=H); b_KTsb = b_hb
            Vp = sb("Vp", [128, H, 129], BF16, stack=s1); b_Vp = Buf()
            b_KTS = Buf(); b_VPS = Buf(); b_OS = Buf()
            P.op("pool", lambda e: e.memset(Vp[:], 1.0), [], [b_Vp])

            for v in range(NV):
                scan = 0 if v < S2 else 1
                is_ctx = v in (0, 1, S2, S2 + 1)
                own = (OWN0 <= v < S2) or (v >= OWN1)
                kv = v < NKV
                if v in (0, S2):
                    for h in range(H):
                        P.op("pool", lambda e, h=h: e.memset(Sst[:, h, :], 0.0), [], [b_S[h]])
                make_HT(v, is_ctx)
                PA = 3
                for k in range(8):
                    P.op("pe", lambda e, k=k, scan=scan: e.matmul(psum[PA][:, 0:16], lhsT=HT[:, k, 2:130], rhs=wab[:, scan, k, :], start=(k == 0), stop=(k == 7)), [b_HT, b_wab], [PB[PA]])
                P.op("dve", lambda e, scan=scan: e.tensor_tensor(out=gb[:, 4, :], in0=psum[PA][:, 0:8], in1=dtb[:, scan, :], op=ALU.add), [PB[PA], b_dtb], [b_gb])
                P.op("act", lambda e: e.activation(out=gb[:, 4, :], in_=gb[:, 4, :], func=AF.Exp), [b_gb], [b_gb])
                P.op("act", lambda e: e.activation(out=gb[:, 4, :], in_=gb[:, 4, :], func=AF.Ln, bias=1.0), [b_gb], [b_gb])
                P.op("dve", lambda e, v=v, scan=scan: e.scalar_tensor_tensor(out=gb[:, 0, :], in0=gb[:, 4, :], scalar=gm[:, v:v + 1], in1=negA[:, scan, :], op0=ALU.mult, op1=ALU.mult), [b_gb, b_gm, b_negA], [b_gb])
                P.op("act", lambda e: e.activation(out=gb[:, 5, :], in_=psum[PA][:, 8:16], func=AF.Sigmoid), [PB[PA], b_gb], [b_gb])
                P.op("dve", lambda e, v=v: e.tensor_scalar_mul(out=gb[:, 1, :], in0=gb[:, 5, :], scalar1=gm[:, v:v + 1]), [b_gb, b_gm], [b_gb])
                P.op("pe", lambda e: e.matmul(psum[PA][:, 16:24], lhsT=Ltri, rhs=gb[:, 0, :], start=True, stop=True), [b_cst, b_gb], [PB[PA]])
                P.op("dve", lambda e: e.tensor_copy(out=gb[:, 2, :], in_=psum[PA][:, 16:24]), [PB[PA], b_gb], [b_gb])
                P.op("act", lambda e: e.activation(out=gb[:, 3, :], in_=psum[PA][:, 16:24], func=AF.Exp), [PB[PA], b_gb], [b_gb])
                P.op("dve", lambda e: e.tensor_tensor(out=gb[:, 3, :], in0=gb[:, 3, :], in1=gb[:, 1, :], op=ALU.mult), [b_gb], [b_gb])

                def pre_steps(cbs):
                    for gi in range(0, len(cbs), 3):
                        grp = cbs[gi:gi + 3]
                        bank = 6 + (gi // 3) % 2
                        for si, cb in enumerate(grp):
                            for k in range(8):
                                P.op("pe", lambda e, k=k, cb=cb, si=si, bank=bank: e.matmul(psum[bank][:, si * 132:(si + 1) * 132], lhsT=W1[:, k, cb * 128:(cb + 1) * 128], rhs=HT[:, k, :], start=(k == 0), stop=(k == 7)),
                                     [b_W1, b_HT], [PB[bank]])
                                yield
                        n = len(grp)
                        P.op("act", lambda e, bank=bank, n=n: e.activation(out=pre[:, 0:n, :], in_=psum[bank][:, 0:n * 132].rearrange("p (a b) -> p a b", a=n), func=AF.Copy), [PB[bank]], [b_pre])
                        yield
                        for si, cb in enumerate(grp):
                            eng = "dve" if (cb % 2 == 0) else "pool"
                            a_, ba_ = acc[cb % 2], b_acc[cb % 2]
                            P.op(eng, lambda e, si=si, cb=cb, a_=a_, scan=scan: e.tensor_scalar_mul(out=a_[:], in0=pre[:, si, 0:128], scalar1=cw[:, scan, cb, 0:1]), [b_pre, b_cw], [ba_])
                            yield
                            for j in range(1, 5):
                                if eng == "dve":
                                    P.op(eng, lambda e, si=si, cb=cb, a_=a_, j=j, scan=scan: e.scalar_tensor_tensor(out=a_[:], in0=pre[:, si, j:j + 128], scalar=cw[:, scan, cb, j:j + 1], in1=a_[:], op0=ALU.mult, op1=ALU.add),
                                         [b_pre, b_cw, ba_], [ba_])
                                    yield
                                else:
                                    P.op(eng, lambda e, si=si, cb=cb, j=j, scan=scan: e.tensor_scalar_mul(out=ptmp[:], in0=pre[:, si, j:j + 128], scalar1=cw[:, scan, cb, j:j + 1]), [b_pre, b_cw], [b_ptmp])
                                    yield
                                    P.op(eng, lambda e, a_=a_: e.tensor_tensor(out=a_[:], in0=a_[:], in1=ptmp[:], op=ALU.add), [b_ptmp, ba_], [ba_])
                                    yield
                            P.op("act", lambda e, cb=cb, a_=a_: e.activation(out=qkvT[:, cb, :], in_=a_[:], func=AF.Silu), [ba_], [b_qkvT[cb]])
                            yield

                    return
                    yield
                def kv_steps():
                    P.dma(tab[:], ROPE[v * 128:(v + 1) * 128, :], [], [b_tab], b_tab)
                    yield
                    for hf in range(2):
                        for k in range(8):
                            P.op("pe", lambda e, k=k, hf=hf: e.matmul(psum[6 + hf][:, :], lhsT=HT[:, k, 2:130], rhs=W1[:, k, 3072 + hf * 512:3072 + (hf + 1) * 512], start=(k == 0), stop=(k == 7)),
                                 [b_HT, b_W1], [PB[6 + hf]])
                            yield
                    qk_norm_rope([psum[6], psum[7]], [PB[6], PB[7]], knw, tab, b_tab, krot, b_krot, tmpA, b_tmpA)
                    yield
                    for h in range(H):
                        bank = 6 + h // 4
                        P.op("pe", lambda e, h=h, bank=bank: e.matmul(psum[bank][:, (h % 4) * 128:(h % 4 + 1) * 128], lhsT=krot[:, h * 128:(h + 1) * 128], rhs=identb, start=True, stop=True),
                             [b_krot, b_cbf], [PB[bank]])
                        yield
                    for hf in range(2):
                        P.op("act", lambda e, hf=hf: e.activation(out=KTsb[:, hf * 4:(hf + 1) * 4, :], in_=psum[6 + hf][:, :].rearrange("p (a b) -> p a b", a=4), func=AF.Copy), [PB[6 + hf]], [b_KTsb])
                        yield
                    P.dma(KTS[:, :, v * 128:(v + 1) * 128].rearrange("h p t -> p h t"), KTsb[:], [b_KTsb], [b_KTS], b_KTS)
                    yield
                    for hf in range(2):
                        for k in range(8):
                            P.op("pe", lambda e, k=k, hf=hf: e.matmul(psum[6 + hf][:, :], lhsT=HT[:, k, 2:130], rhs=W1[:, k, 4096 + hf * 512:4096 + (hf + 1) * 512], start=(k == 0), stop=(k == 7)),
                                 [b_HT, b_W1], [PB[6 + hf]])
                            yield
                    for hf in range(2):
                        P.op("act", lambda e, hf=hf: e.activation(out=Vp[:, hf * 4:(hf + 1) * 4, 0:128], in_=psum[6 + hf][:, :].rearrange("p (a b) -> p a b", a=4), func=AF.Copy), [PB[6 + hf]], [b_Vp])
                        yield
                    P.dma(VPS[:, :, v, :].rearrange("h p c -> p h c"), Vp[:], [b_Vp], [b_VPS], b_VPS)
                    yield

                    return
                    yield
                def head_steps(h, T):
                        qcb, kcb, vcb = h, 8 + h, 16 + h
                        PR, PG, PS_, PU = T.banks
                        P.op("pool", lambda e, kcb=kcb: e.tensor_tensor(out=T.sq[:], in0=qkvT[:, kcb, :], in1=qkvT[:, kcb, :], op=ALU.mult), [b_qkvT[kcb]], [T.b_sq])
                        yield
                        P.op("pe", lambda e: e.matmul(psum[PR][:, 256:384], lhsT=ones, rhs=T.sq[:], start=True, stop=True), [b_cst, T.b_sq], [PB[PR]])
                        yield
                        P.op("act", lambda e: e.activation(out=T.rs[:], in_=psum[PR][:, 256:384], func=AF.Ln, bias=epsc[:, 0:1]), [PB[PR], b_eps], [T.b_rs])
                        yield
                        P.op("act", lambda e: e.activation(out=T.rs[:], in_=T.rs[:], func=AF.Exp, scale=-0.5), [T.b_rs], [T.b_rs])
                        yield
                        P.op("dve", lambda e, kcb=kcb: e.tensor_tensor(out=T.kTn[:], in0=qkvT[:, kcb, :], in1=T.rs[:], op=ALU.mult), [b_qkvT[kcb], T.b_rs], [T.b_kTn])
                        yield
                        if own:
                            P.op("pool", lambda e, qcb=qcb: e.tensor_tensor(out=T.sq[:], in0=qkvT[:, qcb, :], in1=qkvT[:, qcb, :], op=ALU.mult), [b_qkvT[qcb]], [T.b_sq])
                            yield
                            P.op("pe", lambda e: e.matmul(psum[PR][:, 384:512], lhsT=ones, rhs=T.sq[:], start=True, stop=True), [b_cst, T.b_sq], [PB[PR]])
                            yield
                            P.op("act", lambda e: e.activation(out=T.rs[:], in_=psum[PR][:, 384:512], func=AF.Ln, bias=epsc[:, 0:1]), [PB[PR], b_eps], [T.b_rs])
                            yield
                            P.op("act", lambda e: e.activation(out=T.rs[:], in_=T.rs[:], func=AF.Exp, scale=-0.5), [T.b_rs], [T.b_rs])
                            yield
                            P.op("dve", lambda e, qcb=qcb: e.scalar_tensor_tensor(out=T.qTn[:], in0=qkvT[:, qcb, :], scalar=128.0 ** -0.5, in1=T.rs[:], op0=ALU.mult, op1=ALU.mult), [b_qkvT[qcb], T.b_rs], [T.b_qTn])
                            yield
                        P.op("dve", lambda e, h=h: e.tensor_scalar_mul(out=T.LgIb[:, 0:128], in0=Ltri, scalar1=gb[:, 0, h:h + 1]), [b_cst, b_gb], [T.b_LgIb])
                        yield
                        P.op("pool", lambda e, h=h: e.tensor_scalar_mul(out=T.LgIb[:, 128:256], in0=ident, scalar1=gb[:, 1, h:h + 1]), [b_cst, b_gb], [T.b_LgIb])
                        yield
                        P.op("pe", lambda e: e.matmul(psum[PR][:, 0:256], lhsT=ones, rhs=T.LgIb[:], start=True, stop=True), [b_cst, T.b_LgIb], [PB[PR]])
                        yield
                        P.op("dve", lambda e, h=h: e.tensor_scalar(out=T.npos[:, 0:128], in0=psum[PR][:, 0:128], scalar1=gb[:, 2, h:h + 1], scalar2=0.0, op0=ALU.subtract, op1=ALU.max), [PB[PR], b_gb], [T.b_npos])
                        yield
                        P.op("dve", lambda e, h=h: e.tensor_scalar(out=T.npos[:, 128:256], in0=psum[PR][:, 0:128], scalar1=gb[:, 2, h:h + 1], scalar2=0.0, op0=ALU.subtract, op1=ALU.min), [PB[PR], b_gb], [T.b_npos])
                        yield
                        P.op("act", lambda e: e.activation(out=T.Em[:], in_=T.npos[:, 0:128], func=AF.Exp, scale=-1.0), [T.b_npos], [T.b_Em])
                        yield
                        P.op("act", lambda e: e.activation(out=T.ETm[:], in_=T.npos[:, 128:256], func=AF.Exp), [T.b_npos], [T.b_ETm])
                        yield
                        P.op("pool", lambda e: e.tensor_tensor(out=T.EMs[:], in0=T.Em[:], in1=negMs, op=ALU.mult), [T.b_Em, b_cst], [T.b_EMs])
                        yield
                        P.op("pool", lambda e: e.tensor_tensor(out=T.ETs[:], in0=T.ETm[:], in1=negMsT, op=ALU.mult), [T.b_ETm, b_cst], [T.b_ETs])
                        yield
                        P.op("pe", lambda e: e.matmul(psum[PG][:, 0:128], lhsT=T.kTn[:], rhs=T.kTn[:], start=True, stop=True), [T.b_kTn], [PB[PG]])
                        yield
                        if own:
                            P.op("pe", lambda e: e.matmul(psum[PG][:, 128:256], lhsT=T.kTn[:], rhs=T.qTn[:], start=True, stop=True), [T.b_kTn, T.b_qTn], [PB[PG]])
                            yield
                        P.op("dve", lambda e, h=h: e.scalar_tensor_tensor(out=T.NY[:, 0:128], in0=psum[PG][:, 0:128], scalar=gb[:, 1, h:h + 1], in1=T.EMs[:], op0=ALU.mult, op1=ALU.mult), [PB[PG], b_gb, T.b_EMs], [T.b_NY])
                        yield
                        P.op("dve", lambda e: e.tensor_tensor(out=T.NY[:, 128:256], in0=psum[PG][:, 0:128], in1=T.ETs[:], op=ALU.mult), [PB[PG], T.b_ETs], [T.b_NY])
                        yield
                        P.op("dve", lambda e: e.tensor_tensor(out=T.NY[:, 128:256], in0=T.NY[:, 128:256], in1=psum[PR][:, 128:256], op=ALU.mult), [PB[PR], T.b_NY], [T.b_NY])
                        yield
                        P.op("pool", lambda e: e.tensor_tensor(out=T.Wb[0][:, 0:128], in0=T.NY[:, 0:128], in1=BD32, op=ALU.mult), [T.b_NY, b_cst], [T.b_Wb[0]])
                        yield
                        P.op("pool", lambda e: e.tensor_tensor(out=T.Wb[0][:, 256:384], in0=T.NY[:, 128:256], in1=BD32, op=ALU.mult), [T.b_NY, b_cst], [T.b_Wb[0]])
                        yield
                        P.op("pool", lambda e: e.tensor_tensor(out=T.Wb[1][:, 128:256], in0=T.Wb[0][:, 0:128], in1=ident, op=ALU.add), [T.b_Wb[0], b_cst], [T.b_Wb[1]])
                        yield
                        P.op("pool", lambda e: e.tensor_tensor(out=T.Wb[1][:, 384:512], in0=T.Wb[0][:, 256:384], in1=ident, op=ALU.add), [T.b_Wb[0], b_cst], [T.b_Wb[1]])
                        yield
                        for n in range(4):
                            cur, nxt = T.Wb[n % 2], T.Wb[(n + 1) % 2]
                            bc, bn = T.b_Wb[n % 2], T.b_Wb[(n + 1) % 2]
                            w_ = 128 if n == 0 else 256
                            P.op("pe", lambda e, cur=cur, w_=w_: e.matmul(psum[PS_][:, 0:w_], lhsT=cur[:, 256:384], rhs=cur[:, 0:w_], start=True, stop=True), [bc], [PB[PS_]])
                            yield
                            P.op("pe", lambda e, cur=cur, w_=w_: e.matmul(psum[PS_][:, 256:256 + w_], lhsT=cur[:, 0:128], rhs=cur[:, 256:256 + w_], start=True, stop=True), [bc], [PB[PS_]])
                            yield
                            P.op("act", lambda e, nxt=nxt: e.activation(out=nxt[:].rearrange("p (a b) -> p a b", a=2)[:, :, 0:128], in_=psum[PS_][:, :].rearrange("p (a b) -> p a b", a=2)[:, :, 0:128], func=AF.Copy), [PB[PS_]], [bn])
                            yield
                            if n > 0:
                                P.op("dve", lambda e, cur=cur, nxt=nxt: e.tensor_tensor(out=nxt[:].rearrange("p (a b) -> p a b", a=2)[:, :, 128:256], in0=psum[PS_][:, :].rearrange("p (a b) -> p a b", a=2)[:, :, 128:256],
                                                                                        in1=cur[:].rearrange("p (a b) -> p a b", a=2)[:, :, 128:256], op=ALU.add), [PB[PS_], bc], [bn])
                                yield
                        P.op("pe", lambda e: e.matmul(psum[PS_][:, 0:128], lhsT=T.Wb[0][:, 256:384], rhs=T.Wb[0][:, 128:256], start=True, stop=True), [T.b_Wb[0]], [PB[PS_]])
                        yield
                        P.op("pe", lambda e: e.matmul(psum[PS_][:, 128:256], lhsT=T.Wb[0][:, 0:128], rhs=T.Wb[0][:, 384:512], start=True, stop=True), [T.b_Wb[0]], [PB[PS_]])
                        yield
                        P.op("dve", lambda e: e.tensor_tensor(out=T.TU[:].rearrange("p (a b) -> p a b", a=2), in0=psum[PS_][:, 0:256].rearrange("p (a b) -> p a b", a=2),
                                                              in1=T.Wb[0][:].rearrange("p (a b) -> p a b", a=2)[:, :, 128:256], op=ALU.add), [PB[PS_], T.b_Wb[0]], [T.b_TU])
                        yield
                        P.op("pool", lambda e: e.tensor_tensor(out=T.NYm[:, 0:128], in0=T.NY[:, 128:256], in1=M64, op=ALU.mult), [T.b_NY, b_cst], [T.b_NYm])
                        yield
                        P.op("pool", lambda e: e.tensor_tensor(out=T.NYm[:, 128:256], in0=T.NY[:, 0:128], in1=M64, op=ALU.mult), [T.b_NY, b_cst], [T.b_NYm])
                        yield
                        P.op("pe", lambda e: e.matmul(psum[PS_][:, 256:384], lhsT=T.NYm[:, 0:128], rhs=T.TU[:, 0:128], start=True, stop=True), [T.b_NYm, T.b_TU], [PB[PS_]])
                        yield
                        P.op("pe", lambda e: e.matmul(psum[PS_][:, 384:512], lhsT=T.NYm[:, 128:256], rhs=T.TU[:, 128:256], start=True, stop=True), [T.b_NYm, T.b_TU], [PB[PS_]])
                        yield
                        P.op("act", lambda e: e.activation(out=T.Wm[:], in_=psum[PS_][:, 256:512], func=AF.Copy), [PB[PS_]], [T.b_Wm])
                        yield
                        P.op("pe", lambda e: e.matmul(psum[PS_][:, 0:128], lhsT=T.TU[:, 128:256], rhs=T.Wm[:, 0:128], start=True, stop=True), [T.b_TU, T.b_Wm], [PB[PS_]])
                        yield
                        P.op("pe", lambda e: e.matmul(psum[PS_][:, 128:256], lhsT=T.TU[:, 0:128], rhs=T.Wm[:, 128:256], start=True, stop=True), [T.b_TU, T.b_Wm], [PB[PS_]])
                        yield
                        P.op("dve", lambda e: e.tensor_tensor(out=T.TU2[:], in0=psum[PS_][:, 0:256], in1=T.TU[:], op=ALU.add), [PB[PS_], T.b_TU], [T.b_TU2])
                        yield
                        P.op("pool", lambda e: e.tensor_tensor(out=T.NYm[:, 128:256], in0=T.NY[:, 0:128], in1=M128, op=ALU.mult), [T.b_NY, b_cst, T.b_NYm], [T.b_NYm])
                        yield
                        P.op("pe", lambda e: e.matmul(psum[PS_][:, 384:512], lhsT=T.NYm[:, 128:256], rhs=T.TU2[:, 128:256], start=True, stop=True), [T.b_NYm, T.b_TU2], [PB[PS_]])
                        yield
                        P.op("act", lambda e: e.activation(out=T.Wm[:, 128:256], in_=psum[PS_][:, 384:512], func=AF.Copy), [PB[PS_]], [T.b_Wm])
                        yield
                        P.op("pe", lambda e: e.matmul(psum[PS_][:, 128:256], lhsT=T.TU2[:, 0:128], rhs=T.Wm[:, 128:256], start=True, stop=True), [T.b_TU2, T.b_Wm], [PB[PS_]])
                        yield
                        P.op("dve", lambda e: e.tensor_tensor(out=T.TT[:], in0=psum[PS_][:, 128:256], in1=T.TU2[:, 128:256], op=ALU.add), [PB[PS_], T.b_TU2], [T.b_TT])
                        yield
                        P.op("pe", lambda e: e.matmul(psum[PR][:, 256:384], lhsT=T.kTn[:], rhs=ident, start=True, stop=True), [T.b_kTn, b_cst], [PB[PR]])
                        yield
                        P.op("pe", lambda e, vcb=vcb: e.matmul(psum[PR][:, 384:512], lhsT=qkvT[:, vcb, :], rhs=ident, start=True, stop=True), [b_qkvT[vcb], b_cst], [PB[PR]])
                        yield
                        P.op("dve", lambda e, h=h: e.tensor_scalar_mul(out=T.Kbp[:], in0=psum[PR][:, 256:384], scalar1=gb[:, 3, h:h + 1]), [PB[PR], b_gb], [T.b_Kbp])
                        yield
                        P.op("dve", lambda e: e.tensor_scalar_mul(out=T.kd[:], in0=psum[PR][:, 256:384], scalar1=T.ETm[:, 127:128]), [PB[PR], T.b_ETm], [T.b_kd])
                        yield
                        P.op("dve", lambda e, h=h: e.tensor_scalar_mul(out=T.Vb[:], in0=psum[PR][:, 384:512], scalar1=gb[:, 1, h:h + 1]), [PB[PR], b_gb], [T.b_Vb])
                        yield
                        P.op("pe", lambda e: e.matmul(psum[PU][:, 0:128], lhsT=T.TT[:], rhs=T.Vb[:], start=True, stop=True), [T.b_TT, T.b_Vb], [PB[PU]])
                        yield
                        P.op("pe", lambda e: e.matmul(psum[PU][:, 128:256], lhsT=T.Kbp[:], rhs=T.TT[:], start=True, stop=True), [T.b_TT, T.b_Kbp], [PB[PU]])
                        yield
                        P.op("act", lambda e: e.activation(out=T.wT[:], in_=psum[PU][:, 128:256], func=AF.Copy), [PB[PU]], [T.b_wT])
                        yield
                        P.op("act", lambda e: e.activation(out=T.usb[:], in_=psum[PU][:, 0:128], func=AF.Copy), [PB[PU]], [T.b_usb])
                        yield
                        P.op("pe", lambda e, h=h: e.matmul(psum[PU][:, 256:384], lhsT=T.wT[:], rhs=Sst[:, h, :], start=True, stop=True), [T.b_wT, b_S[h]], [PB[PU]])
                        yield
                        P.op("dve", lambda e: e.tensor_tensor(out=T.vnew[:], in0=T.usb[:], in1=psum[PU][:, 256:384], op=ALU.subtract), [T.b_usb, PB[PU]], [T.b_vnew])
                        yield
                        if own:
                            P.op("act", lambda e: e.activation(out=T.eR[:], in_=psum[PR][:, 0:128], func=AF.Exp), [PB[PR]], [T.b_eR])
                            yield
                            P.op("dve", lambda e: e.tensor_tensor(out=T.QdT[:], in0=T.qTn[:], in1=T.eR[:], op=ALU.mult), [T.b_qTn, T.b_eR], [T.b_QdT])
                            yield
                            P.op("pool", lambda e: e.tensor_tensor(out=T.ETi[:], in0=T.ETm[:], in1=MiT, op=ALU.mult), [T.b_ETm, b_cst], [T.b_ETi])
                            yield
                            P.op("dve", lambda e: e.tensor_tensor(out=T.qkT[:], in0=psum[PG][:, 128:256], in1=T.ETi[:], op=ALU.mult), [PB[PG], T.b_ETi], [T.b_qkT])
                            yield
                            P.op("pe", lambda e, h=h: e.matmul(psum[PG][:, 256:384], lhsT=T.QdT[:], rhs=Sst[:, h, :], start=True, stop=False), [T.b_QdT, b_S[h]], [PB[PG]])
                            yield
                            P.op("pe", lambda e: e.matmul(psum[PG][:, 256:384], lhsT=T.qkT[:], rhs=T.vnew[:], start=False, stop=True), [T.b_qkT, T.b_vnew], [PB[PG]])
                            yield
                            P.op("act", lambda e, h=h: e.activation(out=osb[:, h * 128:(h + 1) * 128], in_=psum[PG][:, 256:384], func=AF.Copy), [PB[PG]], [b_osb])
                            yield
                        P.op("act", lambda e: e.activation(out=T.egl[:], in_=psum[PR][:, 127:128], func=AF.Exp), [PB[PR]], [T.b_egl])
                        yield
                        P.op("pe", lambda e: e.matmul(psum[PU][:, 384:512], lhsT=T.kd[:], rhs=T.vnew[:], start=True, stop=True), [T.b_kd, T.b_vnew], [PB[PU]])
                        yield
                        P.op("dve", lambda e, h=h: e.scalar_tensor_tensor(out=Sst[:, h, :], in0=Sst[:, h, :], scalar=T.egl[:, 0:1], in1=psum[PU][:, 384:512], op0=ALU.mult, op1=ALU.add), [b_S[h], T.b_egl, PB[PU]], [b_S[h]])
                        yield
                def pair_cbs(p):
                    hs = (2 * p, 2 * p + 1)
                    c = [hs[0], hs[1]] if own else []
                    return c + [8 + hs[0], 8 + hs[1], 16 + hs[0], 16 + hs[1]]
                for _ in pre_steps(pair_cbs(0)):
                    pass
                for p_ in range(4):
                    if p_ < 3:
                        aux = pre_steps(pair_cbs(p_ + 1))
                    elif kv:
                        aux = kv_steps()
                    else:
                        aux = iter(())
                    gens = [head_steps(2 * p_, TS[0]), head_steps(2 * p_ + 1, TS[1]), aux]
                    alive = [True, True, True]
                    while any(alive):
                        for gi_ in range(3):
                            if alive[gi_]:
                                try:
                                    next(gens[gi_])
                                except StopIteration:
                                    alive[gi_] = False
                if own:
                    oi = (v - OWN0) if scan == 0 else (v - OWN1)
                    P.dma(OS[scan, oi, :, :], osb[:], [b_osb], [b_OS], b_OS)

        P.barrier()

        b_ZS = Buf(); b_GS = Buf()
        YbT = sb("YbT", [128, H, 2048], BF16); b_YbT = Buf()
        s_qt = ExitStack()
        QT = sb("QT", [128, H, 2048], BF16, stack=s_qt); b_QT = Buf()
        with ExitStack() as s2:
            W2 = sb("W2", [128, 8, 3072], BF16, stack=s2); b_W2 = Buf()
            stage = [junk, xt]
            bstage = [b_junk, b_xt]
            tabB = sb("tabB2", [128, 64], stack=s2); b_tabB = Buf()
            tmpB = [sb("tmpB2_%d" % i, [128, D], stack=s2) for i in range(3)] + [sb("tmpB32", [128, 16], stack=s2)]
            b_tmpB = [Buf() for _ in range(4)]
            krotB = sb("krotB2", [128, D], BF16, stack=s2); b_krotB = Buf()
            load_w_bf16(W2, b_W2, 3072, 1024, stage, bstage, 0)
            load_w_bf16(W2, b_W2, 4128, 1024, stage, bstage, 1024)
            load_w_bf16(W2, b_W2, 7200, 1024, stage, bstage, 2048)
            zsb = sb("zsb", [128, 2048], BF16, stack=s2); b_zsb = Buf()
            for i in range(16):
                v = OWN0 + i
                make_HT(v, False)
                P.dma(tabB[:], ROPE[v * 128:(v + 1) * 128, :], [], [b_tabB], b_tabB)
                for hf in range(2):
                    for k in range(8):
                        P.op("pe", lambda e, k=k, hf=hf: e.matmul(psum[6 + hf][:, :], lhsT=HT[:, k, 2:130], rhs=W2[:, k, 1024 + hf * 512:1024 + (hf + 1) * 512], start=(k == 0), stop=(k == 7)),
                             [b_HT, b_W2], [PB[6 + hf]])
                qk_norm_rope([psum[6], psum[7]], [PB[6], PB[7]], qnw, tabB, b_tabB, krotB, b_krotB, tmpB, b_tmpB)
                for h in range(H):
                    bank = 6 + h // 4
                    P.op("pe", lambda e, h=h, bank=bank: e.matmul(psum[bank][:, (h % 4) * 128:(h % 4 + 1) * 128], lhsT=krotB[:, h * 128:(h + 1) * 128], rhs=identb, start=True, stop=True),
                         [b_krotB, b_cbf], [PB[bank]])
                for hf in range(2):
                    P.op("act", lambda e, hf=hf, i=i: e.activation(out=QT[:, hf * 4:(hf + 1) * 4, i * 128:(i + 1) * 128], in_=psum[6 + hf][:, :].rearrange("p (a b) -> p a b", a=4), func=AF.Copy), [PB[6 + hf]], [b_QT])
                for (c0, dstt, bd, off, fn) in ((0, zsb, b_zsb, 0, AF.Silu), (2048, zsb, b_zsb, 1024, AF.Silu)):
                    for hf in range(2):
                        bank = 4 + hf
                        for k in range(8):
                            P.op("pe", lambda e, k=k, hf=hf, c0=c0, bank=bank: e.matmul(psum[bank][:, :], lhsT=HT[:, k, 2:130], rhs=W2[:, k, c0 + hf * 512:c0 + (hf + 1) * 512], start=(k == 0), stop=(k == 7)),
                                 [b_HT, b_W2], [PB[bank]])
                        P.op("act", lambda e, hf=hf, bank=bank, dstt=dstt, off=off, fn=fn: e.activation(out=dstt[:, off + hf * 512:off + (hf + 1) * 512], in_=psum[bank][:, :], func=fn), [PB[bank]], [bd])
                P.dma(ZS[i, :, :], zsb[:], [b_zsb], [b_ZS], b_ZS)

        P.barrier()
        with ExitStack() as s3:
            KTh = sb("KTh", [128, NKV * 128], BF16, stack=s3); b_KTh = Buf()
            Vph = sb("Vph", [128, NKV, 129], BF16, stack=s3); b_Vph = Buf()
            Eb = [sb("Eb%d" % i, [128, 1024], BF16, stack=s3) for i in range(2)]; b_Eb = [Buf(), Buf()]
            zb = sb("zb", [128, 4, 128], BF16, stack=s3); b_zb = Buf()
            fo = sb("fo", [128, 8, 128], stack=s3); b_fo = Buf()
            fs = sb("fs", [128, 8], stack=s3); b_fs = Buf()
            yb = sb("yb", [128, 128], BF16, stack=s3); b_yb = Buf()
            for h in range(H):
                P.dma(KTh[:], KTS[h, :, :], [b_KTS], [b_KTh], b_KTh)
                P.dma(Vph[:], VPS[h, :, :, :], [b_VPS], [b_Vph], b_Vph)
                for qt in range(4):
                    P.dma(zb[:], ZS[qt * 4:(qt + 1) * 4, :, 1024 + h * 128:1024 + (h + 1) * 128].rearrange("a p c -> p a c"), [b_ZS], [b_zb], b_zb)
                    def emit_qk(kb):
                        sbk = (kb % 2) * 2
                        for m in range(2):
                            P.op("pe", lambda e, m=m, kb=kb, qt=qt, h=h, sbk=sbk: e.matmul(psum[sbk + m][:, :], lhsT=KTh[m * 64:(m + 1) * 64, kb * 128:(kb + 1) * 128],
                                                                                   rhs=QT[m * 64:(m + 1) * 64, h, qt * 512:(qt + 1) * 512], start=True, stop=True),
                                 [b_KTh, b_QT], [PB[sbk + m]])
                    emit_qk(0)
                    for kb in range(NKV):
                        sbk = (kb % 2) * 2
                        if kb + 1 < NKV:
                            emit_qk(kb + 1)
                        for m in range(2):
                            P.op("act", lambda e, m=m, kb=kb, sbk=sbk: e.activation(out=Eb[kb % 2][:, m * 512:(m + 1) * 512], in_=psum[sbk + m][:, :], func=AF.Exp, scale=0.125, bias=lamt[:, 1:2]),
                                 [PB[sbk + m], b_lam], [b_Eb[kb % 2]])
                        for qb in range(4):
                            for m in range(2):
                                P.op("pe", lambda e, m=m, kb=kb, qb=qb: e.matmul(psum[4 + qb][:, m * 129:(m + 1) * 129], lhsT=Eb[kb % 2][:, m * 512 + qb * 128:m * 512 + (qb + 1) * 128],
                                                                                 rhs=Vph[:, kb, :], start=(kb == 0 and m == 0), stop=(kb == NKV - 1 and m == 1)),
                                     [b_Eb[kb % 2], b_Vph], [PB[4 + qb]])
                    for qb in range(4):
                        blk = qt * 4 + qb
                        pso = psum[4 + qb]
                        P.op("dve", lambda e, pso=pso: e.tensor_copy(out=fs[:, 0:1], in_=pso[:, 128:129]), [PB[4 + qb]], [b_fs])
                        P.op("dve", lambda e, pso=pso: e.tensor_copy(out=fs[:, 1:2], in_=pso[:, 257:258]), [PB[4 + qb]], [b_fs])
                        P.op("dve", lambda e: e.reciprocal(out=fs[:, 2:4], in_=fs[:, 0:2]), [b_fs], [b_fs])
                        P.op("dve", lambda e: e.tensor_tensor(out=fs[:, 3:4], in0=fs[:, 3:4], in1=lamt[:, 0:1], op=ALU.mult), [b_fs, b_lam], [b_fs])
                        P.op("dve", lambda e, pso=pso: e.tensor_scalar_mul(out=fo[:, 0, :], in0=pso[:, 129:257], scalar1=fs[:, 3:4]), [PB[4 + qb], b_fs], [b_fo])
                        P.op("dve", lambda e, pso=pso: e.scalar_tensor_tensor(out=fo[:, 1, :], in0=pso[:, 0:128], scalar=fs[:, 2:3], in1=fo[:, 0, :], op0=ALU.mult, op1=ALU.add), [PB[4 + qb], b_fs, b_fo], [b_fo])
                        P.op("act", lambda e: e.activation(out=fo[:, 2, :], in_=fo[:, 1, :], func=AF.Square, accum_out=fs[:, 4:5]), [b_fo, b_fs], [b_fo, b_fs])
                        P.op("act", lambda e: e.activation(out=fs[:, 5:6], in_=fs[:, 4:5], func=AF.Ln, scale=1.0 / 128, bias=epsc[:, 0:1]), [b_fs, b_eps], [b_fs])
                        P.op("act", lambda e: e.activation(out=fs[:, 5:6], in_=fs[:, 5:6], func=AF.Exp, scale=-0.5), [b_fs], [b_fs])
                        P.op("dve", lambda e: e.scalar_tensor_tensor(out=fo[:, 3, :], in0=fo[:, 1, :], scalar=fs[:, 5:6], in1=dnw, op0=ALU.mult, op1=ALU.mult), [b_fo, b_fs, b_vecs], [b_fo])
                        P.op("dve", lambda e, qb=qb: e.scalar_tensor_tensor(out=yb[:], in0=fo[:, 3, :], scalar=(1.0 - LAMBDA_INIT), in1=zb[:, qb, :], op0=ALU.mult, op1=ALU.mult), [b_fo, b_zb], [b_yb])
                        P.op("pe", lambda e: e.matmul(psum[0][:, 0:128], lhsT=yb[:], rhs=identb, start=True, stop=True), [b_yb, b_cbf], [PB[0]])
                        P.op("act", lambda e, h=h, blk=blk: e.activation(out=YbT[:, h, blk * 128:(blk + 1) * 128], in_=psum[0][:, 0:128], func=AF.Copy), [PB[0]], [b_YbT])

        s_qt.close()
        P.barrier()
        with ExitStack() as s4:
            Wo = [sb("Wo%d" % i, [128, 8, D], BF16, stack=s4) for i in range(3)]; b_Wo = [Buf() for _ in range(3)]
            stage = [junk, xt]
            bstage = [b_junk, b_xt]
            YaT = sb("YaT", [128, H, 128], BF16, stack=s4); b_YaT = Buf()
            Wg = sb("Wg", [128, 8, 2048], BF16, stack=s4); b_Wg = Buf()
            load_w_bf16(Wg, b_Wg, 8224, 2048, stage, bstage, 0)
            ii = 0
            for wi, src in enumerate((WOA, WOB, WOUT)):
                for k in range(8):
                    s_, bs_ = stage[ii % 2], bstage[ii % 2]
                    P.dma(s_[:], src[k * 128:(k + 1) * 128, :], [], [bs_], bs_)
                    P.op(["dve", "pool"][ii % 2], lambda e, s_=s_, wi=wi, k=k: e.tensor_copy(out=Wo[wi][:, k, :], in_=s_[:]), [bs_], [b_Wo[wi]])
                    ii += 1
            o1 = sb("o1", [128, D], stack=s4); b_o1 = Buf()
            o2 = sb("o2", [128, D], stack=s4); b_o2 = Buf()
            zs4 = sb("zs4", [128, 1024], BF16, stack=s4); b_zs4 = Buf()
            gs4 = sb("gs4", [128, 2048], BF16, stack=s4); b_gs4 = Buf()
            ya = sb("ya", [128, D], BF16, stack=s4); b_ya = Buf()
            mg = sb("mg", [128, D], stack=s4); b_mg = Buf()
            mgb = sb("mgb", [128, D], BF16, stack=s4); b_mgb = Buf()
            mT = sb("mT", [128, 8, 128], BF16, stack=s4); b_mT = Buf()
            res = sb("res", [128, D], stack=s4); b_res = Buf()
            b_OUT = Buf()
            out_toks = []
            for i in range(16):
                P.dma(o1[:], OS[0, i, :, :], [b_OS], [b_o1], b_o1)
                P.dma(o2[:], OS[1, 15 - i, :, :], [b_OS], [b_o2], b_o2)
                P.dma(zs4[:], ZS[i, :, 0:1024], [b_ZS], [b_zs4], b_zs4)
                make_HT(OWN0 + i, False)
                for g4 in range(4):
                    bank = 6 + g4 % 2
                    for k in range(8):
                        P.op("pe", lambda e, k=k, g4=g4, bank=bank: e.matmul(psum[bank][:, :], lhsT=HT[:, k, 2:130], rhs=Wg[:, k, g4 * 512:(g4 + 1) * 512], start=(k == 0), stop=(k == 7)), [b_HT, b_Wg], [PB[bank]])
                    P.op("act", lambda e, g4=g4, bank=bank: e.activation(out=gs4[:, g4 * 512:(g4 + 1) * 512], in_=psum[bank][:, :], func=AF.Sigmoid), [PB[bank]], [b_gs4])
                for hf in range(2):
                    P.op("pe", lambda e, hf=hf: e.matmul(psum[hf][:, :], lhsT=Jm, rhs=o2[:, hf * 512:(hf + 1) * 512], start=True, stop=True), [b_cst, b_o2], [PB[hf]])
                    P.op("dve", lambda e, hf=hf: e.tensor_tensor(out=o1[:, hf * 512:(hf + 1) * 512], in0=o1[:, hf * 512:(hf + 1) * 512], in1=psum[hf][:, :], op=ALU.add), [PB[hf], b_o1], [b_o1])
                P.op("act", lambda e: e.activation(out=junk[:], in_=o1[:], func=AF.Square), [b_o1], [b_junk])
                P.op("dve", lambda e: e.tensor_reduce(out=st[:, 0:8], in_=junk[:].rearrange("p (g d) -> p g d", d=128), axis=AX.X, op=ALU.add), [b_junk], [b_st])
                P.op("act", lambda e: e.activation(out=st[:, 0:8], in_=st[:, 0:8], func=AF.Ln, scale=1.0 / 128, bias=epsc[:, 0:1]), [b_st, b_eps], [b_st])
                P.op("act", lambda e: e.activation(out=st[:, 0:8], in_=st[:, 0:8], func=AF.Exp, scale=-0.5), [b_st], [b_st])
                P.op("dve", lambda e: e.tensor_tensor(out=junk[:].rearrange("p (g d) -> p g d", d=128), in0=o1[:].rearrange("p (g d) -> p g d", d=128), in1=st[:, 0:8].unsqueeze(2).to_broadcast([128, 8, 128]), op=ALU.mult), [b_o1, b_st], [b_junk])
                P.op("pool", lambda e: e.tensor_tensor(out=junk[:].rearrange("p (g d) -> p g d", d=128), in0=junk[:].rearrange("p (g d) -> p g d", d=128), in1=gnw.unsqueeze(1).to_broadcast([128, 8, 128]), op=ALU.mult), [b_junk, b_vecs], [b_junk])
                P.op("dve", lambda e: e.tensor_tensor(out=ya[:], in0=junk[:], in1=zs4[:], op=ALU.mult), [b_junk, b_zs4], [b_ya])
                for h in range(H):
                    bank = h // 4
                    P.op("pe", lambda e, h=h, bank=bank: e.matmul(psum[bank][:, (h % 4) * 128:(h % 4 + 1) * 128], lhsT=ya[:, h * 128:(h + 1) * 128], rhs=identb, start=True, stop=True), [b_ya, b_cbf], [PB[bank]])
                for hf in range(2):
                    P.op("act", lambda e, hf=hf, i=i: e.activation(out=YaT[:, hf * 4:(hf + 1) * 4, :], in_=psum[hf][:, :].rearrange("p (a b) -> p a b", a=4), func=AF.Copy), [PB[hf]], [b_YaT])
                for (wi, Y, bY, pb0, c0_) in ((0, YaT, b_YaT, 2, 0), (1, YbT, b_YbT, 4, i * 128)):
                    for hf in range(2):
                        for k in range(8):
                            P.op("pe", lambda e, wi=wi, Y=Y, hf=hf, k=k, pb0=pb0, c0_=c0_: e.matmul(psum[pb0 + hf][:, :], lhsT=Y[:, k, c0_:c0_ + 128], rhs=Wo[wi][:, k, hf * 512:(hf + 1) * 512], start=(k == 0), stop=(k == 7)),
                                 [bY, b_Wo[wi]], [PB[pb0 + hf]])
                for hf in range(2):
                    P.op("dve", lambda e, hf=hf: e.tensor_tensor(out=mg[:, hf * 512:(hf + 1) * 512], in0=psum[2 + hf][:, :], in1=gs4[:, hf * 512:(hf + 1) * 512], op=ALU.mult), [PB[2 + hf], b_gs4], [b_mg])
                    P.op("dve", lambda e, hf=hf: e.tensor_tensor(out=junk[:, hf * 512:(hf + 1) * 512], in0=psum[4 + hf][:, :], in1=gs4[:, 1024 + hf * 512:1024 + (hf + 1) * 512], op=ALU.mult), [PB[4 + hf], b_gs4], [b_junk])
                P.op("pool", lambda e: e.tensor_tensor(out=mgb[:], in0=mg[:], in1=junk[:], op=ALU.add), [b_mg, b_junk], [b_mgb])
                for k in range(8):
                    bank = 6 + k // 4
                    P.op("pe", lambda e, k=k, bank=bank: e.matmul(psum[bank][:, (k % 4) * 128:(k % 4 + 1) * 128], lhsT=mgb[:, k * 128:(k + 1) * 128], rhs=identb, start=True, stop=True), [b_mgb, b_cbf], [PB[bank]])
                for hf in range(2):
                    P.op("act", lambda e, hf=hf: e.activation(out=mT[:, hf * 4:(hf + 1) * 4, :], in_=psum[6 + hf][:, :].rearrange("p (a b) -> p a b", a=4), func=AF.Copy), [PB[6 + hf]], [b_mT])
                for hf in range(2):
                    for k in range(8):
                        P.op("pe", lambda e, hf=hf, k=k: e.matmul(psum[hf][:, :], lhsT=mT[:, k, :], rhs=Wo[2][:, k, hf * 512:(hf + 1) * 512], start=(k == 0), stop=(k == 7)), [b_mT, b_Wo[2]], [PB[hf]])
                    P.op("dve", lambda e, hf=hf: e.tensor_tensor(out=res[:, hf * 512:(hf + 1) * 512], in0=psum[hf][:, :], in1=mod[:, 2, hf * 512:(hf + 1) * 512], op=ALU.mult), [PB[hf], b_mod], [b_res])
                P.op("pool", lambda e: e.tensor_tensor(out=res[:], in0=res[:], in1=xt[:], op=ALU.add), [b_res, b_xt], [b_res])
                tok = P.dma(OUT[i * 128:(i + 1) * 128, :], res[:], [b_res], [b_OUT], b_OUT)
            P.final_wait("sp", [tok])

        print('total ops', P.gidx, {n: len(P.q[n]) for n in P.names})
        P.emit()
    return nc


def _consts():
    i = np.arange(128)
    ident = np.eye(128, dtype=np.float32)
    J = ident[::-1].copy()
    ones = np.ones((128, 128), np.float32)
    Ltri = (i[:, None] <= i[None, :]).astype(np.float32)
    negMs = -(i[:, None] > i[None, :]).astype(np.float32)
    negMsT = -(i[None, :] > i[:, None]).astype(np.float32)
    MiT = (i[None, :] >= i[:, None]).astype(np.float32)
    SelM = np.zeros((128, 132), np.float32)
    SelM[i, i + 2] = 1
    SelH = np.zeros((128, 132), np.float32)
    SelH[0, 0] = 1
    SelH[1, 1] = 1
    SelH[2, 130] = 1
    SelH[3, 131] = 1
    blk = lambda b: ((i[:, None] // b) == (i[None, :] // b)).astype(np.float32)
    BD32 = blk(32)
    M64 = blk(64) - blk(32)
    M128 = 1.0 - blk(64)
    return np.concatenate([ident, J, ones, Ltri, negMs, negMsT, MiT, SelM, SelH, BD32, M64, M128], axis=1)


def _rope_table(pos):
    pos = np.asarray(pos)
    row = (pos // 64).astype(np.float32)
    col = (pos % 64).astype(np.float32)
    inv = (10000.0 ** (-np.arange(16, dtype=np.float32) / 16)).astype(np.float32)
    ar = row[:, None] * inv
    ac = col[:, None] * inv
    t = np.concatenate([np.cos(ar), np.sin(ar), np.cos(ac), np.sin(ac)], axis=1).astype(np.float32)
    t[pos < 0] = np.array([1.0] * 16 + [0.0] * 16 + [1.0] * 16 + [0.0] * 16, np.float32)
    return t


def _core_plan(r):
    fwd_first = r >= 2
    own = list(range(16 * r, 16 * r + 16))
    left = list(range(0, 16 * r))
    right = list(range(16 * r + 16, 64))

    def scan(forward, nslots, filler):
        rev = not forward
        vis = [("c", 0, rev, 1.0), ("c", 1, rev, 1.0)]
        if rev:
            vis = [("c", 1, rev, 1.0), ("c", 0, rev, 1.0)]
        pref = left if forward else right[::-1]
        fill = [("l", b, rev, 0.0) for b in filler]
        vis += fill[:nslots - len(pref)] + [("l", b, rev, 1.0) for b in pref]
        o = own if forward else own[::-1]
        vis += [("l", b, rev, 1.0) for b in o]
        return vis

    if fwd_first:
        filler = right[::-1]
        v1 = scan(True, 48, filler)
        v2 = scan(False, 16, [own[0]] * 16)
    else:
        filler = left
        v1 = scan(False, 48, filler)
        v2 = scan(True, 16, [own[0]] * 16)
    assert len(v1) == 66 and len(v2) == 34
    return v1 + v2, fwd_first


_NC_CACHE = {}


def kernel(x, c, ctx, c_ctx, w_ada, b_ada, w_in, conv_w, a_log, dt_bias, gdn_norm_w,
           q_norm_w, k_norm_w, lambda_q1, lambda_k1, lambda_q2, lambda_k2, diff_norm_w,
           w_oa, w_ob, w_out):
    f = lambda a: np.ascontiguousarray(np.asarray(a, dtype=np.float32))
    x, c, ctx, c_ctx = f(x), f(c), f(ctx), f(c_ctx)
    w_ada, b_ada, w_in, conv_w = f(w_ada)[0], f(b_ada)[0], f(w_in)[0], f(conv_w)[0]
    a_log, dt_bias = f(a_log)[0], f(dt_bias)[0]
    vec = lambda a: f(a)[0]
    if "nc" not in _NC_CACHE:
        _NC_CACHE["nc"] = build_nc()
    nc = _NC_CACHE["nc"]
    cst = _consts()
    vecs = np.zeros((768,), np.float32)
    vecs[0:128] = vec(gdn_norm_w)
    vecs[128:192] = vec(q_norm_w)
    vecs[192:256] = vec(k_norm_w)
    vecs[256:320] = vec(lambda_q1)
    vecs[320:384] = vec(lambda_k1)
    vecs[384:448] = vec(lambda_q2)
    vecs[448:512] = vec(lambda_k2)
    vecs[512:640] = vec(diff_norm_w)
    VECS = np.ascontiguousarray(np.broadcast_to(vecs, (128, 768)))
    BADA = np.ascontiguousarray(np.broadcast_to(b_ada, (128, 3 * D)))
    in_maps = []
    plans = []
    for core in range(8):
        b, r = divmod(core, 4)
        plan, fwd_first = _core_plan(r)
        plans.append((plan, fwd_first))
        XV = np.zeros((NV * 128, D), np.float32)
        XH = np.zeros((NV * 4, D), np.float32)
        HM = np.zeros((4, NV), np.float32)
        GM = np.zeros((128, NV), np.float32)
        ROPE = np.zeros((NKV * 128, 64), np.float32)
        for v, (seq, blk, rev, gmask) in enumerate(plan):
            src = ctx[b] if seq == "c" else x[b]
            n = src.shape[0]
            idx = np.arange(blk * 128, blk * 128 + 128)
            halo = np.array([blk * 128 - 2, blk * 128 - 1, blk * 128 + 128, blk * 128 + 129])
            if rev:
                idx = idx[::-1]
                halo = halo[::-1]
            XV[v * 128:(v + 1) * 128] = src[idx]
            valid = (halo >= 0) & (halo < n)
            XH[v * 4:(v + 1) * 4][valid] = src[halo[valid]]
            HM[:, v] = valid.astype(np.float32)
            GM[:, v] = gmask
            if v < NKV:
                pos = idx if seq == "l" else -np.ones(128, np.int64)
                ROPE[v * 128:(v + 1) * 128] = _rope_table(pos)
        d1 = 0 if fwd_first else 1
        dirs = (d1, 1 - d1)
        WAB = np.stack([np.concatenate([w_in[:, 4096 + d * 8:4096 + d * 8 + 8], w_in[:, 4112 + d * 8:4112 + d * 8 + 8]], axis=1) for d in dirs])
        cwt = np.ascontiguousarray(conv_w.T)
        CW = np.stack([cwt if d == 0 else np.ascontiguousarray(cwt[:, ::-1]) for d in dirs])
        ALG = np.stack([np.broadcast_to(a_log[d], (128, 8)) for d in dirs])
        DTB = np.stack([np.broadcast_to(dt_bias[d], (128, 8)) for d in dirs])
        cv = np.stack([c[b], c_ctx])
        CT = np.ascontiguousarray(cv.reshape(2, 8, 128).transpose(2, 0, 1).reshape(128, 16))
        in_maps.append({
            "XV": XV, "XH": XH, "HM": HM, "GM": GM, "ROPE": ROPE, "CT": CT, "WADA": w_ada, "BADA": BADA,
            "WIN": w_in, "WAB": np.ascontiguousarray(WAB), "CW": np.ascontiguousarray(CW),
            "ALG": np.ascontiguousarray(ALG), "DTB": np.ascontiguousarray(DTB), "VECS": VECS, "CST": cst,
            "WOA": f(w_oa)[0], "WOB": f(w_ob)[0], "WOUT": f(w_out)[0],
        })
    if _NC_CACHE.get("maps_only"):
        return in_maps, plans
    res = run_bass_kernel_spmd(nc, in_maps, core_ids=list(range(8)))
    out = np.zeros((2, L, D), np.float32)
    for core in range(8):
        b, r = divmod(core, 4)
        plan, fwd_first = plans[core]
        o = np.asarray(res.results[core]["OUT"])
        for i in range(16):
            seq, blk, rev, _ = plan[OWN0 + i]
            idx = np.arange(blk * 128, blk * 128 + 128)
            if rev:
                idx = idx[::-1]
            out[b, idx] = o[i * 128:(i + 1) * 128]
    return out
```

```python
import math
from contextlib import ExitStack

import numpy as np
import concourse.bass as bass
import concourse.mybir as mybir
from concourse.bass_utils import run_bass_kernel_spmd

F32 = mybir.dt.float32
BF16 = mybir.dt.bfloat16
AF = mybir.ActivationFunctionType
ALU = mybir.AluOpType
AX = mybir.AxisListType

D = 1024
L = 8192
NCTX = 256
H = 8
EPS = 1e-6
NV = 100
NKV = 66
OWN0 = 50
S2 = 66
OWN1 = 84
IN_COLS = 10272
LAMBDA_INIT = 0.8 - 0.6 * math.exp(-0.3 * 0)
DEBUG = False
import os
MAXOPS = int(os.environ.get('KMAXOPS', '1000000000'))
TRACE_LO = int(os.environ.get('KTLO', '0'))
TRACE_HI = int(os.environ.get('KTHI', '-1'))


class Buf:
    __slots__ = ("w", "r", "dsem", "dcnt")

    def __init__(self):
        self.w = None
        self.r = {}
        self.dsem = None
        self.dcnt = 0


class Prog:
    def __init__(self, nc, es):
        self.nc = nc
        self.es = es
        self.names = ["pe", "act", "dve", "pool", "sp"]
        self.sem = {n: es.enter_context(nc.semaphore("s_" + n)) for n in self.names}
        self.cnt = {n: 0 for n in self.names}
        self.waited = {n: {} for n in self.names}
        self.q = {n: [] for n in self.names}
        self.nsem = 0
        self.dbufs = []
        self.gidx = 0
        self.maxops = MAXOPS

    def _waits(self, eng, reads, writes):
        need = {}

        def add(tok):
            if tok is None:
                return
            s, v = tok
            k = id(s)
            if k not in need or need[k][1] < v:
                need[k] = (s, v)

        for b in reads:
            add(b.w)
        for b in writes:
            add(b.w)
            for t in b.r.values():
                add(t)
        out = []
        own = id(self.sem[eng])
        for k, (s, v) in need.items():
            if eng == "pe" and k == own:
                continue
            if self.waited[eng].get(k, 0) >= v:
                continue
            self.waited[eng][k] = v
            out.append((s, v))
        return out

    def _commit(self, tok, reads, writes):
        for b in reads:
            b.r[id(tok[0])] = tok
        for b in writes:
            b.w = tok
            b.r = {}

    def op(self, eng, fn, reads=(), writes=(), sig=True):
        self.gidx += 1
        if TRACE_LO <= self.gidx <= TRACE_HI:
            import inspect
            print("OP", self.gidx, eng, inspect.currentframe().f_back.f_lineno)
        if self.gidx > self.maxops:
            return
        waits = self._waits(eng, reads, writes)
        if sig:
            self.cnt[eng] += 1
            tok = (self.sem[eng], self.cnt[eng])
            self.q[eng].append((waits, fn, self.sem[eng], 1))
        else:
            tok = (self.sem[eng], self.cnt[eng] + 1)
            self.q[eng].append((waits, fn, None, 0))
        self._commit(tok, reads, writes)

    def dma(self, out_ap, in_ap, reads, writes, dst):
        eng = "sp"
        self.gidx += 1
        if TRACE_LO <= self.gidx <= TRACE_HI:
            import inspect
            print("DMA", self.gidx, inspect.currentframe().f_back.f_lineno)
        if self.gidx > self.maxops:
            return (None, 0)
        waits = self._waits(eng, reads, writes)
        if dst.dsem is None:
            dst.dsem = self.es.enter_context(self.nc.semaphore("d%d" % self.nsem))
            self.nsem += 1
            self.dbufs.append(dst)
        dst.dcnt += 16
        tok = (dst.dsem, dst.dcnt)
        self.q[eng].append((waits, lambda e: e.dma_start(out=out_ap, in_=in_ap), dst.dsem, 16))
        self._commit(tok, reads, writes)
        return tok

    def barrier(self):
        toks = [(self.sem[n], self.cnt[n]) for n in self.names if self.cnt[n] > 0]
        toks += [(b.dsem, b.dcnt) for b in self.dbufs]
        for eng in self.names:
            waits = []
            for (s_, v) in toks:
                k = id(s_)
                if eng == "pe" and k == id(self.sem["pe"]):
                    continue
                if self.waited[eng].get(k, 0) >= v:
                    continue
                self.waited[eng][k] = v
                waits.append((s_, v))
            if waits:
                self.q[eng].append((waits, None, None, 0))

    def final_wait(self, eng, toks):
        toks = [t for t in toks if t[0] is not None]
        if not toks:
            toks = [(self.sem[n], self.cnt[n]) for n in self.names if self.cnt[n] > 0 and n != eng]
            toks += [(b.dsem, b.dcnt) for b in self.dbufs]
        self.q[eng].append((toks, None, None, 0))

    def emit(self):
        nc = self.nc
        with nc.Block() as block:
            def run(name):
                def body(e):
                    for waits, fn, sem, inc in self.q[name]:
                        for s, v in waits:
                            e.wait_ge(s, v)
                        if fn is not None:
                            ins_ = fn(e)
                            if inc:
                                ins_.then_inc(sem, inc)
                return body
            block.tensor(run("pe"))
            block.scalar(run("act"))
            block.vector(run("dve"))
            block.gpsimd(run("pool"))
            block.sync(run("sp"))


def build_nc():
    nc = bass.Bass("TRN2", target_bir_lowering=False)

    def din(name, shape, dt=F32):
        return nc.dram_tensor(name, list(shape), dt, kind="ExternalInput").ap()

    XV = din("XV", [NV * 128, D])
    XH = din("XH", [NV * 4, D])
    HM = din("HM", [4, NV])
    GM = din("GM", [128, NV])
    ROPE = din("ROPE", [NKV * 128, 64])
    CT = din("CT", [128, 16])
    WADA = din("WADA", [D, 3 * D])
    BADA = din("BADA", [128, 3 * D])
    WIN = din("WIN", [D, IN_COLS])
    WAB = din("WAB", [2, D, 16])
    CW = din("CW", [2, 3072, 5])
    ALG = din("ALG", [2, 128, 8])
    DTB = din("DTB", [2, 128, 8])
    VECS = din("VECS", [128, 768])
    CST = din("CST", [128, 10 * 128 + 2 * 132])
    WOA = din("WOA", [D, D])
    WOB = din("WOB", [D, D])
    WOUT = din("WOUT", [D, D])
    OUT = nc.dram_tensor("OUT", [2048, D], F32, kind="ExternalOutput").ap()

    KTS = nc.dram_tensor("KTS", [H, 128, NKV * 128], BF16).ap()
    VPS = nc.dram_tensor("VPS", [H, 128, NKV, 129], BF16).ap()
    OS = nc.dram_tensor("OS", [2, 16, 128, D], F32).ap()
    ZS = nc.dram_tensor("ZS", [16, 128, 2048], BF16).ap()
    GS = nc.dram_tensor("GS", [16, 128, 2048], BF16).ap()

    es = ExitStack()
    with es:
        P = Prog(nc, es)

        def sb(name, shape, dt=F32, stack=es):
            return stack.enter_context(nc.sbuf_tensor(name, list(shape), dt))

        psum = [es.enter_context(nc.psum_tensor("ps%d" % i, [128, 512], F32)) for i in range(8)]
        PB = [Buf() for _ in range(8)]

        cst = sb("cst", [128, 10 * 128 + 2 * 132]); b_cst = Buf()
        ident = cst[:, 0:128]
        Jm = cst[:, 128:256]
        ones = cst[:, 256:384]
        Ltri = cst[:, 384:512]
        negMs = cst[:, 512:640]
        negMsT = cst[:, 640:768]
        MiT = cst[:, 768:896]
        BD32 = cst[:, 1160:1288]
        M64 = cst[:, 1288:1416]
        M128 = cst[:, 1416:1544]
        cbf = sb("cbf", [128, 128 + 2 * 132], BF16); b_cbf = Buf()
        identb = cbf[:, 0:128]
        SelM = cbf[:, 128:260]
        SelH = cbf[0:4, 260:392]
        vecs = sb("vecs", [128, 768]); b_vecs = Buf()
        gnw = vecs[:, 0:128]
        qnw = vecs[:, 128:192]
        knw = vecs[:, 192:256]
        dnw = vecs[:, 512:640]
        hm = sb("hm", [4, NV]); b_hm = Buf()
        gm = sb("gm", [128, NV]); b_gm = Buf()
        mod = sb("mod", [128, 5, D]); b_mod = Buf()
        negA = sb("negA", [128, 2, 8]); b_negA = Buf()
        dtb = sb("dtb", [128, 2, 8]); b_dtb = Buf()
        lamt = sb("lamt", [128, 4]); b_lam = Buf()
        epsc = sb("epsc", [128, 1]); b_eps = Buf()

        P.dma(cst[:], CST[:, :], [], [b_cst], b_cst)
        P.dma(vecs[:], VECS[:, :], [], [b_vecs], b_vecs)
        P.dma(hm[:], HM[:, :], [], [b_hm], b_hm)
        P.dma(gm[:], GM[:, :], [], [b_gm], b_gm)
        P.dma(negA[:], ALG.rearrange("s p h -> p s h"), [], [b_negA], b_negA)
        P.dma(dtb[:], DTB.rearrange("s p h -> p s h"), [], [b_dtb], b_dtb)
        P.op("act", lambda e: e.activation(out=negA[:], in_=negA[:], func=AF.Exp), [b_negA], [b_negA])
        P.op("dve", lambda e: e.tensor_scalar_mul(out=negA[:], in0=negA[:], scalar1=-1.0), [b_negA], [b_negA])
        P.op("dve", lambda e: e.tensor_copy(out=cbf[:, 0:128], in_=cst[:, 0:128]), [b_cst], [b_cbf])
        P.op("dve", lambda e: e.tensor_copy(out=cbf[:, 128:392], in_=cst[:, 896:1160]), [b_cst], [b_cbf])
        P.op("pool", lambda e: e.memset(epsc[:], EPS), [], [b_eps])

        ltmp = sb("ltmp", [128, 128]); b_ltmp = Buf()
        P.op("dve", lambda e: e.tensor_tensor(out=ltmp[:, 0:64], in0=vecs[:, 256:320], in1=vecs[:, 320:384], op=ALU.mult), [b_vecs], [b_ltmp])
        P.op("dve", lambda e: e.tensor_tensor(out=ltmp[:, 64:128], in0=vecs[:, 384:448], in1=vecs[:, 448:512], op=ALU.mult), [b_vecs], [b_ltmp])
        P.op("dve", lambda e: e.tensor_reduce(out=lamt[:, 2:4], in_=ltmp[:].rearrange("p (a b) -> p a b", a=2), axis=AX.X, op=ALU.add), [b_ltmp], [b_lam])
        P.op("act", lambda e: e.activation(out=lamt[:, 2:4], in_=lamt[:, 2:4], func=AF.Exp), [b_lam], [b_lam])
        P.op("dve", lambda e: e.tensor_tensor(out=lamt[:, 0:1], in0=lamt[:, 3:4], in1=lamt[:, 2:3], op=ALU.subtract), [b_lam], [b_lam])
        P.op("dve", lambda e: e.tensor_scalar_add(out=lamt[:, 0:1], in0=lamt[:, 0:1], scalar1=-LAMBDA_INIT), [b_lam], [b_lam])
        P.op("dve", lambda e: e.tensor_scalar_mul(out=ltmp[:], in0=vecs[:, 128:256], scalar1=-1.0), [b_vecs, b_lam], [b_ltmp])
        P.op("dve", lambda e: e.tensor_tensor(out=ltmp[:], in0=ltmp[:], in1=vecs[:, 128:256], op=ALU.max), [b_vecs, b_ltmp], [b_ltmp])
        P.op("dve", lambda e: e.tensor_reduce(out=lamt[:, 2:4], in_=ltmp[:].rearrange("p (a b) -> p a b", a=2), axis=AX.X, op=ALU.max), [b_ltmp, b_lam], [b_lam])
        P.op("dve", lambda e: e.scalar_tensor_tensor(out=lamt[:, 1:2], in0=lamt[:, 2:3], scalar=-8.0, in1=lamt[:, 3:4], op0=ALU.mult, op1=ALU.mult), [b_lam], [b_lam])

        with ExitStack() as s0:
            ct = sb("ct", [128, 16], stack=s0); b_ct = Buf()
            cbl = sb("cbl", [128, 16, 128], stack=s0); b_cbl = Buf()
            wst = [sb("wada%d" % i, [128, 3 * D], stack=s0) for i in range(2)]
            b_wst = [Buf(), Buf()]
            bada = sb("bada", [128, 3 * D], stack=s0); b_bada = Buf()
            P.dma(ct[:], CT[:, :], [], [b_ct], b_ct)
            P.dma(bada[:], BADA[:, :], [], [b_bada], b_bada)
            P.op("act", lambda e: e.activation(out=ct[:], in_=ct[:], func=AF.Silu), [b_ct], [b_ct])
            for j in range(16):
                P.op("dve", lambda e, j=j: e.tensor_scalar_mul(out=cbl[:, j, :], in0=ones, scalar1=ct[:, j:j + 1]), [b_ct, b_cst], [b_cbl])
            for k in range(8):
                P.dma(wst[k % 2][:], WADA[k * 128:(k + 1) * 128, :], [], [b_wst[k % 2]], b_wst[k % 2])
                for g in range(6):
                    P.op("pe", lambda e, k=k, g=g: e.matmul(psum[g][:, :], lhsT=cbl[:, k, :], rhs=wst[k % 2][:, g * 512:(g + 1) * 512], start=(k == 0), stop=(k == 7)),
                         [b_cbl, b_wst[k % 2]], [PB[g]])
            for g in range(6):
                dsti = [1, 1, 0, 0, 2, 2][g]
                addc = 1.0 if dsti == 0 else 0.0
                P.op("dve", lambda e, g=g, dsti=dsti: e.tensor_tensor(out=mod[:, dsti, (g % 2) * 512:(g % 2) * 512 + 512], in0=psum[g][:, :], in1=bada[:, g * 512:(g + 1) * 512], op=ALU.add),
                     [PB[g], b_bada], [b_mod])
            P.op("pool", lambda e: e.tensor_scalar_add(out=mod[:, 0, :], in0=mod[:, 0, :], scalar1=1.0), [b_mod], [b_mod])
            for k in range(8):
                P.dma(wst[k % 2][:], WADA[k * 128:(k + 1) * 128, :], [], [b_wst[k % 2]], b_wst[k % 2])
                for g in range(4):
                    P.op("pe", lambda e, k=k, g=g: e.matmul(psum[g][:, :], lhsT=cbl[:, 8 + k, :], rhs=wst[k % 2][:, g * 512:(g + 1) * 512], start=(k == 0), stop=(k == 7)),
                         [b_cbl, b_wst[k % 2]], [PB[g]])
            for g in range(4):
                dsti = [4, 4, 3, 3][g]
                P.op("dve", lambda e, g=g, dsti=dsti: e.tensor_tensor(out=mod[:, dsti, (g % 2) * 512:(g % 2) * 512 + 512], in0=psum[g][:, :], in1=bada[:, g * 512:(g + 1) * 512], op=ALU.add),
                     [PB[g], b_bada], [b_mod])
            P.op("pool", lambda e: e.tensor_scalar_add(out=mod[:, 3, :], in0=mod[:, 3, :], scalar1=1.0), [b_mod], [b_mod])

        P.barrier()
        xt = sb("xt", [128, D]); b_xt = Buf()
        xh = sb("xh", [4, D]); b_xh = Buf()
        junk = sb("junk", [128, D]); b_junk = Buf()
        st = sb("st", [128, 8]); b_st = Buf()
        hb = sb("hb", [128, D], BF16); b_hb = Buf()
        hhb = sb("hhb", [4, D], BF16); b_hhb = Buf()
        HT = sb("HT", [128, 8, 132], BF16); b_HT = Buf()

        def make_HT(v, is_ctx):
            mi = 3 if is_ctx else 0
            P.dma(xt[:], XV[v * 128:(v + 1) * 128, :], [], [b_xt], b_xt)
            P.dma(xh[:], XH[v * 4:(v + 1) * 4, :], [], [b_xh], b_xh)
            for (src, bsrc, np_, col, dst, bdst) in ((xt, b_xt, 128, 0, hb, b_hb), (xh, b_xh, 4, 2, hhb, b_hhb)):
                P.op("act", lambda e, src=src, np_=np_, col=col: e.activation(out=junk[0:np_, :], in_=src[0:np_, :], func=AF.Square, accum_out=st[0:np_, col:col + 1]),
                     [bsrc], [b_junk, b_st])
                P.op("act", lambda e, np_=np_, col=col: e.activation(out=st[0:np_, col + 1:col + 2], in_=st[0:np_, col:col + 1], func=AF.Ln, scale=1.0 / D, bias=epsc[0:np_, 0:1]),
                          [b_st, b_eps], [b_st])
                P.op("act", lambda e, np_=np_, col=col: e.activation(out=st[0:np_, col + 1:col + 2], in_=st[0:np_, col + 1:col + 2], func=AF.Exp, scale=-0.5),
                     [b_st], [b_st])
                if np_ == 4:
                    P.op("dve", lambda e, v=v: e.tensor_tensor(out=st[0:4, 3:4], in0=st[0:4, 3:4], in1=hm[:, v:v + 1], op=ALU.mult), [b_st, b_hm], [b_st])
                P.op("dve", lambda e, src=src, np_=np_, col=col: e.scalar_tensor_tensor(out=junk[0:np_, :], in0=src[0:np_, :], scalar=st[0:np_, col + 1:col + 2], in1=mod[0:np_, mi, :], op0=ALU.mult, op1=ALU.mult),
                     [bsrc, b_st, b_mod, b_junk], [b_junk])
                if np_ == 4:
                    P.op("dve", lambda e, v=v, dst=dst: e.scalar_tensor_tensor(out=dst[0:4, :], in0=mod[0:4, mi + 1, :], scalar=hm[:, v:v + 1], in1=junk[0:4, :], op0=ALU.mult, op1=ALU.add),
                         [b_junk, b_mod, b_hm], [bdst])
                else:
                    P.op("pool", lambda e, dst=dst: e.tensor_tensor(out=dst[:, :], in0=junk[:, :], in1=mod[:, mi + 1, :], op=ALU.add), [b_junk, b_mod], [bdst])
            for k in range(8):
                bank, slot = divmod(k, 3)
                o = psum[bank][:, slot * 132:(slot + 1) * 132]
                P.op("pe", lambda e, k=k, o=o: e.matmul(o, lhsT=hb[:, k * 128:(k + 1) * 128], rhs=SelM, start=True, stop=False), [b_hb, b_cbf], [PB[bank]], sig=False)
                P.op("pe", lambda e, k=k, o=o: e.matmul(o, lhsT=hhb[0:4, k * 128:(k + 1) * 128], rhs=SelH, start=False, stop=True), [b_hhb, b_cbf], [PB[bank]])
            for bank in range(3):
                n = 3 if bank < 2 else 2
                P.op("act", lambda e, bank=bank, n=n: e.activation(out=HT[:, bank * 3:bank * 3 + n, :], in_=psum[bank][:, 0:n * 132].rearrange("p (a b) -> p a b", a=n), func=AF.Copy),
                     [PB[bank]], [b_HT])

        def load_w_bf16(dst, bdst, col0, ncols, stage, bstage, dcol0=0):
            i = 0
            for k in range(8):
                for c0 in range(0, ncols, 1024):
                    n = min(1024, ncols - c0)
                    s_, bs_ = stage[i % 2], bstage[i % 2]
                    P.dma(s_[:, 0:n], WIN[k * 128:(k + 1) * 128, col0 + c0:col0 + c0 + n], [], [bs_], bs_)
                    eng = ["act", "pool", "dve"][i % 3]
                    if eng == "act":
                        P.op("act", lambda e, s_=s_, k=k, c0=c0, n=n: e.activation(out=dst[:, k, dcol0 + c0:dcol0 + c0 + n], in_=s_[:, 0:n], func=AF.Copy), [bs_], [bdst])
                    else:
                        P.op(eng, lambda e, s_=s_, k=k, c0=c0, n=n: e.tensor_copy(out=dst[:, k, dcol0 + c0:dcol0 + c0 + n], in_=s_[:, 0:n]), [bs_], [bdst])
                    i += 1

        def qk_norm_rope(ps_banks, pbufs, gain, tab, b_tab, dst, b_dst, tmp, b_tmp):
            for hf in range(2):
                P.op("act", lambda e, hf=hf: e.activation(out=tmp[0][:, hf * 512:(hf + 1) * 512], in_=ps_banks[hf][:, :], func=AF.Square), [pbufs[hf]], [b_tmp[0]])
            P.op("dve", lambda e: e.tensor_reduce(out=tmp[3][:, 0:16], in_=tmp[0][:].rearrange("p (g d) -> p g d", d=64), axis=AX.X, op=ALU.add),
                 [b_tmp[0]], [b_tmp[3]])
            P.op("act", lambda e: e.activation(out=tmp[3][:, 0:16], in_=tmp[3][:, 0:16], func=AF.Ln, scale=1.0 / 64, bias=epsc[:, 0:1]), [b_tmp[3], b_eps], [b_tmp[3]])
            P.op("act", lambda e: e.activation(out=tmp[3][:, 0:16], in_=tmp[3][:, 0:16], func=AF.Exp, scale=-0.5), [b_tmp[3]], [b_tmp[3]])
            for hf in range(2):
                P.op("dve", lambda e, hf=hf: e.tensor_tensor(out=tmp[0][:, hf * 512:(hf + 1) * 512].rearrange("p (g d) -> p g d", d=64),
                                                            in0=ps_banks[hf][:, :].rearrange("p (g d) -> p g d", d=64),
                                                            in1=tmp[3][:, hf * 8:(hf + 1) * 8].unsqueeze(2).to_broadcast([128, 8, 64]), op=ALU.mult),
                     [pbufs[hf], b_tmp[3], b_tmp[0]], [b_tmp[0]])
            P.op("pool", lambda e: e.tensor_tensor(out=tmp[1][:].rearrange("p (g d) -> p g d", d=64), in0=tmp[0][:].rearrange("p (g d) -> p g d", d=64),
                                                   in1=gain.unsqueeze(1).to_broadcast([128, 16, 64]), op=ALU.mult), [b_tmp[0], b_vecs], [b_tmp[1]])
            xv = tmp[1][:].rearrange("p (g h t f) -> p g h t f", g=16, h=2, t=2)
            tv = tab[:].rearrange("p (h t f) -> p h t f", h=2, t=2)
            dv = dst[:].rearrange("p (g h t f) -> p g h t f", g=16, h=2, t=2)
            av = tmp[2][:].rearrange("p (g h t f) -> p g h t f", g=16, h=2, t=2)
            for hh in range(2):
                x1 = xv[:, :, hh, 0, :]
                x2 = xv[:, :, hh, 1, :]
                cs = tv[:, hh, 0, :].unsqueeze(1).to_broadcast([128, 16, 16])
                sn = tv[:, hh, 1, :].unsqueeze(1).to_broadcast([128, 16, 16])
                a1 = av[:, :, hh, 0, :]
                a2 = av[:, :, hh, 1, :]
                e1, e2 = ("dve", "pool")
                P.op(e1, lambda e, x1=x1, cs=cs, a1=a1: e.tensor_tensor(out=a1, in0=x1, in1=cs, op=ALU.mult), [b_tmp[1], b_tab], [b_tmp[2]])
                P.op(e2, lambda e, x2=x2, sn=sn, a2=a2: e.tensor_tensor(out=a2, in0=x2, in1=sn, op=ALU.mult), [b_tmp[1], b_tab], [b_tmp[2]])
                P.op(e1, lambda e, a1=a1, a2=a2, hh=hh: e.tensor_tensor(out=dv[:, :, hh, 0, :], in0=a1, in1=a2, op=ALU.subtract), [b_tmp[2]], [b_dst])
                P.op(e1, lambda e, x2=x2, cs=cs, a1=a1: e.tensor_tensor(out=a1, in0=x2, in1=cs, op=ALU.mult), [b_tmp[1], b_tab, b_tmp[2]], [b_tmp[2]])
                P.op(e2, lambda e, x1=x1, sn=sn, a2=a2: e.tensor_tensor(out=a2, in0=x1, in1=sn, op=ALU.mult), [b_tmp[1], b_tab, b_tmp[2]], [b_tmp[2]])
                P.op(e1, lambda e, a1=a1, a2=a2, hh=hh: e.tensor_tensor(out=dv[:, :, hh, 1, :], in0=a1, in1=a2, op=ALU.add), [b_tmp[2]], [b_dst])


        with ExitStack() as s1:
            W1 = sb("W1", [128, 8, 5120], BF16, stack=s1); b_W1 = Buf()
            wab = sb("wab", [128, 2, 8, 16], BF16, stack=s1); b_wab = Buf()
            wabs = sb("wabs", [128, 2, 8, 16], stack=s1); b_wabs = Buf()
            cw = sb("cw", [128, 2, 24, 5], stack=s1); b_cw = Buf()
            stage = [junk, xt]
            bstage = [b_junk, b_xt]
            tab = sb("tab1", [128, 64], stack=s1); b_tab = Buf()
            tmpA = [sb("tmpA1_%d" % i, [128, D], stack=s1) for i in range(3)] + [sb("tmpA31", [128, 16], stack=s1)]
            b_tmpA = [Buf() for _ in range(4)]
            krot = sb("krot1", [128, D], BF16, stack=s1); b_krot = Buf()
            load_w_bf16(W1, b_W1, 0, 3072, stage, bstage, 0)
            load_w_bf16(W1, b_W1, 5152, 2048, stage, bstage, 3072)
            P.dma(wabs[:], WAB.rearrange("s (k p) c -> p s k c", p=128), [], [b_wabs], b_wabs)
            P.op("dve", lambda e: e.tensor_copy(out=wab[:], in_=wabs[:]), [b_wabs], [b_wab])
            P.dma(cw[:], CW.rearrange("s (c p) j -> p s c j", p=128), [], [b_cw], b_cw)

            Sst = sb("Sst", [128, H, 128], stack=s1); b_S = [Buf() for _ in range(H)]
            pre = sb("pre", [128, 3, 132], stack=s1); b_pre = Buf()
            acc = [sb("acc%d" % i, [128, 128], stack=s1) for i in range(2)]; b_acc = [Buf(), Buf()]
            ptmp = sb("ptmp", [128, 128], stack=s1); b_ptmp = Buf()
            qkvT = sb("qkvT", [128, 24, 128], stack=s1); b_qkvT = [Buf() for _ in range(24)]
            gb = sb("gb", [128, 6, 8], stack=s1); b_gb = Buf()
            from types import SimpleNamespace
            TS = []
            for si_ in range(2):
                T = SimpleNamespace()
                T.Wb = [sb("Wb%d_%d" % (i, si_), [128, 512], stack=s1) for i in range(2)]; T.b_Wb = [Buf(), Buf()]
                T.NY = sb("NY_%d" % si_, [128, 256], stack=s1); T.b_NY = Buf()
                T.NYm = sb("NYm_%d" % si_, [128, 256], stack=s1); T.b_NYm = Buf()
                T.TU = sb("TU_%d" % si_, [128, 256], stack=s1); T.b_TU = Buf()
                T.TU2 = sb("TU2_%d" % si_, [128, 256], stack=s1); T.b_TU2 = Buf()
                T.Wm = sb("Wm_%d" % si_, [128, 256], stack=s1); T.b_Wm = Buf()
                T.LgIb = sb("LgIb_%d" % si_, [128, 256], stack=s1); T.b_LgIb = Buf()
                T.Em = sb("Em_%d" % si_, [128, 128], stack=s1); T.b_Em = Buf()
                T.ETm = sb("ETm_%d" % si_, [128, 128], stack=s1); T.b_ETm = Buf()
                T.EMs = sb("EMs_%d" % si_, [128, 128], stack=s1); T.b_EMs = Buf()
                T.ETs = sb("ETs_%d" % si_, [128, 128], stack=s1); T.b_ETs = Buf()
                T.ETi = sb("ETi_%d" % si_, [128, 128], stack=s1); T.b_ETi = Buf()
                T.npos = T.LgIb; T.b_npos = T.b_LgIb
                T.sq = sb("sq_%d" % si_, [128, 128], stack=s1); T.b_sq = Buf()
                T.rs = sb("rs_%d" % si_, [128, 128], stack=s1); T.b_rs = Buf()
                T.kTn = sb("kTn_%d" % si_, [128, 128], stack=s1); T.b_kTn = Buf()
                T.qTn = sb("qTn_%d" % si_, [128, 128], stack=s1); T.b_qTn = Buf()
                T.Kbp = sb("Kbp_%d" % si_, [128, 128], stack=s1); T.b_Kbp = Buf()
                T.kd = sb("kd_%d" % si_, [128, 128], stack=s1); T.b_kd = Buf()
                T.Vb = sb("Vb_%d" % si_, [128, 128], stack=s1); T.b_Vb = Buf()
                T.TT = sb("TT_%d" % si_, [128, 128], stack=s1); T.b_TT = Buf()
                T.wT = sb("wT_%d" % si_, [128, 128], stack=s1); T.b_wT = Buf()
                T.usb = sb("usb_%d" % si_, [128, 128], stack=s1); T.b_usb = Buf()
                T.vnew = sb("vnew_%d" % si_, [128, 128], stack=s1); T.b_vnew = Buf()
                T.egl = sb("egl_%d" % si_, [128, 1], stack=s1); T.b_egl = Buf()
                T.eR = sb("eR_%d" % si_, [128, 128], stack=s1); T.b_eR = Buf()
                T.QdT = sb("QdT_%d" % si_, [128, 128], stack=s1); T.b_QdT = Buf()
                T.qkT = sb("qkT_%d" % si_, [128, 128], stack=s1); T.b_qkT = Buf()
                T.banks = (0, 1, 2, 2) if si_ == 0 else (3, 4, 5, 5)
                TS.append(T)
            osb = junk; b_osb = b_junk
            KTsb = hb[:].rearrange("p (h t) -> p h t", h=H); b_KTsb = b_hb
            Vp = sb("Vp", [128, H, 129], BF16, stack=s1); b_Vp = Buf()
            b_KTS = Buf(); b_VPS = Buf(); b_OS = Buf()
            P.op("pool", lambda e: e.memset(Vp[:], 1.0), [], [b_Vp])

            for v in range(NV):
                scan = 0 if v < S2 else 1
                is_ctx = v in (0, 1, S2, S2 + 1)
                own = (OWN0 <= v < S2) or (v >= OWN1)
                kv = v < NKV
                if v in (0, S2):
                    for h in range(H):
                        P.op("pool", lambda e, h=h: e.memset(Sst[:, h, :], 0.0), [], [b_S[h]])
                make_HT(v, is_ctx)
                PA = 3
                for k in range(8):
                    P.op("pe", lambda e, k=k, scan=scan: e.matmul(psum[PA][:, 0:16], lhsT=HT[:, k, 2:130], rhs=wab[:, scan, k, :], start=(k == 0), stop=(k == 7)), [b_HT, b_wab], [PB[PA]], sig=(k == 7))
                P.op("dve", lambda e, scan=scan: e.tensor_tensor(out=gb[:, 4, :], in0=psum[PA][:, 0:8], in1=dtb[:, scan, :], op=ALU.add), [PB[PA], b_dtb], [b_gb])
                P.op("act", lambda e: e.activation(out=gb[:, 4, :], in_=gb[:, 4, :], func=AF.Exp), [b_gb], [b_gb])
                P.op("act", lambda e: e.activation(out=gb[:, 4, :], in_=gb[:, 4, :], func=AF.Ln, bias=1.0), [b_gb], [b_gb])
                P.op("dve", lambda e, v=v, scan=scan: e.scalar_tensor_tensor(out=gb[:, 0, :], in0=gb[:, 4, :], scalar=gm[:, v:v + 1], in1=negA[:, scan, :], op0=ALU.mult, op1=ALU.mult), [b_gb, b_gm, b_negA], [b_gb])
                P.op("act", lambda e: e.activation(out=gb[:, 5, :], in_=psum[PA][:, 8:16], func=AF.Sigmoid), [PB[PA], b_gb], [b_gb])
                P.op("dve", lambda e, v=v: e.tensor_scalar_mul(out=gb[:, 1, :], in0=gb[:, 5, :], scalar1=gm[:, v:v + 1]), [b_gb, b_gm], [b_gb])
                P.op("pe", lambda e: e.matmul(psum[PA][:, 16:24], lhsT=Ltri, rhs=gb[:, 0, :], start=True, stop=True), [b_cst, b_gb], [PB[PA]])
                P.op("dve", lambda e: e.tensor_copy(out=gb[:, 2, :], in_=psum[PA][:, 16:24]), [PB[PA], b_gb], [b_gb])
                P.op("act", lambda e: e.activation(out=gb[:, 3, :], in_=psum[PA][:, 16:24], func=AF.Exp), [PB[PA], b_gb], [b_gb])
                P.op("dve", lambda e: e.tensor_tensor(out=gb[:, 3, :], in0=gb[:, 3, :], in1=gb[:, 1, :], op=ALU.mult), [b_gb], [b_gb])

                def pre_steps(cbs):
                    for gi in range(0, len(cbs), 3):
                        grp = cbs[gi:gi + 3]
                        bank = 6 + (gi // 3) % 2
                        for si, cb in enumerate(grp):
                            for k in range(8):
                                P.op("pe", lambda e, k=k, cb=cb, si=si, bank=bank: e.matmul(psum[bank][:, si * 132:(si + 1) * 132], lhsT=W1[:, k, cb * 128:(cb + 1) * 128], rhs=HT[:, k, :], start=(k == 0), stop=(k == 7)),
                                     [b_W1, b_HT], [PB[bank]], sig=(k == 7))
                                yield
                        n = len(grp)
                        P.op("act", lambda e, bank=bank, n=n: e.activation(out=pre[:, 0:n, :], in_=psum[bank][:, 0:n * 132].rearrange("p (a b) -> p a b", a=n), func=AF.Copy), [PB[bank]], [b_pre])
                        yield
                        for si, cb in enumerate(grp):
                            eng = "dve" if (cb % 2 == 0) else "pool"
                            a_, ba_ = acc[cb % 2], b_acc[cb % 2]
                            P.op(eng, lambda e, si=si, cb=cb, a_=a_, scan=scan: e.tensor_scalar_mul(out=a_[:], in0=pre[:, si, 0:128], scalar1=cw[:, scan, cb, 0:1]), [b_pre, b_cw], [ba_])
                            yield
                            for j in range(1, 5):
                                if eng == "dve":
                                    P.op(eng, lambda e, si=si, cb=cb, a_=a_, j=j, scan=scan: e.scalar_tensor_tensor(out=a_[:], in0=pre[:, si, j:j + 128], scalar=cw[:, scan, cb, j:j + 1], in1=a_[:], op0=ALU.mult, op1=ALU.add),
                                         [b_pre, b_cw, ba_], [ba_])
                                    yield
                                else:
                                    P.op(eng, lambda e, si=si, cb=cb, j=j, scan=scan: e.tensor_scalar_mul(out=ptmp[:], in0=pre[:, si, j:j + 128], scalar1=cw[:, scan, cb, j:j + 1]), [b_pre, b_cw], [b_ptmp])
                                    yield
                                    P.op(eng, lambda e, a_=a_: e.tensor_tensor(out=a_[:], in0=a_[:], in1=ptmp[:], op=ALU.add), [b_ptmp, ba_], [ba_])
                                    yield
                            P.op("act", lambda e, cb=cb, a_=a_: e.activation(out=qkvT[:, cb, :], in_=a_[:], func=AF.Silu), [ba_], [b_qkvT[cb]])
                            yield

                    return
                    yield
                def kv_steps():
                    P.dma(tab[:], ROPE[v * 128:(v + 1) * 128, :], [], [b_tab], b_tab)
                    yield
                    for hf in range(2):
                        for k in range(8):
                            P.op("pe", lambda e, k=k, hf=hf: e.matmul(psum[6 + hf][:, :], lhsT=HT[:, k, 2:130], rhs=W1[:, k, 3072 + hf * 512:3072 + (hf + 1) * 512], start=(k == 0), stop=(k == 7)),
                                 [b_HT, b_W1], [PB[6 + hf]], sig=(k == 7))
                            yield
                    qk_norm_rope([psum[6], psum[7]], [PB[6], PB[7]], knw, tab, b_tab, krot, b_krot, tmpA, b_tmpA)
                    yield
                    for h in range(H):
                        bank = 6 + h // 4
                        P.op("pe", lambda e, h=h, bank=bank: e.matmul(psum[bank][:, (h % 4) * 128:(h % 4 + 1) * 128], lhsT=krot[:, h * 128:(h + 1) * 128], rhs=identb, start=True, stop=True),
                             [b_krot, b_cbf], [PB[bank]])
                        yield
                    for hf in range(2):
                        P.op("act", lambda e, hf=hf: e.activation(out=KTsb[:, hf * 4:(hf + 1) * 4, :], in_=psum[6 + hf][:, :].rearrange("p (a b) -> p a b", a=4), func=AF.Copy), [PB[6 + hf]], [b_KTsb])
                        yield
                    P.dma(KTS[:, :, v * 128:(v + 1) * 128].rearrange("h p t -> p h t"), KTsb[:], [b_KTsb], [b_KTS], b_KTS)
                    yield
                    for hf in range(2):
                        for k in range(8):
                            P.op("pe", lambda e, k=k, hf=hf: e.matmul(psum[6 + hf][:, :], lhsT=HT[:, k, 2:130], rhs=W1[:, k, 4096 + hf * 512:4096 + (hf + 1) * 512], start=(k == 0), stop=(k == 7)),
                                 [b_HT, b_W1], [PB[6 + hf]], sig=(k == 7))
                            yield
                    for hf in range(2):
                        P.op("act", lambda e, hf=hf: e.activation(out=Vp[:, hf * 4:(hf + 1) * 4, 0:128], in_=psum[6 + hf][:, :].rearrange("p (a b) -> p a b", a=4), func=AF.Copy), [PB[6 + hf]], [b_Vp])
                        yield
                    P.dma(VPS[:, :, v, :].rearrange("h p c -> p h c"), Vp[:], [b_Vp], [b_VPS], b_VPS)
                    yield

                    return
                    yield
                def head_steps(h, T):
                        qcb, kcb, vcb = h, 8 + h, 16 + h
                        PR, PG, PS_, PU = T.banks
                        P.op("pool", lambda e, kcb=kcb: e.tensor_tensor(out=T.sq[:], in0=qkvT[:, kcb, :], in1=qkvT[:, kcb, :], op=ALU.mult), [b_qkvT[kcb]], [T.b_sq])
                        yield
                        P.op("pe", lambda e: e.matmul(psum[PR][:, 256:384], lhsT=ones, rhs=T.sq[:], start=True, stop=True), [b_cst, T.b_sq], [PB[PR]])
                        yield
                        P.op("act", lambda e: e.activation(out=T.rs[:], in_=psum[PR][:, 256:384], func=AF.Ln, bias=epsc[:, 0:1]), [PB[PR], b_eps], [T.b_rs])
                        yield
                        P.op("act", lambda e: e.activation(out=T.rs[:], in_=T.rs[:], func=AF.Exp, scale=-0.5), [T.b_rs], [T.b_rs])
                        yield
                        P.op("dve", lambda e, kcb=kcb: e.tensor_tensor(out=T.kTn[:], in0=qkvT[:, kcb, :], in1=T.rs[:], op=ALU.mult), [b_qkvT[kcb], T.b_rs], [T.b_kTn])
                        yield
                        if own:
                            P.op("pool", lambda e, qcb=qcb: e.tensor_tensor(out=T.sq[:], in0=qkvT[:, qcb, :], in1=qkvT[:, qcb, :], op=ALU.mult), [b_qkvT[qcb]], [T.b_sq])
                            yield
                            P.op("pe", lambda e: e.matmul(psum[PR][:, 384:512], lhsT=ones, rhs=T.sq[:], start=True, stop=True), [b_cst, T.b_sq], [PB[PR]])
                            yield
                            P.op("act", lambda e: e.activation(out=T.rs[:], in_=psum[PR][:, 384:512], func=AF.Ln, bias=epsc[:, 0:1]), [PB[PR], b_eps], [T.b_rs])
                            yield
                            P.op("act", lambda e: e.activation(out=T.rs[:], in_=T.rs[:], func=AF.Exp, scale=-0.5), [T.b_rs], [T.b_rs])
                            yield
                            P.op("dve", lambda e, qcb=qcb: e.scalar_tensor_tensor(out=T.qTn[:], in0=qkvT[:, qcb, :], scalar=128.0 ** -0.5, in1=T.rs[:], op0=ALU.mult, op1=ALU.mult), [b_qkvT[qcb], T.b_rs], [T.b_qTn])
                            yield
                        P.op("dve", lambda e, h=h: e.tensor_scalar_mul(out=T.LgIb[:, 0:128], in0=Ltri, scalar1=gb[:, 0, h:h + 1]), [b_cst, b_gb], [T.b_LgIb])
                        yield
                        P.op("pool", lambda e, h=h: e.tensor_scalar_mul(out=T.LgIb[:, 128:256], in0=ident, scalar1=gb[:, 1, h:h + 1]), [b_cst, b_gb], [T.b_LgIb])
                        yield
                        P.op("pe", lambda e: e.matmul(psum[PR][:, 0:256], lhsT=ones, rhs=T.LgIb[:], start=True, stop=True), [b_cst, T.b_LgIb], [PB[PR]])
                        yield
                        P.op("dve", lambda e, h=h: e.tensor_scalar(out=T.npos[:, 0:128], in0=psum[PR][:, 0:128], scalar1=gb[:, 2, h:h + 1], scalar2=0.0, op0=ALU.subtract, op1=ALU.max), [PB[PR], b_gb], [T.b_npos])
                        yield
                        P.op("dve", lambda e, h=h: e.tensor_scalar(out=T.npos[:, 128:256], in0=psum[PR][:, 0:128], scalar1=gb[:, 2, h:h + 1], scalar2=0.0, op0=ALU.subtract, op1=ALU.min), [PB[PR], b_gb], [T.b_npos])
                        yield
                        P.op("act", lambda e: e.activation(out=T.Em[:], in_=T.npos[:, 0:128], func=AF.Exp, scale=-1.0), [T.b_npos], [T.b_Em])
                        yield
                        P.op("act", lambda e: e.activation(out=T.ETm[:], in_=T.npos[:, 128:256], func=AF.Exp), [T.b_npos], [T.b_ETm])
                        yield
                        P.op("pool", lambda e: e.tensor_tensor(out=T.EMs[:], in0=T.Em[:], in1=negMs, op=ALU.mult), [T.b_Em, b_cst], [T.b_EMs])
                        yield
                        P.op("pool", lambda e: e.tensor_tensor(out=T.ETs[:], in0=T.ETm[:], in1=negMsT, op=ALU.mult), [T.b_ETm, b_cst], [T.b_ETs])
                        yield
                        P.op("pe", lambda e: e.matmul(psum[PG][:, 0:128], lhsT=T.kTn[:], rhs=T.kTn[:], start=True, stop=True), [T.b_kTn], [PB[PG]])
                        yield
                        if own:
                            P.op("pe", lambda e: e.matmul(psum[PG][:, 128:256], lhsT=T.kTn[:], rhs=T.qTn[:], start=True, stop=True), [T.b_kTn, T.b_qTn], [PB[PG]])
                            yield
                        P.op("dve", lambda e, h=h: e.scalar_tensor_tensor(out=T.NY[:, 0:128], in0=psum[PG][:, 0:128], scalar=gb[:, 1, h:h + 1], in1=T.EMs[:], op0=ALU.mult, op1=ALU.mult), [PB[PG], b_gb, T.b_EMs], [T.b_NY])
                        yield
                        P.op("dve", lambda e: e.tensor_tensor(out=T.NY[:, 128:256], in0=psum[PG][:, 0:128], in1=T.ETs[:], op=ALU.mult), [PB[PG], T.b_ETs], [T.b_NY])
                        yield
                        P.op("dve", lambda e: e.tensor_tensor(out=T.NY[:, 128:256], in0=T.NY[:, 128:256], in1=psum[PR][:, 128:256], op=ALU.mult), [PB[PR], T.b_NY], [T.b_NY])
                        yield
                        P.op("pool", lambda e: e.tensor_tensor(out=T.Wb[0][:, 0:128], in0=T.NY[:, 0:128], in1=BD32, op=ALU.mult), [T.b_NY, b_cst], [T.b_Wb[0]])
                        yield
                        P.op("pool", lambda e: e.tensor_tensor(out=T.Wb[0][:, 256:384], in0=T.NY[:, 128:256], in1=BD32, op=ALU.mult), [T.b_NY, b_cst], [T.b_Wb[0]])
                        yield
                        P.op("pool", lambda e: e.tensor_tensor(out=T.Wb[1][:, 128:256], in0=T.Wb[0][:, 0:128], in1=ident, op=ALU.add), [T.b_Wb[0], b_cst], [T.b_Wb[1]])
                        yield
                        P.op("pool", lambda e: e.tensor_tensor(out=T.Wb[1][:, 384:512], in0=T.Wb[0][:, 256:384], in1=ident, op=ALU.add), [T.b_Wb[0], b_cst], [T.b_Wb[1]])
                        yield
                        for n in range(4):
                            cur, nxt = T.Wb[n % 2], T.Wb[(n + 1) % 2]
                            bc, bn = T.b_Wb[n % 2], T.b_Wb[(n + 1) % 2]
                            w_ = 128 if n == 0 else 256
                            P.op("pe", lambda e, cur=cur, w_=w_: e.matmul(psum[PS_][:, 0:w_], lhsT=cur[:, 256:384], rhs=cur[:, 0:w_], start=True, stop=True), [bc], [PB[PS_]])
                            yield
                            P.op("pe", lambda e, cur=cur, w_=w_: e.matmul(psum[PS_][:, 256:256 + w_], lhsT=cur[:, 0:128], rhs=cur[:, 256:256 + w_], start=True, stop=True), [bc], [PB[PS_]])
                            yield
                            P.op("act", lambda e, nxt=nxt: e.activation(out=nxt[:].rearrange("p (a b) -> p a b", a=2)[:, :, 0:128], in_=psum[PS_][:, :].rearrange("p (a b) -> p a b", a=2)[:, :, 0:128], func=AF.Copy), [PB[PS_]], [bn])
                            yield
                            if n > 0:
                                P.op("dve", lambda e, cur=cur, nxt=nxt: e.tensor_tensor(out=nxt[:].rearrange("p (a b) -> p a b", a=2)[:, :, 128:256], in0=psum[PS_][:, :].rearrange("p (a b) -> p a b", a=2)[:, :, 128:256],
                                                                                        in1=cur[:].rearrange("p (a b) -> p a b", a=2)[:, :, 128:256], op=ALU.add), [PB[PS_], bc], [bn])
                                yield
                        P.op("pe", lambda e: e.matmul(psum[PS_][:, 0:128], lhsT=T.Wb[0][:, 256:384], rhs=T.Wb[0][:, 128:256], start=True, stop=True), [T.b_Wb[0]], [PB[PS_]])
                        yield
                        P.op("pe", lambda e: e.matmul(psum[PS_][:, 128:256], lhsT=T.Wb[0][:, 0:128], rhs=T.Wb[0][:, 384:512], start=True, stop=True), [T.b_Wb[0]], [PB[PS_]])
                        yield
                        P.op("dve", lambda e: e.tensor_tensor(out=T.TU[:].rearrange("p (a b) -> p a b", a=2), in0=psum[PS_][:, 0:256].rearrange("p (a b) -> p a b", a=2),
                                                              in1=T.Wb[0][:].rearrange("p (a b) -> p a b", a=2)[:, :, 128:256], op=ALU.add), [PB[PS_], T.b_Wb[0]], [T.b_TU])
                        yield
                        P.op("pool", lambda e: e.tensor_tensor(out=T.NYm[:, 0:128], in0=T.NY[:, 128:256], in1=M64, op=ALU.mult), [T.b_NY, b_cst], [T.b_NYm])
                        yield
                        P.op("pool", lambda e: e.tensor_tensor(out=T.NYm[:, 128:256], in0=T.NY[:, 0:128], in1=M64, op=ALU.mult), [T.b_NY, b_cst], [T.b_NYm])
                        yield
                        P.op("pe", lambda e: e.matmul(psum[PS_][:, 256:384], lhsT=T.NYm[:, 0:128], rhs=T.TU[:, 0:128], start=True, stop=True), [T.b_NYm, T.b_TU], [PB[PS_]])
                        yield
                        P.op("pe", lambda e: e.matmul(psum[PS_][:, 384:512], lhsT=T.NYm[:, 128:256], rhs=T.TU[:, 128:256], start=True, stop=True), [T.b_NYm, T.b_TU], [PB[PS_]])
                        yield
                        P.op("act", lambda e: e.activation(out=T.Wm[:], in_=psum[PS_][:, 256:512], func=AF.Copy), [PB[PS_]], [T.b_Wm])
                        yield
                        P.op("pe", lambda e: e.matmul(psum[PS_][:, 0:128], lhsT=T.TU[:, 128:256], rhs=T.Wm[:, 0:128], start=True, stop=True), [T.b_TU, T.b_Wm], [PB[PS_]])
                        yield
                        P.op("pe", lambda e: e.matmul(psum[PS_][:, 128:256], lhsT=T.TU[:, 0:128], rhs=T.Wm[:, 128:256], start=True, stop=True), [T.b_TU, T.b_Wm], [PB[PS_]])
                        yield
                        P.op("dve", lambda e: e.tensor_tensor(out=T.TU2[:], in0=psum[PS_][:, 0:256], in1=T.TU[:], op=ALU.add), [PB[PS_], T.b_TU], [T.b_TU2])
                        yield
                        P.op("pool", lambda e: e.tensor_tensor(out=T.NYm[:, 128:256], in0=T.NY[:, 0:128], in1=M128, op=ALU.mult), [T.b_NY, b_cst, T.b_NYm], [T.b_NYm])
                        yield
                        P.op("pe", lambda e: e.matmul(psum[PS_][:, 384:512], lhsT=T.NYm[:, 128:256], rhs=T.TU2[:, 128:256], start=True, stop=True), [T.b_NYm, T.b_TU2], [PB[PS_]])
                        yield
                        P.op("act", lambda e: e.activation(out=T.Wm[:, 128:256], in_=psum[PS_][:, 384:512], func=AF.Copy), [PB[PS_]], [T.b_Wm])
                        yield
                        P.op("pe", lambda e: e.matmul(psum[PS_][:, 128:256], lhsT=T.TU2[:, 0:128], rhs=T.Wm[:, 128:256], start=True, stop=True), [T.b_TU2, T.b_Wm], [PB[PS_]])
                        yield
                        P.op("dve", lambda e: e.tensor_tensor(out=T.TT[:], in0=psum[PS_][:, 128:256], in1=T.TU2[:, 128:256], op=ALU.add), [PB[PS_], T.b_TU2], [T.b_TT])
                        yield
                        P.op("pe", lambda e: e.matmul(psum[PR][:, 256:384], lhsT=T.kTn[:], rhs=ident, start=True, stop=True), [T.b_kTn, b_cst], [PB[PR]])
                        yield
                        P.op("pe", lambda e, vcb=vcb: e.matmul(psum[PR][:, 384:512], lhsT=qkvT[:, vcb, :], rhs=ident, start=True, stop=True), [b_qkvT[vcb], b_cst], [PB[PR]])
                        yield
                        P.op("dve", lambda e, h=h: e.tensor_scalar_mul(out=T.Kbp[:], in0=psum[PR][:, 256:384], scalar1=gb[:, 3, h:h + 1]), [PB[PR], b_gb], [T.b_Kbp])
                        yield
                        P.op("dve", lambda e: e.tensor_scalar_mul(out=T.kd[:], in0=psum[PR][:, 256:384], scalar1=T.ETm[:, 127:128]), [PB[PR], T.b_ETm], [T.b_kd])
                        yield
                        P.op("dve", lambda e, h=h: e.tensor_scalar_mul(out=T.Vb[:], in0=psum[PR][:, 384:512], scalar1=gb[:, 1, h:h + 1]), [PB[PR], b_gb], [T.b_Vb])
                        yield
                        P.op("pe", lambda e: e.matmul(psum[PU][:, 0:128], lhsT=T.TT[:], rhs=T.Vb[:], start=True, stop=True), [T.b_TT, T.b_Vb], [PB[PU]])
                        yield
                        P.op("pe", lambda e: e.matmul(psum[PU][:, 128:256], lhsT=T.Kbp[:], rhs=T.TT[:], start=True, stop=True), [T.b_TT, T.b_Kbp], [PB[PU]])
                        yield
                        P.op("act", lambda e: e.activation(out=T.wT[:], in_=psum[PU][:, 128:256], func=AF.Copy), [PB[PU]], [T.b_wT])
                        yield
                        P.op("act", lambda e: e.activation(out=T.usb[:], in_=psum[PU][:, 0:128], func=AF.Copy), [PB[PU]], [T.b_usb])
                        yield
                        P.op("pe", lambda e, h=h: e.matmul(psum[PU][:, 256:384], lhsT=T.wT[:], rhs=Sst[:, h, :], start=True, stop=True), [T.b_wT, b_S[h]], [PB[PU]])
                        yield
                        P.op("dve", lambda e: e.tensor_tensor(out=T.vnew[:], in0=T.usb[:], in1=psum[PU][:, 256:384], op=ALU.subtract), [T.b_usb, PB[PU]], [T.b_vnew])
                        yield
                        if own:
                            P.op("act", lambda e: e.activation(out=T.eR[:], in_=psum[PR][:, 0:128], func=AF.Exp), [PB[PR]], [T.b_eR])
                            yield
                            P.op("dve", lambda e: e.tensor_tensor(out=T.QdT[:], in0=T.qTn[:], in1=T.eR[:], op=ALU.mult), [T.b_qTn, T.b_eR], [T.b_QdT])
                            yield
                            P.op("pool", lambda e: e.tensor_tensor(out=T.ETi[:], in0=T.ETm[:], in1=MiT, op=ALU.mult), [T.b_ETm, b_cst], [T.b_ETi])
                            yield
                            P.op("dve", lambda e: e.tensor_tensor(out=T.qkT[:], in0=psum[PG][:, 128:256], in1=T.ETi[:], op=ALU.mult), [PB[PG], T.b_ETi], [T.b_qkT])
                            yield
                            P.op("pe", lambda e, h=h: e.matmul(psum[PG][:, 256:384], lhsT=T.QdT[:], rhs=Sst[:, h, :], start=True, stop=False), [T.b_QdT, b_S[h]], [PB[PG]])
                            yield
                            P.op("pe", lambda e: e.matmul(psum[PG][:, 256:384], lhsT=T.qkT[:], rhs=T.vnew[:], start=False, stop=True), [T.b_qkT, T.b_vnew], [PB[PG]])
                            yield
                            P.op("act", lambda e, h=h: e.activation(out=osb[:, h * 128:(h + 1) * 128], in_=psum[PG][:, 256:384], func=AF.Copy), [PB[PG]], [b_osb])
                            yield
                        P.op("act", lambda e: e.activation(out=T.egl[:], in_=psum[PR][:, 127:128], func=AF.Exp), [PB[PR]], [T.b_egl])
                        yield
                        P.op("pe", lambda e: e.matmul(psum[PU][:, 384:512], lhsT=T.kd[:], rhs=T.vnew[:], start=True, stop=True), [T.b_kd, T.b_vnew], [PB[PU]])
                        yield
                        P.op("dve", lambda e, h=h: e.scalar_tensor_tensor(out=Sst[:, h, :], in0=Sst[:, h, :], scalar=T.egl[:, 0:1], in1=psum[PU][:, 384:512], op0=ALU.mult, op1=ALU.add), [b_S[h], T.b_egl, PB[PU]], [b_S[h]])
                        yield
                def pair_cbs(p):
                    hs = (2 * p, 2 * p + 1)
                    c = [hs[0], hs[1]] if own else []
                    return c + [8 + hs[0], 8 + hs[1], 16 + hs[0], 16 + hs[1]]
                for _ in pre_steps(pair_cbs(0)):
                    pass
                for p_ in range(4):
                    if p_ < 3:
                        aux = pre_steps(pair_cbs(p_ + 1))
                    elif kv:
                        aux = kv_steps()
                    else:
                        aux = iter(())
                    gens = [head_steps(2 * p_, TS[0]), head_steps(2 * p_ + 1, TS[1]), aux]
                    alive = [True, True, True]
                    while any(alive):
                        for gi_ in range(3):
                            if alive[gi_]:
                                try:
                                    next(gens[gi_])
                                except StopIteration:
                                    alive[gi_] = False
                if own:
                    oi = (v - OWN0) if scan == 0 else (v - OWN1)
                    P.dma(OS[scan, oi, :, :], osb[:], [b_osb], [b_OS], b_OS)

        P.barrier()

        b_ZS = Buf(); b_GS = Buf()
        YbT = sb("YbT", [128, H, 2048], BF16); b_YbT = Buf()
        s_qt = ExitStack()
        QT = sb("QT", [128, H, 2048], BF16, stack=s_qt); b_QT = Buf()
        with ExitStack() as s2:
            W2 = sb("W2", [128, 8, 3072], BF16, stack=s2); b_W2 = Buf()
            stage = [junk, xt]
            bstage = [b_junk, b_xt]
            tabB = sb("tabB2", [128, 64], stack=s2); b_tabB = Buf()
            tmpB = [sb("tmpB2_%d" % i, [128, D], stack=s2) for i in range(3)] + [sb("tmpB32", [128, 16], stack=s2)]
            b_tmpB = [Buf() for _ in range(4)]
            krotB = sb("krotB2", [128, D], BF16, stack=s2); b_krotB = Buf()
            load_w_bf16(W2, b_W2, 3072, 1024, stage, bstage, 0)
            load_w_bf16(W2, b_W2, 4128, 1024, stage, bstage, 1024)
            load_w_bf16(W2, b_W2, 7200, 1024, stage, bstage, 2048)
            zsb = sb("zsb", [128, 2048], BF16, stack=s2); b_zsb = Buf()
            for i in range(16):
                v = OWN0 + i
                make_HT(v, False)
                P.dma(tabB[:], ROPE[v * 128:(v + 1) * 128, :], [], [b_tabB], b_tabB)
                for hf in range(2):
                    for k in range(8):
                        P.op("pe", lambda e, k=k, hf=hf: e.matmul(psum[6 + hf][:, :], lhsT=HT[:, k, 2:130], rhs=W2[:, k, 1024 + hf * 512:1024 + (hf + 1) * 512], start=(k == 0), stop=(k == 7)),
                             [b_HT, b_W2], [PB[6 + hf]], sig=(k == 7))
                qk_norm_rope([psum[6], psum[7]], [PB[6], PB[7]], qnw, tabB, b_tabB, krotB, b_krotB, tmpB, b_tmpB)
                for h in range(H):
                    bank = 6 + h // 4
                    P.op("pe", lambda e, h=h, bank=bank: e.matmul(psum[bank][:, (h % 4) * 128:(h % 4 + 1) * 128], lhsT=krotB[:, h * 128:(h + 1) * 128], rhs=identb, start=True, stop=True),
                         [b_krotB, b_cbf], [PB[bank]])
                for hf in range(2):
                    P.op("act", lambda e, hf=hf, i=i: e.activation(out=QT[:, hf * 4:(hf + 1) * 4, i * 128:(i + 1) * 128], in_=psum[6 + hf][:, :].rearrange("p (a b) -> p a b", a=4), func=AF.Copy), [PB[6 + hf]], [b_QT])
                for (c0, dstt, bd, off, fn) in ((0, zsb, b_zsb, 0, AF.Silu), (2048, zsb, b_zsb, 1024, AF.Silu)):
                    for hf in range(2):
                        bank = 4 + hf
                        for k in range(8):
                            P.op("pe", lambda e, k=k, hf=hf, c0=c0, bank=bank: e.matmul(psum[bank][:, :], lhsT=HT[:, k, 2:130], rhs=W2[:, k, c0 + hf * 512:c0 + (hf + 1) * 512], start=(k == 0), stop=(k == 7)),
                                 [b_HT, b_W2], [PB[bank]], sig=(k == 7))
                        P.op("act", lambda e, hf=hf, bank=bank, dstt=dstt, off=off, fn=fn: e.activation(out=dstt[:, off + hf * 512:off + (hf + 1) * 512], in_=psum[bank][:, :], func=fn), [PB[bank]], [bd])
                P.dma(ZS[i, :, :], zsb[:], [b_zsb], [b_ZS], b_ZS)

        P.barrier()
        with ExitStack() as s3:
            KTh = sb("KTh", [128, NKV * 128], BF16, stack=s3); b_KTh = Buf()
            Vph = sb("Vph", [128, NKV, 129], BF16, stack=s3); b_Vph = Buf()
            Eb = [sb("Eb%d" % i, [128, 1024], BF16, stack=s3) for i in range(2)]; b_Eb = [Buf(), Buf()]
            zb = sb("zb", [128, 4, 128], BF16, stack=s3); b_zb = Buf()
            fo = sb("fo", [128, 8, 128], stack=s3); b_fo = Buf()
            fs = sb("fs", [128, 8], stack=s3); b_fs = Buf()
            yb = sb("yb", [128, 128], BF16, stack=s3); b_yb = Buf()
            for h in range(H):
                P.dma(KTh[:], KTS[h, :, :], [b_KTS], [b_KTh], b_KTh)
                P.dma(Vph[:], VPS[h, :, :, :], [b_VPS], [b_Vph], b_Vph)
                for qt in range(4):
                    P.dma(zb[:], ZS[qt * 4:(qt + 1) * 4, :, 1024 + h * 128:1024 + (h + 1) * 128].rearrange("a p c -> p a c"), [b_ZS], [b_zb], b_zb)
                    def emit_qk(kb):
                        sbk = (kb % 2) * 2
                        for m in range(2):
                            P.op("pe", lambda e, m=m, kb=kb, qt=qt, h=h, sbk=sbk: e.matmul(psum[sbk + m][:, :], lhsT=KTh[m * 64:(m + 1) * 64, kb * 128:(kb + 1) * 128],
                                                                                   rhs=QT[m * 64:(m + 1) * 64, h, qt * 512:(qt + 1) * 512], start=True, stop=True),
                                 [b_KTh, b_QT], [PB[sbk + m]])
                    emit_qk(0)
                    for kb in range(NKV):
                        sbk = (kb % 2) * 2
                        if kb + 1 < NKV:
                            emit_qk(kb + 1)
                        for m in range(2):
                            P.op("act", lambda e, m=m, kb=kb, sbk=sbk: e.activation(out=Eb[kb % 2][:, m * 512:(m + 1) * 512], in_=psum[sbk + m][:, :], func=AF.Exp, scale=0.125, bias=lamt[:, 1:2]),
                                 [PB[sbk + m], b_lam], [b_Eb[kb % 2]])
                        for qb in range(4):
                            for m in range(2):
                                P.op("pe", lambda e, m=m, kb=kb, qb=qb: e.matmul(psum[4 + qb][:, m * 129:(m + 1) * 129], lhsT=Eb[kb % 2][:, m * 512 + qb * 128:m * 512 + (qb + 1) * 128],
                                                                                 rhs=Vph[:, kb, :], start=(kb == 0 and m == 0), stop=(kb == NKV - 1 and m == 1)),
                                     [b_Eb[kb % 2], b_Vph], [PB[4 + qb]], sig=(qb == 3 and m == 1))
                    for qb in range(4):
                        blk = qt * 4 + qb
                        pso = psum[4 + qb]
                        P.op("dve", lambda e, pso=pso: e.tensor_copy(out=fs[:, 0:1], in_=pso[:, 128:129]), [PB[4 + qb]], [b_fs])
                        P.op("dve", lambda e, pso=pso: e.tensor_copy(out=fs[:, 1:2], in_=pso[:, 257:258]), [PB[4 + qb]], [b_fs])
                        P.op("dve", lambda e: e.reciprocal(out=fs[:, 2:4], in_=fs[:, 0:2]), [b_fs], [b_fs])
                        P.op("dve", lambda e: e.tensor_tensor(out=fs[:, 3:4], in0=fs[:, 3:4], in1=lamt[:, 0:1], op=ALU.mult), [b_fs, b_lam], [b_fs])
                        P.op("dve", lambda e, pso=pso: e.tensor_scalar_mul(out=fo[:, 0, :], in0=pso[:, 129:257], scalar1=fs[:, 3:4]), [PB[4 + qb], b_fs], [b_fo])
                        P.op("dve", lambda e, pso=pso: e.scalar_tensor_tensor(out=fo[:, 1, :], in0=pso[:, 0:128], scalar=fs[:, 2:3], in1=fo[:, 0, :], op0=ALU.mult, op1=ALU.add), [PB[4 + qb], b_fs, b_fo], [b_fo])
                        P.op("act", lambda e: e.activation(out=fo[:, 2, :], in_=fo[:, 1, :], func=AF.Square, accum_out=fs[:, 4:5]), [b_fo, b_fs], [b_fo, b_fs])
                        P.op("act", lambda e: e.activation(out=fs[:, 5:6], in_=fs[:, 4:5], func=AF.Ln, scale=1.0 / 128, bias=epsc[:, 0:1]), [b_fs, b_eps], [b_fs])
                        P.op("act", lambda e: e.activation(out=fs[:, 5:6], in_=fs[:, 5:6], func=AF.Exp, scale=-0.5), [b_fs], [b_fs])
                        P.op("dve", lambda e: e.scalar_tensor_tensor(out=fo[:, 3, :], in0=fo[:, 1, :], scalar=fs[:, 5:6], in1=dnw, op0=ALU.mult, op1=ALU.mult), [b_fo, b_fs, b_vecs], [b_fo])
                        P.op("dve", lambda e, qb=qb: e.scalar_tensor_tensor(out=yb[:], in0=fo[:, 3, :], scalar=(1.0 - LAMBDA_INIT), in1=zb[:, qb, :], op0=ALU.mult, op1=ALU.mult), [b_fo, b_zb], [b_yb])
                        P.op("pe", lambda e: e.matmul(psum[0][:, 0:128], lhsT=yb[:], rhs=identb, start=True, stop=True), [b_yb, b_cbf], [PB[0]])
                        P.op("act", lambda e, h=h, blk=blk: e.activation(out=YbT[:, h, blk * 128:(blk + 1) * 128], in_=psum[0][:, 0:128], func=AF.Copy), [PB[0]], [b_YbT])

        s_qt.close()
        P.barrier()
        with ExitStack() as s4:
            Wo = [sb("Wo%d" % i, [128, 8, D], BF16, stack=s4) for i in range(3)]; b_Wo = [Buf() for _ in range(3)]
            stage = [junk, xt]
            bstage = [b_junk, b_xt]
            YaT = sb("YaT", [128, H, 128], BF16, stack=s4); b_YaT = Buf()
            Wg = sb("Wg", [128, 8, 2048], BF16, stack=s4); b_Wg = Buf()
            load_w_bf16(Wg, b_Wg, 8224, 2048, stage, bstage, 0)
            ii = 0
            for wi, src in enumerate((WOA, WOB, WOUT)):
                for k in range(8):
                    s_, bs_ = stage[ii % 2], bstage[ii % 2]
                    P.dma(s_[:], src[k * 128:(k + 1) * 128, :], [], [bs_], bs_)
                    P.op(["dve", "pool"][ii % 2], lambda e, s_=s_, wi=wi, k=k: e.tensor_copy(out=Wo[wi][:, k, :], in_=s_[:]), [bs_], [b_Wo[wi]])
                    ii += 1
            o1 = sb("o1", [128, D], stack=s4); b_o1 = Buf()
            o2 = sb("o2", [128, D], stack=s4); b_o2 = Buf()
            zs4 = sb("zs4", [128, 1024], BF16, stack=s4); b_zs4 = Buf()
            gs4 = sb("gs4", [128, 2048], BF16, stack=s4); b_gs4 = Buf()
            ya = sb("ya", [128, D], BF16, stack=s4); b_ya = Buf()
            mg = sb("mg", [128, D], stack=s4); b_mg = Buf()
            mgb = sb("mgb", [128, D], BF16, stack=s4); b_mgb = Buf()
            mT = sb("mT", [128, 8, 128], BF16, stack=s4); b_mT = Buf()
            res = sb("res", [128, D], stack=s4); b_res = Buf()
            b_OUT = Buf()
            out_toks = []
            for i in range(16):
                P.dma(o1[:], OS[0, i, :, :], [b_OS], [b_o1], b_o1)
                P.dma(o2[:], OS[1, 15 - i, :, :], [b_OS], [b_o2], b_o2)
                P.dma(zs4[:], ZS[i, :, 0:1024], [b_ZS], [b_zs4], b_zs4)
                make_HT(OWN0 + i, False)
                for g4 in range(4):
                    bank = 6 + g4 % 2
                    for k in range(8):
                        P.op("pe", lambda e, k=k, g4=g4, bank=bank: e.matmul(psum[bank][:, :], lhsT=HT[:, k, 2:130], rhs=Wg[:, k, g4 * 512:(g4 + 1) * 512], start=(k == 0), stop=(k == 7)), [b_HT, b_Wg], [PB[bank]], sig=(k == 7))
                    P.op("act", lambda e, g4=g4, bank=bank: e.activation(out=gs4[:, g4 * 512:(g4 + 1) * 512], in_=psum[bank][:, :], func=AF.Sigmoid), [PB[bank]], [b_gs4])
                for hf in range(2):
                    P.op("pe", lambda e, hf=hf: e.matmul(psum[hf][:, :], lhsT=Jm, rhs=o2[:, hf * 512:(hf + 1) * 512], start=True, stop=True), [b_cst, b_o2], [PB[hf]])
                    P.op("dve", lambda e, hf=hf: e.tensor_tensor(out=o1[:, hf * 512:(hf + 1) * 512], in0=o1[:, hf * 512:(hf + 1) * 512], in1=psum[hf][:, :], op=ALU.add), [PB[hf], b_o1], [b_o1])
                P.op("act", lambda e: e.activation(out=junk[:], in_=o1[:], func=AF.Square), [b_o1], [b_junk])
                P.op("dve", lambda e: e.tensor_reduce(out=st[:, 0:8], in_=junk[:].rearrange("p (g d) -> p g d", d=128), axis=AX.X, op=ALU.add), [b_junk], [b_st])
                P.op("act", lambda e: e.activation(out=st[:, 0:8], in_=st[:, 0:8], func=AF.Ln, scale=1.0 / 128, bias=epsc[:, 0:1]), [b_st, b_eps], [b_st])
                P.op("act", lambda e: e.activation(out=st[:, 0:8], in_=st[:, 0:8], func=AF.Exp, scale=-0.5), [b_st], [b_st])
                P.op("dve", lambda e: e.tensor_tensor(out=junk[:].rearrange("p (g d) -> p g d", d=128), in0=o1[:].rearrange("p (g d) -> p g d", d=128), in1=st[:, 0:8].unsqueeze(2).to_broadcast([128, 8, 128]), op=ALU.mult), [b_o1, b_st], [b_junk])
                P.op("pool", lambda e: e.tensor_tensor(out=junk[:].rearrange("p (g d) -> p g d", d=128), in0=junk[:].rearrange("p (g d) -> p g d", d=128), in1=gnw.unsqueeze(1).to_broadcast([128, 8, 128]), op=ALU.mult), [b_junk, b_vecs], [b_junk])
                P.op("dve", lambda e: e.tensor_tensor(out=ya[:], in0=junk[:], in1=zs4[:], op=ALU.mult), [b_junk, b_zs4], [b_ya])
                for h in range(H):
                    bank = h // 4
                    P.op("pe", lambda e, h=h, bank=bank: e.matmul(psum[bank][:, (h % 4) * 128:(h % 4 + 1) * 128], lhsT=ya[:, h * 128:(h + 1) * 128], rhs=identb, start=True, stop=True), [b_ya, b_cbf], [PB[bank]])
                for hf in range(2):
                    P.op("act", lambda e, hf=hf, i=i: e.activation(out=YaT[:, hf * 4:(hf + 1) * 4, :], in_=psum[hf][:, :].rearrange("p (a b) -> p a b", a=4), func=AF.Copy), [PB[hf]], [b_YaT])
                for (wi, Y, bY, pb0, c0_) in ((0, YaT, b_YaT, 2, 0), (1, YbT, b_YbT, 4, i * 128)):
                    for hf in range(2):
                        for k in range(8):
                            P.op("pe", lambda e, wi=wi, Y=Y, hf=hf, k=k, pb0=pb0, c0_=c0_: e.matmul(psum[pb0 + hf][:, :], lhsT=Y[:, k, c0_:c0_ + 128], rhs=Wo[wi][:, k, hf * 512:(hf + 1) * 512], start=(k == 0), stop=(k == 7)),
                                 [bY, b_Wo[wi]], [PB[pb0 + hf]], sig=(k == 7))
                for hf in range(2):
                    P.op("dve", lambda e, hf=hf: e.tensor_tensor(out=mg[:, hf * 512:(hf + 1) * 512], in0=psum[2 + hf][:, :], in1=gs4[:, hf * 512:(hf + 1) * 512], op=ALU.mult), [PB[2 + hf], b_gs4], [b_mg])
                    P.op("dve", lambda e, hf=hf: e.tensor_tensor(out=junk[:, hf * 512:(hf + 1) * 512], in0=psum[4 + hf][:, :], in1=gs4[:, 1024 + hf * 512:1024 + (hf + 1) * 512], op=ALU.mult), [PB[4 + hf], b_gs4], [b_junk])
                P.op("pool", lambda e: e.tensor_tensor(out=mgb[:], in0=mg[:], in1=junk[:], op=ALU.add), [b_mg, b_junk], [b_mgb])
                for k in range(8):
                    bank = 6 + k // 4
                    P.op("pe", lambda e, k=k, bank=bank: e.matmul(psum[bank][:, (k % 4) * 128:(k % 4 + 1) * 128], lhsT=mgb[:, k * 128:(k + 1) * 128], rhs=identb, start=True, stop=True), [b_mgb, b_cbf], [PB[bank]])
                for hf in range(2):
                    P.op("act", lambda e, hf=hf: e.activation(out=mT[:, hf * 4:(hf + 1) * 4, :], in_=psum[6 + hf][:, :].rearrange("p (a b) -> p a b", a=4), func=AF.Copy), [PB[6 + hf]], [b_mT])
                for hf in range(2):
                    for k in range(8):
                        P.op("pe", lambda e, hf=hf, k=k: e.matmul(psum[hf][:, :], lhsT=mT[:, k, :], rhs=Wo[2][:, k, hf * 512:(hf + 1) * 512], start=(k == 0), stop=(k == 7)), [b_mT, b_Wo[2]], [PB[hf]], sig=(k == 7))
                    P.op("dve", lambda e, hf=hf: e.tensor_tensor(out=res[:, hf * 512:(hf + 1) * 512], in0=psum[hf][:, :], in1=mod[:, 2, hf * 512:(hf + 1) * 512], op=ALU.mult), [PB[hf], b_mod], [b_res])
                P.op("pool", lambda e: e.tensor_tensor(out=res[:], in0=res[:], in1=xt[:], op=ALU.add), [b_res, b_xt], [b_res])
                tok = P.dma(OUT[i * 128:(i + 1) * 128, :], res[:], [b_res], [b_OUT], b_OUT)
            P.final_wait("sp", [tok])

        print('total ops', P.gidx, {n: len(P.q[n]) for n in P.names})
        P.emit()
    return nc


def _consts():
    i = np.arange(128)
    ident = np.eye(128, dtype=np.float32)
    J = ident[::-1].copy()
    ones = np.ones((128, 128), np.float32)
    Ltri = (i[:, None] <= i[None, :]).astype(np.float32)
    negMs = -(i[:, None] > i[None, :]).astype(np.float32)
    negMsT = -(i[None, :] > i[:, None]).astype(np.float32)
    MiT = (i[None, :] >= i[:, None]).astype(np.float32)
    SelM = np.zeros((128, 132), np.float32)
    SelM[i, i + 2] = 1
    SelH = np.zeros((128, 132), np.float32)
    SelH[0, 0] = 1
    SelH[1, 1] = 1
    SelH[2, 130] = 1
    SelH[3, 131] = 1
    blk = lambda b: ((i[:, None] // b) == (i[None, :] // b)).astype(np.float32)
    BD32 = blk(32)
    M64 = blk(64) - blk(32)
    M128 = 1.0 - blk(64)
    return np.concatenate([ident, J, ones, Ltri, negMs, negMsT, MiT, SelM, SelH, BD32, M64, M128], axis=1)


def _rope_table(pos):
    pos = np.asarray(pos)
    row = (pos // 64).astype(np.float32)
    col = (pos % 64).astype(np.float32)
    inv = (10000.0 ** (-np.arange(16, dtype=np.float32) / 16)).astype(np.float32)
    ar = row[:, None] * inv
    ac = col[:, None] * inv
    t = np.concatenate([np.cos(ar), np.sin(ar), np.cos(ac), np.sin(ac)], axis=1).astype(np.float32)
    t[pos < 0] = np.array([1.0] * 16 + [0.0] * 16 + [1.0] * 16 + [0.0] * 16, np.float32)
    return t


def _core_plan(r):
    fwd_first = r >= 2
    own = list(range(16 * r, 16 * r + 16))
    left = list(range(0, 16 * r))
    right = list(range(16 * r + 16, 64))

    def scan(forward, nslots, filler):
        rev = not forward
        vis = [("c", 0, rev, 1.0), ("c", 1, rev, 1.0)]
        if rev:
            vis = [("c", 1, rev, 1.0), ("c", 0, rev, 1.0)]
        pref = left if forward else right[::-1]
        fill = [("l", b, rev, 0.0) for b in filler]
        vis += fill[:nslots - len(pref)] + [("l", b, rev, 1.0) for b in pref]
        o = own if forward else own[::-1]
        vis += [("l", b, rev, 1.0) for b in o]
        return vis

    if fwd_first:
        filler = right[::-1]
        v1 = scan(True, 48, filler)
        v2 = scan(False, 16, [own[0]] * 16)
    else:
        filler = left
        v1 = scan(False, 48, filler)
        v2 = scan(True, 16, [own[0]] * 16)
    assert len(v1) == 66 and len(v2) == 34
    return v1 + v2, fwd_first


_NC_CACHE = {}


def kernel(x, c, ctx, c_ctx, w_ada, b_ada, w_in, conv_w, a_log, dt_bias, gdn_norm_w,
           q_norm_w, k_norm_w, lambda_q1, lambda_k1, lambda_q2, lambda_k2, diff_norm_w,
           w_oa, w_ob, w_out):
    f = lambda a: np.ascontiguousarray(np.asarray(a, dtype=np.float32))
    x, c, ctx, c_ctx = f(x), f(c), f(ctx), f(c_ctx)
    w_ada, b_ada, w_in, conv_w = f(w_ada)[0], f(b_ada)[0], f(w_in)[0], f(conv_w)[0]
    a_log, dt_bias = f(a_log)[0], f(dt_bias)[0]
    vec = lambda a: f(a)[0]
    if "nc" not in _NC_CACHE:
        _NC_CACHE["nc"] = build_nc()
    nc = _NC_CACHE["nc"]
    cst = _consts()
    vecs = np.zeros((768,), np.float32)
    vecs[0:128] = vec(gdn_norm_w)
    vecs[128:192] = vec(q_norm_w)
    vecs[192:256] = vec(k_norm_w)
    vecs[256:320] = vec(lambda_q1)
    vecs[320:384] = vec(lambda_k1)
    vecs[384:448] = vec(lambda_q2)
    vecs[448:512] = vec(lambda_k2)
    vecs[512:640] = vec(diff_norm_w)
    VECS = np.ascontiguousarray(np.broadcast_to(vecs, (128, 768)))
    BADA = np.ascontiguousarray(np.broadcast_to(b_ada, (128, 3 * D)))
    in_maps = []
    plans = []
    for core in range(8):
        b, r = divmod(core, 4)
        plan, fwd_first = _core_plan(r)
        plans.append((plan, fwd_first))
        XV = np.zeros((NV * 128, D), np.float32)
        XH = np.zeros((NV * 4, D), np.float32)
        HM = np.zeros((4, NV), np.float32)
        GM = np.zeros((128, NV), np.float32)
        ROPE = np.zeros((NKV * 128, 64), np.float32)
        for v, (seq, blk, rev, gmask) in enumerate(plan):
            src = ctx[b] if seq == "c" else x[b]
            n = src.shape[0]
            idx = np.arange(blk * 128, blk * 128 + 128)
            halo = np.array([blk * 128 - 2, blk * 128 - 1, blk * 128 + 128, blk * 128 + 129])
            if rev:
                idx = idx[::-1]
                halo = halo[::-1]
            XV[v * 128:(v + 1) * 128] = src[idx]
            valid = (halo >= 0) & (halo < n)
            XH[v * 4:(v + 1) * 4][valid] = src[halo[valid]]
            HM[:, v] = valid.astype(np.float32)
            GM[:, v] = gmask
            if v < NKV:
                pos = idx if seq == "l" else -np.ones(128, np.int64)
                ROPE[v * 128:(v + 1) * 128] = _rope_table(pos)
        d1 = 0 if fwd_first else 1
        dirs = (d1, 1 - d1)
        WAB = np.stack([np.concatenate([w_in[:, 4096 + d * 8:4096 + d * 8 + 8], w_in[:, 4112 + d * 8:4112 + d * 8 + 8]], axis=1) for d in dirs])
        cwt = np.ascontiguousarray(conv_w.T)
        CW = np.stack([cwt if d == 0 else np.ascontiguousarray(cwt[:, ::-1]) for d in dirs])
        ALG = np.stack([np.broadcast_to(a_log[d], (128, 8)) for d in dirs])
        DTB = np.stack([np.broadcast_to(dt_bias[d], (128, 8)) for d in dirs])
        cv = np.stack([c[b], c_ctx])
        CT = np.ascontiguousarray(cv.reshape(2, 8, 128).transpose(2, 0, 1).reshape(128, 16))
        in_maps.append({
            "XV": XV, "XH": XH, "HM": HM, "GM": GM, "ROPE": ROPE, "CT": CT, "WADA": w_ada, "BADA": BADA,
            "WIN": w_in, "WAB": np.ascontiguousarray(WAB), "CW": np.ascontiguousarray(CW),
            "ALG": np.ascontiguousarray(ALG), "DTB": np.ascontiguousarray(DTB), "VECS": VECS, "CST": cst,
            "WOA": f(w_oa)[0], "WOB": f(w_ob)[0], "WOUT": f(w_out)[0],
        })
    if _NC_CACHE.get("maps_only"):
        return in_maps, plans
    res = run_bass_kernel_spmd(nc, in_maps, core_ids=list(range(8)))
    out = np.zeros((2, L, D), np.float32)
    for core in range(8):
        b, r = divmod(core, 4)
        plan, fwd_first = plans[core]
        o = np.asarray(res.results[core]["OUT"])
        for i in range(16):
            seq, blk, rev, _ = plan[OWN0 + i]
            idx = np.arange(blk * 128, blk * 128 + 128)
            if rev:
                idx = idx[::-1]
            out[b, idx] = o[i * 128:(i + 1) * 128]
    return out
```

```python
import math
from contextlib import ExitStack

import numpy as np
import concourse.bass as bass
import concourse.mybir as mybir
from concourse.bass_utils import run_bass_kernel_spmd

F32 = mybir.dt.float32
BF16 = mybir.dt.bfloat16
AF = mybir.ActivationFunctionType
ALU = mybir.AluOpType
AX = mybir.AxisListType

D = 1024
L = 8192
NCTX = 256
H = 8
EPS = 1e-6
NV = 100
NKV = 66
OWN0 = 50
S2 = 66
OWN1 = 84
IN_COLS = 10272
LAMBDA_INIT = 0.8 - 0.6 * math.exp(-0.3 * 0)
DEBUG = False
import os
MAXOPS = int(os.environ.get('KMAXOPS', '1000000000'))
TRACE_LO = int(os.environ.get('KTLO', '0'))
TRACE_HI = int(os.environ.get('KTHI', '-1'))


class Buf:
    __slots__ = ("w", "r", "dsem", "dcnt")

    def __init__(self):
        self.w = None
        self.r = {}
        self.dsem = None
        self.dcnt = 0


class Prog:
    def __init__(self, nc, es):
        self.nc = nc
        self.es = es
        self.names = ["pe", "act", "dve", "pool", "sp"]
        self.sem = {n: es.enter_context(nc.semaphore("s_" + n)) for n in self.names}
        self.cnt = {n: 0 for n in self.names}
        self.waited = {n: {} for n in self.names}
        self.q = {n: [] for n in self.names}
        self.nsem = 0
        self.dbufs = []
        self.gidx = 0
        self.maxops = MAXOPS

    def _waits(self, eng, reads, writes):
        need = {}

        def add(tok):
            if tok is None:
                return
            s, v = tok
            k = id(s)
            if k not in need or need[k][1] < v:
                need[k] = (s, v)

        for b in reads:
            add(b.w)
        for b in writes:
            add(b.w)
            for t in b.r.values():
                add(t)
        out = []
        own = id(self.sem[eng])
        for k, (s, v) in need.items():
            if eng == "pe" and k == own:
                continue
            if self.waited[eng].get(k, 0) >= v:
                continue
            self.waited[eng][k] = v
            out.append((s, v))
        return out

    def _commit(self, tok, reads, writes):
        for b in reads:
            b.r[id(tok[0])] = tok
        for b in writes:
            b.w = tok
            b.r = {}

    def op(self, eng, fn, reads=(), writes=(), sig=True):
        self.gidx += 1
        if TRACE_LO <= self.gidx <= TRACE_HI:
            import inspect
            print("OP", self.gidx, eng, inspect.currentframe().f_back.f_lineno)
        if self.gidx > self.maxops:
            return
        waits = self._waits(eng, reads, writes)
        if sig:
            self.cnt[eng] += 1
            tok = (self.sem[eng], self.cnt[eng])
            self.q[eng].append((waits, fn, self.sem[eng], 1))
        else:
            tok = (self.sem[eng], self.cnt[eng] + 1)
            self.q[eng].append((waits, fn, None, 0))
        self._commit(tok, reads, writes)

    def dma(self, out_ap, in_ap, reads, writes, dst):
        eng = "sp"
        self.gidx += 1
        if TRACE_LO <= self.gidx <= TRACE_HI:
            import inspect
            print("DMA", self.gidx, inspect.currentframe().f_back.f_lineno)
        if self.gidx > self.maxops:
            return (None, 0)
        waits = self._waits(eng, reads, writes)
        if dst.dsem is None:
            dst.dsem = self.es.enter_context(self.nc.semaphore("d%d" % self.nsem))
            self.nsem += 1
            self.dbufs.append(dst)
        dst.dcnt += 16
        tok = (dst.dsem, dst.dcnt)
        self.q[eng].append((waits, lambda e: e.dma_start(out=out_ap, in_=in_ap), dst.dsem, 16))
        self._commit(tok, reads, writes)
        return tok

    def barrier(self):
        toks = [(self.sem[n], self.cnt[n]) for n in self.names if self.cnt[n] > 0]
        toks += [(b.dsem, b.dcnt) for b in self.dbufs]
        for eng in self.names:
            waits = []
            for (s_, v) in toks:
                k = id(s_)
                if eng == "pe" and k == id(self.sem["pe"]):
                    continue
                if self.waited[eng].get(k, 0) >= v:
                    continue
                self.waited[eng][k] = v
                waits.append((s_, v))
            if waits:
                self.q[eng].append((waits, None, None, 0))

    def final_wait(self, eng, toks):
        toks = [t for t in toks if t[0] is not None]
        if not toks:
            toks = [(self.sem[n], self.cnt[n]) for n in self.names if self.cnt[n] > 0 and n != eng]
            toks += [(b.dsem, b.dcnt) for b in self.dbufs]
        self.q[eng].append((toks, None, None, 0))

    def emit(self):
        nc = self.nc
        with nc.Block() as block:
            def run(name):
                def body(e):
                    for waits, fn, sem, inc in self.q[name]:
                        for s, v in waits:
                            e.wait_ge(s, v)
                        if fn is not None:
                            ins_ = fn(e)
                            if inc:
                                ins_.then_inc(sem, inc)
                return body
            block.tensor(run("pe"))
            block.scalar(run("act"))
            block.vector(run("dve"))
            block.gpsimd(run("pool"))
            block.sync(run("sp"))


def build_nc():
    nc = bass.Bass("TRN2", target_bir_lowering=False)

    def din(name, shape, dt=F32):
        return nc.dram_tensor(name, list(shape), dt, kind="ExternalInput").ap()

    XV = din("XV", [NV * 128, D])
    XH = din("XH", [NV * 4, D])
    HM = din("HM", [4, NV])
    GM = din("GM", [128, NV])
    ROPE = din("ROPE", [NKV * 128, 64])
    CT = din("CT", [128, 16])
    WADA = din("WADA", [D, 3 * D])
    BADA = din("BADA", [128, 3 * D])
    WIN = din("WIN", [D, IN_COLS])
    WAB = din("WAB", [2, D, 16])
    CW = din("CW", [2, 3072, 5])
    ALG = din("ALG", [2, 128, 8])
    DTB = din("DTB", [2, 128, 8])
    VECS = din("VECS", [128, 768])
    CST = din("CST", [128, 10 * 128 + 2 * 132])
    WOA = din("WOA", [D, D])
    WOB = din("WOB", [D, D])
    WOUT = din("WOUT", [D, D])
    OUT = nc.dram_tensor("OUT", [2048, D], F32, kind="ExternalOutput").ap()

    KTS = nc.dram_tensor("KTS", [H, 128, NKV * 128], BF16).ap()
    VPS = nc.dram_tensor("VPS", [H, 128, NKV, 129], BF16).ap()
    OS = nc.dram_tensor("OS", [2, 16, 128, D], F32).ap()
    ZS = nc.dram_tensor("ZS", [16, 128, 2048], BF16).ap()
    GS = nc.dram_tensor("GS", [16, 128, 2048], BF16).ap()

    es = ExitStack()
    with es:
        P = Prog(nc, es)

        def sb(name, shape, dt=F32, stack=es):
            return stack.enter_context(nc.sbuf_tensor(name, list(shape), dt))

        psum = [es.enter_context(nc.psum_tensor("ps%d" % i, [128, 512], F32)) for i in range(8)]
        PB = [Buf() for _ in range(8)]

        cst = sb("cst", [128, 10 * 128 + 2 * 132]); b_cst = Buf()
        ident = cst[:, 0:128]
        Jm = cst[:, 128:256]
        ones = cst[:, 256:384]
        Ltri = cst[:, 384:512]
        negMs = cst[:, 512:640]
        negMsT = cst[:, 640:768]
        MiT = cst[:, 768:896]
        BD32 = cst[:, 1160:1288]
        M64 = cst[:, 1288:1416]
        M128 = cst[:, 1416:1544]
        cbf = sb("cbf", [128, 128 + 2 * 132], BF16); b_cbf = Buf()
        identb = cbf[:, 0:128]
        SelM = cbf[:, 128:260]
        SelH = cbf[0:4, 260:392]
        vecs = sb("vecs", [128, 768]); b_vecs = Buf()
        gnw = vecs[:, 0:128]
        qnw = vecs[:, 128:192]
        knw = vecs[:, 192:256]
        dnw = vecs[:, 512:640]
        hm = sb("hm", [4, NV]); b_hm = Buf()
        gm = sb("gm", [128, NV]); b_gm = Buf()
        mod = sb("mod", [128, 5, D]); b_mod = Buf()
        negA = sb("negA", [128, 2, 8]); b_negA = Buf()
        dtb = sb("dtb", [128, 2, 8]); b_dtb = Buf()
        lamt = sb("lamt", [128, 4]); b_lam = Buf()
        epsc = sb("epsc", [128, 1]); b_eps = Buf()

        P.dma(cst[:], CST[:, :], [], [b_cst], b_cst)
        P.dma(vecs[:], VECS[:, :], [], [b_vecs], b_vecs)
        P.dma(hm[:], HM[:, :], [], [b_hm], b_hm)
        P.dma(gm[:], GM[:, :], [], [b_gm], b_gm)
        P.dma(negA[:], ALG.rearrange("s p h -> p s h"), [], [b_negA], b_negA)
        P.dma(dtb[:], DTB.rearrange("s p h -> p s h"), [], [b_dtb], b_dtb)
        P.op("act", lambda e: e.activation(out=negA[:], in_=negA[:], func=AF.Exp), [b_negA], [b_negA])
        P.op("dve", lambda e: e.tensor_scalar_mul(out=negA[:], in0=negA[:], scalar1=-1.0), [b_negA], [b_negA])
        P.op("dve", lambda e: e.tensor_copy(out=cbf[:, 0:128], in_=cst[:, 0:128]), [b_cst], [b_cbf])
        P.op("dve", lambda e: e.tensor_copy(out=cbf[:, 128:392], in_=cst[:, 896:1160]), [b_cst], [b_cbf])
        P.op("pool", lambda e: e.memset(epsc[:], EPS), [], [b_eps])

        ltmp = sb("ltmp", [128, 128]); b_ltmp = Buf()
        P.op("dve", lambda e: e.tensor_tensor(out=ltmp[:, 0:64], in0=vecs[:, 256:320], in1=vecs[:, 320:384], op=ALU.mult), [b_vecs], [b_ltmp])
        P.op("dve", lambda e: e.tensor_tensor(out=ltmp[:, 64:128], in0=vecs[:, 384:448], in1=vecs[:, 448:512], op=ALU.mult), [b_vecs], [b_ltmp])
        P.op("dve", lambda e: e.tensor_reduce(out=lamt[:, 2:4], in_=ltmp[:].rearrange("p (a b) -> p a b", a=2), axis=AX.X, op=ALU.add), [b_ltmp], [b_lam])
        P.op("act", lambda e: e.activation(out=lamt[:, 2:4], in_=lamt[:, 2:4], func=AF.Exp), [b_lam], [b_lam])
        P.op("dve", lambda e: e.tensor_tensor(out=lamt[:, 0:1], in0=lamt[:, 3:4], in1=lamt[:, 2:3], op=ALU.subtract), [b_lam], [b_lam])
        P.op("dve", lambda e: e.tensor_scalar_add(out=lamt[:, 0:1], in0=lamt[:, 0:1], scalar1=-LAMBDA_INIT), [b_lam], [b_lam])
        P.op("dve", lambda e: e.tensor_scalar_mul(out=ltmp[:], in0=vecs[:, 128:256], scalar1=-1.0), [b_vecs, b_lam], [b_ltmp])
        P.op("dve", lambda e: e.tensor_tensor(out=ltmp[:], in0=ltmp[:], in1=vecs[:, 128:256], op=ALU.max), [b_vecs, b_ltmp], [b_ltmp])
        P.op("dve", lambda e: e.tensor_reduce(out=lamt[:, 2:4], in_=ltmp[:].rearrange("p (a b) -> p a b", a=2), axis=AX.X, op=ALU.max), [b_ltmp, b_lam], [b_lam])
        P.op("dve", lambda e: e.scalar_tensor_tensor(out=lamt[:, 1:2], in0=lamt[:, 2:3], scalar=-8.0, in1=lamt[:, 3:4], op0=ALU.mult, op1=ALU.mult), [b_lam], [b_lam])

        with ExitStack() as s0:
            ct = sb("ct", [128, 16], stack=s0); b_ct = Buf()
            cbl = sb("cbl", [128, 16, 128], stack=s0); b_cbl = Buf()
            wst = [sb("wada%d" % i, [128, 3 * D], stack=s0) for i in range(2)]
            b_wst = [Buf(), Buf()]
            bada = sb("bada", [128, 3 * D], stack=s0); b_bada = Buf()
            P.dma(ct[:], CT[:, :], [], [b_ct], b_ct)
            P.dma(bada[:], BADA[:, :], [], [b_bada], b_bada)
            P.op("act", lambda e: e.activation(out=ct[:], in_=ct[:], func=AF.Silu), [b_ct], [b_ct])
            for j in range(16):
                P.op("dve", lambda e, j=j: e.tensor_scalar_mul(out=cbl[:, j, :], in0=ones, scalar1=ct[:, j:j + 1]), [b_ct, b_cst], [b_cbl])
            for k in range(8):
                P.dma(wst[k % 2][:], WADA[k * 128:(k + 1) * 128, :], [], [b_wst[k % 2]], b_wst[k % 2])
                for g in range(6):
                    P.op("pe", lambda e, k=k, g=g: e.matmul(psum[g][:, :], lhsT=cbl[:, k, :], rhs=wst[k % 2][:, g * 512:(g + 1) * 512], start=(k == 0), stop=(k == 7)),
                         [b_cbl, b_wst[k % 2]], [PB[g]])
            for g in range(6):
                dsti = [1, 1, 0, 0, 2, 2][g]
                addc = 1.0 if dsti == 0 else 0.0
                P.op("dve", lambda e, g=g, dsti=dsti: e.tensor_tensor(out=mod[:, dsti, (g % 2) * 512:(g % 2) * 512 + 512], in0=psum[g][:, :], in1=bada[:, g * 512:(g + 1) * 512], op=ALU.add),
                     [PB[g], b_bada], [b_mod])
            P.op("pool", lambda e: e.tensor_scalar_add(out=mod[:, 0, :], in0=mod[:, 0, :], scalar1=1.0), [b_mod], [b_mod])
            for k in range(8):
                P.dma(wst[k % 2][:], WADA[k * 128:(k + 1) * 128, :], [], [b_wst[k % 2]], b_wst[k % 2])
                for g in range(4):
                    P.op("pe", lambda e, k=k, g=g: e.matmul(psum[g][:, :], lhsT=cbl[:, 8 + k, :], rhs=wst[k % 2][:, g * 512:(g + 1) * 512], start=(k == 0), stop=(k == 7)),
                         [b_cbl, b_wst[k % 2]], [PB[g]])
            for g in range(4):
                dsti = [4, 4, 3, 3][g]
                P.op("dve", lambda e, g=g, dsti=dsti: e.tensor_tensor(out=mod[:, dsti, (g % 2) * 512:(g % 2) * 512 + 512], in0=psum[g][:, :], in1=bada[:, g * 512:(g + 1) * 512], op=ALU.add),
                     [PB[g], b_bada], [b_mod])
            P.op("pool", lambda e: e.tensor_scalar_add(out=mod[:, 3, :], in0=mod[:, 3, :], scalar1=1.0), [b_mod], [b_mod])

        P.barrier()
        xt = sb("xt", [128, D]); b_xt = Buf()
        xh = sb("xh", [4, D]); b_xh = Buf()
        junk = sb("junk", [128, D]); b_junk = Buf()
        st = sb("st", [128, 8]); b_st = Buf()
        hb = sb("hb", [128, D], BF16); b_hb = Buf()
        hhb = sb("hhb", [4, D], BF16); b_hhb = Buf()
        HT = sb("HT", [128, 8, 132], BF16); b_HT = Buf()

        def make_HT(v, is_ctx):
            mi = 3 if is_ctx else 0
            P.dma(xt[:], XV[v * 128:(v + 1) * 128, :], [], [b_xt], b_xt)
            P.dma(xh[:], XH[v * 4:(v + 1) * 4, :], [], [b_xh], b_xh)
            for (src, bsrc, np_, col, dst, bdst) in ((xt, b_xt, 128, 0, hb, b_hb), (xh, b_xh, 4, 2, hhb, b_hhb)):
                P.op("act", lambda e, src=src, np_=np_, col=col: e.activation(out=junk[0:np_, :], in_=src[0:np_, :], func=AF.Square, accum_out=st[0:np_, col:col + 1]),
                     [bsrc], [b_junk, b_st])
                P.op("act", lambda e, np_=np_, col=col: e.activation(out=st[0:np_, col + 1:col + 2], in_=st[0:np_, col:col + 1], func=AF.Ln, scale=1.0 / D, bias=epsc[0:np_, 0:1]),
                          [b_st, b_eps], [b_st])
                P.op("act", lambda e, np_=np_, col=col: e.activation(out=st[0:np_, col + 1:col + 2], in_=st[0:np_, col + 1:col + 2], func=AF.Exp, scale=-0.5),
                     [b_st], [b_st])
                if np_ == 4:
                    P.op("dve", lambda e, v=v: e.tensor_tensor(out=st[0:4, 3:4], in0=st[0:4, 3:4], in1=hm[:, v:v + 1], op=ALU.mult), [b_st, b_hm], [b_st])
                P.op("dve", lambda e, src=src, np_=np_, col=col: e.scalar_tensor_tensor(out=junk[0:np_, :], in0=src[0:np_, :], scalar=st[0:np_, col + 1:col + 2], in1=mod[0:np_, mi, :], op0=ALU.mult, op1=ALU.mult),
                     [bsrc, b_st, b_mod, b_junk], [b_junk])
                if np_ == 4:
                    P.op("dve", lambda e, v=v, dst=dst: e.scalar_tensor_tensor(out=dst[0:4, :], in0=mod[0:4, mi + 1, :], scalar=hm[:, v:v + 1], in1=junk[0:4, :], op0=ALU.mult, op1=ALU.add),
                         [b_junk, b_mod, b_hm], [bdst])
                else:
                    P.op("pool", lambda e, dst=dst: e.tensor_tensor(out=dst[:, :], in0=junk[:, :], in1=mod[:, mi + 1, :], op=ALU.add), [b_junk, b_mod], [bdst])
            for k in range(8):
                bank, slot = divmod(k, 3)
                o = psum[bank][:, slot * 132:(slot + 1) * 132]
                P.op("pe", lambda e, k=k, o=o: e.matmul(o, lhsT=hb[:, k * 128:(k + 1) * 128], rhs=SelM, start=True, stop=False), [b_hb, b_cbf], [PB[bank]], sig=False)
                P.op("pe", lambda e, k=k, o=o: e.matmul(o, lhsT=hhb[0:4, k * 128:(k + 1) * 128], rhs=SelH, start=False, stop=True), [b_hhb, b_cbf], [PB[bank]])
            for bank in range(3):
                n = 3 if bank < 2 else 2
                P.op("act", lambda e, bank=bank, n=n: e.activation(out=HT[:, bank * 3:bank * 3 + n, :], in_=psum[bank][:, 0:n * 132].rearrange("p (a b) -> p a b", a=n), func=AF.Copy),
                     [PB[bank]], [b_HT])

        def load_w_bf16(dst, bdst, col0, ncols, stage, bstage, dcol0=0):
            i = 0
            for k in range(8):
                for c0 in range(0, ncols, 1024):
                    n = min(1024, ncols - c0)
                    s_, bs_ = stage[i % 2], bstage[i % 2]
                    P.dma(s_[:, 0:n], WIN[k * 128:(k + 1) * 128, col0 + c0:col0 + c0 + n], [], [bs_], bs_)
                    eng = ["act", "pool", "dve"][i % 3]
                    if eng == "act":
                        P.op("act", lambda e, s_=s_, k=k, c0=c0, n=n: e.activation(out=dst[:, k, dcol0 + c0:dcol0 + c0 + n], in_=s_[:, 0:n], func=AF.Copy), [bs_], [bdst])
                    else:
                        P.op(eng, lambda e, s_=s_, k=k, c0=c0, n=n: e.tensor_copy(out=dst[:, k, dcol0 + c0:dcol0 + c0 + n], in_=s_[:, 0:n]), [bs_], [bdst])
                    i += 1

        def qk_norm_rope(ps_banks, pbufs, gain, tab, b_tab, dst, b_dst, tmp, b_tmp):
            for hf in range(2):
                P.op("act", lambda e, hf=hf: e.activation(out=tmp[0][:, hf * 512:(hf + 1) * 512], in_=ps_banks[hf][:, :], func=AF.Square), [pbufs[hf]], [b_tmp[0]])
            P.op("dve", lambda e: e.tensor_reduce(out=tmp[3][:, 0:16], in_=tmp[0][:].rearrange("p (g d) -> p g d", d=64), axis=AX.X, op=ALU.add),
                 [b_tmp[0]], [b_tmp[3]])
            P.op("act", lambda e: e.activation(out=tmp[3][:, 0:16], in_=tmp[3][:, 0:16], func=AF.Ln, scale=1.0 / 64, bias=epsc[:, 0:1]), [b_tmp[3], b_eps], [b_tmp[3]])
            P.op("act", lambda e: e.activation(out=tmp[3][:, 0:16], in_=tmp[3][:, 0:16], func=AF.Exp, scale=-0.5), [b_tmp[3]], [b_tmp[3]])
            for hf in range(2):
                P.op("dve", lambda e, hf=hf: e.tensor_tensor(out=tmp[0][:, hf * 512:(hf + 1) * 512].rearrange("p (g d) -> p g d", d=64),
                                                            in0=ps_banks[hf][:, :].rearrange("p (g d) -> p g d", d=64),
                                                            in1=tmp[3][:, hf * 8:(hf + 1) * 8].unsqueeze(2).to_broadcast([128, 8, 64]), op=ALU.mult),
                     [pbufs[hf], b_tmp[3], b_tmp[0]], [b_tmp[0]])
            P.op("pool", lambda e: e.tensor_tensor(out=tmp[1][:].rearrange("p (g d) -> p g d", d=64), in0=tmp[0][:].rearrange("p (g d) -> p g d", d=64),
                                                   in1=gain.unsqueeze(1).to_broadcast([128, 16, 64]), op=ALU.mult), [b_tmp[0], b_vecs], [b_tmp[1]])
            xv = tmp[1][:].rearrange("p (g h t f) -> p g h t f", g=16, h=2, t=2)
            tv = tab[:].rearrange("p (h t f) -> p h t f", h=2, t=2)
            dv = dst[:].rearrange("p (g h t f) -> p g h t f", g=16, h=2, t=2)
            av = tmp[2][:].rearrange("p (g h t f) -> p g h t f", g=16, h=2, t=2)
            for hh in range(2):
                x1 = xv[:, :, hh, 0, :]
                x2 = xv[:, :, hh, 1, :]
                cs = tv[:, hh, 0, :].unsqueeze(1).to_broadcast([128, 16, 16])
                sn = tv[:, hh, 1, :].unsqueeze(1).to_broadcast([128, 16, 16])
                a1 = av[:, :, hh, 0, :]
                a2 = av[:, :, hh, 1, :]
                e1, e2 = ("dve", "pool")
                P.op(e1, lambda e, x1=x1, cs=cs, a1=a1: e.tensor_tensor(out=a1, in0=x1, in1=cs, op=ALU.mult), [b_tmp[1], b_tab], [b_tmp[2]])
                P.op(e2, lambda e, x2=x2, sn=sn, a2=a2: e.tensor_tensor(out=a2, in0=x2, in1=sn, op=ALU.mult), [b_tmp[1], b_tab], [b_tmp[2]])
                P.op(e1, lambda e, a1=a1, a2=a2, hh=hh: e.tensor_tensor(out=dv[:, :, hh, 0, :], in0=a1, in1=a2, op=ALU.subtract), [b_tmp[2]], [b_dst])
                P.op(e1, lambda e, x2=x2, cs=cs, a1=a1: e.tensor_tensor(out=a1, in0=x2, in1=cs, op=ALU.mult), [b_tmp[1], b_tab, b_tmp[2]], [b_tmp[2]])
                P.op(e2, lambda e, x1=x1, sn=sn, a2=a2: e.tensor_tensor(out=a2, in0=x1, in1=sn, op=ALU.mult), [b_tmp[1], b_tab, b_tmp[2]], [b_tmp[2]])
                P.op(e1, lambda e, a1=a1, a2=a2, hh=hh: e.tensor_tensor(out=dv[:, :, hh, 1, :], in0=a1, in1=a2, op=ALU.add), [b_tmp[2]], [b_dst])


        with ExitStack() as s1:
            W1 = sb("W1", [128, 8, 5120], BF16, stack=s1); b_W1 = Buf()
            wab = sb("wab", [128, 2, 8, 16], BF16, stack=s1); b_wab = Buf()
            wabs = sb("wabs", [128, 2, 8, 16], stack=s1); b_wabs = Buf()
            cw = sb("cw", [128, 2, 24, 5], stack=s1); b_cw = Buf()
            stage = [junk, xt]
            bstage = [b_junk, b_xt]
            tab = sb("tab1", [128, 64], stack=s1); b_tab = Buf()
            tmpA = [sb("tmpA1_%d" % i, [128, D], stack=s1) for i in range(3)] + [sb("tmpA31", [128, 16], stack=s1)]
            b_tmpA = [Buf() for _ in range(4)]
            krot = sb("krot1", [128, D], BF16, stack=s1); b_krot = Buf()
            load_w_bf16(W1, b_W1, 0, 3072, stage, bstage, 0)
            load_w_bf16(W1, b_W1, 5152, 2048, stage, bstage, 3072)
            P.dma(wabs[:], WAB.rearrange("s (k p) c -> p s k c", p=128), [], [b_wabs], b_wabs)
            P.op("dve", lambda e: e.tensor_copy(out=wab[:], in_=wabs[:]), [b_wabs], [b_wab])
            P.dma(cw[:], CW.rearrange("s (c p) j -> p s c j", p=128), [], [b_cw], b_cw)

            Sst = sb("Sst", [128, H, 128], stack=s1); b_S = [Buf() for _ in range(H)]
            pre = sb("pre", [128, 3, 132], stack=s1); b_pre = Buf()
            acc = [sb("acc%d" % i, [128, 128], stack=s1) for i in range(2)]; b_acc = [Buf(), Buf()]
            ptmp = sb("ptmp", [128, 128], stack=s1); b_ptmp = Buf()
            qkvT = sb("qkvT", [128, 24, 128], stack=s1); b_qkvT = [Buf() for _ in range(24)]
            gb = sb("gb", [128, 6, 8], stack=s1); b_gb = Buf()
            from types import SimpleNamespace
            TS = []
            for si_ in range(2):
                T = SimpleNamespace()
                T.Wb = [sb("Wb%d_%d" % (i, si_), [128, 512], stack=s1) for i in range(2)]; T.b_Wb = [Buf(), Buf()]
                T.NY = sb("NY_%d" % si_, [128, 256], stack=s1); T.b_NY = Buf()
                T.NYm = sb("NYm_%d" % si_, [128, 256], stack=s1); T.b_NYm = Buf()
                T.TU = sb("TU_%d" % si_, [128, 256], stack=s1); T.b_TU = Buf()
                T.TU2 = sb("TU2_%d" % si_, [128, 256], stack=s1); T.b_TU2 = Buf()
                T.Wm = sb("Wm_%d" % si_, [128, 256], stack=s1); T.b_Wm = Buf()
                T.LgIb = sb("LgIb_%d" % si_, [128, 256], stack=s1); T.b_LgIb = Buf()
                T.Em = sb("Em_%d" % si_, [128, 128], stack=s1); T.b_Em = Buf()
                T.ETm = sb("ETm_%d" % si_, [128, 128], stack=s1); T.b_ETm = Buf()
                T.EMs = sb("EMs_%d" % si_, [128, 128], stack=s1); T.b_EMs = Buf()
                T.ETs = sb("ETs_%d" % si_, [128, 128], stack=s1); T.b_ETs = Buf()
                T.ETi = sb("ETi_%d" % si_, [128, 128], stack=s1); T.b_ETi = Buf()
                T.npos = T.LgIb; T.b_npos = T.b_LgIb
                T.sq = sb("sq_%d" % si_, [128, 128], stack=s1); T.b_sq = Buf()
                T.rs = sb("rs_%d" % si_, [128, 128], stack=s1); T.b_rs = Buf()
                T.kTn = sb("kTn_%d" % si_, [128, 128], stack=s1); T.b_kTn = Buf()
                T.qTn = sb("qTn_%d" % si_, [128, 128], stack=s1); T.b_qTn = Buf()
                T.Kbp = sb("Kbp_%d" % si_, [128, 128], stack=s1); T.b_Kbp = Buf()
                T.kd = sb("kd_%d" % si_, [128, 128], stack=s1); T.b_kd = Buf()
                T.Vb = sb("Vb_%d" % si_, [128, 128], stack=s1); T.b_Vb = Buf()
                T.TT = sb("TT_%d" % si_, [128, 128], stack=s1); T.b_TT = Buf()
                T.wT = sb("wT_%d" % si_, [128, 128], stack=s1); T.b_wT = Buf()
                T.usb = sb("usb_%d" % si_, [128, 128], stack=s1); T.b_usb = Buf()
                T.vnew = sb("vnew_%d" % si_, [128, 128], stack=s1); T.b_vnew = Buf()
                T.egl = sb("egl_%d" % si_, [128, 1], stack=s1); T.b_egl = Buf()
                T.eR = sb("eR_%d" % si_, [128, 128], stack=s1); T.b_eR = Buf()
                T.QdT = sb("QdT_%d" % si_, [128, 128], stack=s1); T.b_QdT = Buf()
                T.qkT = sb("qkT_%d" % si_, [128, 128], stack=s1); T.b_qkT = Buf()
                T.banks = (0, 1, 2, 2) if si_ == 0 else (3, 4, 5, 5)
                TS.append(T)
            osb = junk; b_osb = b_junk
            KTsb = hb[:].rearrange("p (h t) -> p h t", h=H); b_KTsb = b_hb
            Vp = sb("Vp", [128, H, 129], BF16, stack=s1); b_Vp = Buf()
            b_KTS = Buf(); b_VPS = Buf(); b_OS = Buf()
            P.op("pool", lambda e: e.memset(Vp[:], 1.0), [], [b_Vp])

            for v in range(NV):
                scan = 0 if v < S2 else 1
                is_ctx = v in (0, 1, S2, S2 + 1)
                own = (OWN0 <= v < S2) or (v >= OWN1)
                kv = v < NKV
                if v in (0, S2):
                    for h in range(H):
                        P.op("pool", lambda e, h=h: e.memset(Sst[:, h, :], 0.0), [], [b_S[h]])
                make_HT(v, is_ctx)
                PA = 3
                for k in range(8):
                    P.op("pe", lambda e, k=k, scan=scan: e.matmul(psum[PA][:, 0:16], lhsT=HT[:, k, 2:130], rhs=wab[:, scan, k, :], start=(k == 0), stop=(k == 7)), [b_HT, b_wab], [PB[PA]], sig=(k == 7))
                P.op("dve", lambda e, scan=scan: e.tensor_tensor(out=gb[:, 4, :], in0=psum[PA][:, 0:8], in1=dtb[:, scan, :], op=ALU.add), [PB[PA], b_dtb], [b_gb])
                P.op("act", lambda e: e.activation(out=gb[:, 4, :], in_=gb[:, 4, :], func=AF.Exp), [b_gb], [b_gb])
                P.op("act", lambda e: e.activation(out=gb[:, 4, :], in_=gb[:, 4, :], func=AF.Ln, bias=1.0), [b_gb], [b_gb])
                P.op("dve", lambda e, v=v, scan=scan: e.scalar_tensor_tensor(out=gb[:, 0, :], in0=gb[:, 4, :], scalar=gm[:, v:v + 1], in1=negA[:, scan, :], op0=ALU.mult, op1=ALU.mult), [b_gb, b_gm, b_negA], [b_gb])
                P.op("act", lambda e: e.activation(out=gb[:, 5, :], in_=psum[PA][:, 8:16], func=AF.Sigmoid), [PB[PA], b_gb], [b_gb])
                P.op("dve", lambda e, v=v: e.tensor_scalar_mul(out=gb[:, 1, :], in0=gb[:, 5, :], scalar1=gm[:, v:v + 1]), [b_gb, b_gm], [b_gb])
                P.op("pe", lambda e: e.matmul(psum[PA][:, 16:24], lhsT=Ltri, rhs=gb[:, 0, :], start=True, stop=True), [b_cst, b_gb], [PB[PA]])
                P.op("dve", lambda e: e.tensor_copy(out=gb[:, 2, :], in_=psum[PA][:, 16:24]), [PB[PA], b_gb], [b_gb])
                P.op("act", lambda e: e.activation(out=gb[:, 3, :], in_=psum[PA][:, 16:24], func=AF.Exp), [PB[PA], b_gb], [b_gb])
                P.op("dve", lambda e: e.tensor_tensor(out=gb[:, 3, :], in0=gb[:, 3, :], in1=gb[:, 1, :], op=ALU.mult), [b_gb], [b_gb])

                def pre_steps(cbs):
                    for gi in range(0, len(cbs), 3):
                        grp = cbs[gi:gi + 3]
                        bank = 6 + (gi // 3) % 2
                        for si, cb in enumerate(grp):
                            for k in range(8):
                                P.op("pe", lambda e, k=k, cb=cb, si=si, bank=bank: e.matmul(psum[bank][:, si * 132:(si + 1) * 132], lhsT=W1[:, k, cb * 128:(cb + 1) * 128], rhs=HT[:, k, :], start=(k == 0), stop=(k == 7)),
                                     [b_W1, b_HT], [PB[bank]], sig=(k == 7))
                                yield
                        n = len(grp)
                        P.op("act", lambda e, bank=bank, n=n: e.activation(out=pre[:, 0:n, :], in_=psum[bank][:, 0:n * 132].rearrange("p (a b) -> p a b", a=n), func=AF.Copy), [PB[bank]], [b_pre])
                        yield
                        for si, cb in enumerate(grp):
                            eng = "dve" if (cb % 2 == 0) else "pool"
                            a_, ba_ = acc[cb % 2], b_acc[cb % 2]
                            P.op(eng, lambda e, si=si, cb=cb, a_=a_, scan=scan: e.tensor_scalar_mul(out=a_[:], in0=pre[:, si, 0:128], scalar1=cw[:, scan, cb, 0:1]), [b_pre, b_cw], [ba_])
                            yield
                            for j in range(1, 5):
                                if eng == "dve":
                                    P.op(eng, lambda e, si=si, cb=cb, a_=a_, j=j, scan=scan: e.scalar_tensor_tensor(out=a_[:], in0=pre[:, si, j:j + 128], scalar=cw[:, scan, cb, j:j + 1], in1=a_[:], op0=ALU.mult, op1=ALU.add),
                                         [b_pre, b_cw, ba_], [ba_])
                                    yield
                                else:
                                    P.op(eng, lambda e, si=si, cb=cb, j=j, scan=scan: e.tensor_scalar_mul(out=ptmp[:], in0=pre[:, si, j:j + 128], scalar1=cw[:, scan, cb, j:j + 1]), [b_pre, b_cw], [b_ptmp])
                                    yield
                                    P.op(eng, lambda e, a_=a_: e.tensor_tensor(out=a_[:], in0=a_[:], in1=ptmp[:], op=ALU.add), [b_ptmp, ba_], [ba_])
                                    yield
                            P.op("act", lambda e, cb=cb, a_=a_: e.activation(out=qkvT[:, cb, :], in_=a_[:], func=AF.Silu), [ba_], [b_qkvT[cb]])
                            yield

                    return
                    yield
                def kv_steps():
                    P.dma(tab[:], ROPE[v * 128:(v + 1) * 128, :], [], [b_tab], b_tab)
                    yield
                    for hf in range(2):
                        for k in range(8):
                            P.op("pe", lambda e, k=k, hf=hf: e.matmul(psum[6 + hf][:, :], lhsT=HT[:, k, 2:130], rhs=W1[:, k, 3072 + hf * 512:3072 + (hf + 1) * 512], start=(k == 0), stop=(k == 7)),
                                 [b_HT, b_W1], [PB[6 + hf]], sig=(k == 7))
                            yield
                    qk_norm_rope([psum[6], psum[7]], [PB[6], PB[7]], knw, tab, b_tab, krot, b_krot, tmpA, b_tmpA)
                    yield
                    for h in range(H):
                        bank = 6 + h // 4
                        P.op("pe", lambda e, h=h, bank=bank: e.matmul(psum[bank][:, (h % 4) * 128:(h % 4 + 1) * 128], lhsT=krot[:, h * 128:(h + 1) * 128], rhs=identb, start=True, stop=True),
                             [b_krot, b_cbf], [PB[bank]])
                        yield
                    for hf in range(2):
                        P.op("act", lambda e, hf=hf: e.activation(out=KTsb[:, hf * 4:(hf + 1) * 4, :], in_=psum[6 + hf][:, :].rearrange("p (a b) -> p a b", a=4), func=AF.Copy), [PB[6 + hf]], [b_KTsb])
                        yield
                    P.dma(KTS[:, :, v * 128:(v + 1) * 128].rearrange("h p t -> p h t"), KTsb[:], [b_KTsb], [b_KTS], b_KTS)
                    yield
                    for hf in range(2):
                        for k in range(8):
                            P.op("pe", lambda e, k=k, hf=hf: e.matmul(psum[6 + hf][:, :], lhsT=HT[:, k, 2:130], rhs=W1[:, k, 4096 + hf * 512:4096 + (hf + 1) * 512], start=(k == 0), stop=(k == 7)),
                                 [b_HT, b_W1], [PB[6 + hf]], sig=(k == 7))
                            yield
                    for hf in range(2):
                        P.op("act", lambda e, hf=hf: e.activation(out=Vp[:, hf * 4:(hf + 1) * 4, 0:128], in_=psum[6 + hf][:, :].rearrange("p (a b) -> p a b", a=4), func=AF.Copy), [PB[6 + hf]], [b_Vp])
                        yield
                    P.dma(VPS[:, :, v, :].rearrange("h p c -> p h c"), Vp[:], [b_Vp], [b_VPS], b_VPS)
                    yield

                    return
                    yield
                def head_steps(h, T):
                        qcb, kcb, vcb = h, 8 + h, 16 + h
                        PR, PG, PS_, PU = T.banks
                        P.op("pool", lambda e, kcb=kcb: e.tensor_tensor(out=T.sq[:], in0=qkvT[:, kcb, :], in1=qkvT[:, kcb, :], op=ALU.mult), [b_qkvT[kcb]], [T.b_sq])
                        yield
                        P.op("pe", lambda e: e.matmul(psum[PR][:, 256:384], lhsT=ones, rhs=T.sq[:], start=True, stop=True), [b_cst, T.b_sq], [PB[PR]])
                        yield
                        P.op("act", lambda e: e.activation(out=T.rs[:], in_=psum[PR][:, 256:384], func=AF.Ln, bias=epsc[:, 0:1]), [PB[PR], b_eps], [T.b_rs])
                        yield
                        P.op("act", lambda e: e.activation(out=T.rs[:], in_=T.rs[:], func=AF.Exp, scale=-0.5), [T.b_rs], [T.b_rs])
                        yield
                        P.op("dve", lambda e, kcb=kcb: e.tensor_tensor(out=T.kTn[:], in0=qkvT[:, kcb, :], in1=T.rs[:], op=ALU.mult), [b_qkvT[kcb], T.b_rs], [T.b_kTn])
                        yield
                        if own:
                            P.op("pool", lambda e, qcb=qcb: e.tensor_tensor(out=T.sq[:], in0=qkvT[:, qcb, :], in1=qkvT[:, qcb, :], op=ALU.mult), [b_qkvT[qcb]], [T.b_sq])
                            yield
                            P.op("pe", lambda e: e.matmul(psum[PR][:, 384:512], lhsT=ones, rhs=T.sq[:], start=True, stop=True), [b_cst, T.b_sq], [PB[PR]])
                            yield
                            P.op("act", lambda e: e.activation(out=T.rs[:], in_=psum[PR][:, 384:512], func=AF.Ln, bias=epsc[:, 0:1]), [PB[PR], b_eps], [T.b_rs])
                            yield
                            P.op("act", lambda e: e.activation(out=T.rs[:], in_=T.rs[:], func=AF.Exp, scale=-0.5), [T.b_rs], [T.b_rs])
                            yield
                            P.op("dve", lambda e, qcb=qcb: e.scalar_tensor_tensor(out=T.qTn[:], in0=qkvT[:, qcb, :], scalar=128.0 ** -0.5, in1=T.rs[:], op0=ALU.mult, op1=ALU.mult), [b_qkvT[qcb], T.b_rs], [T.b_qTn])
                            yield
                        P.op("dve", lambda e, h=h: e.tensor_scalar_mul(out=T.LgIb[:, 0:128], in0=Ltri, scalar1=gb[:, 0, h:h + 1]), [b_cst, b_gb], [T.b_LgIb])
                        yield
                        P.op("pool", lambda e, h=h: e.tensor_scalar_mul(out=T.LgIb[:, 128:256], in0=ident, scalar1=gb[:, 1, h:h + 1]), [b_cst, b_gb], [T.b_LgIb])
                        yield
                        P.op("pe", lambda e: e.matmul(psum[PR][:, 0:256], lhsT=ones, rhs=T.LgIb[:], start=True, stop=True), [b_cst, T.b_LgIb], [PB[PR]])
                        yield
                        P.op("dve", lambda e, h=h: e.tensor_scalar(out=T.npos[:, 0:128], in0=psum[PR][:, 0:128], scalar1=gb[:, 2, h:h + 1], scalar2=0.0, op0=ALU.subtract, op1=ALU.max), [PB[PR], b_gb], [T.b_npos])
                        yield
                        P.op("dve", lambda e, h=h: e.tensor_scalar(out=T.npos[:, 128:256], in0=psum[PR][:, 0:128], scalar1=gb[:, 2, h:h + 1], scalar2=0.0, op0=ALU.subtract, op1=ALU.min), [PB[PR], b_gb], [T.b_npos])
                        yield
                        P.op("act", lambda e: e.activation(out=T.Em[:], in_=T.npos[:, 0:128], func=AF.Exp, scale=-1.0), [T.b_npos], [T.b_Em])
                        yield
                        P.op("act", lambda e: e.activation(out=T.ETm[:], in_=T.npos[:, 128:256], func=AF.Exp), [T.b_npos], [T.b_ETm])
                        yield
                        P.op("pool", lambda e: e.tensor_tensor(out=T.EMs[:], in0=T.Em[:], in1=negMs, op=ALU.mult), [T.b_Em, b_cst], [T.b_EMs])
                        yield
                        P.op("pool", lambda e: e.tensor_tensor(out=T.ETs[:], in0=T.ETm[:], in1=negMsT, op=ALU.mult), [T.b_ETm, b_cst], [T.b_ETs])
                        yield
                        P.op("pe", lambda e: e.matmul(psum[PG][:, 0:128], lhsT=T.kTn[:], rhs=T.kTn[:], start=True, stop=True), [T.b_kTn], [PB[PG]])
                        yield
                        if own:
                            P.op("pe", lambda e: e.matmul(psum[PG][:, 128:256], lhsT=T.kTn[:], rhs=T.qTn[:], start=True, stop=True), [T.b_kTn, T.b_qTn], [PB[PG]])
                            yield
                        P.op("dve", lambda e, h=h: e.scalar_tensor_tensor(out=T.NY[:, 0:128], in0=psum[PG][:, 0:128], scalar=gb[:, 1, h:h + 1], in1=T.EMs[:], op0=ALU.mult, op1=ALU.mult), [PB[PG], b_gb, T.b_EMs], [T.b_NY])
                        yield
                        P.op("dve", lambda e: e.tensor_tensor(out=T.NY[:, 128:256], in0=psum[PG][:, 0:128], in1=T.ETs[:], op=ALU.mult), [PB[PG], T.b_ETs], [T.b_NY])
                        yield
                        P.op("dve", lambda e: e.tensor_tensor(out=T.NY[:, 128:256], in0=T.NY[:, 128:256], in1=psum[PR][:, 128:256], op=ALU.mult), [PB[PR], T.b_NY], [T.b_NY])
                        yield
                        P.op("pe", lambda e: e.matmul(psum[PR][:, 256:384], lhsT=T.kTn[:], rhs=ident, start=True, stop=True), [T.b_kTn, b_cst], [PB[PR]])
                        yield
                        P.op("pe", lambda e, vcb=vcb: e.matmul(psum[PR][:, 384:512], lhsT=qkvT[:, vcb, :], rhs=ident, start=True, stop=True), [b_qkvT[vcb], b_cst], [PB[PR]])
                        yield
                        P.op("dve", lambda e, h=h: e.tensor_scalar_mul(out=T.Kbp[:], in0=psum[PR][:, 256:384], scalar1=gb[:, 3, h:h + 1]), [PB[PR], b_gb], [T.b_Kbp])
                        yield
                        P.op("dve", lambda e: e.tensor_scalar_mul(out=T.kd[:], in0=psum[PR][:, 256:384], scalar1=T.ETm[:, 127:128]), [PB[PR], T.b_ETm], [T.b_kd])
                        yield
                        P.op("dve", lambda e, h=h: e.tensor_scalar_mul(out=T.Vb[:], in0=psum[PR][:, 384:512], scalar1=gb[:, 1, h:h + 1]), [PB[PR], b_gb], [T.b_Vb])
                        yield
                        P.op("pool", lambda e: e.tensor_tensor(out=T.Wb[0][:, 0:128], in0=T.NY[:, 0:128], in1=BD32, op=ALU.mult), [T.b_NY, b_cst], [T.b_Wb[0]])
                        yield
                        P.op("pool", lambda e: e.tensor_tensor(out=T.Wb[0][:, 256:384], in0=T.NY[:, 128:256], in1=BD32, op=ALU.mult), [T.b_NY, b_cst], [T.b_Wb[0]])
                        yield
                        P.op("pool", lambda e: e.tensor_tensor(out=T.Wb[1][:, 128:256], in0=T.Wb[0][:, 0:128], in1=ident, op=ALU.add), [T.b_Wb[0], b_cst], [T.b_Wb[1]])
                        yield
                        P.op("pool", lambda e: e.tensor_tensor(out=T.Wb[1][:, 384:512], in0=T.Wb[0][:, 256:384], in1=ident, op=ALU.add), [T.b_Wb[0], b_cst], [T.b_Wb[1]])
                        yield
                        for n in range(4):
                            cur, nxt = T.Wb[n % 2], T.Wb[(n + 1) % 2]
                            bc, bn = T.b_Wb[n % 2], T.b_Wb[(n + 1) % 2]
                            w_ = 128 if n == 0 else 256
                            P.op("pe", lambda e, cur=cur, w_=w_: e.matmul(psum[PS_][:, 0:w_], lhsT=cur[:, 256:384], rhs=cur[:, 0:w_], start=True, stop=True), [bc], [PB[PS_]])
                            yield
                            P.op("pe", lambda e, cur=cur, w_=w_: e.matmul(psum[PS_][:, 256:256 + w_], lhsT=cur[:, 0:128], rhs=cur[:, 256:256 + w_], start=True, stop=True), [bc], [PB[PS_]])
                            yield
                            P.op("act", lambda e, nxt=nxt: e.activation(out=nxt[:].rearrange("p (a b) -> p a b", a=2)[:, :, 0:128], in_=psum[PS_][:, :].rearrange("p (a b) -> p a b", a=2)[:, :, 0:128], func=AF.Copy), [PB[PS_]], [bn])
                            yield
                            if n > 0:
                                P.op("dve", lambda e, cur=cur, nxt=nxt: e.tensor_tensor(out=nxt[:].rearrange("p (a b) -> p a b", a=2)[:, :, 128:256], in0=psum[PS_][:, :].rearrange("p (a b) -> p a b", a=2)[:, :, 128:256],
                                                                                        in1=cur[:].rearrange("p (a b) -> p a b", a=2)[:, :, 128:256], op=ALU.add), [PB[PS_], bc], [bn])
                                yield
                        P.op("pe", lambda e: e.matmul(psum[PS_][:, 0:128], lhsT=T.Wb[0][:, 256:384], rhs=T.Wb[0][:, 128:256], start=True, stop=True), [T.b_Wb[0]], [PB[PS_]])
                        yield
                        P.op("pe", lambda e: e.matmul(psum[PS_][:, 128:256], lhsT=T.Wb[0][:, 0:128], rhs=T.Wb[0][:, 384:512], start=True, stop=True), [T.b_Wb[0]], [PB[PS_]])
                        yield
                        P.op("dve", lambda e: e.tensor_tensor(out=T.TU[:].rearrange("p (a b) -> p a b", a=2), in0=psum[PS_][:, 0:256].rearrange("p (a b) -> p a b", a=2),
                                                              in1=T.Wb[0][:].rearrange("p (a b) -> p a b", a=2)[:, :, 128:256], op=ALU.add), [PB[PS_], T.b_Wb[0]], [T.b_TU])
                        yield
                        P.op("pool", lambda e: e.tensor_tensor(out=T.NYm[:, 0:128], in0=T.NY[:, 128:256], in1=M64, op=ALU.mult), [T.b_NY, b_cst], [T.b_NYm])
                        yield
                        P.op("pool", lambda e: e.tensor_tensor(out=T.NYm[:, 128:256], in0=T.NY[:, 0:128], in1=M64, op=ALU.mult), [T.b_NY, b_cst], [T.b_NYm])
                        yield
                        P.op("pe", lambda e: e.matmul(psum[PS_][:, 256:384], lhsT=T.NYm[:, 0:128], rhs=T.TU[:, 0:128], start=True, stop=True), [T.b_NYm, T.b_TU], [PB[PS_]])
                        yield
                        P.op("pe", lambda e: e.matmul(psum[PS_][:, 384:512], lhsT=T.NYm[:, 128:256], rhs=T.TU[:, 128:256], start=True, stop=True), [T.b_NYm, T.b_TU], [PB[PS_]])
                        yield
                        P.op("act", lambda e: e.activation(out=T.Wm[:], in_=psum[PS_][:, 256:512], func=AF.Copy), [PB[PS_]], [T.b_Wm])
                        yield
                        P.op("pe", lambda e: e.matmul(psum[PS_][:, 0:128], lhsT=T.TU[:, 128:256], rhs=T.Wm[:, 0:128], start=True, stop=True), [T.b_TU, T.b_Wm], [PB[PS_]])
                        yield
                        P.op("pe", lambda e: e.matmul(psum[PS_][:, 128:256], lhsT=T.TU[:, 0:128], rhs=T.Wm[:, 128:256], start=True, stop=True), [T.b_TU, T.b_Wm], [PB[PS_]])
                        yield
                        P.op("dve", lambda e: e.tensor_tensor(out=T.TU2[:], in0=psum[PS_][:, 0:256], in1=T.TU[:], op=ALU.add), [PB[PS_], T.b_TU], [T.b_TU2])
                        yield
                        P.op("pool", lambda e: e.tensor_tensor(out=T.NYm[:, 128:256], in0=T.NY[:, 0:128], in1=M128, op=ALU.mult), [T.b_NY, b_cst, T.b_NYm], [T.b_NYm])
                        yield
                        P.op("pe", lambda e: e.matmul(psum[PS_][:, 384:512], lhsT=T.NYm[:, 128:256], rhs=T.TU2[:, 128:256], start=True, stop=True), [T.b_NYm, T.b_TU2], [PB[PS_]])
                        yield
                        P.op("act", lambda e: e.activation(out=T.Wm[:, 128:256], in_=psum[PS_][:, 384:512], func=AF.Copy), [PB[PS_]], [T.b_Wm])
                        yield
                        P.op("pe", lambda e: e.matmul(psum[PS_][:, 128:256], lhsT=T.TU2[:, 0:128], rhs=T.Wm[:, 128:256], start=True, stop=True), [T.b_TU2, T.b_Wm], [PB[PS_]])
                        yield
                        P.op("dve", lambda e: e.tensor_tensor(out=T.TT[:], in0=psum[PS_][:, 128:256], in1=T.TU2[:, 128:256], op=ALU.add), [PB[PS_], T.b_TU2], [T.b_TT])
                        yield
                        P.op("pe", lambda e: e.matmul(psum[PU][:, 0:128], lhsT=T.TT[:], rhs=T.Vb[:], start=True, stop=True), [T.b_TT, T.b_Vb], [PB[PU]])
                        yield
                        P.op("pe", lambda e: e.matmul(psum[PU][:, 128:256], lhsT=T.Kbp[:], rhs=T.TT[:], start=True, stop=True), [T.b_TT, T.b_Kbp], [PB[PU]])
                        yield
                        P.op("act", lambda e: e.activation(out=T.wT[:], in_=psum[PU][:, 128:256], func=AF.Copy), [PB[PU]], [T.b_wT])
                        yield
                        P.op("act", lambda e: e.activation(out=T.usb[:], in_=psum[PU][:, 0:128], func=AF.Copy), [PB[PU]], [T.b_usb])
                        yield
                        P.op("pe", lambda e, h=h: e.matmul(psum[PU][:, 256:384], lhsT=T.wT[:], rhs=Sst[:, h, :], start=True, stop=True), [T.b_wT, b_S[h]], [PB[PU]])
                        yield
                        P.op("dve", lambda e: e.tensor_tensor(out=T.vnew[:], in0=T.usb[:], in1=psum[PU][:, 256:384], op=ALU.subtract), [T.b_usb, PB[PU]], [T.b_vnew])
                        yield
                        if own:
                            P.op("act", lambda e: e.activation(out=T.eR[:], in_=psum[PR][:, 0:128], func=AF.Exp), [PB[PR]], [T.b_eR])
                            yield
                            P.op("dve", lambda e: e.tensor_tensor(out=T.QdT[:], in0=T.qTn[:], in1=T.eR[:], op=ALU.mult), [T.b_qTn, T.b_eR], [T.b_QdT])
                            yield
                            P.op("pool", lambda e: e.tensor_tensor(out=T.ETi[:], in0=T.ETm[:], in1=MiT, op=ALU.mult), [T.b_ETm, b_cst], [T.b_ETi])
                            yield
                            P.op("dve", lambda e: e.tensor_tensor(out=T.qkT[:], in0=psum[PG][:, 128:256], in1=T.ETi[:], op=ALU.mult), [PB[PG], T.b_ETi], [T.b_qkT])
                            yield
                            P.op("pe", lambda e, h=h: e.matmul(psum[PG][:, 256:384], lhsT=T.QdT[:], rhs=Sst[:, h, :], start=True, stop=False), [T.b_QdT, b_S[h]], [PB[PG]])
                            yield
                            P.op("pe", lambda e: e.matmul(psum[PG][:, 256:384], lhsT=T.qkT[:], rhs=T.vnew[:], start=False, stop=True), [T.b_qkT, T.b_vnew], [PB[PG]])
                            yield
                            P.op("act", lambda e, h=h: e.activation(out=osb[:, h * 128:(h + 1) * 128], in_=psum[PG][:, 256:384], func=AF.Copy), [PB[PG]], [b_osb])
                            yield
                        P.op("act", lambda e: e.activation(out=T.egl[:], in_=psum[PR][:, 127:128], func=AF.Exp), [PB[PR]], [T.b_egl])
                        yield
                        P.op("pe", lambda e: e.matmul(psum[PU][:, 384:512], lhsT=T.kd[:], rhs=T.vnew[:], start=True, stop=True), [T.b_kd, T.b_vnew], [PB[PU]])
                        yield
                        P.op("dve", lambda e, h=h: e.scalar_tensor_tensor(out=Sst[:, h, :], in0=Sst[:, h, :], scalar=T.egl[:, 0:1], in1=psum[PU][:, 384:512], op0=ALU.mult, op1=ALU.add), [b_S[h], T.b_egl, PB[PU]], [b_S[h]])
                        yield
                def pair_cbs(p):
                    hs = (2 * p, 2 * p + 1)
                    c = [hs[0], hs[1]] if own else []
                    return c + [8 + hs[0], 8 + hs[1], 16 + hs[0], 16 + hs[1]]
                for _ in pre_steps(pair_cbs(0)):
                    pass
                for p_ in range(4):
                    if p_ < 3:
                        aux = pre_steps(pair_cbs(p_ + 1))
                    elif kv:
                        aux = kv_steps()
                    else:
                        aux = iter(())
                    gens = [head_steps(2 * p_, TS[0]), head_steps(2 * p_ + 1, TS[1]), aux]
                    alive = [True, True, True]
                    while any(alive):
                        for gi_ in range(3):
                            if alive[gi_]:
                                try:
                                    next(gens[gi_])
                                except StopIteration:
                                    alive[gi_] = False
                if own:
                    oi = (v - OWN0) if scan == 0 else (v - OWN1)
                    P.dma(OS[scan, oi, :, :], osb[:], [b_osb], [b_OS], b_OS)

        P.barrier()

        b_ZS = Buf(); b_GS = Buf()
        YbT = sb("YbT", [128, H, 2048], BF16); b_YbT = Buf()
        s_qt = ExitStack()
        QT = sb("QT", [128, H, 2048], BF16, stack=s_qt); b_QT = Buf()
        with ExitStack() as s2:
            W2 = sb("W2", [128, 8, 3072], BF16, stack=s2); b_W2 = Buf()
            stage = [junk, xt]
            bstage = [b_junk, b_xt]
            tabB = sb("tabB2", [128, 64], stack=s2); b_tabB = Buf()
            tmpB = [sb("tmpB2_%d" % i, [128, D], stack=s2) for i in range(3)] + [sb("tmpB32", [128, 16], stack=s2)]
            b_tmpB = [Buf() for _ in range(4)]
            krotB = sb("krotB2", [128, D], BF16, stack=s2); b_krotB = Buf()
            load_w_bf16(W2, b_W2, 3072, 1024, stage, bstage, 0)
            load_w_bf16(W2, b_W2, 4128, 1024, stage, bstage, 1024)
            load_w_bf16(W2, b_W2, 7200, 1024, stage, bstage, 2048)
            zsb = sb("zsb", [128, 2048], BF16, stack=s2); b_zsb = Buf()
            for i in range(16):
                v = OWN0 + i
                make_HT(v, False)
                P.dma(tabB[:], ROPE[v * 128:(v + 1) * 128, :], [], [b_tabB], b_tabB)
                for hf in range(2):
                    for k in range(8):
                        P.op("pe", lambda e, k=k, hf=hf: e.matmul(psum[6 + hf][:, :], lhsT=HT[:, k, 2:130], rhs=W2[:, k, 1024 + hf * 512:1024 + (hf + 1) * 512], start=(k == 0), stop=(k == 7)),
                             [b_HT, b_W2], [PB[6 + hf]], sig=(k == 7))
                qk_norm_rope([psum[6], psum[7]], [PB[6], PB[7]], qnw, tabB, b_tabB, krotB, b_krotB, tmpB, b_tmpB)
                for h in range(H):
                    bank = 6 + h // 4
                    P.op("pe", lambda e, h=h, bank=bank: e.matmul(psum[bank][:, (h % 4) * 128:(h % 4 + 1) * 128], lhsT=krotB[:, h * 128:(h + 1) * 128], rhs=identb, start=True, stop=True),
                         [b_krotB, b_cbf], [PB[bank]])
                for hf in range(2):
                    P.op("act", lambda e, hf=hf, i=i: e.activation(out=QT[:, hf * 4:(hf + 1) * 4, i * 128:(i + 1) * 128], in_=psum[6 + hf][:, :].rearrange("p (a b) -> p a b", a=4), func=AF.Copy), [PB[6 + hf]], [b_QT])
                for (c0, dstt, bd, off, fn) in ((0, zsb, b_zsb, 0, AF.Silu), (2048, zsb, b_zsb, 1024, AF.Silu)):
                    for hf in range(2):
                        bank = 4 + hf
                        for k in range(8):
                            P.op("pe", lambda e, k=k, hf=hf, c0=c0, bank=bank: e.matmul(psum[bank][:, :], lhsT=HT[:, k, 2:130], rhs=W2[:, k, c0 + hf * 512:c0 + (hf + 1) * 512], start=(k == 0), stop=(k == 7)),
                                 [b_HT, b_W2], [PB[bank]], sig=(k == 7))
                        P.op("act", lambda e, hf=hf, bank=bank, dstt=dstt, off=off, fn=fn: e.activation(out=dstt[:, off + hf * 512:off + (hf + 1) * 512], in_=psum[bank][:, :], func=fn), [PB[bank]], [bd])
                P.dma(ZS[i, :, :], zsb[:], [b_zsb], [b_ZS], b_ZS)

        P.barrier()
        with ExitStack() as s3:
            KTh = sb("KTh", [128, NKV * 128], BF16, stack=s3); b_KTh = Buf()
            Vph = sb("Vph", [128, NKV, 129], BF16, stack=s3); b_Vph = Buf()
            Eb = [sb("Eb%d" % i, [128, 1024], BF16, stack=s3) for i in range(2)]; b_Eb = [Buf(), Buf()]
            zb = sb("zb", [128, 4, 128], BF16, stack=s3); b_zb = Buf()
            fo = sb("fo", [128, 8, 128], stack=s3); b_fo = Buf()
            fs = sb("fs", [128, 8], stack=s3); b_fs = Buf()
            yb = sb("yb", [128, 128], BF16, stack=s3); b_yb = Buf()
            for h in range(H):
                P.dma(KTh[:], KTS[h, :, :], [b_KTS], [b_KTh], b_KTh)
                P.dma(Vph[:], VPS[h, :, :, :], [b_VPS], [b_Vph], b_Vph)
                for qt in range(4):
                    P.dma(zb[:], ZS[qt * 4:(qt + 1) * 4, :, 1024 + h * 128:1024 + (h + 1) * 128].rearrange("a p c -> p a c"), [b_ZS], [b_zb], b_zb)
                    def emit_qk(kb):
                        sbk = (kb % 2) * 2
                        for m in range(2):
                            P.op("pe", lambda e, m=m, kb=kb, qt=qt, h=h, sbk=sbk: e.matmul(psum[sbk + m][:, :], lhsT=KTh[m * 64:(m + 1) * 64, kb * 128:(kb + 1) * 128],
                                                                                   rhs=QT[m * 64:(m + 1) * 64, h, qt * 512:(qt + 1) * 512], start=True, stop=True),
                                 [b_KTh, b_QT], [PB[sbk + m]])
                    emit_qk(0)
                    for kb in range(NKV):
                        sbk = (kb % 2) * 2
                        if kb + 1 < NKV:
                            emit_qk(kb + 1)
                        for m in range(2):
                            P.op("act", lambda e, m=m, kb=kb, sbk=sbk: e.activation(out=Eb[kb % 2][:, m * 512:(m + 1) * 512], in_=psum[sbk + m][:, :], func=AF.Exp, scale=0.125, bias=lamt[:, 1:2]),
                                 [PB[sbk + m], b_lam], [b_Eb[kb % 2]])
                        for qb in range(4):
                            for m in range(2):
                                P.op("pe", lambda e, m=m, kb=kb, qb=qb: e.matmul(psum[4 + qb][:, m * 129:(m + 1) * 129], lhsT=Eb[kb % 2][:, m * 512 + qb * 128:m * 512 + (qb + 1) * 128],
                                                                                 rhs=Vph[:, kb, :], start=(kb == 0 and m == 0), stop=(kb == NKV - 1 and m == 1)),
                                     [b_Eb[kb % 2], b_Vph], [PB[4 + qb]], sig=(qb == 3 and m == 1))
                    for qb in range(4):
                        blk = qt * 4 + qb
                        pso = psum[4 + qb]
                        P.op("dve", lambda e, pso=pso: e.tensor_copy(out=fs[:, 0:1], in_=pso[:, 128:129]), [PB[4 + qb]], [b_fs])
                        P.op("dve", lambda e, pso=pso: e.tensor_copy(out=fs[:, 1:2], in_=pso[:, 257:258]), [PB[4 + qb]], [b_fs])
                        P.op("dve", lambda e: e.reciprocal(out=fs[:, 2:4], in_=fs[:, 0:2]), [b_fs], [b_fs])
                        P.op("dve", lambda e: e.tensor_tensor(out=fs[:, 3:4], in0=fs[:, 3:4], in1=lamt[:, 0:1], op=ALU.mult), [b_fs, b_lam], [b_fs])
                        P.op("dve", lambda e, pso=pso: e.tensor_scalar_mul(out=fo[:, 0, :], in0=pso[:, 129:257], scalar1=fs[:, 3:4]), [PB[4 + qb], b_fs], [b_fo])
                        P.op("dve", lambda e, pso=pso: e.scalar_tensor_tensor(out=fo[:, 1, :], in0=pso[:, 0:128], scalar=fs[:, 2:3], in1=fo[:, 0, :], op0=ALU.mult, op1=ALU.add), [PB[4 + qb], b_fs, b_fo], [b_fo])
                        P.op("act", lambda e: e.activation(out=fo[:, 2, :], in_=fo[:, 1, :], func=AF.Square, accum_out=fs[:, 4:5]), [b_fo, b_fs], [b_fo, b_fs])
                        P.op("act", lambda e: e.activation(out=fs[:, 5:6], in_=fs[:, 4:5], func=AF.Ln, scale=1.0 / 128, bias=epsc[:, 0:1]), [b_fs, b_eps], [b_fs])
                        P.op("act", lambda e: e.activation(out=fs[:, 5:6], in_=fs[:, 5:6], func=AF.Exp, scale=-0.5), [b_fs], [b_fs])
                        P.op("dve", lambda e: e.scalar_tensor_tensor(out=fo[:, 3, :], in0=fo[:, 1, :], scalar=fs[:, 5:6], in1=dnw, op0=ALU.mult, op1=ALU.mult), [b_fo, b_fs, b_vecs], [b_fo])
                        P.op("dve", lambda e, qb=qb: e.scalar_tensor_tensor(out=yb[:], in0=fo[:, 3, :], scalar=(1.0 - LAMBDA_INIT), in1=zb[:, qb, :], op0=ALU.mult, op1=ALU.mult), [b_fo, b_zb], [b_yb])
                        P.op("pe", lambda e: e.matmul(psum[0][:, 0:128], lhsT=yb[:], rhs=identb, start=True, stop=True), [b_yb, b_cbf], [PB[0]])
                        P.op("act", lambda e, h=h, blk=blk: e.activation(out=YbT[:, h, blk * 128:(blk + 1) * 128], in_=psum[0][:, 0:128], func=AF.Copy), [PB[0]], [b_YbT])

        s_qt.close()
        P.barrier()
        with ExitStack() as s4:
            Wo = [sb("Wo%d" % i, [128, 8, D], BF16, stack=s4) for i in range(3)]; b_Wo = [Buf() for _ in range(3)]
            stage = [junk, xt]
            bstage = [b_junk, b_xt]
            YaT = sb("YaT", [128, H, 128], BF16, stack=s4); b_YaT = Buf()
            Wg = sb("Wg", [128, 8, 2048], BF16, stack=s4); b_Wg = Buf()
            load_w_bf16(Wg, b_Wg, 8224, 2048, stage, bstage, 0)
            ii = 0
            for wi, src in enumerate((WOA, WOB, WOUT)):
                for k in range(8):
                    s_, bs_ = stage[ii % 2], bstage[ii % 2]
                    P.dma(s_[:], src[k * 128:(k + 1) * 128, :], [], [bs_], bs_)
                    P.op(["dve", "pool"][ii % 2], lambda e, s_=s_, wi=wi, k=k: e.tensor_copy(out=Wo[wi][:, k, :], in_=s_[:]), [bs_], [b_Wo[wi]])
                    ii += 1
            o1 = sb("o1", [128, D], stack=s4); b_o1 = Buf()
            o2 = sb("o2", [128, D], stack=s4); b_o2 = Buf()
            zs4 = sb("zs4", [128, 1024], BF16, stack=s4); b_zs4 = Buf()
            gs4 = sb("gs4", [128, 2048], BF16, stack=s4); b_gs4 = Buf()
            ya = sb("ya", [128, D], BF16, stack=s4); b_ya = Buf()
            mg = sb("mg", [128, D], stack=s4); b_mg = Buf()
            mgb = sb("mgb", [128, D], BF16, stack=s4); b_mgb = Buf()
            mT = sb("mT", [128, 8, 128], BF16, stack=s4); b_mT = Buf()
            res = sb("res", [128, D], stack=s4); b_res = Buf()
            b_OUT = Buf()
            out_toks = []
            for i in range(16):
                P.dma(o1[:], OS[0, i, :, :], [b_OS], [b_o1], b_o1)
                P.dma(o2[:], OS[1, 15 - i, :, :], [b_OS], [b_o2], b_o2)
                P.dma(zs4[:], ZS[i, :, 0:1024], [b_ZS], [b_zs4], b_zs4)
                make_HT(OWN0 + i, False)
                for g4 in range(4):
                    bank = 6 + g4 % 2
                    for k in range(8):
                        P.op("pe", lambda e, k=k, g4=g4, bank=bank: e.matmul(psum[bank][:, :], lhsT=HT[:, k, 2:130], rhs=Wg[:, k, g4 * 512:(g4 + 1) * 512], start=(k == 0), stop=(k == 7)), [b_HT, b_Wg], [PB[bank]], sig=(k == 7))
                    P.op("act", lambda e, g4=g4, bank=bank: e.activation(out=gs4[:, g4 * 512:(g4 + 1) * 512], in_=psum[bank][:, :], func=AF.Sigmoid), [PB[bank]], [b_gs4])
                for hf in range(2):
                    P.op("pe", lambda e, hf=hf: e.matmul(psum[hf][:, :], lhsT=Jm, rhs=o2[:, hf * 512:(hf + 1) * 512], start=True, stop=True), [b_cst, b_o2], [PB[hf]])
                    P.op("dve", lambda e, hf=hf: e.tensor_tensor(out=o1[:, hf * 512:(hf + 1) * 512], in0=o1[:, hf * 512:(hf + 1) * 512], in1=psum[hf][:, :], op=ALU.add), [PB[hf], b_o1], [b_o1])
                P.op("act", lambda e: e.activation(out=junk[:], in_=o1[:], func=AF.Square), [b_o1], [b_junk])
                P.op("dve", lambda e: e.tensor_reduce(out=st[:, 0:8], in_=junk[:].rearrange("p (g d) -> p g d", d=128), axis=AX.X, op=ALU.add), [b_junk], [b_st])
                P.op("act", lambda e: e.activation(out=st[:, 0:8], in_=st[:, 0:8], func=AF.Ln, scale=1.0 / 128, bias=epsc[:, 0:1]), [b_st, b_eps], [b_st])
                P.op("act", lambda e: e.activation(out=st[:, 0:8], in_=st[:, 0:8], func=AF.Exp, scale=-0.5), [b_st], [b_st])
                P.op("dve", lambda e: e.tensor_tensor(out=junk[:].rearrange("p (g d) -> p g d", d=128), in0=o1[:].rearrange("p (g d) -> p g d", d=128), in1=st[:, 0:8].unsqueeze(2).to_broadcast([128, 8, 128]), op=ALU.mult), [b_o1, b_st], [b_junk])
                P.op("pool", lambda e: e.tensor_tensor(out=junk[:].rearrange("p (g d) -> p g d", d=128), in0=junk[:].rearrange("p (g d) -> p g d", d=128), in1=gnw.unsqueeze(1).to_broadcast([128, 8, 128]), op=ALU.mult), [b_junk, b_vecs], [b_junk])
                P.op("dve", lambda e: e.tensor_tensor(out=ya[:], in0=junk[:], in1=zs4[:], op=ALU.mult), [b_junk, b_zs4], [b_ya])
                for h in range(H):
                    bank = h // 4
                    P.op("pe", lambda e, h=h, bank=bank: e.matmul(psum[bank][:, (h % 4) * 128:(h % 4 + 1) * 128], lhsT=ya[:, h * 128:(h + 1) * 128], rhs=identb, start=True, stop=True), [b_ya, b_cbf], [PB[bank]])
                for hf in range(2):
                    P.op("act", lambda e, hf=hf, i=i: e.activation(out=YaT[:, hf * 4:(hf + 1) * 4, :], in_=psum[hf][:, :].rearrange("p (a b) -> p a b", a=4), func=AF.Copy), [PB[hf]], [b_YaT])
                for (wi, Y, bY, pb0, c0_) in ((0, YaT, b_YaT, 2, 0), (1, YbT, b_YbT, 4, i * 128)):
                    for hf in range(2):
                        for k in range(8):
                            P.op("pe", lambda e, wi=wi, Y=Y, hf=hf, k=k, pb0=pb0, c0_=c0_: e.matmul(psum[pb0 + hf][:, :], lhsT=Y[:, k, c0_:c0_ + 128], rhs=Wo[wi][:, k, hf * 512:(hf + 1) * 512], start=(k == 0), stop=(k == 7)),
                                 [bY, b_Wo[wi]], [PB[pb0 + hf]], sig=(k == 7))
                for hf in range(2):
                    P.op("dve", lambda e, hf=hf: e.tensor_tensor(out=mg[:, hf * 512:(hf + 1) * 512], in0=psum[2 + hf][:, :], in1=gs4[:, hf * 512:(hf + 1) * 512], op=ALU.mult), [PB[2 + hf], b_gs4], [b_mg])
                    P.op("dve", lambda e, hf=hf: e.tensor_tensor(out=junk[:, hf * 512:(hf + 1) * 512], in0=psum[4 + hf][:, :], in1=gs4[:, 1024 + hf * 512:1024 + (hf + 1) * 512], op=ALU.mult), [PB[4 + hf], b_gs4], [b_junk])
                P.op("pool", lambda e: e.tensor_tensor(out=mgb[:], in0=mg[:], in1=junk[:], op=ALU.add), [b_mg, b_junk], [b_mgb])
                for k in range(8):
                    bank = 6 + k // 4
                    P.op("pe", lambda e, k=k, bank=bank: e.matmul(psum[bank][:, (k % 4) * 128:(k % 4 + 1) * 128], lhsT=mgb[:, k * 128:(k + 1) * 128], rhs=identb, start=True, stop=True), [b_mgb, b_cbf], [PB[bank]])
                for hf in range(2):
                    P.op("act", lambda e, hf=hf: e.activation(out=mT[:, hf * 4:(hf + 1) * 4, :], in_=psum[6 + hf][:, :].rearrange("p (a b) -> p a b", a=4), func=AF.Copy), [PB[6 + hf]], [b_mT])
                for hf in range(2):
                    for k in range(8):
                        P.op("pe", lambda e, hf=hf, k=k: e.matmul(psum[hf][:, :], lhsT=mT[:, k, :], rhs=Wo[2][:, k, hf * 512:(hf + 1) * 512], start=(k == 0), stop=(k == 7)), [b_mT, b_Wo[2]], [PB[hf]], sig=(k == 7))
                    P.op("dve", lambda e, hf=hf: e.tensor_tensor(out=res[:, hf * 512:(hf + 1) * 512], in0=psum[hf][:, :], in1=mod[:, 2, hf * 512:(hf + 1) * 512], op=ALU.mult), [PB[hf], b_mod], [b_res])
                P.op("pool", lambda e: e.tensor_tensor(out=res[:], in0=res[:], in1=xt[:], op=ALU.add), [b_res, b_xt], [b_res])
                tok = P.dma(OUT[i * 128:(i + 1) * 128, :], res[:], [b_res], [b_OUT], b_OUT)
            P.final_wait("sp", [tok])

        print('total ops', P.gidx, {n: len(P.q[n]) for n in P.names})
        P.emit()
    return nc


def _consts():
    i = np.arange(128)
    ident = np.eye(128, dtype=np.float32)
    J = ident[::-1].copy()
    ones = np.ones((128, 128), np.float32)
    Ltri = (i[:, None] <= i[None, :]).astype(np.float32)
    negMs = -(i[:, None] > i[None, :]).astype(np.float32)
    negMsT = -(i[None, :] > i[:, None]).astype(np.float32)
    MiT = (i[None, :] >= i[:, None]).astype(np.float32)
    SelM = np.zeros((128, 132), np.float32)
    SelM[i, i + 2] = 1
    SelH = np.zeros((128, 132), np.float32)
    SelH[0, 0] = 1
    SelH[1, 1] = 1
    SelH[2, 130] = 1
    SelH[3, 131] = 1
    blk = lambda b: ((i[:, None] // b) == (i[None, :] // b)).astype(np.float32)
    BD32 = blk(32)
    M64 = blk(64) - blk(32)
    M128 = 1.0 - blk(64)
    return np.concatenate([ident, J, ones, Ltri, negMs, negMsT, MiT, SelM, SelH, BD32, M64, M128], axis=1)


def _rope_table(pos):
    pos = np.asarray(pos)
    row = (pos // 64).astype(np.float32)
    col = (pos % 64).astype(np.float32)
    inv = (10000.0 ** (-np.arange(16, dtype=np.float32) / 16)).astype(np.float32)
    ar = row[:, None] * inv
    ac = col[:, None] * inv
    t = np.concatenate([np.cos(ar), np.sin(ar), np.cos(ac), np.sin(ac)], axis=1).astype(np.float32)
    t[pos < 0] = np.array([1.0] * 16 + [0.0] * 16 + [1.0] * 16 + [0.0] * 16, np.float32)
    return t


def _core_plan(r):
    fwd_first = r >= 2
    own = list(range(16 * r, 16 * r + 16))
    left = list(range(0, 16 * r))
    right = list(range(16 * r + 16, 64))

    def scan(forward, nslots, filler):
        rev = not forward
        vis = [("c", 0, rev, 1.0), ("c", 1, rev, 1.0)]
        if rev:
            vis = [("c", 1, rev, 1.0), ("c", 0, rev, 1.0)]
        pref = left if forward else right[::-1]
        fill = [("l", b, rev, 0.0) for b in filler]
        vis += fill[:nslots - len(pref)] + [("l", b, rev, 1.0) for b in pref]
        o = own if forward else own[::-1]
        vis += [("l", b, rev, 1.0) for b in o]
        return vis

    if fwd_first:
        filler = right[::-1]
        v1 = scan(True, 48, filler)
        v2 = scan(False, 16, [own[0]] * 16)
    else:
        filler = left
        v1 = scan(False, 48, filler)
        v2 = scan(True, 16, [own[0]] * 16)
    assert len(v1) == 66 and len(v2) == 34
    return v1 + v2, fwd_first


_NC_CACHE = {}


def kernel(x, c, ctx, c_ctx, w_ada, b_ada, w_in, conv_w, a_log, dt_bias, gdn_norm_w,
           q_norm_w, k_norm_w, lambda_q1, lambda_k1, lambda_q2, lambda_k2, diff_norm_w,
           w_oa, w_ob, w_out):
    f = lambda a: np.ascontiguousarray(np.asarray(a, dtype=np.float32))
    x, c, ctx, c_ctx = f(x), f(c), f(ctx), f(c_ctx)
    w_ada, b_ada, w_in, conv_w = f(w_ada)[0], f(b_ada)[0], f(w_in)[0], f(conv_w)[0]
    a_log, dt_bias = f(a_log)[0], f(dt_bias)[0]
    vec = lambda a: f(a)[0]
    if "nc" not in _NC_CACHE:
        _NC_CACHE["nc"] = build_nc()
    nc = _NC_CACHE["nc"]
    cst = _consts()
    vecs = np.zeros((768,), np.float32)
    vecs[0:128] = vec(gdn_norm_w)
    vecs[128:192] = vec(q_norm_w)
    vecs[192:256] = vec(k_norm_w)
    vecs[256:320] = vec(lambda_q1)
    vecs[320:384] = vec(lambda_k1)
    vecs[384:448] = vec(lambda_q2)
    vecs[448:512] = vec(lambda_k2)
    vecs[512:640] = vec(diff_norm_w)
    VECS = np.ascontiguousarray(np.broadcast_to(vecs, (128, 768)))
    BADA = np.ascontiguousarray(np.broadcast_to(b_ada, (128, 3 * D)))
    in_maps = []
    plans = []
    for core in range(8):
        b, r = divmod(core, 4)
        plan, fwd_first = _core_plan(r)
        plans.append((plan, fwd_first))
        XV = np.zeros((NV * 128, D), np.float32)
        XH = np.zeros((NV * 4, D), np.float32)
        HM = np.zeros((4, NV), np.float32)
        GM = np.zeros((128, NV), np.float32)
        ROPE = np.zeros((NKV * 128, 64), np.float32)
        for v, (seq, blk, rev, gmask) in enumerate(plan):
            src = ctx[b] if seq == "c" else x[b]
            n = src.shape[0]
            idx = np.arange(blk * 128, blk * 128 + 128)
            halo = np.array([blk * 128 - 2, blk * 128 - 1, blk * 128 + 128, blk * 128 + 129])
            if rev:
                idx = idx[::-1]
                halo = halo[::-1]
            XV[v * 128:(v + 1) * 128] = src[idx]
            valid = (halo >= 0) & (halo < n)
            XH[v * 4:(v + 1) * 4][valid] = src[halo[valid]]
            HM[:, v] = valid.astype(np.float32)
            GM[:, v] = gmask
            if v < NKV:
                pos = idx if seq == "l" else -np.ones(128, np.int64)
                ROPE[v * 128:(v + 1) * 128] = _rope_table(pos)
        d1 = 0 if fwd_first else 1
        dirs = (d1, 1 - d1)
        WAB = np.stack([np.concatenate([w_in[:, 4096 + d * 8:4096 + d * 8 + 8], w_in[:, 4112 + d * 8:4112 + d * 8 + 8]], axis=1) for d in dirs])
        cwt = np.ascontiguousarray(conv_w.T)
        CW = np.stack([cwt if d == 0 else np.ascontiguousarray(cwt[:, ::-1]) for d in dirs])
        ALG = np.stack([np.broadcast_to(a_log[d], (128, 8)) for d in dirs])
        DTB = np.stack([np.broadcast_to(dt_bias[d], (128, 8)) for d in dirs])
        cv = np.stack([c[b], c_ctx])
        CT = np.ascontiguousarray(cv.reshape(2, 8, 128).transpose(2, 0, 1).reshape(128, 16))
        in_maps.append({
            "XV": XV, "XH": XH, "HM": HM, "GM": GM, "ROPE": ROPE, "CT": CT, "WADA": w_ada, "BADA": BADA,
            "WIN": w_in, "WAB": np.ascontiguousarray(WAB), "CW": np.ascontiguousarray(CW),
            "ALG": np.ascontiguousarray(ALG), "DTB": np.ascontiguousarray(DTB), "VECS": VECS, "CST": cst,
            "WOA": f(w_oa)[0], "WOB": f(w_ob)[0], "WOUT": f(w_out)[0],
        })
    if _NC_CACHE.get("maps_only"):
        return in_maps, plans
    res = run_bass_kernel_spmd(nc, in_maps, core_ids=list(range(8)))
    out = np.zeros((2, L, D), np.float32)
    for core in range(8):
        b, r = divmod(core, 4)
        plan, fwd_first = plans[core]
        o = np.asarray(res.results[core]["OUT"])
        for i in range(16):
            seq, blk, rev, _ = plan[OWN0 + i]
            idx = np.arange(blk * 128, blk * 128 + 128)
            if rev:
                idx = idx[::-1]
            out[b, idx] = o[i * 128:(i + 1) * 128]
    return out
```

```python
import math
from contextlib import ExitStack

import numpy as np
import concourse.bass as bass
import concourse.mybir as mybir
from concourse.bass_utils import run_bass_kernel_spmd

F32 = mybir.dt.float32
BF16 = mybir.dt.bfloat16
AF = mybir.ActivationFunctionType
ALU = mybir.AluOpType
AX = mybir.AxisListType

D = 1024
L = 8192
NCTX = 256
H = 8
EPS = 1e-6
NV = 100
NKV = 66
OWN0 = 50
S2 = 66
OWN1 = 84
IN_COLS = 10272
LAMBDA_INIT = 0.8 - 0.6 * math.exp(-0.3 * 0)
DEBUG = False
import os
MAXOPS = int(os.environ.get('KMAXOPS', '1000000000'))
TRACE_LO = int(os.environ.get('KTLO', '0'))
TRACE_HI = int(os.environ.get('KTHI', '-1'))


class Buf:
    __slots__ = ("w", "r", "dsem", "dcnt")

    def __init__(self):
        self.w = None
        self.r = {}
        self.dsem = None
        self.dcnt = 0


class Prog:
    def __init__(self, nc, es):
        self.nc = nc
        self.es = es
        self.names = ["pe", "act", "dve", "pool", "sp"]
        self.sem = {n: es.enter_context(nc.semaphore("s_" + n)) for n in self.names}
        self.cnt = {n: 0 for n in self.names}
        self.waited = {n: {} for n in self.names}
        self.q = {n: [] for n in self.names}
        self.nsem = 0
        self.dbufs = []
        self.gidx = 0
        self.maxops = MAXOPS

    def _waits(self, eng, reads, writes):
        need = {}

        def add(tok):
            if tok is None:
                return
            s, v = tok
            k = id(s)
            if k not in need or need[k][1] < v:
                need[k] = (s, v)

        for b in reads:
            add(b.w)
        for b in writes:
            add(b.w)
            for t in b.r.values():
                add(t)
        out = []
        own = id(self.sem[eng])
        for k, (s, v) in need.items():
            if eng == "pe" and k == own:
                continue
            if self.waited[eng].get(k, 0) >= v:
                continue
            self.waited[eng][k] = v
            out.append((s, v))
        return out

    def _commit(self, tok, reads, writes):
        for b in reads:
            b.r[id(tok[0])] = tok
        for b in writes:
            b.w = tok
            b.r = {}

    def op(self, eng, fn, reads=(), writes=(), sig=True):
        self.gidx += 1
        if TRACE_LO <= self.gidx <= TRACE_HI:
            import inspect
            print("OP", self.gidx, eng, inspect.currentframe().f_back.f_lineno)
        if self.gidx > self.maxops:
            return
        waits = self._waits(eng, reads, writes)
        if sig:
            self.cnt[eng] += 1
            tok = (self.sem[eng], self.cnt[eng])
            self.q[eng].append((waits, fn, self.sem[eng], 1))
        else:
            tok = (self.sem[eng], self.cnt[eng] + 1)
            self.q[eng].append((waits, fn, None, 0))
        self._commit(tok, reads, writes)

    def dma(self, out_ap, in_ap, reads, writes, dst):
        eng = "sp"
        self.gidx += 1
        if TRACE_LO <= self.gidx <= TRACE_HI:
            import inspect
            print("DMA", self.gidx, inspect.currentframe().f_back.f_lineno)
        if self.gidx > self.maxops:
            return (None, 0)
        waits = self._waits(eng, reads, writes)
        if dst.dsem is None:
            dst.dsem = self.es.enter_context(self.nc.semaphore("d%d" % self.nsem))
            self.nsem += 1
            self.dbufs.append(dst)
        dst.dcnt += 16
        tok = (dst.dsem, dst.dcnt)
        self.q[eng].append((waits, lambda e: e.dma_start(out=out_ap, in_=in_ap), dst.dsem, 16))
        self._commit(tok, reads, writes)
        return tok

    def barrier(self):
        toks = [(self.sem[n], self.cnt[n]) for n in self.names if self.cnt[n] > 0]
        toks += [(b.dsem, b.dcnt) for b in self.dbufs]
        for eng in self.names:
            waits = []
            for (s_, v) in toks:
                k = id(s_)
                if eng == "pe" and k == id(self.sem["pe"]):
                    continue
                if self.waited[eng].get(k, 0) >= v:
                    continue
                self.waited[eng][k] = v
                waits.append((s_, v))
            if waits:
                self.q[eng].append((waits, None, None, 0))

    def final_wait(self, eng, toks):
        toks = [t for t in toks if t[0] is not None]
        if not toks:
            toks = [(self.sem[n], self.cnt[n]) for n in self.names if self.cnt[n] > 0 and n != eng]
            toks += [(b.dsem, b.dcnt) for b in self.dbufs]
        self.q[eng].append((toks, None, None, 0))

    def emit(self):
        nc = self.nc
        with nc.Block() as block:
            def run(name):
                def body(e):
                    for waits, fn, sem, inc in self.q[name]:
                        for s, v in waits:
                            e.wait_ge(s, v)
                        if fn is not None:
                            ins_ = fn(e)
                            if inc:
                                ins_.then_inc(sem, inc)
                return body
            block.tensor(run("pe"))
            block.scalar(run("act"))
            block.vector(run("dve"))
            block.gpsimd(run("pool"))
            block.sync(run("sp"))


def build_nc():
    nc = bass.Bass("TRN2", target_bir_lowering=False)

    def din(name, shape, dt=F32):
        return nc.dram_tensor(name, list(shape), dt, kind="ExternalInput").ap()

    XV = din("XV", [NV * 128, D])
    XH = din("XH", [NV * 4, D])
    HM = din("HM", [4, NV])
    GM = din("GM", [128, NV])
    ROPE = din("ROPE", [NKV * 128, 64])
    CT = din("CT", [128, 16])
    WADA = din("WADA", [D, 3 * D])
    BADA = din("BADA", [128, 3 * D])
    WIN = din("WIN", [D, IN_COLS])
    WAB = din("WAB", [2, D, 16])
    CW = din("CW", [2, 3072, 5])
    ALG = din("ALG", [2, 128, 8])
    DTB = din("DTB", [2, 128, 8])
    VECS = din("VECS", [128, 768])
    CST = din("CST", [128, 10 * 128 + 2 * 132])
    WOA = din("WOA", [D, D])
    WOB = din("WOB", [D, D])
    WOUT = din("WOUT", [D, D])
    OUT = nc.dram_tensor("OUT", [2048, D], F32, kind="ExternalOutput").ap()

    KTS = nc.dram_tensor("KTS", [H, 128, NKV * 128], BF16).ap()
    VPS = nc.dram_tensor("VPS", [H, 128, NKV, 129], BF16).ap()
    OS = nc.dram_tensor("OS", [2, 16, 128, D], F32).ap()
    ZS = nc.dram_tensor("ZS", [16, 128, 2048], BF16).ap()
    GS = nc.dram_tensor("GS", [16, 128, 2048], BF16).ap()

    es = ExitStack()
    with es:
        P = Prog(nc, es)

        def sb(name, shape, dt=F32, stack=es):
            return stack.enter_context(nc.sbuf_tensor(name, list(shape), dt))

        psum = [es.enter_context(nc.psum_tensor("ps%d" % i, [128, 512], F32)) for i in range(8)]
        PB = [Buf() for _ in range(8)]

        cst = sb("cst", [128, 10 * 128 + 2 * 132]); b_cst = Buf()
        ident = cst[:, 0:128]
        Jm = cst[:, 128:256]
        ones = cst[:, 256:384]
        Ltri = cst[:, 384:512]
        negMs = cst[:, 512:640]
        negMsT = cst[:, 640:768]
        MiT = cst[:, 768:896]
        BD32 = cst[:, 1160:1288]
        M64 = cst[:, 1288:1416]
        M128 = cst[:, 1416:1544]
        cbf = sb("cbf", [128, 128 + 2 * 132], BF16); b_cbf = Buf()
        identb = cbf[:, 0:128]
        SelM = cbf[:, 128:260]
        SelH = cbf[0:4, 260:392]
        vecs = sb("vecs", [128, 768]); b_vecs = Buf()
        gnw = vecs[:, 0:128]
        qnw = vecs[:, 128:192]
        knw = vecs[:, 192:256]
        dnw = vecs[:, 512:640]
        hm = sb("hm", [4, NV]); b_hm = Buf()
        gm = sb("gm", [128, NV]); b_gm = Buf()
        mod = sb("mod", [128, 5, D]); b_mod = Buf()
        negA = sb("negA", [128, 2, 8]); b_negA = Buf()
        dtb = sb("dtb", [128, 2, 8]); b_dtb = Buf()
        lamt = sb("lamt", [128, 4]); b_lam = Buf()
        epsc = sb("epsc", [128, 1]); b_eps = Buf()

        P.dma(cst[:], CST[:, :], [], [b_cst], b_cst)
        P.dma(vecs[:], VECS[:, :], [], [b_vecs], b_vecs)
        P.dma(hm[:], HM[:, :], [], [b_hm], b_hm)
        P.dma(gm[:], GM[:, :], [], [b_gm], b_gm)
        P.dma(negA[:], ALG.rearrange("s p h -> p s h"), [], [b_negA], b_negA)
        P.dma(dtb[:], DTB.rearrange("s p h -> p s h"), [], [b_dtb], b_dtb)
        P.op("act", lambda e: e.activation(out=negA[:], in_=negA[:], func=AF.Exp), [b_negA], [b_negA])
        P.op("dve", lambda e: e.tensor_scalar_mul(out=negA[:], in0=negA[:], scalar1=-1.0), [b_negA], [b_negA])
        P.op("dve", lambda e: e.tensor_copy(out=cbf[:, 0:128], in_=cst[:, 0:128]), [b_cst], [b_cbf])
        P.op("dve", lambda e: e.tensor_copy(out=cbf[:, 128:392], in_=cst[:, 896:1160]), [b_cst], [b_cbf])
        P.op("pool", lambda e: e.memset(epsc[:], EPS), [], [b_eps])

        ltmp = sb("ltmp", [128, 128]); b_ltmp = Buf()
        P.op("dve", lambda e: e.tensor_tensor(out=ltmp[:, 0:64], in0=vecs[:, 256:320], in1=vecs[:, 320:384], op=ALU.mult), [b_vecs], [b_ltmp])
        P.op("dve", lambda e: e.tensor_tensor(out=ltmp[:, 64:128], in0=vecs[:, 384:448], in1=vecs[:, 448:512], op=ALU.mult), [b_vecs], [b_ltmp])
        P.op("dve", lambda e: e.tensor_reduce(out=lamt[:, 2:4], in_=ltmp[:].rearrange("p (a b) -> p a b", a=2), axis=AX.X, op=ALU.add), [b_ltmp], [b_lam])
        P.op("act", lambda e: e.activation(out=lamt[:, 2:4], in_=lamt[:, 2:4], func=AF.Exp), [b_lam], [b_lam])
        P.op("dve", lambda e: e.tensor_tensor(out=lamt[:, 0:1], in0=lamt[:, 3:4], in1=lamt[:, 2:3], op=ALU.subtract), [b_lam], [b_lam])
        P.op("dve", lambda e: e.tensor_scalar_add(out=lamt[:, 0:1], in0=lamt[:, 0:1], scalar1=-LAMBDA_INIT), [b_lam], [b_lam])
        P.op("dve", lambda e: e.tensor_scalar_mul(out=ltmp[:], in0=vecs[:, 128:256], scalar1=-1.0), [b_vecs, b_lam], [b_ltmp])
        P.op("dve", lambda e: e.tensor_tensor(out=ltmp[:], in0=ltmp[:], in1=vecs[:, 128:256], op=ALU.max), [b_vecs, b_ltmp], [b_ltmp])
        P.op("dve", lambda e: e.tensor_reduce(out=lamt[:, 2:4], in_=ltmp[:].rearrange("p (a b) -> p a b", a=2), axis=AX.X, op=ALU.max), [b_ltmp, b_lam], [b_lam])
        P.op("dve", lambda e: e.scalar_tensor_tensor(out=lamt[:, 1:2], in0=lamt[:, 2:3], scalar=-8.0, in1=lamt[:, 3:4], op0=ALU.mult, op1=ALU.mult), [b_lam], [b_lam])

        with ExitStack() as s0:
            ct = sb("ct", [128, 16], stack=s0); b_ct = Buf()
            cbl = sb("cbl", [128, 16, 128], stack=s0); b_cbl = Buf()
            wst = [sb("wada%d" % i, [128, 3 * D], stack=s0) for i in range(2)]
            b_wst = [Buf(), Buf()]
            bada = sb("bada", [128, 3 * D], stack=s0); b_bada = Buf()
            P.dma(ct[:], CT[:, :], [], [b_ct], b_ct)
            P.dma(bada[:], BADA[:, :], [], [b_bada], b_bada)
            P.op("act", lambda e: e.activation(out=ct[:], in_=ct[:], func=AF.Silu), [b_ct], [b_ct])
            for j in range(16):
                P.op("dve", lambda e, j=j: e.tensor_scalar_mul(out=cbl[:, j, :], in0=ones, scalar1=ct[:, j:j + 1]), [b_ct, b_cst], [b_cbl])
            for k in range(8):
                P.dma(wst[k % 2][:], WADA[k * 128:(k + 1) * 128, :], [], [b_wst[k % 2]], b_wst[k % 2])
                for g in range(6):
                    P.op("pe", lambda e, k=k, g=g: e.matmul(psum[g][:, :], lhsT=cbl[:, k, :], rhs=wst[k % 2][:, g * 512:(g + 1) * 512], start=(k == 0), stop=(k == 7)),
                         [b_cbl, b_wst[k % 2]], [PB[g]])
            for g in range(6):
                dsti = [1, 1, 0, 0, 2, 2][g]
                addc = 1.0 if dsti == 0 else 0.0
                P.op("dve", lambda e, g=g, dsti=dsti: e.tensor_tensor(out=mod[:, dsti, (g % 2) * 512:(g % 2) * 512 + 512], in0=psum[g][:, :], in1=bada[:, g * 512:(g + 1) * 512], op=ALU.add),
                     [PB[g], b_bada], [b_mod])
            P.op("pool", lambda e: e.tensor_scalar_add(out=mod[:, 0, :], in0=mod[:, 0, :], scalar1=1.0), [b_mod], [b_mod])
            for k in range(8):
                P.dma(wst[k % 2][:], WADA[k * 128:(k + 1) * 128, :], [], [b_wst[k % 2]], b_wst[k % 2])
                for g in range(4):
                    P.op("pe", lambda e, k=k, g=g: e.matmul(psum[g][:, :], lhsT=cbl[:, 8 + k, :], rhs=wst[k % 2][:, g * 512:(g + 1) * 512], start=(k == 0), stop=(k == 7)),
                         [b_cbl, b_wst[k % 2]], [PB[g]])
            for g in range(4):
                dsti = [4, 4, 3, 3][g]
                P.op("dve", lambda e, g=g, dsti=dsti: e.tensor_tensor(out=mod[:, dsti, (g % 2) * 512:(g % 2) * 512 + 512], in0=psum[g][:, :], in1=bada[:, g * 512:(g + 1) * 512], op=ALU.add),
                     [PB[g], b_bada], [b_mod])
            P.op("pool", lambda e: e.tensor_scalar_add(out=mod[:, 3, :], in0=mod[:, 3, :], scalar1=1.0), [b_mod], [b_mod])

        P.barrier()
        xt = sb("xt", [128, D]); b_xt = Buf()
        xh = sb("xh", [4, D]); b_xh = Buf()
        junk = sb("junk", [128, D]); b_junk = Buf()
        st = sb("st", [128, 8]); b_st = Buf()
        hb = sb("hb", [128, D], BF16); b_hb = Buf()
        hhb = sb("hhb", [4, D], BF16); b_hhb = Buf()
        HT = sb("HT", [128, 8, 132], BF16); b_HT = Buf()

        def make_HT(v, is_ctx):
            mi = 3 if is_ctx else 0
            P.dma(xt[:], XV[v * 128:(v + 1) * 128, :], [], [b_xt], b_xt)
            P.dma(xh[:], XH[v * 4:(v + 1) * 4, :], [], [b_xh], b_xh)
            for (src, bsrc, np_, col, dst, bdst) in ((xt, b_xt, 128, 0, hb, b_hb), (xh, b_xh, 4, 2, hhb, b_hhb)):
                P.op("act", lambda e, src=src, np_=np_, col=col: e.activation(out=junk[0:np_, :], in_=src[0:np_, :], func=AF.Square, accum_out=st[0:np_, col:col + 1]),
                     [bsrc], [b_junk, b_st])
                P.op("act", lambda e, np_=np_, col=col: e.activation(out=st[0:np_, col + 1:col + 2], in_=st[0:np_, col:col + 1], func=AF.Ln, scale=1.0 / D, bias=epsc[0:np_, 0:1]),
                          [b_st, b_eps], [b_st])
                P.op("act", lambda e, np_=np_, col=col: e.activation(out=st[0:np_, col + 1:col + 2], in_=st[0:np_, col + 1:col + 2], func=AF.Exp, scale=-0.5),
                     [b_st], [b_st])
                if np_ == 4:
                    P.op("dve", lambda e, v=v: e.tensor_tensor(out=st[0:4, 3:4], in0=st[0:4, 3:4], in1=hm[:, v:v + 1], op=ALU.mult), [b_st, b_hm], [b_st])
                P.op("dve", lambda e, src=src, np_=np_, col=col: e.scalar_tensor_tensor(out=junk[0:np_, :], in0=src[0:np_, :], scalar=st[0:np_, col + 1:col + 2], in1=mod[0:np_, mi, :], op0=ALU.mult, op1=ALU.mult),
                     [bsrc, b_st, b_mod, b_junk], [b_junk])
                if np_ == 4:
                    P.op("dve", lambda e, v=v, dst=dst: e.scalar_tensor_tensor(out=dst[0:4, :], in0=mod[0:4, mi + 1, :], scalar=hm[:, v:v + 1], in1=junk[0:4, :], op0=ALU.mult, op1=ALU.add),
                         [b_junk, b_mod, b_hm], [bdst])
                else:
                    P.op("pool", lambda e, dst=dst: e.tensor_tensor(out=dst[:, :], in0=junk[:, :], in1=mod[:, mi + 1, :], op=ALU.add), [b_junk, b_mod], [bdst])
            for k in range(8):
                bank, slot = divmod(k, 3)
                o = psum[bank][:, slot * 132:(slot + 1) * 132]
                P.op("pe", lambda e, k=k, o=o: e.matmul(o, lhsT=hb[:, k * 128:(k + 1) * 128], rhs=SelM, start=True, stop=False), [b_hb, b_cbf], [PB[bank]], sig=False)
                P.op("pe", lambda e, k=k, o=o: e.matmul(o, lhsT=hhb[0:4, k * 128:(k + 1) * 128], rhs=SelH, start=False, stop=True), [b_hhb, b_cbf], [PB[bank]])
            for bank in range(3):
                n = 3 if bank < 2 else 2
                P.op("act", lambda e, bank=bank, n=n: e.activation(out=HT[:, bank * 3:bank * 3 + n, :], in_=psum[bank][:, 0:n * 132].rearrange("p (a b) -> p a b", a=n), func=AF.Copy),
                     [PB[bank]], [b_HT])

        def load_w_bf16(dst, bdst, col0, ncols, stage, bstage, dcol0=0):
            i = 0
            for k in range(8):
                for c0 in range(0, ncols, 1024):
                    n = min(1024, ncols - c0)
                    s_, bs_ = stage[i % 2], bstage[i % 2]
                    P.dma(s_[:, 0:n], WIN[k * 128:(k + 1) * 128, col0 + c0:col0 + c0 + n], [], [bs_], bs_)
                    eng = ["act", "pool", "dve"][i % 3]
                    if eng == "act":
                        P.op("act", lambda e, s_=s_, k=k, c0=c0, n=n: e.activation(out=dst[:, k, dcol0 + c0:dcol0 + c0 + n], in_=s_[:, 0:n], func=AF.Copy), [bs_], [bdst])
                    else:
                        P.op(eng, lambda e, s_=s_, k=k, c0=c0, n=n: e.tensor_copy(out=dst[:, k, dcol0 + c0:dcol0 + c0 + n], in_=s_[:, 0:n]), [bs_], [bdst])
                    i += 1

        def qk_norm_rope(ps_banks, pbufs, gain, tab, b_tab, dst, b_dst, tmp, b_tmp):
            for hf in range(2):
                P.op("act", lambda e, hf=hf: e.activation(out=tmp[0][:, hf * 512:(hf + 1) * 512], in_=ps_banks[hf][:, :], func=AF.Square), [pbufs[hf]], [b_tmp[0]])
            P.op("dve", lambda e: e.tensor_reduce(out=tmp[3][:, 0:16], in_=tmp[0][:].rearrange("p (g d) -> p g d", d=64), axis=AX.X, op=ALU.add),
                 [b_tmp[0]], [b_tmp[3]])
            P.op("act", lambda e: e.activation(out=tmp[3][:, 0:16], in_=tmp[3][:, 0:16], func=AF.Ln, scale=1.0 / 64, bias=epsc[:, 0:1]), [b_tmp[3], b_eps], [b_tmp[3]])
            P.op("act", lambda e: e.activation(out=tmp[3][:, 0:16], in_=tmp[3][:, 0:16], func=AF.Exp, scale=-0.5), [b_tmp[3]], [b_tmp[3]])
            for hf in range(2):
                P.op("dve", lambda e, hf=hf: e.tensor_tensor(out=tmp[0][:, hf * 512:(hf + 1) * 512].rearrange("p (g d) -> p g d", d=64),
                                                            in0=ps_banks[hf][:, :].rearrange("p (g d) -> p g d", d=64),
                                                            in1=tmp[3][:, hf * 8:(hf + 1) * 8].unsqueeze(2).to_broadcast([128, 8, 64]), op=ALU.mult),
                     [pbufs[hf], b_tmp[3], b_tmp[0]], [b_tmp[0]])
            P.op("pool", lambda e: e.tensor_tensor(out=tmp[1][:].rearrange("p (g d) -> p g d", d=64), in0=tmp[0][:].rearrange("p (g d) -> p g d", d=64),
                                                   in1=gain.unsqueeze(1).to_broadcast([128, 16, 64]), op=ALU.mult), [b_tmp[0], b_vecs], [b_tmp[1]])
            xv = tmp[1][:].rearrange("p (g h t f) -> p g h t f", g=16, h=2, t=2)
            tv = tab[:].rearrange("p (h t f) -> p h t f", h=2, t=2)
            dv = dst[:].rearrange("p (g h t f) -> p g h t f", g=16, h=2, t=2)
            av = tmp[2][:].rearrange("p (g h t f) -> p g h t f", g=16, h=2, t=2)
            for hh in range(2):
                x1 = xv[:, :, hh, 0, :]
                x2 = xv[:, :, hh, 1, :]
                cs = tv[:, hh, 0, :].unsqueeze(1).to_broadcast([128, 16, 16])
                sn = tv[:, hh, 1, :].unsqueeze(1).to_broadcast([128, 16, 16])
                a1 = av[:, :, hh, 0, :]
                a2 = av[:, :, hh, 1, :]
                e1, e2 = ("dve", "pool")
                P.op(e1, lambda e, x1=x1, cs=cs, a1=a1: e.tensor_tensor(out=a1, in0=x1, in1=cs, op=ALU.mult), [b_tmp[1], b_tab], [b_tmp[2]])
                P.op(e2, lambda e, x2=x2, sn=sn, a2=a2: e.tensor_tensor(out=a2, in0=x2, in1=sn, op=ALU.mult), [b_tmp[1], b_tab], [b_tmp[2]])
                P.op(e1, lambda e, a1=a1, a2=a2, hh=hh: e.tensor_tensor(out=dv[:, :, hh, 0, :], in0=a1, in1=a2, op=ALU.subtract), [b_tmp[2]], [b_dst])
                P.op(e1, lambda e, x2=x2, cs=cs, a1=a1: e.tensor_tensor(out=a1, in0=x2, in1=cs, op=ALU.mult), [b_tmp[1], b_tab, b_tmp[2]], [b_tmp[2]])
                P.op(e2, lambda e, x1=x1, sn=sn, a2=a2: e.tensor_tensor(out=a2, in0=x1, in1=sn, op=ALU.mult), [b_tmp[1], b_tab, b_tmp[2]], [b_tmp[2]])
                P.op(e1, lambda e, a1=a1, a2=a2, hh=hh: e.tensor_tensor(out=dv[:, :, hh, 1, :], in0=a1, in1=a2, op=ALU.add), [b_tmp[2]], [b_dst])


        with ExitStack() as s1:
            W1 = sb("W1", [128, 8, 5120], BF16, stack=s1); b_W1 = Buf()
            wab = sb("wab", [128, 2, 8, 16], BF16, stack=s1); b_wab = Buf()
            wabs = sb("wabs", [128, 2, 8, 16], stack=s1); b_wabs = Buf()
            cw = sb("cw", [128, 2, 24, 5], stack=s1); b_cw = Buf()
            stage = [junk, xt]
            bstage = [b_junk, b_xt]
            tab = sb("tab1", [128, 64], stack=s1); b_tab = Buf()
            tmpA = [sb("tmpA1_%d" % i, [128, D], stack=s1) for i in range(3)] + [sb("tmpA31", [128, 16], stack=s1)]
            b_tmpA = [Buf() for _ in range(4)]
            krot = sb("krot1", [128, D], BF16, stack=s1); b_krot = Buf()
            load_w_bf16(W1, b_W1, 0, 3072, stage, bstage, 0)
            load_w_bf16(W1, b_W1, 5152, 2048, stage, bstage, 3072)
            P.dma(wabs[:], WAB.rearrange("s (k p) c -> p s k c", p=128), [], [b_wabs], b_wabs)
            P.op("dve", lambda e: e.tensor_copy(out=wab[:], in_=wabs[:]), [b_wabs], [b_wab])
            P.dma(cw[:], CW.rearrange("s (c p) j -> p s c j", p=128), [], [b_cw], b_cw)

            Sst = sb("Sst", [128, H, 128], stack=s1); b_S = [Buf() for _ in range(H)]
            pre = sb("pre", [128, 3, 132], stack=s1); b_pre = Buf()
            acc = [sb("acc%d" % i, [128, 128], stack=s1) for i in range(2)]; b_acc = [Buf(), Buf()]
            ptmp = sb("ptmp", [128, 128], stack=s1); b_ptmp = Buf()
            qkvT = sb("qkvT", [128, 24, 128], stack=s1); b_qkvT = [Buf() for _ in range(24)]
            gb = sb("gb", [128, 6, 8], stack=s1); b_gb = Buf()
            from types import SimpleNamespace
            TS = []
            for si_ in range(2):
                T = SimpleNamespace()
                T.Wb = [sb("Wb%d_%d" % (i, si_), [128, 512], stack=s1) for i in range(2)]; T.b_Wb = [Buf(), Buf()]
                T.NY = sb("NY_%d" % si_, [128, 256], stack=s1); T.b_NY = Buf()
                T.NYm = sb("NYm_%d" % si_, [128, 256], stack=s1); T.b_NYm = Buf()
                T.TU = sb("TU_%d" % si_, [128, 256], stack=s1); T.b_TU = Buf()
                T.TU2 = sb("TU2_%d" % si_, [128, 256], stack=s1); T.b_TU2 = Buf()
                T.Wm = sb("Wm_%d" % si_, [128, 256], stack=s1); T.b_Wm = Buf()
                T.LgIb = sb("LgIb_%d" % si_, [128, 256], stack=s1); T.b_LgIb = Buf()
                T.Em = sb("Em_%d" % si_, [128, 128], stack=s1); T.b_Em = Buf()
                T.ETm = sb("ETm_%d" % si_, [128, 128], stack=s1); T.b_ETm = Buf()
                T.EMs = sb("EMs_%d" % si_, [128, 128], stack=s1); T.b_EMs = Buf()
                T.ETs = sb("ETs_%d" % si_, [128, 128], stack=s1); T.b_ETs = Buf()
                T.ETi = sb("ETi_%d" % si_, [128, 128], stack=s1); T.b_ETi = Buf()
                T.npos = T.LgIb; T.b_npos = T.b_LgIb
                T.sq = sb("sq_%d" % si_, [128, 128], stack=s1); T.b_sq = Buf()
                T.rs = sb("rs_%d" % si_, [128, 128], stack=s1); T.b_rs = Buf()
                T.kTn = sb("kTn_%d" % si_, [128, 128], stack=s1); T.b_kTn = Buf()
                T.qTn = sb("qTn_%d" % si_, [128, 128], stack=s1); T.b_qTn = Buf()
                T.Kbp = sb("Kbp_%d" % si_, [128, 128], stack=s1); T.b_Kbp = Buf()
                T.kd = sb("kd_%d" % si_, [128, 128], stack=s1); T.b_kd = Buf()
                T.Vb = sb("Vb_%d" % si_, [128, 128], stack=s1); T.b_Vb = Buf()
                T.TT = sb("TT_%d" % si_, [128, 128], stack=s1); T.b_TT = Buf()
                T.wT = sb("wT_%d" % si_, [128, 128], stack=s1); T.b_wT = Buf()
                T.usb = sb("usb_%d" % si_, [128, 128], stack=s1); T.b_usb = Buf()
                T.vnew = sb("vnew_%d" % si_, [128, 128], stack=s1); T.b_vnew = Buf()
                T.egl = sb("egl_%d" % si_, [128, 1], stack=s1); T.b_egl = Buf()
                T.eR = sb("eR_%d" % si_, [128, 128], stack=s1); T.b_eR = Buf()
                T.QdT = sb("QdT_%d" % si_, [128, 128], stack=s1); T.b_QdT = Buf()
                T.qkT = sb("qkT_%d" % si_, [128, 128], stack=s1); T.b_qkT = Buf()
                T.banks = (0, 1, 2, 2) if si_ == 0 else (3, 4, 5, 5)
                TS.append(T)
            osb = junk; b_osb = b_junk
            KTsb = hb[:].rearrange("p (h t) -> p h t", h=H); b_KTsb = b_hb
            Vp = sb("Vp", [128, H, 129], BF16, stack=s1); b_Vp = Buf()
            b_KTS = Buf(); b_VPS = Buf(); b_OS = Buf()
            P.op("pool", lambda e: e.memset(Vp[:], 1.0), [], [b_Vp])

            for v in range(NV):
                scan = 0 if v < S2 else 1
                is_ctx = v in (0, 1, S2, S2 + 1)
                own = (OWN0 <= v < S2) or (v >= OWN1)
                kv = v < NKV
                if v in (0, S2):
                    for h in range(H):
                        P.op("pool", lambda e, h=h: e.memset(Sst[:, h, :], 0.0), [], [b_S[h]])
                make_HT(v, is_ctx)
                PA = 3
                for k in range(8):
                    P.op("pe", lambda e, k=k, scan=scan: e.matmul(psum[PA][:, 0:16], lhsT=HT[:, k, 2:130], rhs=wab[:, scan, k, :], start=(k == 0), stop=(k == 7)), [b_HT, b_wab], [PB[PA]], sig=(k == 7))
                P.op("dve", lambda e, scan=scan: e.tensor_tensor(out=gb[:, 4, :], in0=psum[PA][:, 0:8], in1=dtb[:, scan, :], op=ALU.add), [PB[PA], b_dtb], [b_gb])
                P.op("act", lambda e: e.activation(out=gb[:, 4, :], in_=gb[:, 4, :], func=AF.Exp), [b_gb], [b_gb])
                P.op("act", lambda e: e.activation(out=gb[:, 4, :], in_=gb[:, 4, :], func=AF.Ln, bias=1.0), [b_gb], [b_gb])
                P.op("dve", lambda e, v=v, scan=scan: e.scalar_tensor_tensor(out=gb[:, 0, :], in0=gb[:, 4, :], scalar=gm[:, v:v + 1], in1=negA[:, scan, :], op0=ALU.mult, op1=ALU.mult), [b_gb, b_gm, b_negA], [b_gb])
                P.op("act", lambda e: e.activation(out=gb[:, 5, :], in_=psum[PA][:, 8:16], func=AF.Sigmoid), [PB[PA], b_gb], [b_gb])
                P.op("dve", lambda e, v=v: e.tensor_scalar_mul(out=gb[:, 1, :], in0=gb[:, 5, :], scalar1=gm[:, v:v + 1]), [b_gb, b_gm], [b_gb])
                P.op("pe", lambda e: e.matmul(psum[PA][:, 16:24], lhsT=Ltri, rhs=gb[:, 0, :], start=True, stop=True), [b_cst, b_gb], [PB[PA]])
                P.op("dve", lambda e: e.tensor_copy(out=gb[:, 2, :], in_=psum[PA][:, 16:24]), [PB[PA], b_gb], [b_gb])
                P.op("act", lambda e: e.activation(out=gb[:, 3, :], in_=psum[PA][:, 16:24], func=AF.Exp), [PB[PA], b_gb], [b_gb])
                P.op("dve", lambda e: e.tensor_tensor(out=gb[:, 3, :], in0=gb[:, 3, :], in1=gb[:, 1, :], op=ALU.mult), [b_gb], [b_gb])

                def pre_steps(cbs):
                    for gi in range(0, len(cbs), 3):
                        grp = cbs[gi:gi + 3]
                        bank = 6 + (gi // 3) % 2
                        for si, cb in enumerate(grp):
                            for k in range(8):
                                P.op("pe", lambda e, k=k, cb=cb, si=si, bank=bank: e.matmul(psum[bank][:, si * 132:(si + 1) * 132], lhsT=W1[:, k, cb * 128:(cb + 1) * 128], rhs=HT[:, k, :], start=(k == 0), stop=(k == 7)),
                                     [b_W1, b_HT], [PB[bank]], sig=(k == 7))
                                yield
                        n = len(grp)
                        P.op("act", lambda e, bank=bank, n=n: e.activation(out=pre[:, 0:n, :], in_=psum[bank][:, 0:n * 132].rearrange("p (a b) -> p a b", a=n), func=AF.Copy), [PB[bank]], [b_pre])
                        yield
                        for si, cb in enumerate(grp):
                            eng = "dve" if (cb % 2 == 0) else "pool"
                            a_, ba_ = acc[cb % 2], b_acc[cb % 2]
                            P.op(eng, lambda e, si=si, cb=cb, a_=a_, scan=scan: e.tensor_scalar_mul(out=a_[:], in0=pre[:, si, 0:128], scalar1=cw[:, scan, cb, 0:1]), [b_pre, b_cw], [ba_])
                            yield
                            for j in range(1, 5):
                                if eng == "dve":
                                    P.op(eng, lambda e, si=si, cb=cb, a_=a_, j=j, scan=scan: e.scalar_tensor_tensor(out=a_[:], in0=pre[:, si, j:j + 128], scalar=cw[:, scan, cb, j:j + 1], in1=a_[:], op0=ALU.mult, op1=ALU.add),
                                         [b_pre, b_cw, ba_], [ba_])
                                    yield
                                else:
                                    P.op(eng, lambda e, si=si, cb=cb, j=j, scan=scan: e.tensor_scalar_mul(out=ptmp[:], in0=pre[:, si, j:j + 128], scalar1=cw[:, scan, cb, j:j + 1]), [b_pre, b_cw], [b_ptmp])
                                    yield
                                    P.op(eng, lambda e, a_=a_: e.tensor_tensor(out=a_[:], in0=a_[:], in1=ptmp[:], op=ALU.add), [b_ptmp, ba_], [ba_])
                                    yield
                            P.op("act", lambda e, cb=cb, a_=a_: e.activation(out=qkvT[:, cb, :], in_=a_[:], func=AF.Silu), [ba_], [b_qkvT[cb]])
                            yield

                    return
                    yield
                def kv_steps():
                    P.dma(tab[:], ROPE[v * 128:(v + 1) * 128, :], [], [b_tab], b_tab)
                    yield
                    for hf in range(2):
                        for k in range(8):
                            P.op("pe", lambda e, k=k, hf=hf: e.matmul(psum[6 + hf][:, :], lhsT=HT[:, k, 2:130], rhs=W1[:, k, 3072 + hf * 512:3072 + (hf + 1) * 512], start=(k == 0), stop=(k == 7)),
                                 [b_HT, b_W1], [PB[6 + hf]], sig=(k == 7))
                            yield
                    qk_norm_rope([psum[6], psum[7]], [PB[6], PB[7]], knw, tab, b_tab, krot, b_krot, tmpA, b_tmpA)
                    yield
                    for h in range(H):
                        bank = 6 + h // 4
                        P.op("pe", lambda e, h=h, bank=bank: e.matmul(psum[bank][:, (h % 4) * 128:(h % 4 + 1) * 128], lhsT=krot[:, h * 128:(h + 1) * 128], rhs=identb, start=True, stop=True),
                             [b_krot, b_cbf], [PB[bank]])
                        yield
                    for hf in range(2):
                        P.op("act", lambda e, hf=hf: e.activation(out=KTsb[:, hf * 4:(hf + 1) * 4, :], in_=psum[6 + hf][:, :].rearrange("p (a b) -> p a b", a=4), func=AF.Copy), [PB[6 + hf]], [b_KTsb])
                        yield
                    P.dma(KTS[:, :, v * 128:(v + 1) * 128].rearrange("h p t -> p h t"), KTsb[:], [b_KTsb], [b_KTS], b_KTS)
                    yield
                    for hf in range(2):
                        for k in range(8):
                            P.op("pe", lambda e, k=k, hf=hf: e.matmul(psum[6 + hf][:, :], lhsT=HT[:, k, 2:130], rhs=W1[:, k, 4096 + hf * 512:4096 + (hf + 1) * 512], start=(k == 0), stop=(k == 7)),
                                 [b_HT, b_W1], [PB[6 + hf]], sig=(k == 7))
                            yield
                    for hf in range(2):
                        P.op("act", lambda e, hf=hf: e.activation(out=Vp[:, hf * 4:(hf + 1) * 4, 0:128], in_=psum[6 + hf][:, :].rearrange("p (a b) -> p a b", a=4), func=AF.Copy), [PB[6 + hf]], [b_Vp])
                        yield
                    P.dma(VPS[:, :, v, :].rearrange("h p c -> p h c"), Vp[:], [b_Vp], [b_VPS], b_VPS)
                    yield

                    return
                    yield
                def head_steps(h, T):
                        qcb, kcb, vcb = h, 8 + h, 16 + h
                        PR, PG, PS_, PU = T.banks
                        P.op("pool", lambda e, kcb=kcb: e.tensor_tensor(out=T.sq[:], in0=qkvT[:, kcb, :], in1=qkvT[:, kcb, :], op=ALU.mult), [b_qkvT[kcb]], [T.b_sq])
                        yield
                        P.op("pe", lambda e: e.matmul(psum[PR][:, 256:384], lhsT=ones, rhs=T.sq[:], start=True, stop=True), [b_cst, T.b_sq], [PB[PR]])
                        yield
                        P.op("act", lambda e: e.activation(out=T.rs[:], in_=psum[PR][:, 256:384], func=AF.Ln, bias=epsc[:, 0:1]), [PB[PR], b_eps], [T.b_rs])
                        yield
                        P.op("act", lambda e: e.activation(out=T.rs[:], in_=T.rs[:], func=AF.Exp, scale=-0.5), [T.b_rs], [T.b_rs])
                        yield
                        P.op("dve", lambda e, kcb=kcb: e.tensor_tensor(out=T.kTn[:], in0=qkvT[:, kcb, :], in1=T.rs[:], op=ALU.mult), [b_qkvT[kcb], T.b_rs], [T.b_kTn])
                        yield
                        if own:
                            P.op("pool", lambda e, qcb=qcb: e.tensor_tensor(out=T.sq[:], in0=qkvT[:, qcb, :], in1=qkvT[:, qcb, :], op=ALU.mult), [b_qkvT[qcb]], [T.b_sq])
                            yield
                            P.op("pe", lambda e: e.matmul(psum[PR][:, 384:512], lhsT=ones, rhs=T.sq[:], start=True, stop=True), [b_cst, T.b_sq], [PB[PR]])
                            yield
                            P.op("act", lambda e: e.activation(out=T.rs[:], in_=psum[PR][:, 384:512], func=AF.Ln, bias=epsc[:, 0:1]), [PB[PR], b_eps], [T.b_rs])
                            yield
                            P.op("act", lambda e: e.activation(out=T.rs[:], in_=T.rs[:], func=AF.Exp, scale=-0.5), [T.b_rs], [T.b_rs])
                            yield
                            P.op("dve", lambda e, qcb=qcb: e.scalar_tensor_tensor(out=T.qTn[:], in0=qkvT[:, qcb, :], scalar=128.0 ** -0.5, in1=T.rs[:], op0=ALU.mult, op1=ALU.mult), [b_qkvT[qcb], T.b_rs], [T.b_qTn])
                            yield
                        P.op("dve", lambda e, h=h: e.tensor_scalar_mul(out=T.LgIb[:, 0:128], in0=Ltri, scalar1=gb[:, 0, h:h + 1]), [b_cst, b_gb], [T.b_LgIb])
                        yield
                        P.op("pool", lambda e, h=h: e.tensor_scalar_mul(out=T.LgIb[:, 128:256], in0=ident, scalar1=gb[:, 1, h:h + 1]), [b_cst, b_gb], [T.b_LgIb])
                        yield
                        P.op("pe", lambda e: e.matmul(psum[PR][:, 0:256], lhsT=ones, rhs=T.LgIb[:], start=True, stop=True), [b_cst, T.b_LgIb], [PB[PR]])
                        yield
                        P.op("dve", lambda e, h=h: e.tensor_scalar(out=T.npos[:, 0:128], in0=psum[PR][:, 0:128], scalar1=gb[:, 2, h:h + 1], scalar2=0.0, op0=ALU.subtract, op1=ALU.max), [PB[PR], b_gb], [T.b_npos])
                        yield
                        P.op("dve", lambda e, h=h: e.tensor_scalar(out=T.npos[:, 128:256], in0=psum[PR][:, 0:128], scalar1=gb[:, 2, h:h + 1], scalar2=0.0, op0=ALU.subtract, op1=ALU.min), [PB[PR], b_gb], [T.b_npos])
                        yield
                        P.op("act", lambda e: e.activation(out=T.Em[:], in_=T.npos[:, 0:128], func=AF.Exp, scale=-1.0), [T.b_npos], [T.b_Em])
                        yield
                        P.op("act", lambda e: e.activation(out=T.ETm[:], in_=T.npos[:, 128:256], func=AF.Exp), [T.b_npos], [T.b_ETm])
                        yield
                        P.op("pool", lambda e: e.tensor_tensor(out=T.EMs[:], in0=T.Em[:], in1=negMs, op=ALU.mult), [T.b_Em, b_cst], [T.b_EMs])
                        yield
                        P.op("pool", lambda e: e.tensor_tensor(out=T.ETs[:], in0=T.ETm[:], in1=negMsT, op=ALU.mult), [T.b_ETm, b_cst], [T.b_ETs])
                        yield
                        P.op("pe", lambda e: e.matmul(psum[PG][:, 0:128], lhsT=T.kTn[:], rhs=T.kTn[:], start=True, stop=True), [T.b_kTn], [PB[PG]])
                        yield
                        if own:
                            P.op("pe", lambda e: e.matmul(psum[PG][:, 128:256], lhsT=T.kTn[:], rhs=T.qTn[:], start=True, stop=True), [T.b_kTn, T.b_qTn], [PB[PG]])
                            yield
                        P.op("dve", lambda e, h=h: e.scalar_tensor_tensor(out=T.NY[:, 0:128], in0=psum[PG][:, 0:128], scalar=gb[:, 1, h:h + 1], in1=T.EMs[:], op0=ALU.mult, op1=ALU.mult), [PB[PG], b_gb, T.b_EMs], [T.b_NY])
                        yield
                        P.op("dve", lambda e: e.tensor_tensor(out=T.NY[:, 128:256], in0=psum[PG][:, 0:128], in1=T.ETs[:], op=ALU.mult), [PB[PG], T.b_ETs], [T.b_NY])
                        yield
                        P.op("dve", lambda e: e.tensor_tensor(out=T.NY[:, 128:256], in0=T.NY[:, 128:256], in1=psum[PR][:, 128:256], op=ALU.mult), [PB[PR], T.b_NY], [T.b_NY])
                        yield
                        P.op("pe", lambda e: e.matmul(psum[PR][:, 256:384], lhsT=T.kTn[:], rhs=ident, start=True, stop=True), [T.b_kTn, b_cst], [PB[PR]])
                        yield
                        P.op("pe", lambda e, vcb=vcb: e.matmul(psum[PR][:, 384:512], lhsT=qkvT[:, vcb, :], rhs=ident, start=True, stop=True), [b_qkvT[vcb], b_cst], [PB[PR]])
                        yield
                        P.op("dve", lambda e, h=h: e.tensor_scalar_mul(out=T.Kbp[:], in0=psum[PR][:, 256:384], scalar1=gb[:, 3, h:h + 1]), [PB[PR], b_gb], [T.b_Kbp])
                        yield
                        P.op("dve", lambda e: e.tensor_scalar_mul(out=T.kd[:], in0=psum[PR][:, 256:384], scalar1=T.ETm[:, 127:128]), [PB[PR], T.b_ETm], [T.b_kd])
                        yield
                        P.op("dve", lambda e, h=h: e.tensor_scalar_mul(out=T.Vb[:], in0=psum[PR][:, 384:512], scalar1=gb[:, 1, h:h + 1]), [PB[PR], b_gb], [T.b_Vb])
                        yield
                        if own:
                            P.op("pool", lambda e: e.tensor_tensor(out=T.ETi[:], in0=T.ETm[:], in1=MiT, op=ALU.mult), [T.b_ETm, b_cst], [T.b_ETi])
                            yield
                            P.op("dve", lambda e: e.tensor_tensor(out=T.qkT[:], in0=psum[PG][:, 128:256], in1=T.ETi[:], op=ALU.mult), [PB[PG], T.b_ETi], [T.b_qkT])
                            yield
                        P.op("pool", lambda e: e.tensor_tensor(out=T.Wb[0][:, 0:128], in0=T.NY[:, 0:128], in1=BD32, op=ALU.mult), [T.b_NY, b_cst], [T.b_Wb[0]])
                        yield
                        P.op("pool", lambda e: e.tensor_tensor(out=T.Wb[0][:, 256:384], in0=T.NY[:, 128:256], in1=BD32, op=ALU.mult), [T.b_NY, b_cst], [T.b_Wb[0]])
                        yield
                        P.op("pool", lambda e: e.tensor_tensor(out=T.Wb[1][:, 128:256], in0=T.Wb[0][:, 0:128], in1=ident, op=ALU.add), [T.b_Wb[0], b_cst], [T.b_Wb[1]])
                        yield
                        P.op("pool", lambda e: e.tensor_tensor(out=T.Wb[1][:, 384:512], in0=T.Wb[0][:, 256:384], in1=ident, op=ALU.add), [T.b_Wb[0], b_cst], [T.b_Wb[1]])
                        yield
                        for n in range(4):
                            cur, nxt = T.Wb[n % 2], T.Wb[(n + 1) % 2]
                            bc, bn = T.b_Wb[n % 2], T.b_Wb[(n + 1) % 2]
                            w_ = 128 if n == 0 else 256
                            P.op("pe", lambda e, cur=cur, w_=w_: e.matmul(psum[PS_][:, 0:w_], lhsT=cur[:, 256:384], rhs=cur[:, 0:w_], start=True, stop=True), [bc], [PB[PS_]])
                            yield
                            P.op("pe", lambda e, cur=cur, w_=w_: e.matmul(psum[PS_][:, 256:256 + w_], lhsT=cur[:, 0:128], rhs=cur[:, 256:256 + w_], start=True, stop=True), [bc], [PB[PS_]])
                            yield
                            P.op("act", lambda e, nxt=nxt: e.activation(out=nxt[:].rearrange("p (a b) -> p a b", a=2)[:, :, 0:128], in_=psum[PS_][:, :].rearrange("p (a b) -> p a b", a=2)[:, :, 0:128], func=AF.Copy), [PB[PS_]], [bn])
                            yield
                            if n > 0:
                                P.op("dve", lambda e, cur=cur, nxt=nxt: e.tensor_tensor(out=nxt[:].rearrange("p (a b) -> p a b", a=2)[:, :, 128:256], in0=psum[PS_][:, :].rearrange("p (a b) -> p a b", a=2)[:, :, 128:256],
                                                                                        in1=cur[:].rearrange("p (a b) -> p a b", a=2)[:, :, 128:256], op=ALU.add), [PB[PS_], bc], [bn])
                                yield
                        P.op("pe", lambda e: e.matmul(psum[PS_][:, 0:128], lhsT=T.Wb[0][:, 256:384], rhs=T.Wb[0][:, 128:256], start=True, stop=True), [T.b_Wb[0]], [PB[PS_]])
                        yield
                        P.op("pe", lambda e: e.matmul(psum[PS_][:, 128:256], lhsT=T.Wb[0][:, 0:128], rhs=T.Wb[0][:, 384:512], start=True, stop=True), [T.b_Wb[0]], [PB[PS_]])
                        yield
                        P.op("dve", lambda e: e.tensor_tensor(out=T.TU[:].rearrange("p (a b) -> p a b", a=2), in0=psum[PS_][:, 0:256].rearrange("p (a b) -> p a b", a=2),
                                                              in1=T.Wb[0][:].rearrange("p (a b) -> p a b", a=2)[:, :, 128:256], op=ALU.add), [PB[PS_], T.b_Wb[0]], [T.b_TU])
                        yield
                        P.op("pool", lambda e: e.tensor_tensor(out=T.NYm[:, 0:128], in0=T.NY[:, 128:256], in1=M64, op=ALU.mult), [T.b_NY, b_cst], [T.b_NYm])
                        yield
                        P.op("pool", lambda e: e.tensor_tensor(out=T.NYm[:, 128:256], in0=T.NY[:, 0:128], in1=M64, op=ALU.mult), [T.b_NY, b_cst], [T.b_NYm])
                        yield
                        P.op("pe", lambda e: e.matmul(psum[PS_][:, 256:384], lhsT=T.NYm[:, 0:128], rhs=T.TU[:, 0:128], start=True, stop=True), [T.b_NYm, T.b_TU], [PB[PS_]])
                        yield
                        P.op("pe", lambda e: e.matmul(psum[PS_][:, 384:512], lhsT=T.NYm[:, 128:256], rhs=T.TU[:, 128:256], start=True, stop=True), [T.b_NYm, T.b_TU], [PB[PS_]])
                        yield
                        P.op("act", lambda e: e.activation(out=T.Wm[:], in_=psum[PS_][:, 256:512], func=AF.Copy), [PB[PS_]], [T.b_Wm])
                        yield
                        P.op("pe", lambda e: e.matmul(psum[PS_][:, 0:128], lhsT=T.TU[:, 128:256], rhs=T.Wm[:, 0:128], start=True, stop=True), [T.b_TU, T.b_Wm], [PB[PS_]])
                        yield
                        P.op("pe", lambda e: e.matmul(psum[PS_][:, 128:256], lhsT=T.TU[:, 0:128], rhs=T.Wm[:, 128:256], start=True, stop=True), [T.b_TU, T.b_Wm], [PB[PS_]])
                        yield
                        P.op("dve", lambda e: e.tensor_tensor(out=T.TU2[:], in0=psum[PS_][:, 0:256], in1=T.TU[:], op=ALU.add), [PB[PS_], T.b_TU], [T.b_TU2])
                        yield
                        P.op("pool", lambda e: e.tensor_tensor(out=T.NYm[:, 128:256], in0=T.NY[:, 0:128], in1=M128, op=ALU.mult), [T.b_NY, b_cst, T.b_NYm], [T.b_NYm])
                        yield
                        P.op("pe", lambda e: e.matmul(psum[PS_][:, 384:512], lhsT=T.NYm[:, 128:256], rhs=T.TU2[:, 128:256], start=True, stop=True), [T.b_NYm, T.b_TU2], [PB[PS_]])
                        yield
                        P.op("act", lambda e: e.activation(out=T.Wm[:, 128:256], in_=psum[PS_][:, 384:512], func=AF.Copy), [PB[PS_]], [T.b_Wm])
                        yield
                        P.op("pe", lambda e: e.matmul(psum[PS_][:, 128:256], lhsT=T.TU2[:, 0:128], rhs=T.Wm[:, 128:256], start=True, stop=True), [T.b_TU2, T.b_Wm], [PB[PS_]])
                        yield
                        P.op("dve", lambda e: e.tensor_tensor(out=T.TT[:], in0=psum[PS_][:, 128:256], in1=T.TU2[:, 128:256], op=ALU.add), [PB[PS_], T.b_TU2], [T.b_TT])
                        yield
                        P.op("pe", lambda e: e.matmul(psum[PU][:, 0:128], lhsT=T.TT[:], rhs=T.Vb[:], start=True, stop=True), [T.b_TT, T.b_Vb], [PB[PU]])
                        yield
                        P.op("pe", lambda e: e.matmul(psum[PU][:, 128:256], lhsT=T.Kbp[:], rhs=T.TT[:], start=True, stop=True), [T.b_TT, T.b_Kbp], [PB[PU]])
                        yield
                        P.op("act", lambda e: e.activation(out=T.wT[:], in_=psum[PU][:, 128:256], func=AF.Copy), [PB[PU]], [T.b_wT])
                        yield
                        P.op("act", lambda e: e.activation(out=T.usb[:], in_=psum[PU][:, 0:128], func=AF.Copy), [PB[PU]], [T.b_usb])
                        yield
                        P.op("pe", lambda e, h=h: e.matmul(psum[PU][:, 256:384], lhsT=T.wT[:], rhs=Sst[:, h, :], start=True, stop=True), [T.b_wT, b_S[h]], [PB[PU]])
                        yield
                        P.op("dve", lambda e: e.tensor_tensor(out=T.vnew[:], in0=T.usb[:], in1=psum[PU][:, 256:384], op=ALU.subtract), [T.b_usb, PB[PU]], [T.b_vnew])
                        yield
                        if own:
                            P.op("act", lambda e: e.activation(out=T.eR[:], in_=psum[PR][:, 0:128], func=AF.Exp), [PB[PR]], [T.b_eR])
                            yield
                            P.op("dve", lambda e: e.tensor_tensor(out=T.QdT[:], in0=T.qTn[:], in1=T.eR[:], op=ALU.mult), [T.b_qTn, T.b_eR], [T.b_QdT])
                            yield
                            P.op("pe", lambda e, h=h: e.matmul(psum[PG][:, 256:384], lhsT=T.QdT[:], rhs=Sst[:, h, :], start=True, stop=False), [T.b_QdT, b_S[h]], [PB[PG]])
                            yield
                            P.op("pe", lambda e: e.matmul(psum[PG][:, 256:384], lhsT=T.qkT[:], rhs=T.vnew[:], start=False, stop=True), [T.b_qkT, T.b_vnew], [PB[PG]])
                            yield
                            P.op("act", lambda e, h=h: e.activation(out=osb[:, h * 128:(h + 1) * 128], in_=psum[PG][:, 256:384], func=AF.Copy), [PB[PG]], [b_osb])
                            yield
                        P.op("act", lambda e: e.activation(out=T.egl[:], in_=psum[PR][:, 127:128], func=AF.Exp), [PB[PR]], [T.b_egl])
                        yield
                        P.op("pe", lambda e: e.matmul(psum[PU][:, 384:512], lhsT=T.kd[:], rhs=T.vnew[:], start=True, stop=True), [T.b_kd, T.b_vnew], [PB[PU]])
                        yield
                        P.op("dve", lambda e, h=h: e.scalar_tensor_tensor(out=Sst[:, h, :], in0=Sst[:, h, :], scalar=T.egl[:, 0:1], in1=psum[PU][:, 384:512], op0=ALU.mult, op1=ALU.add), [b_S[h], T.b_egl, PB[PU]], [b_S[h]])
                        yield
                def pair_cbs(p):
                    hs = (2 * p, 2 * p + 1)
                    c = [hs[0], hs[1]] if own else []
                    return c + [8 + hs[0], 8 + hs[1], 16 + hs[0], 16 + hs[1]]
                for _ in pre_steps(pair_cbs(0)):
                    pass
                for p_ in range(4):
                    if p_ < 3:
                        aux = pre_steps(pair_cbs(p_ + 1))
                    elif kv:
                        aux = kv_steps()
                    else:
                        aux = iter(())
                    gens = [head_steps(2 * p_, TS[0]), head_steps(2 * p_ + 1, TS[1]), aux]
                    alive = [True, True, True]
                    while any(alive):
                        for gi_ in range(3):
                            if alive[gi_]:
                                try:
                                    next(gens[gi_])
                                except StopIteration:
                                    alive[gi_] = False
                if own:
                    oi = (v - OWN0) if scan == 0 else (v - OWN1)
                    P.dma(OS[scan, oi, :, :], osb[:], [b_osb], [b_OS], b_OS)

        P.barrier()

        b_ZS = Buf(); b_GS = Buf()
        YbT = sb("YbT", [128, H, 2048], BF16); b_YbT = Buf()
        s_qt = ExitStack()
        QT = sb("QT", [128, H, 2048], BF16, stack=s_qt); b_QT = Buf()
        with ExitStack() as s2:
            W2 = sb("W2", [128, 8, 3072], BF16, stack=s2); b_W2 = Buf()
            stage = [junk, xt]
            bstage = [b_junk, b_xt]
            tabB = sb("tabB2", [128, 64], stack=s2); b_tabB = Buf()
            tmpB = [sb("tmpB2_%d" % i, [128, D], stack=s2) for i in range(3)] + [sb("tmpB32", [128, 16], stack=s2)]
            b_tmpB = [Buf() for _ in range(4)]
            krotB = sb("krotB2", [128, D], BF16, stack=s2); b_krotB = Buf()
            load_w_bf16(W2, b_W2, 3072, 1024, stage, bstage, 0)
            load_w_bf16(W2, b_W2, 4128, 1024, stage, bstage, 1024)
            load_w_bf16(W2, b_W2, 7200, 1024, stage, bstage, 2048)
            zsb = sb("zsb", [128, 2048], BF16, stack=s2); b_zsb = Buf()
            for i in range(16):
                v = OWN0 + i
                make_HT(v, False)
                P.dma(tabB[:], ROPE[v * 128:(v + 1) * 128, :], [], [b_tabB], b_tabB)
                for hf in range(2):
                    for k in range(8):
                        P.op("pe", lambda e, k=k, hf=hf: e.matmul(psum[6 + hf][:, :], lhsT=HT[:, k, 2:130], rhs=W2[:, k, 1024 + hf * 512:1024 + (hf + 1) * 512], start=(k == 0), stop=(k == 7)),
                             [b_HT, b_W2], [PB[6 + hf]], sig=(k == 7))
                qk_norm_rope([psum[6], psum[7]], [PB[6], PB[7]], qnw, tabB, b_tabB, krotB, b_krotB, tmpB, b_tmpB)
                for h in range(H):
                    bank = 6 + h // 4
                    P.op("pe", lambda e, h=h, bank=bank: e.matmul(psum[bank][:, (h % 4) * 128:(h % 4 + 1) * 128], lhsT=krotB[:, h * 128:(h + 1) * 128], rhs=identb, start=True, stop=True),
                         [b_krotB, b_cbf], [PB[bank]])
                for hf in range(2):
                    P.op("act", lambda e, hf=hf, i=i: e.activation(out=QT[:, hf * 4:(hf + 1) * 4, i * 128:(i + 1) * 128], in_=psum[6 + hf][:, :].rearrange("p (a b) -> p a b", a=4), func=AF.Copy), [PB[6 + hf]], [b_QT])
                for (c0, dstt, bd, off, fn) in ((0, zsb, b_zsb, 0, AF.Silu), (2048, zsb, b_zsb, 1024, AF.Silu)):
                    for hf in range(2):
                        bank = 4 + hf
                        for k in range(8):
                            P.op("pe", lambda e, k=k, hf=hf, c0=c0, bank=bank: e.matmul(psum[bank][:, :], lhsT=HT[:, k, 2:130], rhs=W2[:, k, c0 + hf * 512:c0 + (hf + 1) * 512], start=(k == 0), stop=(k == 7)),
                                 [b_HT, b_W2], [PB[bank]], sig=(k == 7))
                        P.op("act", lambda e, hf=hf, bank=bank, dstt=dstt, off=off, fn=fn: e.activation(out=dstt[:, off + hf * 512:off + (hf + 1) * 512], in_=psum[bank][:, :], func=fn), [PB[bank]], [bd])
                P.dma(ZS[i, :, :], zsb[:], [b_zsb], [b_ZS], b_ZS)

        P.barrier()
        with ExitStack() as s3:
            KTh = sb("KTh", [128, NKV * 128], BF16, stack=s3); b_KTh = Buf()
            Vph = sb("Vph", [128, NKV, 129], BF16, stack=s3); b_Vph = Buf()
            Eb = [sb("Eb%d" % i, [128, 1024], BF16, stack=s3) for i in range(2)]; b_Eb = [Buf(), Buf()]
            zb = sb("zb", [128, 4, 128], BF16, stack=s3); b_zb = Buf()
            fo = sb("fo", [128, 8, 128], stack=s3); b_fo = Buf()
            fs = sb("fs", [128, 8], stack=s3); b_fs = Buf()
            yb = sb("yb", [128, 128], BF16, stack=s3); b_yb = Buf()
            for h in range(H):
                P.dma(KTh[:], KTS[h, :, :], [b_KTS], [b_KTh], b_KTh)
                P.dma(Vph[:], VPS[h, :, :, :], [b_VPS], [b_Vph], b_Vph)
                for qt in range(4):
                    P.dma(zb[:], ZS[qt * 4:(qt + 1) * 4, :, 1024 + h * 128:1024 + (h + 1) * 128].rearrange("a p c -> p a c"), [b_ZS], [b_zb], b_zb)
                    def emit_qk(kb):
                        sbk = (kb % 2) * 2
                        for m in range(2):
                            P.op("pe", lambda e, m=m, kb=kb, qt=qt, h=h, sbk=sbk: e.matmul(psum[sbk + m][:, :], lhsT=KTh[m * 64:(m + 1) * 64, kb * 128:(kb + 1) * 128],
                                                                                   rhs=QT[m * 64:(m + 1) * 64, h, qt * 512:(qt + 1) * 512], start=True, stop=True),
                                 [b_KTh, b_QT], [PB[sbk + m]])
                    emit_qk(0)
                    for kb in range(NKV):
                        sbk = (kb % 2) * 2
                        if kb + 1 < NKV:
                            emit_qk(kb + 1)
                        for m in range(2):
                            P.op("act", lambda e, m=m, kb=kb, sbk=sbk: e.activation(out=Eb[kb % 2][:, m * 512:(m + 1) * 512], in_=psum[sbk + m][:, :], func=AF.Exp, scale=0.125, bias=lamt[:, 1:2]),
                                 [PB[sbk + m], b_lam], [b_Eb[kb % 2]])
                        for qb in range(4):
                            for m in range(2):
                                P.op("pe", lambda e, m=m, kb=kb, qb=qb: e.matmul(psum[4 + qb][:, m * 129:(m + 1) * 129], lhsT=Eb[kb % 2][:, m * 512 + qb * 128:m * 512 + (qb + 1) * 128],
                                                                                 rhs=Vph[:, kb, :], start=(kb == 0 and m == 0), stop=(kb == NKV - 1 and m == 1)),
                                     [b_Eb[kb % 2], b_Vph], [PB[4 + qb]], sig=(qb == 3 and m == 1))
                    for qb in range(4):
                        blk = qt * 4 + qb
                        pso = psum[4 + qb]
                        P.op("dve", lambda e, pso=pso: e.tensor_copy(out=fs[:, 0:1], in_=pso[:, 128:129]), [PB[4 + qb]], [b_fs])
                        P.op("dve", lambda e, pso=pso: e.tensor_copy(out=fs[:, 1:2], in_=pso[:, 257:258]), [PB[4 + qb]], [b_fs])
                        P.op("dve", lambda e: e.reciprocal(out=fs[:, 2:4], in_=fs[:, 0:2]), [b_fs], [b_fs])
                        P.op("dve", lambda e: e.tensor_tensor(out=fs[:, 3:4], in0=fs[:, 3:4], in1=lamt[:, 0:1], op=ALU.mult), [b_fs, b_lam], [b_fs])
                        P.op("dve", lambda e, pso=pso: e.tensor_scalar_mul(out=fo[:, 0, :], in0=pso[:, 129:257], scalar1=fs[:, 3:4]), [PB[4 + qb], b_fs], [b_fo])
                        P.op("dve", lambda e, pso=pso: e.scalar_tensor_tensor(out=fo[:, 1, :], in0=pso[:, 0:128], scalar=fs[:, 2:3], in1=fo[:, 0, :], op0=ALU.mult, op1=ALU.add), [PB[4 + qb], b_fs, b_fo], [b_fo])
                        P.op("act", lambda e: e.activation(out=fo[:, 2, :], in_=fo[:, 1, :], func=AF.Square, accum_out=fs[:, 4:5]), [b_fo, b_fs], [b_fo, b_fs])
                        P.op("act", lambda e: e.activation(out=fs[:, 5:6], in_=fs[:, 4:5], func=AF.Ln, scale=1.0 / 128, bias=epsc[:, 0:1]), [b_fs, b_eps], [b_fs])
                        P.op("act", lambda e: e.activation(out=fs[:, 5:6], in_=fs[:, 5:6], func=AF.Exp, scale=-0.5), [b_fs], [b_fs])
                        P.op("dve", lambda e: e.scalar_tensor_tensor(out=fo[:, 3, :], in0=fo[:, 1, :], scalar=fs[:, 5:6], in1=dnw, op0=ALU.mult, op1=ALU.mult), [b_fo, b_fs, b_vecs], [b_fo])
                        P.op("dve", lambda e, qb=qb: e.scalar_tensor_tensor(out=yb[:], in0=fo[:, 3, :], scalar=(1.0 - LAMBDA_INIT), in1=zb[:, qb, :], op0=ALU.mult, op1=ALU.mult), [b_fo, b_zb], [b_yb])
                        P.op("pe", lambda e: e.matmul(psum[0][:, 0:128], lhsT=yb[:], rhs=identb, start=True, stop=True), [b_yb, b_cbf], [PB[0]])
                        P.op("act", lambda e, h=h, blk=blk: e.activation(out=YbT[:, h, blk * 128:(blk + 1) * 128], in_=psum[0][:, 0:128], func=AF.Copy), [PB[0]], [b_YbT])

        s_qt.close()
        P.barrier()
        with ExitStack() as s4:
            Wo = [sb("Wo%d" % i, [128, 8, D], BF16, stack=s4) for i in range(3)]; b_Wo = [Buf() for _ in range(3)]
            stage = [junk, xt]
            bstage = [b_junk, b_xt]
            YaT = sb("YaT", [128, H, 128], BF16, stack=s4); b_YaT = Buf()
            Wg = sb("Wg", [128, 8, 2048], BF16, stack=s4); b_Wg = Buf()
            load_w_bf16(Wg, b_Wg, 8224, 2048, stage, bstage, 0)
            ii = 0
            for wi, src in enumerate((WOA, WOB, WOUT)):
                for k in range(8):
                    s_, bs_ = stage[ii % 2], bstage[ii % 2]
                    P.dma(s_[:], src[k * 128:(k + 1) * 128, :], [], [bs_], bs_)
                    P.op(["dve", "pool"][ii % 2], lambda e, s_=s_, wi=wi, k=k: e.tensor_copy(out=Wo[wi][:, k, :], in_=s_[:]), [bs_], [b_Wo[wi]])
                    ii += 1
            o1 = sb("o1", [128, D], stack=s4); b_o1 = Buf()
            o2 = sb("o2", [128, D], stack=s4); b_o2 = Buf()
            zs4 = sb("zs4", [128, 1024], BF16, stack=s4); b_zs4 = Buf()
            gs4 = sb("gs4", [128, 2048], BF16, stack=s4); b_gs4 = Buf()
            ya = sb("ya", [128, D], BF16, stack=s4); b_ya = Buf()
            mg = sb("mg", [128, D], stack=s4); b_mg = Buf()
            mgb = sb("mgb", [128, D], BF16, stack=s4); b_mgb = Buf()
            mT = sb("mT", [128, 8, 128], BF16, stack=s4); b_mT = Buf()
            res = sb("res", [128, D], stack=s4); b_res = Buf()
            b_OUT = Buf()
            out_toks = []
            for i in range(16):
                P.dma(o1[:], OS[0, i, :, :], [b_OS], [b_o1], b_o1)
                P.dma(o2[:], OS[1, 15 - i, :, :], [b_OS], [b_o2], b_o2)
                P.dma(zs4[:], ZS[i, :, 0:1024], [b_ZS], [b_zs4], b_zs4)
                make_HT(OWN0 + i, False)
                for g4 in range(4):
                    bank = 6 + g4 % 2
                    for k in range(8):
                        P.op("pe", lambda e, k=k, g4=g4, bank=bank: e.matmul(psum[bank][:, :], lhsT=HT[:, k, 2:130], rhs=Wg[:, k, g4 * 512:(g4 + 1) * 512], start=(k == 0), stop=(k == 7)), [b_HT, b_Wg], [PB[bank]], sig=(k == 7))
                    P.op("act", lambda e, g4=g4, bank=bank: e.activation(out=gs4[:, g4 * 512:(g4 + 1) * 512], in_=psum[bank][:, :], func=AF.Sigmoid), [PB[bank]], [b_gs4])
                for hf in range(2):
                    P.op("pe", lambda e, hf=hf: e.matmul(psum[hf][:, :], lhsT=Jm, rhs=o2[:, hf * 512:(hf + 1) * 512], start=True, stop=True), [b_cst, b_o2], [PB[hf]])
                    P.op("dve", lambda e, hf=hf: e.tensor_tensor(out=o1[:, hf * 512:(hf + 1) * 512], in0=o1[:, hf * 512:(hf + 1) * 512], in1=psum[hf][:, :], op=ALU.add), [PB[hf], b_o1], [b_o1])
                P.op("act", lambda e: e.activation(out=junk[:], in_=o1[:], func=AF.Square), [b_o1], [b_junk])
                P.op("dve", lambda e: e.tensor_reduce(out=st[:, 0:8], in_=junk[:].rearrange("p (g d) -> p g d", d=128), axis=AX.X, op=ALU.add), [b_junk], [b_st])
                P.op("act", lambda e: e.activation(out=st[:, 0:8], in_=st[:, 0:8], func=AF.Ln, scale=1.0 / 128, bias=epsc[:, 0:1]), [b_st, b_eps], [b_st])
                P.op("act", lambda e: e.activation(out=st[:, 0:8], in_=st[:, 0:8], func=AF.Exp, scale=-0.5), [b_st], [b_st])
                P.op("dve", lambda e: e.tensor_tensor(out=junk[:].rearrange("p (g d) -> p g d", d=128), in0=o1[:].rearrange("p (g d) -> p g d", d=128), in1=st[:, 0:8].unsqueeze(2).to_broadcast([128, 8, 128]), op=ALU.mult), [b_o1, b_st], [b_junk])
                P.op("pool", lambda e: e.tensor_tensor(out=junk[:].rearrange("p (g d) -> p g d", d=128), in0=junk[:].rearrange("p (g d) -> p g d", d=128), in1=gnw.unsqueeze(1).to_broadcast([128, 8, 128]), op=ALU.mult), [b_junk, b_vecs], [b_junk])
                P.op("dve", lambda e: e.tensor_tensor(out=ya[:], in0=junk[:], in1=zs4[:], op=ALU.mult), [b_junk, b_zs4], [b_ya])
                for h in range(H):
                    bank = h // 4
                    P.op("pe", lambda e, h=h, bank=bank: e.matmul(psum[bank][:, (h % 4) * 128:(h % 4 + 1) * 128], lhsT=ya[:, h * 128:(h + 1) * 128], rhs=identb, start=True, stop=True), [b_ya, b_cbf], [PB[bank]])
                for hf in range(2):
                    P.op("act", lambda e, hf=hf, i=i: e.activation(out=YaT[:, hf * 4:(hf + 1) * 4, :], in_=psum[hf][:, :].rearrange("p (a b) -> p a b", a=4), func=AF.Copy), [PB[hf]], [b_YaT])
                for (wi, Y, bY, pb0, c0_) in ((0, YaT, b_YaT, 2, 0), (1, YbT, b_YbT, 4, i * 128)):
                    for hf in range(2):
                        for k in range(8):
                            P.op("pe", lambda e, wi=wi, Y=Y, hf=hf, k=k, pb0=pb0, c0_=c0_: e.matmul(psum[pb0 + hf][:, :], lhsT=Y[:, k, c0_:c0_ + 128], rhs=Wo[wi][:, k, hf * 512:(hf + 1) * 512], start=(k == 0), stop=(k == 7)),
                                 [bY, b_Wo[wi]], [PB[pb0 + hf]], sig=(k == 7))
                for hf in range(2):
                    P.op("dve", lambda e, hf=hf: e.tensor_tensor(out=mg[:, hf * 512:(hf + 1) * 512], in0=psum[2 + hf][:, :], in1=gs4[:, hf * 512:(hf + 1) * 512], op=ALU.mult), [PB[2 + hf], b_gs4], [b_mg])
                    P.op("dve", lambda e, hf=hf: e.tensor_tensor(out=junk[:, hf * 512:(hf + 1) * 512], in0=psum[4 + hf][:, :], in1=gs4[:, 1024 + hf * 512:1024 + (hf + 1) * 512], op=ALU.mult), [PB[4 + hf], b_gs4], [b_junk])
                P.op("pool", lambda e: e.tensor_tensor(out=mgb[:], in0=mg[:], in1=junk[:], op=ALU.add), [b_mg, b_junk], [b_mgb])
                for k in range(8):
                    bank = 6 + k // 4
                    P.op("pe", lambda e, k=k, bank=bank: e.matmul(psum[bank][:, (k % 4) * 128:(k % 4 + 1) * 128], lhsT=mgb[:, k * 128:(k + 1) * 128], rhs=identb, start=True, stop=True), [b_mgb, b_cbf], [PB[bank]])
                for hf in range(2):
                    P.op("act", lambda e, hf=hf: e.activation(out=mT[:, hf * 4:(hf + 1) * 4, :], in_=psum[6 + hf][:, :].rearrange("p (a b) -> p a b", a=4), func=AF.Copy), [PB[6 + hf]], [b_mT])
                for hf in range(2):
                    for k in range(8):
                        P.op("pe", lambda e, hf=hf, k=k: e.matmul(psum[hf][:, :], lhsT=mT[:, k, :], rhs=Wo[2][:, k, hf * 512:(hf + 1) * 512], start=(k == 0), stop=(k == 7)), [b_mT, b_Wo[2]], [PB[hf]], sig=(k == 7))
                    P.op("dve", lambda e, hf=hf: e.tensor_tensor(out=res[:, hf * 512:(hf + 1) * 512], in0=psum[hf][:, :], in1=mod[:, 2, hf * 512:(hf + 1) * 512], op=ALU.mult), [PB[hf], b_mod], [b_res])
                P.op("pool", lambda e: e.tensor_tensor(out=res[:], in0=res[:], in1=xt[:], op=ALU.add), [b_res, b_xt], [b_res])
                tok = P.dma(OUT[i * 128:(i + 1) * 128, :], res[:], [b_res], [b_OUT], b_OUT)
            P.final_wait("sp", [tok])

        print('total ops', P.gidx, {n: len(P.q[n]) for n in P.names})
        P.emit()
    return nc


def _consts():
    i = np.arange(128)
    ident = np.eye(128, dtype=np.float32)
    J = ident[::-1].copy()
    ones = np.ones((128, 128), np.float32)
    Ltri = (i[:, None] <= i[None, :]).astype(np.float32)
    negMs = -(i[:, None] > i[None, :]).astype(np.float32)
    negMsT = -(i[None, :] > i[:, None]).astype(np.float32)
    MiT = (i[None, :] >= i[:, None]).astype(np.float32)
    SelM = np.zeros((128, 132), np.float32)
    SelM[i, i + 2] = 1
    SelH = np.zeros((128, 132), np.float32)
    SelH[0, 0] = 1
    SelH[1, 1] = 1
    SelH[2, 130] = 1
    SelH[3, 131] = 1
    blk = lambda b: ((i[:, None] // b) == (i[None, :] // b)).astype(np.float32)
    BD32 = blk(32)
    M64 = blk(64) - blk(32)
    M128 = 1.0 - blk(64)
    return np.concatenate([ident, J, ones, Ltri, negMs, negMsT, MiT, SelM, SelH, BD32, M64, M128], axis=1)


def _rope_table(pos):
    pos = np.asarray(pos)
    row = (pos // 64).astype(np.float32)
    col = (pos % 64).astype(np.float32)
    inv = (10000.0 ** (-np.arange(16, dtype=np.float32) / 16)).astype(np.float32)
    ar = row[:, None] * inv
    ac = col[:, None] * inv
    t = np.concatenate([np.cos(ar), np.sin(ar), np.cos(ac), np.sin(ac)], axis=1).astype(np.float32)
    t[pos < 0] = np.array([1.0] * 16 + [0.0] * 16 + [1.0] * 16 + [0.0] * 16, np.float32)
    return t


def _core_plan(r):
    fwd_first = r >= 2
    own = list(range(16 * r, 16 * r + 16))
    left = list(range(0, 16 * r))
    right = list(range(16 * r + 16, 64))

    def scan(forward, nslots, filler):
        rev = not forward
        vis = [("c", 0, rev, 1.0), ("c", 1, rev, 1.0)]
        if rev:
            vis = [("c", 1, rev, 1.0), ("c", 0, rev, 1.0)]
        pref = left if forward else right[::-1]
        fill = [("l", b, rev, 0.0) for b in filler]
        vis += fill[:nslots - len(pref)] + [("l", b, rev, 1.0) for b in pref]
        o = own if forward else own[::-1]
        vis += [("l", b, rev, 1.0) for b in o]
        return vis

    if fwd_first:
        filler = right[::-1]
        v1 = scan(True, 48, filler)
        v2 = scan(False, 16, [own[0]] * 16)
    else:
        filler = left
        v1 = scan(False, 48, filler)
        v2 = scan(True, 16, [own[0]] * 16)
    assert len(v1) == 66 and len(v2) == 34
    return v1 + v2, fwd_first


_NC_CACHE = {}


def kernel(x, c, ctx, c_ctx, w_ada, b_ada, w_in, conv_w, a_log, dt_bias, gdn_norm_w,
           q_norm_w, k_norm_w, lambda_q1, lambda_k1, lambda_q2, lambda_k2, diff_norm_w,
           w_oa, w_ob, w_out):
    f = lambda a: np.ascontiguousarray(np.asarray(a, dtype=np.float32))
    x, c, ctx, c_ctx = f(x), f(c), f(ctx), f(c_ctx)
    w_ada, b_ada, w_in, conv_w = f(w_ada)[0], f(b_ada)[0], f(w_in)[0], f(conv_w)[0]
    a_log, dt_bias = f(a_log)[0], f(dt_bias)[0]
    vec = lambda a: f(a)[0]
    if "nc" not in _NC_CACHE:
        _NC_CACHE["nc"] = build_nc()
    nc = _NC_CACHE["nc"]
    cst = _consts()
    vecs = np.zeros((768,), np.float32)
    vecs[0:128] = vec(gdn_norm_w)
    vecs[128:192] = vec(q_norm_w)
    vecs[192:256] = vec(k_norm_w)
    vecs[256:320] = vec(lambda_q1)
    vecs[320:384] = vec(lambda_k1)
    vecs[384:448] = vec(lambda_q2)
    vecs[448:512] = vec(lambda_k2)
    vecs[512:640] = vec(diff_norm_w)
    VECS = np.ascontiguousarray(np.broadcast_to(vecs, (128, 768)))
    BADA = np.ascontiguousarray(np.broadcast_to(b_ada, (128, 3 * D)))
    in_maps = []
    plans = []
    for core in range(8):
        b, r = divmod(core, 4)
        plan, fwd_first = _core_plan(r)
        plans.append((plan, fwd_first))
        XV = np.zeros((NV * 128, D), np.float32)
        XH = np.zeros((NV * 4, D), np.float32)
        HM = np.zeros((4, NV), np.float32)
        GM = np.zeros((128, NV), np.float32)
        ROPE = np.zeros((NKV * 128, 64), np.float32)
        for v, (seq, blk, rev, gmask) in enumerate(plan):
            src = ctx[b] if seq == "c" else x[b]
            n = src.shape[0]
            idx = np.arange(blk * 128, blk * 128 + 128)
            halo = np.array([blk * 128 - 2, blk * 128 - 1, blk * 128 + 128, blk * 128 + 129])
            if rev:
                idx = idx[::-1]
                halo = halo[::-1]
            XV[v * 128:(v + 1) * 128] = src[idx]
            valid = (halo >= 0) & (halo < n)
            XH[v * 4:(v + 1) * 4][valid] = src[halo[valid]]
            HM[:, v] = valid.astype(np.float32)
            GM[:, v] = gmask
            if v < NKV:
                pos = idx if seq == "l" else -np.ones(128, np.int64)
                ROPE[v * 128:(v + 1) * 128] = _rope_table(pos)
        d1 = 0 if fwd_first else 1
        dirs = (d1, 1 - d1)
        WAB = np.stack([np.concatenate([w_in[:, 4096 + d * 8:4096 + d * 8 + 8], w_in[:, 4112 + d * 8:4112 + d * 8 + 8]], axis=1) for d in dirs])
        cwt = np.ascontiguousarray(conv_w.T)
        CW = np.stack([cwt if d == 0 else np.ascontiguousarray(cwt[:, ::-1]) for d in dirs])
        ALG = np.stack([np.broadcast_to(a_log[d], (128, 8)) for d in dirs])
        DTB = np.stack([np.broadcast_to(dt_bias[d], (128, 8)) for d in dirs])
        cv = np.stack([c[b], c_ctx])
        CT = np.ascontiguousarray(cv.reshape(2, 8, 128).transpose(2, 0, 1).reshape(128, 16))
        in_maps.append({
            "XV": XV, "XH": XH, "HM": HM, "GM": GM, "ROPE": ROPE, "CT": CT, "WADA": w_ada, "BADA": BADA,
            "WIN": w_in, "WAB": np.ascontiguousarray(WAB), "CW": np.ascontiguousarray(CW),
            "ALG": np.ascontiguousarray(ALG), "DTB": np.ascontiguousarray(DTB), "VECS": VECS, "CST": cst,
            "WOA": f(w_oa)[0], "WOB": f(w_ob)[0], "WOUT": f(w_out)[0],
        })
    if _NC_CACHE.get("maps_only"):
        return in_maps, plans
    res = run_bass_kernel_spmd(nc, in_maps, core_ids=list(range(8)))
    out = np.zeros((2, L, D), np.float32)
    for core in range(8):
        b, r = divmod(core, 4)
        plan, fwd_first = plans[core]
        o = np.asarray(res.results[core]["OUT"])
        for i in range(16):
            seq, blk, rev, _ = plan[OWN0 + i]
            idx = np.arange(blk * 128, blk * 128 + 128)
            if rev:
                idx = idx[::-1]
            out[b, idx] = o[i * 128:(i + 1) * 128]
    return out
```
